# Optimizing a Trainium2 kernel written in Bass

```python
import math
import jax, jax.numpy as jnp
from jax import lax
import numpy as np

D_MODEL = 1024
BATCH = 8
SEQ = 2048
DEPTH = 2

GRID_W = 64
NORM_EPS = 1e-6

NA_HEADS = 8
NA_HEAD_DIM = 64
NA_WIDTH = NA_HEADS * NA_HEAD_DIM
NA_KH_MAX = 8
NA_KW = 16

SSD_D_INNER = D_MODEL
SSD_HEAD_DIM = 64
SSD_HEADS = SSD_D_INNER // SSD_HEAD_DIM
SSD_GROUPS = 4
SSD_STATE = 128
SSD_CONV = 4
SSD_CHUNK = 128
SSD_CONV_DIM = SSD_D_INNER + 2 * SSD_GROUPS * SSD_STATE
DT_MIN = 0.001
DT_MAX = 0.1

EVEN_SPLITS = [NA_WIDTH, 2 * NA_WIDTH, 3 * NA_WIDTH,
               3 * NA_WIDTH + SSD_D_INNER,
               3 * NA_WIDTH + SSD_D_INNER + SSD_CONV_DIM]
EVEN_IN_WIDTH = 3 * NA_WIDTH + SSD_D_INNER + SSD_CONV_DIM + 2 * SSD_HEADS
EVEN_MIX_WIDTH = NA_WIDTH + SSD_D_INNER

GQA_HEADS = 16
GQA_KV_HEADS = 4
GQA_HEAD_DIM = 64
GQA_Q_WIDTH = GQA_HEADS * GQA_HEAD_DIM
GQA_KV_WIDTH = GQA_KV_HEADS * GQA_HEAD_DIM
ODD_IN_WIDTH = GQA_Q_WIDTH + 2 * GQA_KV_WIDTH
ROPE_THETA = 10000.0
Q_BLOCK = 128

FFN_HIDDEN = -(-8 * D_MODEL // (3 * 256)) * 256

N_EVEN = (DEPTH + 1) // 2
N_ODD = DEPTH // 2

kernel_name = "hybrid_natten_ssd_axial_gqa_encoder"


def rms_norm(x, g):
    xf = x.astype(jnp.float32)
    y = xf * lax.rsqrt(jnp.mean(xf * xf, axis=-1, keepdims=True) + NORM_EPS)
    return (y * g.astype(jnp.float32)).astype(x.dtype)


def neighbourhood_attention(q, k, v, rpb):
    B, S, H, Dh = q.shape
    rows = S // GRID_W
    kh = min(NA_KH_MAX, rows)
    qg = q.reshape(B, rows, GRID_W, H, Dh)
    kg = k.reshape(B, rows, GRID_W, H, Dh)
    vg = v.reshape(B, rows, GRID_W, H, Dh)
    r = jnp.arange(rows)
    row_start = jnp.clip(r - kh // 2, 0, rows - kh)
    key_rows = row_start[:, None] + jnp.arange(kh)[None, :]
    k_blk = kg[:, key_rows]
    v_blk = vg[:, key_rows]
    c = jnp.arange(GRID_W)
    col_start = jnp.clip(c - NA_KW // 2, 0, GRID_W - NA_KW)
    in_win = (c[None, :] >= col_start[:, None]) & (c[None, :] < col_start[:, None] + NA_KW)
    dy = key_rows - r[:, None] + (NA_KH_MAX - 1)
    dx = jnp.clip(c[None, :] - c[:, None], -(NA_KW - 1), NA_KW - 1) + (NA_KW - 1)
    bias = rpb.astype(jnp.float32)[:, dy[:, None, :, None], dx[None, :, None, :]]
    scale = Dh ** -0.5
    s = jnp.einsum('brchd,brjkhd->bhrcjk', qg, k_blk).astype(jnp.float32) * scale + bias[None]
    s = jnp.where(in_win[:, None, :], s, -jnp.inf)
    p = jax.nn.softmax(s.reshape(B, H, rows, GRID_W, kh * GRID_W), axis=-1)
    p = p.reshape(B, H, rows, GRID_W, kh, GRID_W).astype(v.dtype)
    o = jnp.einsum('bhrcjk,brjkhd->brchd', p, v_blk)
    return o.reshape(B, S, H * Dh)


def centred_depthwise_conv(x, w, b):
    K, C = w.shape
    left = K // 2
    xp = jnp.pad(x, ((0, 0), (left, K - 1 - left), (0, 0)))
    y = lax.conv_general_dilated(xp, w[:, None, :].astype(x.dtype), window_strides=(1,), padding='VALID',
                                 dimension_numbers=('NWC', 'WIO', 'NWC'), feature_group_count=C)
    return y + b.astype(x.dtype)


def ssd_scan(x, dt, A, Bm, Cm):
    out_dtype = x.dtype
    Bsz, S, H, P = x.shape
    G, N = Bm.shape[2], Bm.shape[3]
    R = H // G
    L = SSD_CHUNK
    nc = S // L
    f32 = jnp.float32
    x = x.astype(f32); dt = dt.astype(f32)
    xc = (x * dt[..., None]).reshape(Bsz, nc, L, G, R, P)
    Bc = Bm.astype(f32).reshape(Bsz, nc, L, G, N)
    Cc = Cm.astype(f32).reshape(Bsz, nc, L, G, N)
    a = (dt * A.astype(f32)).reshape(Bsz, nc, L, G, R)
    a_cum = jnp.moveaxis(jnp.cumsum(a, axis=2), 2, -1)
    seg = a_cum[..., :, None] - a_cum[..., None, :]
    lower = jnp.tril(jnp.ones((L, L), dtype=bool))
    decay = jnp.exp(jnp.where(lower, seg, -jnp.inf))
    cb = jnp.einsum('bclgn,bcsgn->bcgls', Cc, Bc)
    y_diag = jnp.einsum('bcgls,bcgrls,bcsgrp->bclgrp', cb, decay, xc)
    decay_to_end = jnp.exp(a_cum[..., -1:] - a_cum)
    states = jnp.einsum('bcsgn,bcgrs,bcsgrp->bcgrpn', Bc, decay_to_end, xc)
    chunk_decay = jnp.exp(a_cum[..., -1])

    def step(h, inp):
        s_c, d_c = inp
        return h * d_c[..., None, None] + s_c, h

    h0 = jnp.zeros((Bsz, G, R, P, N), f32)
    _, h_prev = lax.scan(step, h0, (jnp.moveaxis(states, 1, 0), jnp.moveaxis(chunk_decay, 1, 0)))
    h_prev = jnp.moveaxis(h_prev, 0, 1)
    y_off = jnp.einsum('bclgn,bcgrpn,bcgrl->bclgrp', Cc, h_prev, jnp.exp(a_cum))
    return (y_diag + y_off).reshape(Bsz, S, H, P).astype(out_dtype)


def gated_group_rms_norm(y, z, g):
    Bsz, S, Dn = y.shape
    h = (y * jax.nn.silu(z)).astype(jnp.float32).reshape(Bsz, S, SSD_GROUPS, Dn // SSD_GROUPS)
    h = h * lax.rsqrt(jnp.mean(h * h, axis=-1, keepdims=True) + NORM_EPS)
    return (h.reshape(Bsz, S, Dn) * g.astype(jnp.float32)).astype(y.dtype)


def even_mixer(x, mix_norm, w_in, na_q_norm, na_k_norm, na_rel_bias, conv_w, conv_b,
               dt_bias, A_log, D_skip, out_norm, w_out):
    Bsz, S, _ = x.shape
    proj = rms_norm(x, mix_norm) @ w_in
    q, k, v, z, xbc, dt_raw = jnp.split(proj, EVEN_SPLITS, axis=-1)
    q = rms_norm(q.reshape(Bsz, S, NA_HEADS, NA_HEAD_DIM), na_q_norm)
    k = rms_norm(k.reshape(Bsz, S, NA_HEADS, NA_HEAD_DIM), na_k_norm)
    v = v.reshape(Bsz, S, NA_HEADS, NA_HEAD_DIM)
    na_out = neighbourhood_attention(q, k, v, na_rel_bias)
    xbc = jax.nn.silu(centred_depthwise_conv(xbc, conv_w, conv_b))
    xs, bm, cm = jnp.split(xbc, [SSD_D_INNER, SSD_D_INNER + SSD_GROUPS * SSD_STATE], axis=-1)
    xs = xs.reshape(Bsz, S, SSD_HEADS, SSD_HEAD_DIM)
    bm = bm.reshape(Bsz, S, SSD_GROUPS, SSD_STATE)
    cm = cm.reshape(Bsz, S, SSD_GROUPS, SSD_STATE)
    dt = jax.nn.softplus(dt_raw.astype(jnp.float32).reshape(Bsz, S, 2, SSD_HEADS)
                         + dt_bias.astype(jnp.float32))
    A = -jnp.exp(A_log.astype(jnp.float32))
    y_fwd = ssd_scan(xs, dt[:, :, 0], A[0], bm, cm)
    y_bwd = jnp.flip(ssd_scan(jnp.flip(xs, 1), jnp.flip(dt[:, :, 1], 1), A[1],
                              jnp.flip(bm, 1), jnp.flip(cm, 1)), 1)
    y = y_fwd + y_bwd + D_skip.astype(xs.dtype)[:, None] * xs
    ssd_out = gated_group_rms_norm(y.reshape(Bsz, S, SSD_D_INNER), z, out_norm)
    return jnp.concatenate([na_out, ssd_out], axis=-1) @ w_out


def axial_rope_tables(S):
    t = jnp.arange(S)
    row = (t // GRID_W).astype(jnp.float32)
    col = (t % GRID_W).astype(jnp.float32)
    axis_dims = GQA_HEAD_DIM // 2
    freqs = ROPE_THETA ** (-jnp.arange(0, axis_dims, 2, dtype=jnp.float32) / axis_dims)
    ang = jnp.concatenate([row[:, None] * freqs, col[:, None] * freqs], axis=-1)
    return jnp.cos(ang), jnp.sin(ang)


def apply_rope(x, cos, sin):
    xf = x.astype(jnp.float32).reshape(*x.shape[:-1], x.shape[-1] // 2, 2)
    x0, x1 = xf[..., 0], xf[..., 1]
    c = cos[None, :, None, :]
    s = sin[None, :, None, :]
    out = jnp.stack([x0 * c - x1 * s, x0 * s + x1 * c], axis=-1)
    return out.reshape(x.shape).astype(x.dtype)


def gqa_block_attention(q, k, v):
    Bsz, S, H, Dh = q.shape
    KV = k.shape[2]
    rep = H // KV
    nb = S // Q_BLOCK
    qb = jnp.moveaxis(q.reshape(Bsz, nb, Q_BLOCK, KV, rep, Dh), 1, 0)
    scale = Dh ** -0.5

    def one_block(q_blk):
        s = jnp.einsum('bqgrd,bkgd->bgrqk', q_blk, k).astype(jnp.float32) * scale
        p = jax.nn.softmax(s, axis=-1).astype(v.dtype)
        return jnp.einsum('bgrqk,bkgd->bqgrd', p, v)

    o = lax.map(one_block, qb)
    return jnp.moveaxis(o, 0, 1).reshape(Bsz, S, H * Dh)


def odd_mixer(x, mix_norm, w_qkv, q_norm, k_norm, w_out):
    Bsz, S, _ = x.shape
    proj = rms_norm(x, mix_norm) @ w_qkv
    q, k, v = jnp.split(proj, [GQA_Q_WIDTH, GQA_Q_WIDTH + GQA_KV_WIDTH], axis=-1)
    q = rms_norm(q.reshape(Bsz, S, GQA_HEADS, GQA_HEAD_DIM), q_norm)
    k = rms_norm(k.reshape(Bsz, S, GQA_KV_HEADS, GQA_HEAD_DIM), k_norm)
    v = v.reshape(Bsz, S, GQA_KV_HEADS, GQA_HEAD_DIM)
    cos, sin = axial_rope_tables(S)
    q = apply_rope(q, cos, sin)
    k = apply_rope(k, cos, sin)
    return gqa_block_attention(q, k, v) @ w_out


def swiglu_ffn(x, norm_g, w13, w2):
    g, u = jnp.split(rms_norm(x, norm_g) @ w13, 2, axis=-1)
    return (jax.nn.silu(g) * u) @ w2


def setup_inputs(seed: int = 0) -> dict:
    key = jax.random.key(seed)
    ks = jax.random.split(key, 24)
    f32 = jnp.float32

    def normal(k, shape, scale):
        return jax.random.normal(k, shape, f32) * scale

    def gain(k, shape):
        return 1.0 + 0.02 * jax.random.normal(k, shape, f32)

    dt0 = jnp.exp(jax.random.uniform(ks[10], (N_EVEN, 2, SSD_HEADS), f32)
                  * (math.log(DT_MAX) - math.log(DT_MIN)) + math.log(DT_MIN))
    dt_bias = dt0 + jnp.log(-jnp.expm1(-dt0))
    A_log = jnp.log(jax.random.uniform(ks[11], (N_EVEN, 2, SSD_HEADS), f32, 1.0, 16.0))
    return {
        "x": jax.random.normal(ks[0], (BATCH, SEQ, D_MODEL), f32),
        "even_mix_norm": gain(ks[1], (N_EVEN, D_MODEL)),
        "even_w_in": normal(ks[2], (N_EVEN, D_MODEL, EVEN_IN_WIDTH), D_MODEL ** -0.5),
        "na_q_norm": gain(ks[3], (N_EVEN, NA_HEAD_DIM)),
        "na_k_norm": gain(ks[4], (N_EVEN, NA_HEAD_DIM)),
        "na_rel_bias": normal(ks[5], (N_EVEN, NA_HEADS, 2 * NA_KH_MAX - 1, 2 * NA_KW - 1), 0.05),
        "ssd_conv_w": normal(ks[6], (N_EVEN, SSD_CONV, SSD_CONV_DIM), SSD_CONV ** -0.5),
        "ssd_conv_b": normal(ks[7], (N_EVEN, SSD_CONV_DIM), 0.01),
        "ssd_dt_bias": dt_bias,
        "ssd_A_log": A_log,
        "ssd_D": gain(ks[8], (N_EVEN, SSD_HEADS)),
        "ssd_out_norm": gain(ks[9], (N_EVEN, SSD_D_INNER)),
        "even_w_out": normal(ks[12], (N_EVEN, EVEN_MIX_WIDTH, D_MODEL), EVEN_MIX_WIDTH ** -0.5),
        "odd_mix_norm": gain(ks[13], (N_ODD, D_MODEL)),
        "odd_w_qkv": normal(ks[14], (N_ODD, D_MODEL, ODD_IN_WIDTH), D_MODEL ** -0.5),
        "gqa_q_norm": gain(ks[15], (N_ODD, GQA_HEAD_DIM)),
        "gqa_k_norm": gain(ks[16], (N_ODD, GQA_HEAD_DIM)),
        "odd_w_out": normal(ks[17], (N_ODD, GQA_Q_WIDTH, D_MODEL), GQA_Q_WIDTH ** -0.5),
        "ffn_norm": gain(ks[18], (DEPTH, D_MODEL)),
        "ffn_w13": normal(ks[19], (DEPTH, D_MODEL, 2 * FFN_HIDDEN), D_MODEL ** -0.5),
        "ffn_w2": normal(ks[20], (DEPTH, FFN_HIDDEN, D_MODEL), FFN_HIDDEN ** -0.5),
    }


def reference(x, even_mix_norm, even_w_in, na_q_norm, na_k_norm, na_rel_bias, ssd_conv_w, ssd_conv_b,
              ssd_dt_bias, ssd_A_log, ssd_D, ssd_out_norm, even_w_out, odd_mix_norm, odd_w_qkv,
              gqa_q_norm, gqa_k_norm, odd_w_out, ffn_norm, ffn_w13, ffn_w2):
    for layer in range(DEPTH):
        i = layer // 2
        if layer % 2 == 0:
            x = x + even_mixer(x, even_mix_norm[i], even_w_in[i], na_q_norm[i], na_k_norm[i],
                               na_rel_bias[i], ssd_conv_w[i], ssd_conv_b[i], ssd_dt_bias[i],
                               ssd_A_log[i], ssd_D[i], ssd_out_norm[i], even_w_out[i])
        else:
            x = x + odd_mixer(x, odd_mix_norm[i], odd_w_qkv[i], gqa_q_norm[i], gqa_k_norm[i], odd_w_out[i])
        x = x + swiglu_ffn(x, ffn_norm[layer], ffn_w13[layer], ffn_w2[layer])
    return x
```

```python
import numpy as np
from contextlib import ExitStack
import concourse.bass as bass
import concourse.mybir as mybir
from concourse.bass_utils import run_bass_kernel_spmd

F32 = mybir.dt.float32
BF16 = mybir.dt.bfloat16
AF = mybir.ActivationFunctionType
ALU = mybir.AluOpType
AX = mybir.AxisListType

S = 2048
D = 1024
NT = 16
FH = 2816
NHC = 22
EPS = 1e-6
NEG = -30000.0
N_DUMMY = 0


class Ctx:
    SAME_ENGINE_SYNC = ("act", "dve", "pool")

    def __init__(self, nc, es, n_dma_sems=32):
        self.nc = nc
        self.es = es
        self.eng = {"pe": nc.tensor, "act": nc.scalar, "dve": nc.vector, "pool": nc.gpsimd, "sp": nc.sync}
        self.sem = {}
        self.cnt = {}
        self.nsem = 0
        for e in self.eng:
            self._new_sem(e)
        self.dsem = [es.enter_context(nc.semaphore(f"dma{i}")) for i in range(n_dma_sems)]
        self.dcnt = [0] * n_dma_sems
        self.dnext = {"hw": 0, "sw": 0}
        self.dhalf = n_dma_sems // 2
        self.waited = {e: {} for e in self.eng}
        self.last_w = {}
        self.readers = {}
        self.pend = {e: ([], []) for e in self.eng}
        self.uid = 0

    def _new_sem(self, e):
        self.sem[e] = self.es.enter_context(self.nc.semaphore(f"s_{e}_{self.nsem}"))
        self.nsem += 1
        self.cnt[e] = 0

    def _wait(self, e, tok):
        sem, val, src = tok
        if src == e and e not in self.SAME_ENGINE_SYNC:
            return
        key = id(sem)
        if self.waited[e].get(key, 0) >= val:
            return
        self.waited[e][key] = val
        self.eng[e].wait_ge(sem, val)

    def _deps(self, e, reads, writes):
        for r in reads:
            t = self.last_w.get(r)
            if t is not None:
                self._wait(e, t)
        for w in writes:
            t = self.last_w.get(w)
            if t is not None:
                self._wait(e, t)
            for t in self.readers.get(w, ()):
                self._wait(e, t)

    def _commit(self, tok, reads, writes):
        for w in writes:
            self.last_w[w] = tok
            self.readers[w] = []
        for r in reads:
            self.readers.setdefault(r, []).append(tok)

    def op(self, e, ins_fn, reads=(), writes=(), signal=True):
        reads = list(reads)
        writes = list(writes)
        self._deps(e, reads, writes)
        ins = ins_fn()
        pr, pw = self.pend[e]
        if not signal:
            pr.extend(reads)
            pw.extend(writes)
            return ins
        if self.cnt[e] >= 30000:
            self._new_sem(e)
        self.cnt[e] += 1
        ins.then_inc(self.sem[e], 1)
        tok = (self.sem[e], self.cnt[e], e)
        self._commit(tok, reads + pr, writes + pw)
        self.pend[e] = ([], [])
        return ins

    def dma(self, q, out, in_, reads=(), writes=(), **kw):
        reads = list(reads)
        writes = list(writes)
        kind = "sw" if q == "pool" else "hw"
        i = self.dnext[kind] + (self.dhalf if kind == "sw" else 0)
        self.dnext[kind] = (self.dnext[kind] + 1) % self.dhalf
        skey = ("__dsem", i)
        self._deps(q, reads, writes + [skey])
        ins = self.eng[q].dma_start(out=out, in_=in_, **kw)
        self.dcnt[i] += 16
        ins.then_inc(self.dsem[i], 16)
        tok = (self.dsem[i], self.dcnt[i], "dma")
        self._commit(tok, reads, writes + [skey])
        return ins

    def finish(self, e="sp"):
        for i, s in enumerate(self.dsem):
            if self.dcnt[i]:
                self._wait(e, (s, self.dcnt[i], "dma"))
        for x in self.eng:
            if self.cnt[x] and (x != e or e in self.SAME_ENGINE_SYNC):
                self._wait(e, (self.sem[x], self.cnt[x], x))

    def barrier(self):
        for e in self.eng:
            assert not self.pend[e][0] and not self.pend[e][1], "pending unsignalled ops at barrier"
        for e in self.eng:
            self.finish(e)
        self.last_w.clear()
        self.readers.clear()

    def key(self, name):
        self.uid += 1
        return f"{name}#{self.uid}"


class Ring:
    def __init__(self, c, es, name, shape, dtype, n, psum=False):
        alloc = c.nc.psum_tensor if psum else c.nc.sbuf_tensor
        self.t = [es.enter_context(alloc(f"{name}{i}", shape, dtype)) for i in range(n)]
        self.k = [c.key(name) for _ in range(n)]
        self.i = -1
        self.n = n

    def next(self):
        self.i = (self.i + 1) % self.n
        return self.t[self.i], self.k[self.i]


def bcast_row(ap_1d, n, parts=128):
    return ap_1d.rearrange("(o n) -> o n", o=1).to_broadcast([parts, n])


def emit_norm_T(c, es, x_dram, g_dram, xnT, xnT_key, ident, name):
    nc = c.nc
    gb = es.enter_context(nc.sbuf_tensor(f"{name}_gb", [128, D], F32))
    kgb = c.key("gb")
    c.dma("sp", gb[:], bcast_row(g_dram, D), writes=[kgb])
    xt = Ring(c, es, f"{name}_xt", [128, D], F32, 4)
    sq = Ring(c, es, f"{name}_sq", [128, D], BF16, 3)
    xs = Ring(c, es, f"{name}_xs", [128, D], BF16, 3)
    st = Ring(c, es, f"{name}_st", [128, 2], F32, 4)
    pT = Ring(c, es, f"{name}_pT", [128, 8, 128], BF16, 2, psum=True)
    eps = es.enter_context(nc.sbuf_tensor(f"{name}_eps", [128, 1], F32))
    keps = c.key("eps")
    c.op("pool", lambda: nc.gpsimd.memset(eps[:], EPS), writes=[keps])
    def chain(i):
        x_t, kx = xt.next()
        c.dma("sp", x_t[:], x_dram[i * 128:(i + 1) * 128, :], writes=[kx])
        s_t, ks = sq.next()
        st_t, kst = st.next()
        c.op("act", lambda: nc.scalar.activation(out=s_t[:], in_=x_t[:], func=AF.Square, accum_out=st_t[:, 0:1]),
             reads=[kx], writes=[ks, kst])
        c.op("act", lambda: nc.scalar.activation(out=st_t[:, 1:2], in_=st_t[:, 0:1], func=AF.Sqrt,
                                                  scale=1.0 / D, bias=eps[:]),
             reads=[kst, keps], writes=[kst])
        c.op("dve", lambda: nc.vector.reciprocal(out=st_t[:, 1:2], in_=st_t[:, 1:2]), reads=[kst], writes=[kst])
        xs_t, kxs = xs.next()
        c.op("dve", lambda: nc.vector.scalar_tensor_tensor(out=xs_t[:], in0=x_t[:], scalar=st_t[:, 1:2], in1=gb[:],
                                                           op0=ALU.mult, op1=ALU.mult),
             reads=[kx, kst, kgb], writes=[kxs])
        return xs_t, kxs

    def tr(i, xs_t, kxs):
        p_t, kp = pT.next()
        for k in range(8):
            c.op("pe", lambda: nc.tensor.transpose(p_t[:, k, :], xs_t[:, k * 128:(k + 1) * 128], ident[:]),
                 reads=[kxs], writes=[kp], signal=(k == 7))
        c.op("act", lambda: nc.scalar.copy(out=xnT[:, :, i * 128:(i + 1) * 128], in_=p_t[:]),
             reads=[kp], writes=[(xnT_key, i)])

    cur = chain(0)
    for i in range(NT):
        nxt = chain(i + 1) if i + 1 < NT else None
        tr(i, *cur)
        cur = nxt


def emit_ffn(c, x_in, x_out, g_dram, w13, w2, ident, name):
    nc = c.nc
    with ExitStack() as es:
        E = es.enter_context
        xnT = E(nc.sbuf_tensor(f"{name}_xnT", [128, 8, S], BF16))
        kxnT = c.key("xnT")
        hT = E(nc.sbuf_tensor(f"{name}_hT", [128, NHC, S], BF16))
        khT = c.key("hT")
        W2 = E(nc.sbuf_tensor(f"{name}_W2", [128, NHC, D], BF16))
        kW2 = c.key("W2")
        with ExitStack() as es1:
            emit_norm_T(c, es1, x_in, g_dram, xnT, kxnT, ident, name)
            c.barrier()
        with ExitStack() as es2:
            xn_all = [(kxnT, i) for i in range(NT)]
            w13v = w13.rearrange("(k p) n -> p k n", p=128)
            wg = Ring(c, es2, f"{name}_wg", [128, 8, 128], BF16, 3)
            wu = Ring(c, es2, f"{name}_wu", [128, 8, 128], BF16, 3)
            pg = Ring(c, es2, f"{name}_pg", [128, 1024], F32, 2, psum=True)
            pu = Ring(c, es2, f"{name}_pu", [128, 1024], F32, 2, psum=True)
            sg = Ring(c, es2, f"{name}_sg", [128, 1024], F32, 2)
            for hc in range(NHC):
                wg_t, kwg = wg.next()
                wu_t, kwu = wu.next()
                c.dma("pool", wg_t[:], w13v[:, :, hc * 128:(hc + 1) * 128], writes=[kwg])
                c.dma("pool", wu_t[:], w13v[:, :, FH + hc * 128:FH + (hc + 1) * 128], writes=[kwu])
                c.dma("pool", W2[:, hc, :], w2[hc * 128:(hc + 1) * 128, :], writes=[(kW2, hc)])
                for th in range(2):
                    pg_t, kpg = pg.next()
                    pu_t, kpu = pu.next()
                    for (w_t, kw, p_t, kp) in ((wg_t, kwg, pg_t, kpg), (wu_t, kwu, pu_t, kpu)):
                        for k in range(8):
                            for nb in range(2):
                                t0 = th * 1024 + nb * 512
                                c.op("pe", lambda: nc.tensor.matmul(p_t[:, nb * 512:(nb + 1) * 512], lhsT=w_t[:, k, :],
                                                                    rhs=xnT[:, k, t0:t0 + 512],
                                                                    start=(k == 0), stop=(k == 7)),
                                     reads=[kw] + xn_all, writes=[kp], signal=(k == 7 and nb == 1))
                    sg_t, ksg = sg.next()
                    c.op("act", lambda: nc.scalar.activation(out=sg_t[:], in_=pg_t[:], func=AF.Silu),
                         reads=[kpg], writes=[ksg])
                    c.op("dve", lambda: nc.vector.tensor_tensor(out=hT[:, hc, th * 1024:(th + 1) * 1024], in0=sg_t[:],
                                                                in1=pu_t[:], op=ALU.mult),
                         reads=[ksg, kpu], writes=[(khT, hc, th)])
            c.barrier()
        py = [E(nc.psum_tensor(f"{name}_py{j}", [128, D], F32)) for j in range(4)]
        kpy = [c.key("py") for _ in range(4)]
        xr = Ring(c, es, f"{name}_xr", [128, D], F32, 4)
        yo = Ring(c, es, f"{name}_yo", [128, D], F32, 3)
        for tg in range(4):
            xl = []
            for j in range(4):
                x_t, kx = xr.next()
                c.dma("sp", x_t[:], x_in[(tg * 4 + j) * 128:(tg * 4 + j + 1) * 128, :], writes=[kx])
                xl.append((x_t, kx))
            for hc in range(NHC):
                w_t, kw = W2[:, hc, :], (kW2, hc)
                for j in range(4):
                    tt = tg * 4 + j
                    for ob in range(2):
                        c.op("pe", lambda: nc.tensor.matmul(py[j][:, ob * 512:(ob + 1) * 512],
                                                            lhsT=hT[:, hc, tt * 128:(tt + 1) * 128],
                                                            rhs=w_t[:, ob * 512:(ob + 1) * 512],
                                                            start=(hc == 0), stop=(hc == NHC - 1)),
                             reads=[kw, (khT, hc, tt // 8)], writes=[kpy[j]],
                             signal=(ob == 1 and (hc == NHC - 1 or j == 3)))
            for j in range(4):
                tt = tg * 4 + j
                x_t, kx = xl[j]
                y_t, ky = yo.next()
                c.op("dve", lambda: nc.vector.tensor_tensor(out=y_t[:], in0=x_t[:], in1=py[j][:], op=ALU.add),
                     reads=[kx, kpy[j]], writes=[ky])
                c.dma("sp", x_out[tt * 128:(tt + 1) * 128, :], y_t[:], reads=[ky])
        c.barrier()


def warm_pe(c, ptile, pkey, src, n=24):
    nc = c.nc
    for j in range(n):
        c.op("pe", lambda: nc.tensor.matmul(ptile[:, 0:512], lhsT=src[:, 0:128], rhs=src[:, 0:512], start=True, stop=True),
             reads=[], writes=[pkey], signal=(j == n - 1))


def run_pipeline(iters, look=2):
    deferred = []
    N = len(iters)
    for n in range(min(look, N)):
        iters[n]["qk"]()
    for n in range(N):
        due = [d for d in deferred if d[0] <= n]
        for d in due:
            d[1]()
            deferred.remove(d)
        if n + look < N:
            iters[n + look]["qk"]()
        iters[n]["exp"]()
        iters[n]["pv"]()
        posts = iters[n]["post_factory"]() if "post_factory" in iters[n] else ()
        for delay, fn in posts:
            if delay == 0:
                fn()
            else:
                deferred.append((n + delay, fn))
    for d in deferred:
        d[1]()


def make_norm_post(c, o_t, ko, hb, dst_ap, dst_key, osb, rd, pb, ones32, nrows=128):
    nc = c.nc
    dp = 64 if hb == 0 else 0
    st = {}

    def evac():
        st["o"], st["ko"] = osb.next()
        c.op("dve", lambda: nc.vector.tensor_copy(out=st["o"][0:nrows, :], in_=o_t[0:nrows, :]), reads=[ko], writes=[st["ko"]])

    def bcast():
        b_t, kb = pb.next()
        c.op("pe", lambda: nc.tensor.matmul(b_t[:, :], lhsT=ones32[dp:dp + 1, :], rhs=st["o"][dp:dp + 1, :],
                                            start=True, stop=True), reads=[st["ko"]], writes=[kb])
        st["r"], st["kr"] = rd.next()
        c.op("dve", lambda: nc.vector.reciprocal(out=st["r"][hb:hb + 64, :], in_=b_t[hb:hb + 64, :]), reads=[kb], writes=[st["kr"]])
        c.op("dve", lambda: nc.vector.tensor_tensor(out=dst_ap, in0=st["o"][hb:hb + 64, :], in1=st["r"][hb:hb + 64, :], op=ALU.mult),
             reads=[st["ko"], st["kr"]], writes=[dst_key])

    return [(0, evac), (2, bcast)]


def emit_qkv_proj(c, xnT, kxnT, wv, gq, gk, cos_d, sin_d, ident, QT, kQT, VA, VB, kVA, NH, NKV, dupk, name, QZ=None, koff=8):
    nc = c.nc
    HD = 64
    NQK = NH + NKV
    rope = cos_d is not None
    with ExitStack() as es2:
        E2 = es2.enter_context
        W = E2(nc.sbuf_tensor(f"{name}_W", [128, 8, 1536], BF16))
        kW = c.key("W")
        for k in range(8):
            c.dma("pool", W[:, k, :], wv[:, k, :], writes=[(kW, k)])
        G = E2(nc.sbuf_tensor(f"{name}_G", [128, NQK, HD], F32))
        kG = c.key("G")
        c.dma("sp", G[:, 0:NH, :], gq.rearrange("(o h d) -> o h d", o=1, h=1).to_broadcast([128, NH, HD]), writes=[kG])
        c.dma("sp", G[:, NH:NQK, :], gk.rearrange("(o h d) -> o h d", o=1, h=1).to_broadcast([128, NKV, HD]), writes=[kG])
        c.op("dve", lambda: nc.vector.tensor_scalar(out=G[:, 0:NH, :], in0=G[:, 0:NH, :], scalar1=HD ** -0.5,
                                                    scalar2=None, op0=ALU.mult), reads=[kG], writes=[kG])
        eps = E2(nc.sbuf_tensor(f"{name}_eps2", [128, 1], F32))
        keps = c.key("eps")
        c.op("pool", lambda: nc.gpsimd.memset(eps[:], EPS), writes=[keps])
        if VA.shape[-1] > HD + 1:
            c.op("pool", lambda: nc.gpsimd.memset(VA[:, :, :, HD:], 0.0), writes=[(kVA, "ones")])
        c.op("pool", lambda: nc.gpsimd.memset(VA[:, :, :, HD:HD + 1], 1.0), writes=[(kVA, "ones")])
        c.op("pool", lambda: nc.gpsimd.memset(VB[:, :, :, 0:HD], 0.0), writes=[(kVA, "ones")])
        c.op("pool", lambda: nc.gpsimd.memset(VB[:, :, :, 0:1], 1.0), writes=[(kVA, "ones")])
        if rope:
            cs = E2(nc.sbuf_tensor(f"{name}_cs", [128, NT, 2, 32], F32))
            kcs = c.key("cs")
            c.dma("sp", cs[:, :, 0, :], cos_d.rearrange("(i p) f -> p i f", p=128), writes=[kcs])
            c.dma("sp", cs[:, :, 1, :], sin_d.rearrange("(i p) f -> p i f", p=128), writes=[kcs])
        pq = Ring(c, es2, f"{name}_pq", [128, 1536], F32, 2, psum=True)
        pT = Ring(c, es2, f"{name}_pT2", [128, 8, 128], BF16, 2, psum=True)
        nb_ = 1
        sq = Ring(c, es2, f"{name}_sq2", [128, NQK, HD], F32, nb_)
        st = Ring(c, es2, f"{name}_st2", [128, 2, NQK], F32, 2)
        qn = Ring(c, es2, f"{name}_qn", [128, NQK, HD], F32, 2)
        if rope:
            kdr = Ring(c, es2, f"{name}_kd", [128, NKV, 2, HD], BF16, 2)
            tA = Ring(c, es2, f"{name}_tA", [128, NQK, 32], F32, 1)
            tB = Ring(c, es2, f"{name}_tB", [128, NQK, 32], F32, 1)
            tC = Ring(c, es2, f"{name}_tC", [128, NQK, 32], F32, 1)
            tD = Ring(c, es2, f"{name}_tD", [128, NQK, 32], F32, 1)
            ro = Ring(c, es2, f"{name}_ro", [128, NQK, HD], F32, 1)
        qr = Ring(c, es2, f"{name}_qr", [128, NQK, HD], BF16, 2)
        def make_tile(i):
            T = {}

            def mm():
                p_t, kp = pq.next()
                for cb in range(3):
                    for k in range(8):
                        c.op("pe", lambda: nc.tensor.matmul(p_t[:, cb * 512:(cb + 1) * 512],
                                                            lhsT=xnT[:, k, i * 128:(i + 1) * 128],
                                                            rhs=W[:, k, cb * 512:(cb + 1) * 512],
                                                            start=(k == 0), stop=(k == 7)),
                             reads=[(kW, k), (kxnT, i)], writes=[kp], signal=(k == 7 and cb == 2))
                T['p_t'], T['kp'] = p_t, kp

            def chain():
                p_t, kp = T['p_t'], T['kp']
                pqk = p_t[:, 0:NQK * HD].rearrange("p (h d) -> p h d", d=HD)
                s_t, ks = sq.next()
                st_t, kst = st.next()
                q_t, kq = qn.next()
                r_t, kr = qr.next()
                c.op("act", lambda: nc.scalar.activation(out=s_t[:], in_=pqk, func=AF.Square), reads=[kp], writes=[ks])
                c.op("dve", lambda: nc.vector.tensor_tensor(out=q_t[:], in0=pqk, in1=G[:], op=ALU.mult), reads=[kp, ks, kG], writes=[kq])
                c.op("act", lambda: nc.scalar.copy(out=VA[:, i, :, 0:HD],
                                                   in_=p_t[:, NQK * HD:1536].rearrange("p (g d) -> p g d", d=HD)),
                     reads=[kp, kq], writes=[(kVA, i)])
                c.op("act", lambda: nc.scalar.copy(out=VB[:, i, :, HD:2 * HD],
                                                   in_=p_t[:, NQK * HD:1536].rearrange("p (g d) -> p g d", d=HD)),
                     reads=[kp, kq], writes=[(kVA, i, "b")])
                c.op("dve", lambda: nc.vector.tensor_reduce(out=st_t[:, 0, :], in_=s_t[:], axis=AX.X, op=ALU.add),
                     reads=[ks], writes=[kst])
                c.op("act", lambda: nc.scalar.activation(out=st_t[:, 1, :], in_=st_t[:, 0, :], func=AF.Sqrt,
                                                          scale=1.0 / HD, bias=eps[:]), reads=[kst, keps], writes=[kst])
                c.op("dve", lambda: nc.vector.reciprocal(out=st_t[:, 1, :], in_=st_t[:, 1, :]), reads=[kst], writes=[kst])
                rstd_b = st_t[:, 1, :].unsqueeze(2).to_broadcast([128, NQK, HD])
                if not rope:
                    c.op("dve", lambda: nc.vector.tensor_tensor(out=r_t[:], in0=q_t[:], in1=rstd_b, op=ALU.mult),
                         reads=[kq, kst], writes=[(kr, 0), (kr, 1)])
                else:
                    qv = q_t[:].rearrange("p h (f two) -> p h f two", two=2)
                    x0 = qv[:, :, :, 0]
                    x1 = qv[:, :, :, 1]
                    cosb = cs[:, i, 0, :].unsqueeze(1).to_broadcast([128, NQK, 32])
                    sinb = cs[:, i, 1, :].unsqueeze(1).to_broadcast([128, NQK, 32])
                    a_t, ka = tA.next()
                    b_t, kb = tB.next()
                    c_t, kc = tC.next()
                    d_t, kd = tD.next()
                    o_t, ko = ro.next()
                    ov = o_t[:].rearrange("p h (f two) -> p h f two", two=2)
                    c.op("dve", lambda: nc.vector.tensor_tensor(out=a_t[:], in0=x0, in1=cosb, op=ALU.mult), reads=[kq, kcs], writes=[ka])
                    c.op("dve", lambda: nc.vector.tensor_tensor(out=b_t[:], in0=x1, in1=sinb, op=ALU.mult), reads=[kq, kcs], writes=[kb])
                    c.op("dve", lambda: nc.vector.tensor_tensor(out=ov[:, :, :, 0], in0=a_t[:], in1=b_t[:], op=ALU.subtract),
                         reads=[ka, kb], writes=[(ko, 0)])
                    c.op("pool", lambda: nc.gpsimd.tensor_tensor(out=c_t[:], in0=x0, in1=sinb, op=ALU.mult), reads=[kq, kcs], writes=[kc])
                    c.op("pool", lambda: nc.gpsimd.tensor_tensor(out=d_t[:], in0=x1, in1=cosb, op=ALU.mult), reads=[kq, kcs], writes=[kd])
                    c.op("pool", lambda: nc.gpsimd.tensor_tensor(out=ov[:, :, :, 1], in0=c_t[:], in1=d_t[:], op=ALU.add),
                         reads=[kc, kd], writes=[(ko, 1)])
                    c.op("dve", lambda: nc.vector.tensor_tensor(out=r_t[:], in0=o_t[:], in1=rstd_b, op=ALU.mult),
                         reads=[(ko, 0), (ko, 1), kst], writes=[(kr, 0), (kr, 1)])
                    kd_t, kkd = kdr.next()
                    c.op("pool", lambda: nc.gpsimd.tensor_copy(out=kd_t[:], in_=r_t[:, NH:NQK, :].unsqueeze(2).to_broadcast([128, NKV, 2, HD])),
                         reads=[(kr, 0), (kr, 1)], writes=[kkd])
                T['r_t'], T['kr'] = r_t, kr
                if dupk:
                    T['kd_t'], T['kkd'] = kd_t, kkd

            def tr():
                r_t, kr = T['r_t'], T['kr']
                if dupk:
                    kd_t, kkd = T['kd_t'], T['kkd']
                rflat = r_t[:].rearrange("p h d -> p (h d)")
                if dupk:
                    kflat = kd_t[:].rearrange("p g t d -> p (g t d)")
                t_t, kt_ = pT.next()
                for j in range(8):
                    c.op("pe", lambda: nc.tensor.transpose(t_t[:, j, :], rflat[:, j * 128:(j + 1) * 128], ident[:]),
                         reads=[(kr, 0), (kr, 1)], writes=[kt_], signal=(j == 7))
                if QZ is None:
                    c.op("act", lambda: nc.scalar.copy(out=QT[:, 0:8, i * 128:(i + 1) * 128], in_=t_t[:]),
                         reads=[kt_], writes=[(kQT, i, 0)])
                else:
                    nqc = NH // 2
                    qzv = QZ[:].rearrange("p (c two) s -> p c two s", two=2)
                    c.op("act", lambda: nc.scalar.copy(out=qzv[0:64, :, 0, i * 128:(i + 1) * 128], in_=t_t[0:64, 0:nqc, :]),
                         reads=[kt_, "qz0", "qz1"], writes=[(kQT, i, 0)])
                    c.op("dve", lambda: nc.vector.tensor_copy(out=qzv[64:128, :, 1, i * 128:(i + 1) * 128], in_=t_t[64:128, 0:nqc, :]),
                         reads=[kt_, (kQT, i, 0)], writes=[(kQT, i, 2)])
                    if not dupk:
                        c.op("dve", lambda: nc.vector.tensor_copy(out=QT[:, koff:koff + NKV // 2, i * 128:(i + 1) * 128],
                                                                  in_=t_t[:, nqc:nqc + NKV // 2, :]),
                             reads=[kt_, (kQT, i, 2)], writes=[(kQT, i, 3)])
                if dupk:
                    t_t, kt_ = pT.next()
                    for j in range(NKV):
                        c.op("pe", lambda: nc.tensor.transpose(t_t[:, j, :], kflat[:, j * 128:(j + 1) * 128], ident[:]),
                             reads=[kkd], writes=[kt_], signal=(j == NKV - 1))
                    c.op("act", lambda: nc.scalar.copy(out=QT[:, koff:koff + NKV, i * 128:(i + 1) * 128], in_=t_t[:, 0:NKV, :]),
                         reads=[kt_], writes=[(kQT, i, 1)])
            return mm, chain, tr

        tiles = [make_tile(i) for i in range(NT)]
        tiles[0][0]()
        tiles[1][0]()
        tiles[0][1]()
        for i in range(NT):
            if i + 2 < NT:
                tiles[i + 2][0]()
            if i + 1 < NT:
                tiles[i + 1][1]()
            tiles[i][2]()
        c.barrier()


def emit_odd(c, x_in, x_out, g_dram, wqkv, gq, gk, wo, cos_d, sin_d, ident, ones32, name):
    nc = c.nc
    NH, NKV, HD = 16, 4, 64
    NQK = NH + NKV
    with ExitStack() as es:
        E = es.enter_context
        QT = E(nc.sbuf_tensor(f"{name}_QT", [128, NKV, S], BF16))
        kQT = c.key("QT")
        VAB = E(nc.sbuf_tensor(f"{name}_VAB", [128, NT, NKV, 192], BF16))
        VA = VAB[:, :, :, 0:128]
        VB = VAB[:, :, :, 64:192]
        kVA = c.key("VA")
        QZ = E(nc.sbuf_tensor(f"{name}_QZ", [128, NH, S], BF16))
        c.op("pool", lambda: nc.gpsimd.memset(QZ[:, 0:NH // 2, :], 0.0), writes=["qz0"])
        c.op("pool", lambda: nc.gpsimd.memset(QZ[:, NH // 2:NH, :], 0.0), writes=["qz1"])
        with ExitStack() as es0:
            xnT = es0.enter_context(nc.sbuf_tensor(f"{name}_xnT", [128, 8, S], BF16))
            kxnT = c.key("xnT")
            with ExitStack() as es1:
                emit_norm_T(c, es1, x_in, g_dram, xnT, kxnT, ident, name)
                c.barrier()
            emit_qkv_proj(c, xnT, kxnT, wqkv.rearrange("(k p) n -> p k n", p=128), gq, gk, cos_d, sin_d, ident,
                          QT, kQT, VA, VB, kVA, NH, NKV, True, name, QZ=QZ, koff=0)
        OT = E(nc.sbuf_tensor(f"{name}_OT", [128, 8, S], BF16))
        kOT = c.key("OT")
        Wo = E(nc.sbuf_tensor(f"{name}_Wo", [128, 8, D], BF16))
        kWo = c.key("Wo")
        wov = wo.rearrange("(k p) n -> p k n", p=128)
        for k in range(8):
            c.dma("pool", Wo[:, k, :], wov[:, k, :], writes=[(kWo, k)])
        with ExitStack() as es3:
            ps = Ring(c, es3, f"{name}_ps", [128, 2, 512], F32, 3, psum=True)
            po = Ring(c, es3, f"{name}_po", [128, 512], F32, 1, psum=True)
            pb = Ring(c, es3, f"{name}_pb", [128, 512], F32, 1, psum=True)
            PT = Ring(c, es3, f"{name}_PT", [128, 2, 512], BF16, 3)
            osb = Ring(c, es3, f"{name}_osb", [128, 512], F32, 2)
            rd = Ring(c, es3, f"{name}_rd", [128, 512], F32, 2)
            iters = []
            for h in range(NH):
                for qb in range(4):
                    grp = {}
                    for kt2 in range(8):
                        def mk(h=h, qb=qb, kt2=kt2, grp=grp):
                            g = h // 4
                            hb = (h % 2) * 64
                            stt = {}

                            def qk():
                                stt["s"], stt["ks"] = ps.next()
                                for _ in range(N_DUMMY):
                                    c.op("pe", lambda: nc.tensor.matmul(stt["s"][:, 0, :], lhsT=QT[:, 0, 0:128], rhs=QT[:, 0, 0:512],
                                                                        start=True, stop=True), reads=[], writes=[stt["ks"]], signal=False)
                                for j in range(2):
                                    kt = kt2 * 2 + j
                                    c.op("pe", lambda: nc.tensor.matmul(stt["s"][:, j, :], lhsT=QT[:, g, kt * 128:(kt + 1) * 128],
                                                                        rhs=QZ[:, h, qb * 512:(qb + 1) * 512], start=True, stop=True),
                                         reads=[], writes=[stt["ks"]], signal=(j == 1))

                            def ex():
                                stt["p"], stt["kp"] = PT.next()
                                c.op("act", lambda: nc.scalar.activation(out=stt["p"][:], in_=stt["s"][:], func=AF.Exp),
                                     reads=[stt["ks"]], writes=[stt["kp"]])

                            def pv():
                                if kt2 == 0:
                                    grp["o"], grp["ko"] = po.next()
                                o_t, ko = grp["o"], grp["ko"]
                                for j in range(2):
                                    kt = kt2 * 2 + j
                                    if hb == 0:
                                        c.op("pe", lambda: nc.tensor.matmul(o_t[:, :], lhsT=VA[:, kt, g, :], rhs=stt["p"][:, j, :],
                                                                            start=(kt == 0), stop=(kt == 15)),
                                             reads=[stt["kp"]], writes=[ko], signal=(j == 1))
                                    else:
                                        c.op("pe", lambda: nc.tensor.matmul(o_t[:, :], lhsT=VB[:, kt, g, :], rhs=stt["p"][:, j, :],
                                                                            start=(kt == 0), stop=(kt == 15)),
                                             reads=[stt["kp"]], writes=[ko], signal=(j == 1))

                            it = {"qk": qk, "exp": ex, "pv": pv}
                            if kt2 == 7:
                                def post_factory():
                                    return make_norm_post(c, grp["o"], grp["ko"], hb,
                                                          OT[hb:hb + 64, h // 2, qb * 512:(qb + 1) * 512], (kOT, h, qb), osb, rd, pb, ones32)
                                it["post_factory"] = post_factory
                            return it
                        iters.append(mk())
            warm_pe(c, pb.t[0], pb.k[0], QT[:, 0, :])
            run_pipeline(iters)
            c.barrier()
        with ExitStack() as es4:
            py = Ring(c, es4, f"{name}_py", [128, D], F32, 2, psum=True)
            xr = Ring(c, es4, f"{name}_xr", [128, D], F32, 4)
            yo = Ring(c, es4, f"{name}_yo", [128, D], F32, 3)
            xq = []
            for tt in range(3):
                x_t, kx = xr.next()
                c.dma("sp", x_t[:], x_in[tt * 128:(tt + 1) * 128, :], writes=[kx])
                xq.append((x_t, kx))
            for tt in range(NT):
                y_p, kyp = py.next()
                for k in range(8):
                    for ob in range(2):
                        c.op("pe", lambda: nc.tensor.matmul(y_p[:, ob * 512:(ob + 1) * 512], lhsT=OT[:, k, tt * 128:(tt + 1) * 128],
                                                            rhs=Wo[:, k, ob * 512:(ob + 1) * 512], start=(k == 0), stop=(k == 7)),
                             reads=[(kWo, k)], writes=[kyp], signal=(k == 7 and ob == 1))
                x_t, kx = xq.pop(0)
                y_t, ky = yo.next()
                c.op("dve", lambda: nc.vector.tensor_tensor(out=y_t[:], in0=x_t[:], in1=y_p[:], op=ALU.add),
                     reads=[kx, kyp], writes=[ky])
                if tt + 3 < NT:
                    x_n, kxn = xr.next()
                    c.dma("sp", x_n[:], x_in[(tt + 3) * 128:(tt + 4) * 128, :], writes=[kxn])
                    xq.append((x_n, kxn))
                c.dma("sp", x_out[tt * 128:(tt + 1) * 128, :], y_t[:], reads=[ky])
            c.barrier()


NA_TYPES = [(0, j) for j in range(4)] + [(1, j) for j in range(4)] + [(2, j) for j in range(5)] + \
           [(3, j) for j in range(4)] + [(4, j) for j in range(4)]
NA_TIX = {t: n for n, t in enumerate(NA_TYPES)}
NTYPES = len(NA_TYPES)


def na_cls(i):
    if i == 0:
        return 0, 0, 4
    if i == 1:
        return 1, 0, 4
    if i == 14:
        return 3, 12, 4
    if i == 15:
        return 4, 12, 4
    return 2, i - 2, 5


def emit_even(c, x_in, x_out, P, ident, ones32, name):
    nc = c.nc
    HD = 64
    w_in_v = P["w_in"].rearrange("(k p) n -> p k n", p=128)
    with ExitStack() as es:
        E = es.enter_context
        mt_v = P["mt_scr"].rearrange("k p s -> p k s")
        kMT = c.key("MT")
        DT = E(nc.sbuf_tensor(f"{name}_DT", [128, NT, 32], F32))
        AA = E(nc.sbuf_tensor(f"{name}_AA", [128, NT, 32], F32))
        kDT = c.key("DT")
        kAA = c.key("AA")
        eps = E(nc.sbuf_tensor(f"{name}_epsE", [128, 1], F32))
        keps = c.key("eps")
        c.op("pool", lambda: nc.gpsimd.memset(eps[:], EPS), writes=[keps])
        with ExitStack() as esn:
            if True:
                En = esn.enter_context
                MT = En(nc.sbuf_tensor(f"{name}_MTn", [128, 4, S], BF16))
                QT = En(nc.sbuf_tensor(f"{name}_QT", [128, 4, S], BF16))
                kQT = c.key("QT")
                QZ = En(nc.sbuf_tensor(f"{name}_QZ", [128, 8, S], BF16))
                c.op("pool", lambda: nc.gpsimd.memset(QZ[:, 0:4, :], 0.0), writes=["qz0"])
                c.op("pool", lambda: nc.gpsimd.memset(QZ[:, 4:8, :], 0.0), writes=["qz1"])
                VAB = En(nc.sbuf_tensor(f"{name}_VAB", [128, NT, 8, 192], BF16))
                VA = VAB[:, :, :, 0:128]
                VB = VAB[:, :, :, 64:192]
                kVA = c.key("VA")
                with ExitStack() as esx:
                    xnT = esx.enter_context(nc.sbuf_tensor(f"{name}_xnT", [128, 8, S], BF16))
                    kxnT = c.key("xnT")
                    with ExitStack() as es1:
                        emit_norm_T(c, es1, x_in, P["mix_norm"], xnT, kxnT, ident, name)
                        c.barrier()
                    emit_qkv_proj(c, xnT, kxnT, w_in_v[:, :, 0:1536], P["na_gq"], P["na_gk"], None, None, ident,
                                  QT, kQT, VA, VB, kVA, 8, 8, False, name + "n", QZ=QZ, koff=0)
                BM = En(nc.sbuf_tensor(f"{name}_BM", [128, NTYPES, 8, 128], BF16))
                kBM = c.key("BM")
                with ExitStack() as esb:
                    bg = Ring(c, esb, f"{name}_bg", [128, 8, 128], F32, 2)
                    mk = Ring(c, esb, f"{name}_mk", [128, 128], F32, 2)
                    for t in range(NTYPES):
                        b_t, kb = bg.next()
                        m_t, km = mk.next()
                        c.dma("sp", b_t[:], P["biasg"][t], writes=[kb])
                        c.dma("sp", m_t[:], P["namask"][t], writes=[km])
                        c.op("dve", lambda: nc.vector.tensor_tensor(out=BM[:, t, :, :], in0=b_t[:],
                                                                    in1=m_t[:].unsqueeze(1).to_broadcast([128, 8, 128]), op=ALU.add),
                             reads=[kb, km], writes=[kBM])
                    c.barrier()
                with ExitStack() as es3:
                    ps = Ring(c, es3, f"{name}_ps", [128, 8, 128], F32, 3, psum=True)
                    po = Ring(c, es3, f"{name}_po", [128, 512], F32, 1, psum=True)
                    pb = Ring(c, es3, f"{name}_pb", [128, 512], F32, 1, psum=True)
                    PT = Ring(c, es3, f"{name}_PT", [128, 5, 128], BF16, 3)
                    osb = Ring(c, es3, f"{name}_osb", [128, 512], F32, 2)
                    rd = Ring(c, es3, f"{name}_rd", [128, 512], F32, 2)
                    iters = []
                    for h in range(8):
                        for ig in range(4):
                            grp = {}
                            for ii in range(4):
                                def mk(h=h, ig=ig, ii=ii, grp=grp):
                                    hb = (h % 2) * 64
                                    qc = h // 2
                                    kc = 4 + h // 2
                                    i = ig * 4 + ii
                                    cls, kp0, ntl = na_cls(i)
                                    stt = {}

                                    def qk():
                                        stt["s"], stt["ks"] = ps.next()
                                        t0 = NA_TIX[(cls, 0)]
                                        c.op("pe", lambda: nc.tensor.matmul(stt["s"][:, 0:4, :], lhsT=ident[:], rhs=BM[:, t0:t0 + 4, h, :],
                                                                            start=True, stop=False),
                                             reads=[], writes=[stt["ks"]], signal=False)
                                        for j in range(4):
                                            kp = kp0 + j
                                            c.op("pe", lambda: nc.tensor.matmul(stt["s"][:, j, :], lhsT=QT[:, h // 2, kp * 128:(kp + 1) * 128],
                                                                                rhs=QZ[:, h, i * 128:(i + 1) * 128], start=False, stop=(j == 3)),
                                                 reads=[], writes=[stt["ks"]], signal=(j == 3 and ntl == 4))
                                        if ntl == 5:
                                            kp = kp0 + 4
                                            c.op("pe", lambda: nc.tensor.matmul(stt["s"][:, 4, :], lhsT=ident[:], rhs=BM[:, t0 + 4, h, :],
                                                                                start=True, stop=False),
                                                 reads=[], writes=[stt["ks"]], signal=False)
                                            c.op("pe", lambda: nc.tensor.matmul(stt["s"][:, 4, :], lhsT=QT[:, h // 2, kp * 128:(kp + 1) * 128],
                                                                                rhs=QZ[:, h, i * 128:(i + 1) * 128], start=False, stop=True),
                                                 reads=[], writes=[stt["ks"]], signal=True)

                                    def ex():
                                        stt["p"], stt["kp"] = PT.next()
                                        c.op("act", lambda: nc.scalar.activation(out=stt["p"][:, 0:ntl, :], in_=stt["s"][:, 0:ntl, :], func=AF.Exp),
                                             reads=[stt["ks"]], writes=[stt["kp"]])

                                    def pv():
                                        if ii == 0:
                                            grp["o"], grp["ko"] = po.next()
                                        o_t, ko = grp["o"], grp["ko"]
                                        for j in range(ntl):
                                            kp = kp0 + j
                                            if hb == 0:
                                                c.op("pe", lambda: nc.tensor.matmul(o_t[:, ii * 128:(ii + 1) * 128], lhsT=VA[:, kp, h, :],
                                                                                    rhs=stt["p"][:, j, :], start=(j == 0), stop=(j == ntl - 1)),
                                                     reads=[stt["kp"]], writes=[ko], signal=(j == ntl - 1))
                                            else:
                                                c.op("pe", lambda: nc.tensor.matmul(o_t[:, ii * 128:(ii + 1) * 128], lhsT=VB[:, kp, h, :],
                                                                                    rhs=stt["p"][:, j, :], start=(j == 0), stop=(j == ntl - 1)),
                                                     reads=[stt["kp"]], writes=[ko], signal=(j == ntl - 1))

                                    it = {"qk": qk, "exp": ex, "pv": pv}
                                    if ii == 3:
                                        def post_factory():
                                            return make_norm_post(c, grp["o"], grp["ko"], hb,
                                                                  MT[hb:hb + 64, h // 2, ig * 512:(ig + 1) * 512], (kMT, h, ig), osb, rd, pb, ones32,
                                                                  nrows=128)
                                        it["post_factory"] = post_factory
                                    return it
                                iters.append(mk())
                    warm_pe(c, pb.t[0], pb.k[0], QT[:, 0, :])
                    run_pipeline(iters)
                    c.barrier()
                    c.dma("sp", mt_v[:, 0:4, :], MT[:], writes=[("mt_d", "na")])
                    c.barrier()
        XT = E(nc.sbuf_tensor(f"{name}_XT", [128, NT, 1024], BF16))
        kXT = c.key("XT")
        BT = E(nc.sbuf_tensor(f"{name}_BT", [128, 4, S], BF16))
        CT = E(nc.sbuf_tensor(f"{name}_CT", [128, 4, S], BF16))
        kBT = c.key("BT")
        kCT = c.key("CT")
        Btok = E(nc.sbuf_tensor(f"{name}_Btok", [128, NT, 512], BF16))
        kBtok = c.key("Btok")
        with ExitStack() as esx:
            xnT = esx.enter_context(nc.sbuf_tensor(f"{name}_xnTb", [128, 8, S], BF16))
            kxnT = c.key("xnT")
            with ExitStack() as es1:
                emit_norm_T(c, es1, x_in, P["mix_norm"], xnT, kxnT, ident, name + "b")
                c.barrier()
            xn_all = [(kxnT, i) for i in range(NT)]
            with ExitStack() as esz:
                Ez = esz.enter_context
                Wz = Ez(nc.sbuf_tensor(f"{name}_Wz", [128, 8, 1056], BF16))
                kWz = c.key("Wz")
                for k in range(8):
                    c.dma("pool", Wz[:, k, 0:1024], w_in_v[:, k, 1536:2560], writes=[(kWz, k)])
                    c.dma("pool", Wz[:, k, 1024:1056], w_in_v[:, k, 4608:4640], writes=[(kWz, k, "d")])
                dtb = Ez(nc.sbuf_tensor(f"{name}_dtb", [128, 32], F32))
                alb = Ez(nc.sbuf_tensor(f"{name}_alb", [128, 32], F32))
                kdtb = c.key("dtb")
                kalb = c.key("alb")
                c.dma("sp", dtb[:], bcast_row(P["dt_bias"], 32), writes=[kdtb])
                c.dma("sp", alb[:], bcast_row(P["A_log"], 32), writes=[kalb])
                pz = Ring(c, esz, f"{name}_pz", [128, 1024], F32, 2, psum=True)
                pd = Ring(c, esz, f"{name}_pd", [128, 512], F32, 2, psum=True)
                zb = Ring(c, esz, f"{name}_zb", [128, 1024], BF16, 3)
                for i in range(NT):
                    z_p, kzp = pz.next()
                    d_p, kdp = pd.next()
                    for cb in range(2):
                        for k in range(8):
                            c.op("pe", lambda: nc.tensor.matmul(z_p[:, cb * 512:(cb + 1) * 512], lhsT=xnT[:, k, i * 128:(i + 1) * 128],
                                                                rhs=Wz[:, k, cb * 512:(cb + 1) * 512], start=(k == 0), stop=(k == 7)),
                                 reads=[(kWz, k), (kxnT, i)], writes=[kzp], signal=(k == 7 and cb == 1))
                    for k in range(8):
                        c.op("pe", lambda: nc.tensor.matmul(d_p[:, 0:32], lhsT=xnT[:, k, i * 128:(i + 1) * 128],
                                                            rhs=Wz[:, k, 1024:1056], start=(k == 0), stop=(k == 7)),
                             reads=[(kWz, k, "d"), (kxnT, i)], writes=[kdp], signal=(k == 7))
                    z_t, kz = zb.next()
                    c.op("act", lambda: nc.scalar.activation(out=z_t[:], in_=z_p[:], func=AF.Silu), reads=[kzp], writes=[kz])
                    c.dma("sp", P["zs"][i * 128:(i + 1) * 128, :], z_t[:], reads=[kz], writes=[("zs_d", i)])
                    c.op("dve", lambda: nc.vector.tensor_tensor(out=DT[:, i, :], in0=d_p[:, 0:32], in1=dtb[:], op=ALU.add),
                         reads=[kdp, kdtb], writes=[kDT])
                c.op("act", lambda: nc.scalar.activation(out=DT[:], in_=DT[:], func=AF.Exp), reads=[kDT], writes=[kDT])
                c.op("act", lambda: nc.scalar.activation(out=DT[:], in_=DT[:], func=AF.Ln, bias=1.0), reads=[kDT], writes=[kDT])
                c.op("act", lambda: nc.scalar.activation(out=alb[:], in_=alb[:], func=AF.Exp), reads=[kalb], writes=[kalb])
                c.op("dve", lambda: nc.vector.tensor_scalar(out=alb[:], in0=alb[:], scalar1=-1.0, scalar2=None, op0=ALU.mult),
                     reads=[kalb], writes=[kalb])
                c.op("dve", lambda: nc.vector.tensor_tensor(out=AA[:], in0=DT[:], in1=alb[:].unsqueeze(1).to_broadcast([128, NT, 32]),
                                                            op=ALU.mult), reads=[kDT, kalb], writes=[kAA])
                c.barrier()
            with ExitStack() as esc:
                Ec = esc.enter_context
                cw = Ec(nc.sbuf_tensor(f"{name}_cw", [128, 16, 4], F32))
                cbias = Ec(nc.sbuf_tensor(f"{name}_cb", [128, 16], F32))
                kcw = c.key("cw")
                for j in range(4):
                    c.dma("sp", cw[:, :, j], P["conv_w"][j].rearrange("(c p) -> p c", p=128), writes=[kcw], allow_slow_non_contiguous=True)
                c.dma("sp", cbias[:], P["conv_b"].rearrange("(c p) -> p c", p=128), writes=[kcw], allow_slow_non_contiguous=True)
                wx = Ring(c, esc, f"{name}_wx", [128, 8, 128], BF16, 3)
                px = Ring(c, esc, f"{name}_px", [128, 1024], F32, 2, psum=True)
                pTr = Ring(c, esc, f"{name}_pTr", [128, 8, 128], BF16, 2, psum=True)
                Rr = Ring(c, esc, f"{name}_R", [128, S + 3], F32, 2)
                acc = Ring(c, esc, f"{name}_acc", [128, S], F32, 2)
                xo = Ring(c, esc, f"{name}_xo", [128, S], BF16, 2)
                for t_, k_ in zip(Rr.t, Rr.k):
                    c.op("pool", lambda: nc.gpsimd.memset(t_[:, 0:2], 0.0), writes=[k_])
                    c.op("pool", lambda: nc.gpsimd.memset(t_[:, S + 2:S + 3], 0.0), writes=[k_])
                def make_ch(ch):
                    T = {}

                    def front():
                            w_t, kw = wx.next()
                            c.dma("pool", w_t[:], w_in_v[:, :, 2560 + ch * 128:2560 + (ch + 1) * 128], writes=[kw])
                            R_t, kR = Rr.next()
                            for half in range(2):
                                p_t, kp = px.next()
                                for k in range(8):
                                    for nb in range(2):
                                        t0 = half * 1024 + nb * 512
                                        c.op("pe", lambda: nc.tensor.matmul(p_t[:, nb * 512:(nb + 1) * 512], lhsT=w_t[:, k, :],
                                                                            rhs=xnT[:, k, t0:t0 + 512], start=(k == 0), stop=(k == 7)),
                                             reads=[kw] + xn_all, writes=[kp], signal=(k == 7 and nb == 1))
                                c.op("act", lambda: nc.scalar.copy(out=R_t[:, 2 + half * 1024:2 + (half + 1) * 1024], in_=p_t[:]),
                                     reads=[kp], writes=[kR])
                            T['R'] = (R_t, kR)

                    def back1():
                            R_t, kR = T['R']
                            a_t, ka = acc.next()
                            c.op("act", lambda: nc.scalar.activation(out=a_t[:], in_=R_t[:, 0:S], func=AF.Identity,
                                                                      scale=cw[:, ch, 0:1], bias=cbias[:, ch:ch + 1]),
                                 reads=[kR, kcw], writes=[ka])
                            c.op("dve", lambda: nc.vector.scalar_tensor_tensor(out=a_t[:], in0=R_t[:, 1:S + 1], scalar=cw[:, ch, 1:2], in1=a_t[:],
                                                                               op0=ALU.mult, op1=ALU.add), reads=[kR, kcw, ka], writes=[ka])
                            c.op("dve", lambda: nc.vector.scalar_tensor_tensor(out=a_t[:], in0=R_t[:, 2:S + 2], scalar=cw[:, ch, 2:3], in1=a_t[:],
                                                                                op0=ALU.mult, op1=ALU.add), reads=[kR, kcw, ka], writes=[ka])
                            c.op("dve", lambda: nc.vector.scalar_tensor_tensor(out=a_t[:], in0=R_t[:, 3:S + 3], scalar=cw[:, ch, 3:4], in1=a_t[:],
                                                                               op0=ALU.mult, op1=ALU.add), reads=[kR, kcw, ka], writes=[ka])
                            if ch < 8:
                                o_t, ko = xo.next()
                                dst = o_t[:]
                                wk = [ko]
                            elif ch < 12:
                                dst = BT[:, ch - 8, :]
                                ko = (kBT, ch - 8)
                                wk = [ko]
                            else:
                                dst = CT[:, ch - 12, :]
                                wk = [(kCT, ch - 12)]
                            T['dst'] = (dst, wk, a_t, ka)

                    def back2():
                            dst, wk, a_t, ka = T['dst']
                            c.op("act", lambda: nc.scalar.activation(out=dst, in_=a_t[:], func=AF.Silu), reads=[ka], writes=wk)
                            if ch < 12:
                                for half in range(2):
                                    t_t, kt_ = pTr.next()
                                    for tt in range(8):
                                        t0 = (half * 8 + tt) * 128
                                        c.op("pe", lambda: nc.tensor.transpose(t_t[:, tt, :], dst[:, t0:t0 + 128], ident[:]),
                                             reads=wk, writes=[kt_], signal=(tt == 7))
                                    if ch < 8:
                                        c.op("dve", lambda: nc.vector.tensor_copy(out=XT[:, half * 8:(half + 1) * 8, ch * 128:(ch + 1) * 128], in_=t_t[:]),
                                             reads=[kt_], writes=[(kXT, ch, half)])
                                    else:
                                        c.op("dve", lambda: nc.vector.tensor_copy(out=Btok[:, half * 8:(half + 1) * 8, (ch - 8) * 128:(ch - 7) * 128],
                                                                                  in_=t_t[:]), reads=[kt_], writes=[(kBtok, ch, half)])
                    return front, back1, back2

                chs = [make_ch(ch) for ch in range(16)]
                chs[0][0]()
                chs[1][0]()
                chs[0][1]()
                for ch in range(16):
                    if ch + 2 < 16:
                        chs[ch + 2][0]()
                    if ch + 1 < 16:
                        chs[ch + 1][1]()
                    chs[ch][2]()
                c.barrier()
        emit_ssd(c, P, XT, BT, CT, Btok, DT, AA, eps, ident, ones32, name)
        with ExitStack() as es4:
            MT = es4.enter_context(nc.sbuf_tensor(f"{name}_MT", [128, 12, S], BF16))
            for k in range(12):
                c.dma("sp", MT[:, k, :], mt_v[:, k, :], writes=[(kMT, "ld", k)])
            Wo = es4.enter_context(nc.sbuf_tensor(f"{name}_Wo", [128, 12, D], BF16))
            kWo = c.key("Wo")
            wov = P["w_out"].rearrange("(k p) n -> p k n", p=128)
            for k in range(12):
                c.dma("pool", Wo[:, k, :], wov[:, k, :], writes=[(kWo, k)])
            py = Ring(c, es4, f"{name}_py", [128, D], F32, 2, psum=True)
            xr = Ring(c, es4, f"{name}_xr", [128, D], F32, 4)
            yo = Ring(c, es4, f"{name}_yo", [128, D], F32, 3)
            xq = []
            for tt in range(3):
                x_t, kx = xr.next()
                c.dma("sp", x_t[:], x_in[tt * 128:(tt + 1) * 128, :], writes=[kx])
                xq.append((x_t, kx))
            for tt in range(NT):
                y_p, kyp = py.next()
                for k in range(12):
                    for ob in range(2):
                        c.op("pe", lambda: nc.tensor.matmul(y_p[:, ob * 512:(ob + 1) * 512], lhsT=MT[:, k, tt * 128:(tt + 1) * 128],
                                                            rhs=Wo[:, k, ob * 512:(ob + 1) * 512], start=(k == 0), stop=(k == 11)),
                             reads=[(kWo, k), (kMT, "ld", k)], writes=[kyp], signal=(k == 11 and ob == 1))
                x_t, kx = xq.pop(0)
                y_t, ky = yo.next()
                c.op("dve", lambda: nc.vector.tensor_tensor(out=y_t[:], in0=x_t[:], in1=y_p[:], op=ALU.add),
                     reads=[kx, kyp], writes=[ky])
                if tt + 3 < NT:
                    x_n, kxn = xr.next()
                    c.dma("sp", x_n[:], x_in[(tt + 3) * 128:(tt + 4) * 128, :], writes=[kxn])
                    xq.append((x_n, kxn))
                c.dma("sp", x_out[tt * 128:(tt + 1) * 128, :], y_t[:], reads=[ky])
            c.barrier()


def emit_ssd(c, P, XT, BT, CT, Btok, DT, AA, eps, ident, ones32, name):
    nc = c.nc
    NHD = 16
    mt_v = P["mt_scr"].rearrange("k p s -> p k s")
    with ExitStack() as es:
        E = es.enter_context
        U = E(nc.sbuf_tensor(f"{name}_U", [128, 128], F32))
        L = E(nc.sbuf_tensor(f"{name}_L", [128, 128], F32))
        NMf = E(nc.sbuf_tensor(f"{name}_NMf", [128, 128], F32))
        NMb = E(nc.sbuf_tensor(f"{name}_NMb", [128, 128], F32))
        Db = E(nc.sbuf_tensor(f"{name}_Db", [128, NHD], F32))
        gnb = E(nc.sbuf_tensor(f"{name}_gnb", [128, 1024], F32))
        kconst = c.key("ssdconst")
        c.dma("sp", U[:], P["tri_u"], writes=[kconst])
        c.dma("sp", L[:], P["tri_l"], writes=[kconst])
        c.dma("sp", NMf[:], P["negm_f"], writes=[kconst])
        c.dma("sp", NMb[:], P["negm_b"], writes=[kconst])
        c.dma("sp", Db[:], bcast_row(P["ssd_D"], NHD), writes=[kconst])
        c.dma("sp", gnb[:], bcast_row(P["out_norm"], 1024), writes=[kconst])
        Hbst = Ring(c, es, f"{name}_Hbst", [128, 1024], BF16, 2)
        Hbld = Ring(c, es, f"{name}_Hbld", [128, 1024], BF16, 2)
        kHb = c.key("Hball")
        Hf32 = E(nc.sbuf_tensor(f"{name}_Hf32", [128, 1024], F32))
        Hb32 = E(nc.sbuf_tensor(f"{name}_Hb32", [128, 1024], F32))
        kHf = c.key("Hf32")
        kHb32 = c.key("Hb32")
        c.op("pool", lambda: nc.gpsimd.memset(Hf32[:], 0.0), writes=[kHf])
        c.op("pool", lambda: nc.gpsimd.memset(Hb32[:], 0.0), writes=[kHb32])
        Hfb = Ring(c, es, f"{name}_Hfb", [128, 1024], BF16, 2)
        pR = Ring(c, es, f"{name}_pR", [128, 4, 128], F32, 2, psum=True)
        pcb = Ring(c, es, f"{name}_pcb", [128, 4, 128], F32, 1, psum=True)
        py = Ring(c, es, f"{name}_pyd", [128, 1024], F32, 1, psum=True)
        pY2 = Ring(c, es, f"{name}_pY2", [128, 1024], F32, 1, psum=True)
        pTn = Ring(c, es, f"{name}_pTn", [128, 8, 128], BF16, 1, psum=True)
        STr = Ring(c, es, f"{name}_ST", [128, 5, 32], F32, 4)
        wsm = Ring(c, es, f"{name}_wsm", [128, 16], F32, 3)
        xcf = Ring(c, es, f"{name}_xcf", [128, 1024], BF16, 2)
        xcb = Ring(c, es, f"{name}_xcb", [128, 1024], BF16, 2)
        xcd = Ring(c, es, f"{name}_xcd", [128, 1024], BF16, 3)
        cbT = Ring(c, es, f"{name}_cbT", [128, 4, 128], F32, 2)
        t1r = Ring(c, es, f"{name}_t1", [128, 4, 128], F32, 3)
        t2r = Ring(c, es, f"{name}_t2", [128, 4, 128], F32, 3)
        Mr = Ring(c, es, f"{name}_M", [128, 4, 128], BF16, 18)
        yar = Ring(c, es, f"{name}_ya", [128, 1024], F32, 2)
        ybr = Ring(c, es, f"{name}_yb", [128, 1024], F32, 2)
        ydr = Ring(c, es, f"{name}_yd", [128, 1024], F32, 2)
        zsr = Ring(c, es, f"{name}_zs", [128, 1024], BF16, 2)
        hhr = Ring(c, es, f"{name}_hh", [128, 1024], F32, 2)
        sqr = Ring(c, es, f"{name}_sqj", [128, 1024], BF16, 1)
        s4r = Ring(c, es, f"{name}_s4", [128, 2, 4], F32, 2)
        hbr = Ring(c, es, f"{name}_hb16", [128, 1024], BF16, 2)
        mst = Ring(c, es, f"{name}_mst", [128, 8, 128], BF16, 2)
        c.barrier()

        def b16(ap2d, n=NHD, d=64):
            return ap2d.unsqueeze(2).to_broadcast([128, n, d])

        def v3(ap2d, d=64):
            return ap2d.rearrange("p (h d) -> p h d", d=d)

        def chunk_stats(ci):
            p_t, kp = pR.next()
            c.op("pe", lambda: nc.tensor.matmul(p_t[:, 0, 0:16], lhsT=U[:], rhs=AA[:, ci, 0:16], start=True, stop=True),
                 reads=[], writes=[kp], signal=False)
            c.op("pe", lambda: nc.tensor.matmul(p_t[:, 0, 16:32], lhsT=L[:], rhs=AA[:, ci, 16:32], start=True, stop=True),
                 reads=[], writes=[kp], signal=False)
            c.op("pe", lambda: nc.tensor.matmul(p_t[:, 0, 32:64], lhsT=ones32[:], rhs=AA[:, ci, 0:32], start=True, stop=True),
                 reads=[], writes=[kp], signal=True)
            st, kst = STr.next()
            c.op("dve", lambda: nc.vector.tensor_copy(out=st[:, 0, :], in_=p_t[:, 0, 0:32]), reads=[kp], writes=[kst])
            c.op("dve", lambda: nc.vector.tensor_tensor(out=st[:, 4, :], in0=p_t[:, 0, 32:64], in1=st[:, 0, :], op=ALU.subtract),
                 reads=[kp, kst], writes=[kst])
            c.op("act", lambda: nc.scalar.activation(out=st[:, 1, :], in_=st[:, 0, :], func=AF.Exp), reads=[kst], writes=[kst])
            c.op("act", lambda: nc.scalar.activation(out=st[:, 2, :], in_=st[:, 4, :], func=AF.Exp), reads=[kst], writes=[kst])
            c.op("act", lambda: nc.scalar.activation(out=st[:, 3, :], in_=p_t[:, 0, 32:64], func=AF.Exp), reads=[kp, kst], writes=[kst])
            return st, kst

        def states_into(ps_t, kps, ci, x_t, kx):
            for g in range(4):
                c.op("pe", lambda: nc.tensor.matmul(ps_t[:, g * 256:(g + 1) * 256], lhsT=Btok[:, ci, g * 128:(g + 1) * 128],
                                                    rhs=x_t[:, g * 256:(g + 1) * 256], start=True, stop=True),
                     reads=[kx], writes=[kps], signal=(g == 3))

        pre_ps = [(pY2.t[0], pY2.k[0]), (py.t[0], py.k[0])]
        order = list(range(NT - 1, -1, -1))

        def pre_front(n):
            ci = order[n]
            st, kst = chunk_stats(ci)
            w_t, kw = wsm.next()
            c.op("dve", lambda: nc.vector.tensor_tensor(out=w_t[:], in0=DT[:, ci, 16:32], in1=st[:, 2, 16:32], op=ALU.mult),
                 reads=[kst], writes=[kw])
            x_t, kx = xcd.next()
            c.op("pool", lambda: nc.gpsimd.tensor_tensor(out=v3(x_t[:]), in0=v3(XT[:, ci, :]), in1=b16(w_t[:]), op=ALU.mult),
                 reads=[kw], writes=[kx])
            ps_t, kps = pre_ps[n % 2]
            states_into(ps_t, kps, ci, x_t, kx)
            return st, kst, ps_t, kps

        fr = pre_front(0)
        for n in range(NT):
            ci = order[n]
            nxt = pre_front(n + 1) if n + 1 < NT else None
            st, kst, ps_t, kps = fr
            hs_t, khs = Hbst.next()
            c.op("act", lambda: nc.scalar.copy(out=hs_t[:], in_=Hb32[:]), reads=[kHb32], writes=[khs])
            c.dma("sp", P["hb_scr"][ci], hs_t[:], reads=[khs], writes=[(kHb, ci)])
            c.op("pool", lambda: nc.gpsimd.tensor_tensor(out=v3(Hb32[:]), in0=v3(Hb32[:]), in1=b16(st[:, 3, 16:32]), op=ALU.mult),
                 reads=[kHb32, kst], writes=[kHb32])
            c.op("dve", lambda: nc.vector.tensor_tensor(out=Hb32[:], in0=Hb32[:], in1=ps_t[:], op=ALU.add),
                 reads=[kHb32, kps], writes=[kHb32])
            fr = nxt

        def stage_a(ci, T):
            tok = slice(ci * 128, (ci + 1) * 128)
            st, kst = chunk_stats(ci)
            T["st"], T["kst"] = st, kst
            xf_t, kxf = xcf.next()
            xb_t, kxb = xcb.next()
            xd_t, kxd = xcd.next()
            c.op("dve", lambda: nc.vector.tensor_tensor(out=v3(xf_t[:]), in0=v3(XT[:, ci, :]), in1=b16(DT[:, ci, 0:16]), op=ALU.mult),
                 reads=[], writes=[kxf])
            c.op("pool", lambda: nc.gpsimd.tensor_tensor(out=v3(xb_t[:]), in0=v3(XT[:, ci, :]), in1=b16(DT[:, ci, 16:32]), op=ALU.mult),
                 reads=[], writes=[kxb])
            c.op("pool", lambda: nc.gpsimd.tensor_tensor(out=v3(xd_t[:]), in0=v3(xf_t[:]), in1=b16(st[:, 2, 0:16]), op=ALU.mult),
                 reads=[kxf, kst], writes=[kxd])
            T["xf"], T["xb"], T["xd"] = (xf_t, kxf), (xb_t, kxb), (xd_t, kxd)
            yd_t, kyd = ydr.next()
            c.op("pool", lambda: nc.gpsimd.tensor_tensor(out=v3(yd_t[:]), in0=v3(XT[:, ci, :]), in1=b16(Db[:]), op=ALU.mult),
                 reads=[], writes=[kyd])
            T["yd"] = (yd_t, kyd)
            cb_p, kcbp = pcb.next()
            for g in range(4):
                c.op("pe", lambda: nc.tensor.matmul(cb_p[:, g, :], lhsT=BT[:, g, tok], rhs=CT[:, g, tok], start=True, stop=True),
                     reads=[], writes=[kcbp], signal=(g == 3))
            cb_t, kcb = cbT.next()
            c.op("act", lambda: nc.scalar.copy(out=cb_t[:], in_=cb_p[:]), reads=[kcbp], writes=[kcb])
            T["M"] = []
            for g in range(4):
                ms = []
                for d in range(2):
                    tri = U if d == 0 else L
                    NM = NMf if d == 0 else NMb
                    r_p, krp = pR.next()
                    for j in range(4):
                        col = d * 16 + g * 4 + j
                        c.op("pe", lambda: nc.tensor.matmul(r_p[:, j, :], lhsT=AA[:, ci, col:col + 1].to_broadcast([128, 128]),
                                                            rhs=tri[:], start=True, stop=True),
                             reads=[], writes=[krp], signal=(j == 3))
                    c0 = d * 16 + g * 4
                    a_t, ka = t1r.next()
                    for j in range(4):
                        c.op("dve", lambda: nc.vector.scalar_tensor_tensor(out=a_t[:, j, :], in0=r_p[:, j, :], scalar=st[:, 0, c0 + j:c0 + j + 1],
                                                                           in1=NM[:], op0=ALU.subtract, op1=ALU.add),
                             reads=[krp, kst], writes=[ka])
                    e_t, ke = t2r.next()
                    c.op("act", lambda: nc.scalar.activation(out=e_t[:], in_=a_t[:], func=AF.Exp), reads=[ka], writes=[ke])
                    m_t, km = Mr.next()
                    c.op("pool", lambda: nc.gpsimd.tensor_tensor(out=m_t[:], in0=e_t[:], in1=cb_t[:, g, :].unsqueeze(1).to_broadcast([128, 4, 128]),
                                                                 op=ALU.mult), reads=[ke, kcb], writes=[km])
                    ms.append((m_t, km))
                    if d == 1:
                        T["M"].append(ms)
                    yield

        def stage_b_steps(T, hf_t, khf):
            ci = T["ci"]
            tok = slice(ci * 128, (ci + 1) * 128)
            st, kst = T["st"], T["kst"]
            (xf_t, kxf), (xb_t, kxb), (xd_t, kxd) = T["xf"], T["xb"], T["xd"]
            yd_t, kyd = T["yd"]

            def ydiag(g):
                def f():
                    if g == 0:
                        T["yp"] = py.next()
                    y_p, kyp = T["yp"]
                    ms = T["M"][g]
                    for j in range(4):
                        h = g * 4 + j
                        c.op("pe", lambda: nc.tensor.matmul(y_p[:, h * 64:(h + 1) * 64], lhsT=ms[0][0][:, j, :], rhs=xf_t[:, h * 64:(h + 1) * 64],
                                                            start=True, stop=False),
                             reads=[ms[0][1], kxf], writes=[kyp], signal=False)
                        c.op("pe", lambda: nc.tensor.matmul(y_p[:, h * 64:(h + 1) * 64], lhsT=ms[1][0][:, j, :], rhs=xb_t[:, h * 64:(h + 1) * 64],
                                                            start=False, stop=True),
                             reads=[ms[1][1], kxb], writes=[kyp], signal=(j == 3))
                return f

            def state_update():
                s_p, ksp = pY2.next()
                states_into(s_p, ksp, ci, xd_t, kxd)
                c.op("pool", lambda: nc.gpsimd.tensor_tensor(out=v3(Hf32[:]), in0=v3(Hf32[:]), in1=b16(st[:, 3, 0:16]), op=ALU.mult),
                     reads=[kHf, kst], writes=[kHf])
                c.op("dve", lambda: nc.vector.tensor_tensor(out=Hf32[:], in0=Hf32[:], in1=s_p[:], op=ALU.add), reads=[kHf, ksp], writes=[kHf])
                hf_n, khf_n = Hfb.next()
                c.op("act", lambda: nc.scalar.copy(out=hf_n[:], in_=Hf32[:]), reads=[kHf], writes=[khf_n])
                T["hf_n"] = (hf_n, khf_n)

            def yoff_f():
                o_p, kop = pY2.next()
                for g in range(4):
                    c.op("pe", lambda: nc.tensor.matmul(o_p[:, g * 256:(g + 1) * 256], lhsT=CT[:, g, tok], rhs=hf_t[:, g * 256:(g + 1) * 256],
                                                        start=True, stop=True), reads=[khf], writes=[kop], signal=(g == 3))
                ya_t, kya = yar.next()
                c.op("dve", lambda: nc.vector.tensor_tensor(out=v3(ya_t[:]), in0=v3(o_p[:]), in1=b16(st[:, 1, 0:16]), op=ALU.mult),
                     reads=[kop, kst], writes=[kya])
                T["ya"] = (ya_t, kya)

            def yoff_b():
                ya_t, kya = T["ya"]
                y_p, kyp = T["yp"]
                o_p, kop = pY2.next()
                hl_t, khl = Hbld.next()
                c.dma("sp", hl_t[:], P["hb_scr"][ci], reads=[(kHb, ci)], writes=[khl])
                for g in range(4):
                    c.op("pe", lambda: nc.tensor.matmul(o_p[:, g * 256:(g + 1) * 256], lhsT=CT[:, g, tok], rhs=hl_t[:, g * 256:(g + 1) * 256],
                                                        start=True, stop=True), reads=[khl], writes=[kop], signal=(g == 3))
                yb_t, kyb = ybr.next()
                c.op("dve", lambda: nc.vector.tensor_tensor(out=v3(yb_t[:]), in0=v3(o_p[:]), in1=b16(st[:, 1, 16:32]), op=ALU.mult),
                     reads=[kop, kst], writes=[kyb])
                c.op("pool", lambda: nc.gpsimd.tensor_tensor(out=yb_t[:], in0=yb_t[:], in1=yd_t[:], op=ALU.add), reads=[kyb, kyd], writes=[kyb])
                c.op("dve", lambda: nc.vector.tensor_tensor(out=ya_t[:], in0=ya_t[:], in1=yb_t[:], op=ALU.add), reads=[kya, kyb], writes=[kya])
                z_t, kz = zsr.next()
                c.dma("sp", z_t[:], P["zs"][ci * 128:(ci + 1) * 128, :], reads=[("zs_d", ci)], writes=[kz])
                hh_t, khh = hhr.next()
                c.op("dve", lambda: nc.vector.tensor_tensor(out=hh_t[:], in0=y_p[:], in1=ya_t[:], op=ALU.add), reads=[kyp, kya], writes=[khh])
                T["hh"] = (hh_t, khh, z_t, kz)

            return [state_update, yoff_f, ydiag(0), ydiag(1), ydiag(2), ydiag(3), yoff_b]

        def stage_b2(T):
            ci = T["ci"]
            hh_t, khh, z_t, kz = T["hh"]
            c.op("pool", lambda: nc.gpsimd.tensor_tensor(out=hh_t[:], in0=hh_t[:], in1=z_t[:], op=ALU.mult), reads=[khh, kz], writes=[khh])
            sq_t, ksq = sqr.next()
            s4, ks4 = s4r.next()
            for gg in range(4):
                c.op("act", lambda: nc.scalar.activation(out=sq_t[:, gg * 256:(gg + 1) * 256], in_=hh_t[:, gg * 256:(gg + 1) * 256],
                                                          func=AF.Square, accum_out=s4[:, 0, gg:gg + 1]),
                     reads=[khh], writes=[ksq, ks4])
            c.op("act", lambda: nc.scalar.activation(out=s4[:, 1, :], in_=s4[:, 0, :], func=AF.Sqrt, scale=1.0 / 256, bias=eps[:]),
                 reads=[ks4], writes=[ks4])
            c.op("dve", lambda: nc.vector.reciprocal(out=s4[:, 1, :], in_=s4[:, 1, :]), reads=[ks4], writes=[ks4])
            hb_t, khb = hbr.next()
            for gg in range(4):
                c.op("dve", lambda: nc.vector.scalar_tensor_tensor(out=hb_t[:, gg * 256:(gg + 1) * 256], in0=hh_t[:, gg * 256:(gg + 1) * 256],
                                                                   scalar=s4[:, 1, gg:gg + 1], in1=gnb[:, gg * 256:(gg + 1) * 256],
                                                                   op0=ALU.mult, op1=ALU.mult),
                     reads=[khh, ks4], writes=[khb])
            T["hb"] = (hb_t, khb)

        def stage_b3(T):
            ci = T["ci"]
            hb_t, khb = T["hb"]
            t_t, kt_ = pTn.next()
            for k in range(8):
                c.op("pe", lambda: nc.tensor.transpose(t_t[:, k, :], hb_t[:, k * 128:(k + 1) * 128], ident[:]),
                     reads=[khb], writes=[kt_], signal=(k == 7))
            m_s, kms = mst.next()
            c.op("act", lambda: nc.scalar.copy(out=m_s[:], in_=t_t[:]), reads=[kt_], writes=[kms])
            c.dma("sp", mt_v[:, 4:12, ci * 128:(ci + 1) * 128], m_s[:], reads=[kms], writes=[("mt_d", "ssd", ci)])

        hf_t, khf = Hfb.next()
        c.op("pool", lambda: nc.gpsimd.memset(hf_t[:], 0.0), writes=[khf])
        Ta = {"ci": 0}
        for _ in stage_a(0, Ta):
            pass
        Tp1 = Tp2 = None
        for ci in range(NT):
            steps = stage_b_steps(Ta, hf_t, khf)
            Tn = None
            if ci + 1 < NT:
                Tn = {"ci": ci + 1}
                for _ in stage_a(ci + 1, Tn):
                    if steps:
                        steps.pop(0)()
            while steps:
                steps.pop(0)()
            hf_t, khf = Ta["hf_n"]
            if Tp1 is not None:
                stage_b2(Tp1)
            if Tp2 is not None:
                stage_b3(Tp2)
            Tp2 = Tp1
            Tp1 = Ta
            Ta = Tn
        stage_b2(Tp1)
        stage_b3(Tp2)
        stage_b3(Tp1)
        c.barrier()


def _rope_tables():
    t = np.arange(S)
    row = (t // 64).astype(np.float32)
    col = (t % 64).astype(np.float32)
    freqs = (np.float32(10000.0) ** (-np.arange(0, 32, 2, dtype=np.float32) / np.float32(32))).astype(np.float32)
    ang = np.concatenate([row[:, None] * freqs, col[:, None] * freqs], -1).astype(np.float32)
    return np.cos(ang).astype(np.float32), np.sin(ang).astype(np.float32)


def _na_index():
    dyi = np.zeros((NTYPES, 128, 128), np.int64)
    dxi = np.zeros((NTYPES, 128, 128), np.int64)
    msk = np.zeros((NTYPES, 128, 128), np.float32)
    rep = {0: 0, 1: 1, 2: 5, 3: 14, 4: 15}
    kk = np.arange(128)
    kr, ck = kk // 64, kk % 64
    for n, (cls, j) in enumerate(NA_TYPES):
        i = rep[cls]
        _, kp0, _ = na_cls(i)
        r = 2 * i + kr[None, :]
        cq = ck[None, :]
        rk = 2 * (kp0 + j) + kr[:, None]
        ckk = ck[:, None]
        rs = np.clip(r - 4, 0, 24)
        vrow = (rk >= rs) & (rk < rs + 8)
        cs = np.clip(cq - 8, 0, 48)
        vcol = (ckk >= cs) & (ckk < cs + 16)
        dyi[n] = np.clip(rk - r + 7, 0, 14)
        dxi[n] = np.clip(ckk - cq, -15, 15) + 15
        msk[n] = np.where(vrow & vcol, 0.0, NEG)
    return dyi, dxi, msk


_PROGRAM = None


def _inputs_spec():
    return [("x", [S, D], F32), ("even_mix_norm", [D], F32), ("even_w_in", [D, 4640], F32), ("na_q_norm", [64], F32),
            ("na_k_norm", [64], F32), ("biasg", [NTYPES, 128, 8, 128], F32), ("namask", [NTYPES, 128, 128], F32),
            ("ssd_conv_w", [4, 2048], F32), ("ssd_conv_b", [2048], F32), ("ssd_dt_bias", [32], F32), ("ssd_A_log", [32], F32),
            ("ssd_D", [16], F32), ("ssd_out_norm", [1024], F32), ("even_w_out", [1536, D], F32),
            ("odd_mix_norm", [D], F32), ("odd_w_qkv", [D, 1536], F32), ("gqa_q_norm", [64], F32), ("gqa_k_norm", [64], F32),
            ("odd_w_out", [D, D], F32), ("ffn_norm0", [D], F32), ("ffn_norm1", [D], F32),
            ("ffn_w13_0", [D, 2 * FH], F32), ("ffn_w13_1", [D, 2 * FH], F32), ("ffn_w2_0", [FH, D], F32), ("ffn_w2_1", [FH, D], F32),
            ("cos", [S, 32], F32), ("sin", [S, 32], F32), ("tri_u", [128, 128], F32), ("tri_l", [128, 128], F32),
            ("negm_f", [128, 128], F32), ("negm_b", [128, 128], F32), ("ident", [128, 128], BF16)]


def build_program(phases=("even", "ffn0", "odd", "ffn1")):
    nc = bass.Bass("TRN2", target_bir_lowering=False)
    A = {}
    for n, sh, dt in _inputs_spec():
        A[n] = nc.dram_tensor(n, sh, dt, kind="ExternalInput").ap()
    out = nc.dram_tensor("out", [S, D], F32, kind="ExternalOutput").ap()
    x1 = nc.dram_tensor("x1_scr", [S, D], F32, kind="Internal").ap()
    x2 = nc.dram_tensor("x2_scr", [S, D], F32, kind="Internal").ap()
    x3 = nc.dram_tensor("x3_scr", [S, D], F32, kind="Internal").ap()
    zs = nc.dram_tensor("zs_scr", [S, 1024], BF16, kind="Internal").ap()
    hb = nc.dram_tensor("hb_scr", [NT, 128, 1024], BF16, kind="Internal").ap()
    mt = nc.dram_tensor("mt_scr", [12, 128, S], BF16, kind="Internal").ap()
    chain = [A["x"], x1, x2, x3, out]
    order = ["even", "ffn0", "odd", "ffn1"]
    active = [p for p in order if p in phases]
    cur = A["x"]
    with ExitStack() as es:
        c = Ctx(nc, es)
        ident = es.enter_context(nc.sbuf_tensor("ident_sb", [128, 128], BF16))
        ones32 = es.enter_context(nc.sbuf_tensor("ones32", [128, 128], F32))
        c.dma("sp", ident[:], A["ident"], writes=["ident"])
        c.op("pool", lambda: nc.gpsimd.memset(ones32[:], 1.0), writes=["ones32"])
        c.barrier()
        for n, ph in enumerate(active):
            dst = out if n == len(active) - 1 else chain[order.index(ph) + 1]
            if ph == "even":
                P = {"mix_norm": A["even_mix_norm"], "w_in": A["even_w_in"], "na_gq": A["na_q_norm"], "na_gk": A["na_k_norm"],
                     "biasg": A["biasg"], "namask": A["namask"], "conv_w": A["ssd_conv_w"], "conv_b": A["ssd_conv_b"],
                     "dt_bias": A["ssd_dt_bias"], "A_log": A["ssd_A_log"], "ssd_D": A["ssd_D"], "out_norm": A["ssd_out_norm"],
                     "w_out": A["even_w_out"], "tri_u": A["tri_u"], "tri_l": A["tri_l"], "negm_f": A["negm_f"], "negm_b": A["negm_b"],
                     "zs": zs, "hb_scr": hb, "mt_scr": mt}
                emit_even(c, cur, dst, P, ident, ones32, "e0")
            elif ph == "ffn0":
                emit_ffn(c, cur, dst, A["ffn_norm0"], A["ffn_w13_0"], A["ffn_w2_0"], ident, "f0")
            elif ph == "odd":
                emit_odd(c, cur, dst, A["odd_mix_norm"], A["odd_w_qkv"], A["gqa_q_norm"], A["gqa_k_norm"], A["odd_w_out"],
                         A["cos"], A["sin"], ident, ones32, "o0")
            elif ph == "ffn1":
                emit_ffn(c, cur, dst, A["ffn_norm1"], A["ffn_w13_1"], A["ffn_w2_1"], ident, "f1")
            cur = dst
        c.finish("sp")
    return nc


def make_in_maps(inputs, xs):
    import ml_dtypes
    f = lambda a: np.ascontiguousarray(np.asarray(a, dtype=np.float32))
    cos, sin = _rope_tables()
    dyi, dxi, msk = _na_index()
    rpb = f(inputs["na_rel_bias"])[0]
    biasg = np.ascontiguousarray(rpb[:, dyi, dxi].transpose(1, 2, 0, 3))
    tri_u = np.triu(np.ones((128, 128), np.float32))
    kk = np.arange(128)
    negm_f = np.where(kk[None, :] >= kk[:, None], 0.0, NEG).astype(np.float32)
    negm_b = np.where(kk[None, :] <= kk[:, None], 0.0, NEG).astype(np.float32)
    shared = {
        "even_mix_norm": f(inputs["even_mix_norm"])[0], "even_w_in": f(inputs["even_w_in"])[0],
        "na_q_norm": f(inputs["na_q_norm"])[0], "na_k_norm": f(inputs["na_k_norm"])[0], "biasg": biasg, "namask": msk,
        "ssd_conv_w": f(inputs["ssd_conv_w"])[0], "ssd_conv_b": f(inputs["ssd_conv_b"])[0],
        "ssd_dt_bias": f(inputs["ssd_dt_bias"])[0].reshape(32), "ssd_A_log": f(inputs["ssd_A_log"])[0].reshape(32),
        "ssd_D": f(inputs["ssd_D"])[0], "ssd_out_norm": f(inputs["ssd_out_norm"])[0], "even_w_out": f(inputs["even_w_out"])[0],
        "odd_mix_norm": f(inputs["odd_mix_norm"])[0], "odd_w_qkv": f(inputs["odd_w_qkv"])[0],
        "gqa_q_norm": f(inputs["gqa_q_norm"])[0], "gqa_k_norm": f(inputs["gqa_k_norm"])[0], "odd_w_out": f(inputs["odd_w_out"])[0],
        "ffn_norm0": f(inputs["ffn_norm"])[0], "ffn_norm1": f(inputs["ffn_norm"])[1],
        "ffn_w13_0": f(inputs["ffn_w13"])[0], "ffn_w13_1": f(inputs["ffn_w13"])[1],
        "ffn_w2_0": f(inputs["ffn_w2"])[0], "ffn_w2_1": f(inputs["ffn_w2"])[1],
        "cos": cos, "sin": sin, "tri_u": tri_u, "tri_l": np.ascontiguousarray(tri_u.T),
        "negm_f": negm_f, "negm_b": negm_b, "ident": np.eye(128).astype(ml_dtypes.bfloat16),
    }
    return [dict(shared, x=np.ascontiguousarray(xb)) for xb in xs]


def kernel(**inputs):
    global _PROGRAM
    x = np.asarray(inputs["x"], dtype=np.float32)
    B = x.shape[0]
    if _PROGRAM is None:
        _PROGRAM = build_program()
    in_maps = make_in_maps(inputs, [x[b] for b in range(B)])
    res = run_bass_kernel_spmd(_PROGRAM, in_maps, core_ids=list(range(B)))
    return np.stack([np.asarray(r["out"], dtype=np.float32) for r in res.results], axis=0)
```

```python
import numpy as np
from contextlib import ExitStack
import concourse.bass as bass
import concourse.mybir as mybir
from concourse.bass_utils import run_bass_kernel_spmd

F32 = mybir.dt.float32
BF16 = mybir.dt.bfloat16
AF = mybir.ActivationFunctionType
ALU = mybir.AluOpType
AX = mybir.AxisListType

S = 2048
D = 1024
NT = 16
FH = 2816
NHC = 22
EPS = 1e-6
NEG = -30000.0
N_DUMMY = 0


class Ctx:
    SAME_ENGINE_SYNC = ("act", "dve", "pool")

    def __init__(self, nc, es, n_dma_sems=32):
        self.nc = nc
        self.es = es
        self.eng = {"pe": nc.tensor, "act": nc.scalar, "dve": nc.vector, "pool": nc.gpsimd, "sp": nc.sync}
        self.sem = {}
        self.cnt = {}
        self.nsem = 0
        for e in self.eng:
            self._new_sem(e)
        self.dsem = [es.enter_context(nc.semaphore(f"dma{i}")) for i in range(n_dma_sems)]
        self.dcnt = [0] * n_dma_sems
        self.dnext = {"hw": 0, "sw": 0}
        self.dhalf = n_dma_sems // 2
        self.waited = {e: {} for e in self.eng}
        self.last_w = {}
        self.readers = {}
        self.pend = {e: ([], []) for e in self.eng}
        self.uid = 0

    def _new_sem(self, e):
        self.sem[e] = self.es.enter_context(self.nc.semaphore(f"s_{e}_{self.nsem}"))
        self.nsem += 1
        self.cnt[e] = 0

    def _wait(self, e, tok):
        sem, val, src = tok
        if src == e and e not in self.SAME_ENGINE_SYNC:
            return
        key = id(sem)
        if self.waited[e].get(key, 0) >= val:
            return
        self.waited[e][key] = val
        self.eng[e].wait_ge(sem, val)

    def _deps(self, e, reads, writes):
        for r in reads:
            t = self.last_w.get(r)
            if t is not None:
                self._wait(e, t)
        for w in writes:
            t = self.last_w.get(w)
            if t is not None:
                self._wait(e, t)
            for t in self.readers.get(w, ()):
                self._wait(e, t)

    def _commit(self, tok, reads, writes):
        for w in writes:
            self.last_w[w] = tok
            self.readers[w] = []
        for r in reads:
            self.readers.setdefault(r, []).append(tok)

    def op(self, e, ins_fn, reads=(), writes=(), signal=True):
        reads = list(reads)
        writes = list(writes)
        self._deps(e, reads, writes)
        ins = ins_fn()
        pr, pw = self.pend[e]
        if not signal:
            pr.extend(reads)
            pw.extend(writes)
            return ins
        if self.cnt[e] >= 30000:
            self._new_sem(e)
        self.cnt[e] += 1
        ins.then_inc(self.sem[e], 1)
        tok = (self.sem[e], self.cnt[e], e)
        self._commit(tok, reads + pr, writes + pw)
        self.pend[e] = ([], [])
        return ins

    def dma(self, q, out, in_, reads=(), writes=(), **kw):
        reads = list(reads)
        writes = list(writes)
        kind = "sw" if q == "pool" else "hw"
        i = self.dnext[kind] + (self.dhalf if kind == "sw" else 0)
        self.dnext[kind] = (self.dnext[kind] + 1) % self.dhalf
        skey = ("__dsem", i)
        self._deps(q, reads, writes + [skey])
        ins = self.eng[q].dma_start(out=out, in_=in_, **kw)
        self.dcnt[i] += 16
        ins.then_inc(self.dsem[i], 16)
        tok = (self.dsem[i], self.dcnt[i], "dma")
        self._commit(tok, reads, writes + [skey])
        return ins

    def finish(self, e="sp"):
        for i, s in enumerate(self.dsem):
            if self.dcnt[i]:
                self._wait(e, (s, self.dcnt[i], "dma"))
        for x in self.eng:
            if self.cnt[x] and (x != e or e in self.SAME_ENGINE_SYNC):
                self._wait(e, (self.sem[x], self.cnt[x], x))

    def barrier(self):
        for e in self.eng:
            assert not self.pend[e][0] and not self.pend[e][1], "pending unsignalled ops at barrier"
        for e in self.eng:
            self.finish(e)
        self.last_w.clear()
        self.readers.clear()

    def key(self, name):
        self.uid += 1
        return f"{name}#{self.uid}"


class Ring:
    def __init__(self, c, es, name, shape, dtype, n, psum=False):
        alloc = c.nc.psum_tensor if psum else c.nc.sbuf_tensor
        self.t = [es.enter_context(alloc(f"{name}{i}", shape, dtype)) for i in range(n)]
        self.k = [c.key(name) for _ in range(n)]
        self.i = -1
        self.n = n

    def next(self):
        self.i = (self.i + 1) % self.n
        return self.t[self.i], self.k[self.i]


def bcast_row(ap_1d, n, parts=128):
    return ap_1d.rearrange("(o n) -> o n", o=1).to_broadcast([parts, n])


def emit_norm_T(c, es, x_dram, g_dram, xnT, xnT_key, ident, name):
    nc = c.nc
    gb = es.enter_context(nc.sbuf_tensor(f"{name}_gb", [128, D], F32))
    kgb = c.key("gb")
    c.dma("sp", gb[:], bcast_row(g_dram, D), writes=[kgb])
    xt = Ring(c, es, f"{name}_xt", [128, D], F32, 4)
    sq = Ring(c, es, f"{name}_sq", [128, D], BF16, 3)
    xs = Ring(c, es, f"{name}_xs", [128, D], BF16, 3)
    st = Ring(c, es, f"{name}_st", [128, 2], F32, 4)
    pT = Ring(c, es, f"{name}_pT", [128, 8, 128], BF16, 2, psum=True)
    eps = es.enter_context(nc.sbuf_tensor(f"{name}_eps", [128, 1], F32))
    keps = c.key("eps")
    c.op("pool", lambda: nc.gpsimd.memset(eps[:], EPS), writes=[keps])
    def chain(i):
        x_t, kx = xt.next()
        c.dma("sp", x_t[:], x_dram[i * 128:(i + 1) * 128, :], writes=[kx])
        s_t, ks = sq.next()
        st_t, kst = st.next()
        c.op("act", lambda: nc.scalar.activation(out=s_t[:], in_=x_t[:], func=AF.Square, accum_out=st_t[:, 0:1]),
             reads=[kx], writes=[ks, kst])
        c.op("act", lambda: nc.scalar.activation(out=st_t[:, 1:2], in_=st_t[:, 0:1], func=AF.Sqrt,
                                                  scale=1.0 / D, bias=eps[:]),
             reads=[kst, keps], writes=[kst])
        c.op("dve", lambda: nc.vector.reciprocal(out=st_t[:, 1:2], in_=st_t[:, 1:2]), reads=[kst], writes=[kst])
        xs_t, kxs = xs.next()
        c.op("dve", lambda: nc.vector.scalar_tensor_tensor(out=xs_t[:], in0=x_t[:], scalar=st_t[:, 1:2], in1=gb[:],
                                                           op0=ALU.mult, op1=ALU.mult),
             reads=[kx, kst, kgb], writes=[kxs])
        return xs_t, kxs

    def tr(i, xs_t, kxs):
        p_t, kp = pT.next()
        for k in range(8):
            c.op("pe", lambda: nc.tensor.transpose(p_t[:, k, :], xs_t[:, k * 128:(k + 1) * 128], ident[:]),
                 reads=[kxs], writes=[kp], signal=(k == 7))
        c.op("act", lambda: nc.scalar.copy(out=xnT[:, :, i * 128:(i + 1) * 128], in_=p_t[:]),
             reads=[kp], writes=[(xnT_key, i)])

    cur = chain(0)
    for i in range(NT):
        nxt = chain(i + 1) if i + 1 < NT else None
        tr(i, *cur)
        cur = nxt


def emit_ffn(c, x_in, x_out, g_dram, w13, w2, ident, name):
    nc = c.nc
    with ExitStack() as es:
        E = es.enter_context
        xnT = E(nc.sbuf_tensor(f"{name}_xnT", [128, 8, S], BF16))
        kxnT = c.key("xnT")
        hT = E(nc.sbuf_tensor(f"{name}_hT", [128, NHC, S], BF16))
        khT = c.key("hT")
        W2 = E(nc.sbuf_tensor(f"{name}_W2", [128, NHC, D], BF16))
        kW2 = c.key("W2")
        with ExitStack() as es1:
            emit_norm_T(c, es1, x_in, g_dram, xnT, kxnT, ident, name)
            c.barrier()
        with ExitStack() as es2:
            xn_all = [(kxnT, i) for i in range(NT)]
            w13v = w13.rearrange("(k p) n -> p k n", p=128)
            wg = Ring(c, es2, f"{name}_wg", [128, 8, 128], BF16, 3)
            wu = Ring(c, es2, f"{name}_wu", [128, 8, 128], BF16, 3)
            pg = Ring(c, es2, f"{name}_pg", [128, 1024], F32, 2, psum=True)
            pu = Ring(c, es2, f"{name}_pu", [128, 1024], F32, 2, psum=True)
            sg = Ring(c, es2, f"{name}_sg", [128, 1024], F32, 2)
            for hc in range(NHC):
                wg_t, kwg = wg.next()
                wu_t, kwu = wu.next()
                c.dma("pool", wg_t[:], w13v[:, :, hc * 128:(hc + 1) * 128], writes=[kwg])
                c.dma("pool", wu_t[:], w13v[:, :, FH + hc * 128:FH + (hc + 1) * 128], writes=[kwu])
                c.dma("pool", W2[:, hc, :], w2[hc * 128:(hc + 1) * 128, :], writes=[(kW2, hc)])
                for th in range(2):
                    pg_t, kpg = pg.next()
                    pu_t, kpu = pu.next()
                    for (w_t, kw, p_t, kp) in ((wg_t, kwg, pg_t, kpg), (wu_t, kwu, pu_t, kpu)):
                        for k in range(8):
                            for nb in range(2):
                                t0 = th * 1024 + nb * 512
                                c.op("pe", lambda: nc.tensor.matmul(p_t[:, nb * 512:(nb + 1) * 512], lhsT=w_t[:, k, :],
                                                                    rhs=xnT[:, k, t0:t0 + 512],
                                                                    start=(k == 0), stop=(k == 7)),
                                     reads=[kw] + xn_all, writes=[kp], signal=(k == 7 and nb == 1))
                    sg_t, ksg = sg.next()
                    c.op("act", lambda: nc.scalar.activation(out=sg_t[:], in_=pg_t[:], func=AF.Silu),
                         reads=[kpg], writes=[ksg])
                    c.op("dve", lambda: nc.vector.tensor_tensor(out=hT[:, hc, th * 1024:(th + 1) * 1024], in0=sg_t[:],
                                                                in1=pu_t[:], op=ALU.mult),
                         reads=[ksg, kpu], writes=[(khT, hc, th)])
            c.barrier()
        py = [E(nc.psum_tensor(f"{name}_py{j}", [128, D], F32)) for j in range(4)]
        kpy = [c.key("py") for _ in range(4)]
        xr = Ring(c, es, f"{name}_xr", [128, D], F32, 4)
        yo = Ring(c, es, f"{name}_yo", [128, D], F32, 3)
        for tg in range(4):
            xl = []
            for j in range(4):
                x_t, kx = xr.next()
                c.dma("sp", x_t[:], x_in[(tg * 4 + j) * 128:(tg * 4 + j + 1) * 128, :], writes=[kx])
                xl.append((x_t, kx))
            for hc in range(NHC):
                w_t, kw = W2[:, hc, :], (kW2, hc)
                for j in range(4):
                    tt = tg * 4 + j
                    for ob in range(2):
                        c.op("pe", lambda: nc.tensor.matmul(py[j][:, ob * 512:(ob + 1) * 512],
                                                            lhsT=hT[:, hc, tt * 128:(tt + 1) * 128],
                                                            rhs=w_t[:, ob * 512:(ob + 1) * 512],
                                                            start=(hc == 0), stop=(hc == NHC - 1)),
                             reads=[kw, (khT, hc, tt // 8)], writes=[kpy[j]],
                             signal=(ob == 1 and (hc == NHC - 1 or j == 3)))
            for j in range(4):
                tt = tg * 4 + j
                x_t, kx = xl[j]
                y_t, ky = yo.next()
                c.op("dve", lambda: nc.vector.tensor_tensor(out=y_t[:], in0=x_t[:], in1=py[j][:], op=ALU.add),
                     reads=[kx, kpy[j]], writes=[ky])
                c.dma("sp", x_out[tt * 128:(tt + 1) * 128, :], y_t[:], reads=[ky])
        c.barrier()


def warm_pe(c, ptile, pkey, src, n=20):
    nc = c.nc
    for j in range(n):
        c.op("pe", lambda: nc.tensor.matmul(ptile[:, 0:512], lhsT=src[:, 0:128], rhs=src[:, 0:512], start=True, stop=True),
             reads=[], writes=[pkey], signal=(j == n - 1))


def run_pipeline(iters, look=2):
    deferred = []
    N = len(iters)
    for n in range(min(look, N)):
        iters[n]["qk"]()
    for n in range(N):
        due = [d for d in deferred if d[0] <= n]
        for d in due:
            d[1]()
            deferred.remove(d)
        if n + look < N:
            iters[n + look]["qk"]()
        iters[n]["exp"]()
        iters[n]["pv"]()
        posts = iters[n]["post_factory"]() if "post_factory" in iters[n] else ()
        for delay, fn in posts:
            if delay == 0:
                fn()
            else:
                deferred.append((n + delay, fn))
    for d in deferred:
        d[1]()


def make_norm_post(c, o_t, ko, hb, dst_ap, dst_key, osb, rd, pb, ones32, nrows=128):
    nc = c.nc
    dp = 64 if hb == 0 else 0
    st = {}

    def evac():
        st["o"], st["ko"] = osb.next()
        c.op("dve", lambda: nc.vector.tensor_copy(out=st["o"][0:nrows, :], in_=o_t[0:nrows, :]), reads=[ko], writes=[st["ko"]])

    def bcast():
        b_t, kb = pb.next()
        c.op("pe", lambda: nc.tensor.matmul(b_t[:, :], lhsT=ones32[dp:dp + 1, :], rhs=st["o"][dp:dp + 1, :],
                                            start=True, stop=True), reads=[st["ko"]], writes=[kb])
        st["r"], st["kr"] = rd.next()
        c.op("dve", lambda: nc.vector.reciprocal(out=st["r"][hb:hb + 64, :], in_=b_t[hb:hb + 64, :]), reads=[kb], writes=[st["kr"]])
        c.op("dve", lambda: nc.vector.tensor_tensor(out=dst_ap, in0=st["o"][hb:hb + 64, :], in1=st["r"][hb:hb + 64, :], op=ALU.mult),
             reads=[st["ko"], st["kr"]], writes=[dst_key])

    return [(0, evac), (2, bcast)]


def emit_qkv_proj(c, xnT, kxnT, wv, gq, gk, cos_d, sin_d, ident, QT, kQT, VA, VB, kVA, NH, NKV, dupk, name, QZ=None, koff=8):
    nc = c.nc
    HD = 64
    NQK = NH + NKV
    rope = cos_d is not None
    with ExitStack() as es2:
        E2 = es2.enter_context
        W = E2(nc.sbuf_tensor(f"{name}_W", [128, 8, 1536], BF16))
        kW = c.key("W")
        for k in range(8):
            c.dma("pool", W[:, k, :], wv[:, k, :], writes=[(kW, k)])
        G = E2(nc.sbuf_tensor(f"{name}_G", [128, NQK, HD], F32))
        kG = c.key("G")
        c.dma("sp", G[:, 0:NH, :], gq.rearrange("(o h d) -> o h d", o=1, h=1).to_broadcast([128, NH, HD]), writes=[kG])
        c.dma("sp", G[:, NH:NQK, :], gk.rearrange("(o h d) -> o h d", o=1, h=1).to_broadcast([128, NKV, HD]), writes=[kG])
        c.op("dve", lambda: nc.vector.tensor_scalar(out=G[:, 0:NH, :], in0=G[:, 0:NH, :], scalar1=HD ** -0.5,
                                                    scalar2=None, op0=ALU.mult), reads=[kG], writes=[kG])
        eps = E2(nc.sbuf_tensor(f"{name}_eps2", [128, 1], F32))
        keps = c.key("eps")
        c.op("pool", lambda: nc.gpsimd.memset(eps[:], EPS), writes=[keps])
        if VA.shape[-1] > HD + 1:
            c.op("pool", lambda: nc.gpsimd.memset(VA[:, :, :, HD:], 0.0), writes=[(kVA, "ones")])
        c.op("pool", lambda: nc.gpsimd.memset(VA[:, :, :, HD:HD + 1], 1.0), writes=[(kVA, "ones")])
        c.op("pool", lambda: nc.gpsimd.memset(VB[:, :, :, 0:HD], 0.0), writes=[(kVA, "ones")])
        c.op("pool", lambda: nc.gpsimd.memset(VB[:, :, :, 0:1], 1.0), writes=[(kVA, "ones")])
        if rope:
            cs = E2(nc.sbuf_tensor(f"{name}_cs", [128, NT, 2, 32], F32))
            kcs = c.key("cs")
            c.dma("sp", cs[:, :, 0, :], cos_d.rearrange("(i p) f -> p i f", p=128), writes=[kcs])
            c.dma("sp", cs[:, :, 1, :], sin_d.rearrange("(i p) f -> p i f", p=128), writes=[kcs])
        pq = Ring(c, es2, f"{name}_pq", [128, 1536], F32, 2, psum=True)
        pT = Ring(c, es2, f"{name}_pT2", [128, 8, 128], BF16, 2, psum=True)
        nb_ = 1
        sq = Ring(c, es2, f"{name}_sq2", [128, NQK, HD], F32, nb_)
        st = Ring(c, es2, f"{name}_st2", [128, 2, NQK], F32, 2)
        qn = Ring(c, es2, f"{name}_qn", [128, NQK, HD], F32, 2)
        if rope:
            kdr = Ring(c, es2, f"{name}_kd", [128, NKV, 2, HD], BF16, 2)
            tA = Ring(c, es2, f"{name}_tA", [128, NQK, 32], F32, 1)
            tB = Ring(c, es2, f"{name}_tB", [128, NQK, 32], F32, 1)
            tC = Ring(c, es2, f"{name}_tC", [128, NQK, 32], F32, 1)
            tD = Ring(c, es2, f"{name}_tD", [128, NQK, 32], F32, 1)
            ro = Ring(c, es2, f"{name}_ro", [128, NQK, HD], F32, 1)
        qr = Ring(c, es2, f"{name}_qr", [128, NQK, HD], BF16, 2)
        def make_tile(i):
            T = {}

            def mm():
                p_t, kp = pq.next()
                for cb in range(3):
                    for k in range(8):
                        c.op("pe", lambda: nc.tensor.matmul(p_t[:, cb * 512:(cb + 1) * 512],
                                                            lhsT=xnT[:, k, i * 128:(i + 1) * 128],
                                                            rhs=W[:, k, cb * 512:(cb + 1) * 512],
                                                            start=(k == 0), stop=(k == 7)),
                             reads=[(kW, k), (kxnT, i)], writes=[kp], signal=(k == 7 and cb == 2))
                T['p_t'], T['kp'] = p_t, kp

            def chain():
                p_t, kp = T['p_t'], T['kp']
                pqk = p_t[:, 0:NQK * HD].rearrange("p (h d) -> p h d", d=HD)
                s_t, ks = sq.next()
                st_t, kst = st.next()
                q_t, kq = qn.next()
                r_t, kr = qr.next()
                c.op("act", lambda: nc.scalar.activation(out=s_t[:], in_=pqk, func=AF.Square), reads=[kp], writes=[ks])
                c.op("dve", lambda: nc.vector.tensor_tensor(out=q_t[:], in0=pqk, in1=G[:], op=ALU.mult), reads=[kp, ks, kG], writes=[kq])
                c.op("act", lambda: nc.scalar.copy(out=VA[:, i, :, 0:HD],
                                                   in_=p_t[:, NQK * HD:1536].rearrange("p (g d) -> p g d", d=HD)),
                     reads=[kp, kq], writes=[(kVA, i)])
                c.op("act", lambda: nc.scalar.copy(out=VB[:, i, :, HD:2 * HD],
                                                   in_=p_t[:, NQK * HD:1536].rearrange("p (g d) -> p g d", d=HD)),
                     reads=[kp, kq], writes=[(kVA, i, "b")])
                c.op("dve", lambda: nc.vector.tensor_reduce(out=st_t[:, 0, :], in_=s_t[:], axis=AX.X, op=ALU.add),
                     reads=[ks], writes=[kst])
                c.op("act", lambda: nc.scalar.activation(out=st_t[:, 1, :], in_=st_t[:, 0, :], func=AF.Sqrt,
                                                          scale=1.0 / HD, bias=eps[:]), reads=[kst, keps], writes=[kst])
                c.op("dve", lambda: nc.vector.reciprocal(out=st_t[:, 1, :], in_=st_t[:, 1, :]), reads=[kst], writes=[kst])
                rstd_b = st_t[:, 1, :].unsqueeze(2).to_broadcast([128, NQK, HD])
                if not rope:
                    c.op("dve", lambda: nc.vector.tensor_tensor(out=r_t[:], in0=q_t[:], in1=rstd_b, op=ALU.mult),
                         reads=[kq, kst], writes=[(kr, 0), (kr, 1)])
                else:
                    qv = q_t[:].rearrange("p h (f two) -> p h f two", two=2)
                    x0 = qv[:, :, :, 0]
                    x1 = qv[:, :, :, 1]
                    cosb = cs[:, i, 0, :].unsqueeze(1).to_broadcast([128, NQK, 32])
                    sinb = cs[:, i, 1, :].unsqueeze(1).to_broadcast([128, NQK, 32])
                    a_t, ka = tA.next()
                    b_t, kb = tB.next()
                    c_t, kc = tC.next()
                    d_t, kd = tD.next()
                    o_t, ko = ro.next()
                    ov = o_t[:].rearrange("p h (f two) -> p h f two", two=2)
                    c.op("dve", lambda: nc.vector.tensor_tensor(out=a_t[:], in0=x0, in1=cosb, op=ALU.mult), reads=[kq, kcs], writes=[ka])
                    c.op("dve", lambda: nc.vector.tensor_tensor(out=b_t[:], in0=x1, in1=sinb, op=ALU.mult), reads=[kq, kcs], writes=[kb])
                    c.op("dve", lambda: nc.vector.tensor_tensor(out=ov[:, :, :, 0], in0=a_t[:], in1=b_t[:], op=ALU.subtract),
                         reads=[ka, kb], writes=[(ko, 0)])
                    c.op("pool", lambda: nc.gpsimd.tensor_tensor(out=c_t[:], in0=x0, in1=sinb, op=ALU.mult), reads=[kq, kcs], writes=[kc])
                    c.op("pool", lambda: nc.gpsimd.tensor_tensor(out=d_t[:], in0=x1, in1=cosb, op=ALU.mult), reads=[kq, kcs], writes=[kd])
                    c.op("pool", lambda: nc.gpsimd.tensor_tensor(out=ov[:, :, :, 1], in0=c_t[:], in1=d_t[:], op=ALU.add),
                         reads=[kc, kd], writes=[(ko, 1)])
                    c.op("dve", lambda: nc.vector.tensor_tensor(out=r_t[:], in0=o_t[:], in1=rstd_b, op=ALU.mult),
                         reads=[(ko, 0), (ko, 1), kst], writes=[(kr, 0), (kr, 1)])
                    kd_t, kkd = kdr.next()
                    c.op("pool", lambda: nc.gpsimd.tensor_copy(out=kd_t[:], in_=r_t[:, NH:NQK, :].unsqueeze(2).to_broadcast([128, NKV, 2, HD])),
                         reads=[(kr, 0), (kr, 1)], writes=[kkd])
                T['r_t'], T['kr'] = r_t, kr
                if dupk:
                    T['kd_t'], T['kkd'] = kd_t, kkd

            def tr():
                r_t, kr = T['r_t'], T['kr']
                if dupk:
                    kd_t, kkd = T['kd_t'], T['kkd']
                rflat = r_t[:].rearrange("p h d -> p (h d)")
                if dupk:
                    kflat = kd_t[:].rearrange("p g t d -> p (g t d)")
                t_t, kt_ = pT.next()
                for j in range(8):
                    c.op("pe", lambda: nc.tensor.transpose(t_t[:, j, :], rflat[:, j * 128:(j + 1) * 128], ident[:]),
                         reads=[(kr, 0), (kr, 1)], writes=[kt_], signal=(j == 7))
                if QZ is None:
                    c.op("act", lambda: nc.scalar.copy(out=QT[:, 0:8, i * 128:(i + 1) * 128], in_=t_t[:]),
                         reads=[kt_], writes=[(kQT, i, 0)])
                else:
                    nqc = NH // 2
                    qzv = QZ[:].rearrange("p (c two) s -> p c two s", two=2)
                    c.op("act", lambda: nc.scalar.copy(out=qzv[0:64, :, 0, i * 128:(i + 1) * 128], in_=t_t[0:64, 0:nqc, :]),
                         reads=[kt_, "qz0", "qz1"], writes=[(kQT, i, 0)])
                    c.op("dve", lambda: nc.vector.tensor_copy(out=qzv[64:128, :, 1, i * 128:(i + 1) * 128], in_=t_t[64:128, 0:nqc, :]),
                         reads=[kt_, (kQT, i, 0)], writes=[(kQT, i, 2)])
                    if not dupk:
                        c.op("dve", lambda: nc.vector.tensor_copy(out=QT[:, koff:koff + NKV // 2, i * 128:(i + 1) * 128],
                                                                  in_=t_t[:, nqc:nqc + NKV // 2, :]),
                             reads=[kt_, (kQT, i, 2)], writes=[(kQT, i, 3)])
                if dupk:
                    t_t, kt_ = pT.next()
                    for j in range(NKV):
                        c.op("pe", lambda: nc.tensor.transpose(t_t[:, j, :], kflat[:, j * 128:(j + 1) * 128], ident[:]),
                             reads=[kkd], writes=[kt_], signal=(j == NKV - 1))
                    c.op("act", lambda: nc.scalar.copy(out=QT[:, koff:koff + NKV, i * 128:(i + 1) * 128], in_=t_t[:, 0:NKV, :]),
                         reads=[kt_], writes=[(kQT, i, 1)])
            return mm, chain, tr

        tiles = [make_tile(i) for i in range(NT)]
        tiles[0][0]()
        tiles[1][0]()
        tiles[0][1]()
        for i in range(NT):
            if i + 2 < NT:
                tiles[i + 2][0]()
            if i + 1 < NT:
                tiles[i + 1][1]()
            tiles[i][2]()
        c.barrier()


def emit_odd(c, x_in, x_out, g_dram, wqkv, gq, gk, wo, cos_d, sin_d, ident, ones32, name):
    nc = c.nc
    NH, NKV, HD = 16, 4, 64
    NQK = NH + NKV
    with ExitStack() as es:
        E = es.enter_context
        QT = E(nc.sbuf_tensor(f"{name}_QT", [128, NKV, S], BF16))
        kQT = c.key("QT")
        VAB = E(nc.sbuf_tensor(f"{name}_VAB", [128, NT, NKV, 192], BF16))
        VA = VAB[:, :, :, 0:128]
        VB = VAB[:, :, :, 64:192]
        kVA = c.key("VA")
        QZ = E(nc.sbuf_tensor(f"{name}_QZ", [128, NH, S], BF16))
        c.op("pool", lambda: nc.gpsimd.memset(QZ[:, 0:NH // 2, :], 0.0), writes=["qz0"])
        c.op("pool", lambda: nc.gpsimd.memset(QZ[:, NH // 2:NH, :], 0.0), writes=["qz1"])
        with ExitStack() as es0:
            xnT = es0.enter_context(nc.sbuf_tensor(f"{name}_xnT", [128, 8, S], BF16))
            kxnT = c.key("xnT")
            with ExitStack() as es1:
                emit_norm_T(c, es1, x_in, g_dram, xnT, kxnT, ident, name)
                c.barrier()
            emit_qkv_proj(c, xnT, kxnT, wqkv.rearrange("(k p) n -> p k n", p=128), gq, gk, cos_d, sin_d, ident,
                          QT, kQT, VA, VB, kVA, NH, NKV, True, name, QZ=QZ, koff=0)
        OT = E(nc.sbuf_tensor(f"{name}_OT", [128, 8, S], BF16))
        kOT = c.key("OT")
        Wo = E(nc.sbuf_tensor(f"{name}_Wo", [128, 8, D], BF16))
        kWo = c.key("Wo")
        wov = wo.rearrange("(k p) n -> p k n", p=128)
        for k in range(8):
            c.dma("pool", Wo[:, k, :], wov[:, k, :], writes=[(kWo, k)])
        with ExitStack() as es3:
            ps = Ring(c, es3, f"{name}_ps", [128, 2, 512], F32, 3, psum=True)
            po = Ring(c, es3, f"{name}_po", [128, 512], F32, 1, psum=True)
            pb = Ring(c, es3, f"{name}_pb", [128, 512], F32, 1, psum=True)
            PT = Ring(c, es3, f"{name}_PT", [128, 2, 512], BF16, 3)
            osb = Ring(c, es3, f"{name}_osb", [128, 512], F32, 2)
            rd = Ring(c, es3, f"{name}_rd", [128, 512], F32, 2)
            iters = []
            for h in range(NH):
                for qb in range(4):
                    grp = {}
                    for kt2 in range(8):
                        def mk(h=h, qb=qb, kt2=kt2, grp=grp):
                            g = h // 4
                            hb = (h % 2) * 64
                            stt = {}

                            def qk():
                                stt["s"], stt["ks"] = ps.next()
                                for _ in range(N_DUMMY):
                                    c.op("pe", lambda: nc.tensor.matmul(stt["s"][:, 0, :], lhsT=QT[:, 0, 0:128], rhs=QT[:, 0, 0:512],
                                                                        start=True, stop=True), reads=[], writes=[stt["ks"]], signal=False)
                                for j in range(2):
                                    kt = kt2 * 2 + j
                                    c.op("pe", lambda: nc.tensor.matmul(stt["s"][:, j, :], lhsT=QT[:, g, kt * 128:(kt + 1) * 128],
                                                                        rhs=QZ[:, h, qb * 512:(qb + 1) * 512], start=True, stop=True),
                                         reads=[], writes=[stt["ks"]], signal=(j == 1))

                            def ex():
                                stt["p"], stt["kp"] = PT.next()
                                c.op("act", lambda: nc.scalar.activation(out=stt["p"][:], in_=stt["s"][:], func=AF.Exp),
                                     reads=[stt["ks"]], writes=[stt["kp"]])

                            def pv():
                                if kt2 == 0:
                                    grp["o"], grp["ko"] = po.next()
                                o_t, ko = grp["o"], grp["ko"]
                                for j in range(2):
                                    kt = kt2 * 2 + j
                                    if hb == 0:
                                        c.op("pe", lambda: nc.tensor.matmul(o_t[:, :], lhsT=VA[:, kt, g, :], rhs=stt["p"][:, j, :],
                                                                            start=(kt == 0), stop=(kt == 15)),
                                             reads=[stt["kp"]], writes=[ko], signal=(j == 1))
                                    else:
                                        c.op("pe", lambda: nc.tensor.matmul(o_t[:, :], lhsT=VB[:, kt, g, :], rhs=stt["p"][:, j, :],
                                                                            start=(kt == 0), stop=(kt == 15)),
                                             reads=[stt["kp"]], writes=[ko], signal=(j == 1))

                            it = {"qk": qk, "exp": ex, "pv": pv}
                            if kt2 == 7:
                                def post_factory():
                                    return make_norm_post(c, grp["o"], grp["ko"], hb,
                                                          OT[hb:hb + 64, h // 2, qb * 512:(qb + 1) * 512], (kOT, h, qb), osb, rd, pb, ones32)
                                it["post_factory"] = post_factory
                            return it
                        iters.append(mk())
            warm_pe(c, pb.t[0], pb.k[0], QT[:, 0, :])
            run_pipeline(iters)
            c.barrier()
        with ExitStack() as es4:
            py = Ring(c, es4, f"{name}_py", [128, D], F32, 2, psum=True)
            xr = Ring(c, es4, f"{name}_xr", [128, D], F32, 4)
            yo = Ring(c, es4, f"{name}_yo", [128, D], F32, 3)
            xq = []
            for tt in range(3):
                x_t, kx = xr.next()
                c.dma("sp", x_t[:], x_in[tt * 128:(tt + 1) * 128, :], writes=[kx])
                xq.append((x_t, kx))
            for tt in range(NT):
                y_p, kyp = py.next()
                for k in range(8):
                    for ob in range(2):
                        c.op("pe", lambda: nc.tensor.matmul(y_p[:, ob * 512:(ob + 1) * 512], lhsT=OT[:, k, tt * 128:(tt + 1) * 128],
                                                            rhs=Wo[:, k, ob * 512:(ob + 1) * 512], start=(k == 0), stop=(k == 7)),
                             reads=[(kWo, k)], writes=[kyp], signal=(k == 7 and ob == 1))
                x_t, kx = xq.pop(0)
                y_t, ky = yo.next()
                c.op("dve", lambda: nc.vector.tensor_tensor(out=y_t[:], in0=x_t[:], in1=y_p[:], op=ALU.add),
                     reads=[kx, kyp], writes=[ky])
                if tt + 3 < NT:
                    x_n, kxn = xr.next()
                    c.dma("sp", x_n[:], x_in[(tt + 3) * 128:(tt + 4) * 128, :], writes=[kxn])
                    xq.append((x_n, kxn))
                c.dma("sp", x_out[tt * 128:(tt + 1) * 128, :], y_t[:], reads=[ky])
            c.barrier()


NA_TYPES = [(0, j) for j in range(4)] + [(1, j) for j in range(4)] + [(2, j) for j in range(5)] + \
           [(3, j) for j in range(4)] + [(4, j) for j in range(4)]
NA_TIX = {t: n for n, t in enumerate(NA_TYPES)}
NTYPES = len(NA_TYPES)


def na_cls(i):
    if i == 0:
        return 0, 0, 4
    if i == 1:
        return 1, 0, 4
    if i == 14:
        return 3, 12, 4
    if i == 15:
        return 4, 12, 4
    return 2, i - 2, 5


def emit_even(c, x_in, x_out, P, ident, ones32, name):
    nc = c.nc
    HD = 64
    w_in_v = P["w_in"].rearrange("(k p) n -> p k n", p=128)
    with ExitStack() as es:
        E = es.enter_context
        mt_v = P["mt_scr"].rearrange("k p s -> p k s")
        kMT = c.key("MT")
        DT = E(nc.sbuf_tensor(f"{name}_DT", [128, NT, 32], F32))
        AA = E(nc.sbuf_tensor(f"{name}_AA", [128, NT, 32], F32))
        kDT = c.key("DT")
        kAA = c.key("AA")
        eps = E(nc.sbuf_tensor(f"{name}_epsE", [128, 1], F32))
        keps = c.key("eps")
        c.op("pool", lambda: nc.gpsimd.memset(eps[:], EPS), writes=[keps])
        with ExitStack() as esn:
            if True:
                En = esn.enter_context
                MT = En(nc.sbuf_tensor(f"{name}_MTn", [128, 4, S], BF16))
                QT = En(nc.sbuf_tensor(f"{name}_QT", [128, 4, S], BF16))
                kQT = c.key("QT")
                QZ = En(nc.sbuf_tensor(f"{name}_QZ", [128, 8, S], BF16))
                c.op("pool", lambda: nc.gpsimd.memset(QZ[:, 0:4, :], 0.0), writes=["qz0"])
                c.op("pool", lambda: nc.gpsimd.memset(QZ[:, 4:8, :], 0.0), writes=["qz1"])
                VAB = En(nc.sbuf_tensor(f"{name}_VAB", [128, NT, 8, 192], BF16))
                VA = VAB[:, :, :, 0:128]
                VB = VAB[:, :, :, 64:192]
                kVA = c.key("VA")
                with ExitStack() as esx:
                    xnT = esx.enter_context(nc.sbuf_tensor(f"{name}_xnT", [128, 8, S], BF16))
                    kxnT = c.key("xnT")
                    with ExitStack() as es1:
                        emit_norm_T(c, es1, x_in, P["mix_norm"], xnT, kxnT, ident, name)
                        c.barrier()
                    emit_qkv_proj(c, xnT, kxnT, w_in_v[:, :, 0:1536], P["na_gq"], P["na_gk"], None, None, ident,
                                  QT, kQT, VA, VB, kVA, 8, 8, False, name + "n", QZ=QZ, koff=0)
                BM = En(nc.sbuf_tensor(f"{name}_BM", [128, NTYPES, 8, 128], BF16))
                kBM = c.key("BM")
                with ExitStack() as esb:
                    bg = Ring(c, esb, f"{name}_bg", [128, 8, 128], F32, 2)
                    mk = Ring(c, esb, f"{name}_mk", [128, 128], F32, 2)
                    for t in range(NTYPES):
                        b_t, kb = bg.next()
                        m_t, km = mk.next()
                        c.dma("sp", b_t[:], P["biasg"][t], writes=[kb])
                        c.dma("sp", m_t[:], P["namask"][t], writes=[km])
                        c.op("dve", lambda: nc.vector.tensor_tensor(out=BM[:, t, :, :], in0=b_t[:],
                                                                    in1=m_t[:].unsqueeze(1).to_broadcast([128, 8, 128]), op=ALU.add),
                             reads=[kb, km], writes=[kBM])
                    c.barrier()
                with ExitStack() as es3:
                    ps = Ring(c, es3, f"{name}_ps", [128, 8, 128], F32, 3, psum=True)
                    po = Ring(c, es3, f"{name}_po", [128, 512], F32, 1, psum=True)
                    pb = Ring(c, es3, f"{name}_pb", [128, 512], F32, 1, psum=True)
                    PT = Ring(c, es3, f"{name}_PT", [128, 5, 128], BF16, 3)
                    osb = Ring(c, es3, f"{name}_osb", [128, 512], F32, 2)
                    rd = Ring(c, es3, f"{name}_rd", [128, 512], F32, 2)
                    iters = []
                    for h in range(8):
                        for ig in range(4):
                            grp = {}
                            for ii in range(4):
                                def mk(h=h, ig=ig, ii=ii, grp=grp):
                                    hb = (h % 2) * 64
                                    qc = h // 2
                                    kc = 4 + h // 2
                                    i = ig * 4 + ii
                                    cls, kp0, ntl = na_cls(i)
                                    stt = {}

                                    def qk():
                                        stt["s"], stt["ks"] = ps.next()
                                        t0 = NA_TIX[(cls, 0)]
                                        c.op("pe", lambda: nc.tensor.matmul(stt["s"][:, 0:4, :], lhsT=ident[:], rhs=BM[:, t0:t0 + 4, h, :],
                                                                            start=True, stop=False),
                                             reads=[], writes=[stt["ks"]], signal=False)
                                        for j in range(4):
                                            kp = kp0 + j
                                            c.op("pe", lambda: nc.tensor.matmul(stt["s"][:, j, :], lhsT=QT[:, h // 2, kp * 128:(kp + 1) * 128],
                                                                                rhs=QZ[:, h, i * 128:(i + 1) * 128], start=False, stop=(j == 3)),
                                                 reads=[], writes=[stt["ks"]], signal=(j == 3 and ntl == 4))
                                        if ntl == 5:
                                            kp = kp0 + 4
                                            c.op("pe", lambda: nc.tensor.matmul(stt["s"][:, 4, :], lhsT=ident[:], rhs=BM[:, t0 + 4, h, :],
                                                                                start=True, stop=False),
                                                 reads=[], writes=[stt["ks"]], signal=False)
                                            c.op("pe", lambda: nc.tensor.matmul(stt["s"][:, 4, :], lhsT=QT[:, h // 2, kp * 128:(kp + 1) * 128],
                                                                                rhs=QZ[:, h, i * 128:(i + 1) * 128], start=False, stop=True),
                                                 reads=[], writes=[stt["ks"]], signal=True)

                                    def ex():
                                        stt["p"], stt["kp"] = PT.next()
                                        c.op("act", lambda: nc.scalar.activation(out=stt["p"][:, 0:ntl, :], in_=stt["s"][:, 0:ntl, :], func=AF.Exp),
                                             reads=[stt["ks"]], writes=[stt["kp"]])

                                    def pv():
                                        if ii == 0:
                                            grp["o"], grp["ko"] = po.next()
                                        o_t, ko = grp["o"], grp["ko"]
                                        for j in range(ntl):
                                            kp = kp0 + j
                                            if hb == 0:
                                                c.op("pe", lambda: nc.tensor.matmul(o_t[:, ii * 128:(ii + 1) * 128], lhsT=VA[:, kp, h, :],
                                                                                    rhs=stt["p"][:, j, :], start=(j == 0), stop=(j == ntl - 1)),
                                                     reads=[stt["kp"]], writes=[ko], signal=(j == ntl - 1))
                                            else:
                                                c.op("pe", lambda: nc.tensor.matmul(o_t[:, ii * 128:(ii + 1) * 128], lhsT=VB[:, kp, h, :],
                                                                                    rhs=stt["p"][:, j, :], start=(j == 0), stop=(j == ntl - 1)),
                                                     reads=[stt["kp"]], writes=[ko], signal=(j == ntl - 1))

                                    it = {"qk": qk, "exp": ex, "pv": pv}
                                    if ii == 3:
                                        def post_factory():
                                            return make_norm_post(c, grp["o"], grp["ko"], hb,
                                                                  MT[hb:hb + 64, h // 2, ig * 512:(ig + 1) * 512], (kMT, h, ig), osb, rd, pb, ones32,
                                                                  nrows=128)
                                        it["post_factory"] = post_factory
                                    return it
                                iters.append(mk())
                    warm_pe(c, pb.t[0], pb.k[0], QT[:, 0, :])
                    run_pipeline(iters)
                    c.barrier()
                    c.dma("sp", mt_v[:, 0:4, :], MT[:], writes=[("mt_d", "na")])
                    c.barrier()
        XT = E(nc.sbuf_tensor(f"{name}_XT", [128, NT, 1024], BF16))
        kXT = c.key("XT")
        BT = E(nc.sbuf_tensor(f"{name}_BT", [128, 4, S], BF16))
        CT = E(nc.sbuf_tensor(f"{name}_CT", [128, 4, S], BF16))
        kBT = c.key("BT")
        kCT = c.key("CT")
        Btok = E(nc.sbuf_tensor(f"{name}_Btok", [128, NT, 512], BF16))
        kBtok = c.key("Btok")
        with ExitStack() as esx:
            xnT = esx.enter_context(nc.sbuf_tensor(f"{name}_xnTb", [128, 8, S], BF16))
            kxnT = c.key("xnT")
            with ExitStack() as es1:
                emit_norm_T(c, es1, x_in, P["mix_norm"], xnT, kxnT, ident, name + "b")
                c.barrier()
            xn_all = [(kxnT, i) for i in range(NT)]
            with ExitStack() as esz:
                Ez = esz.enter_context
                Wz = Ez(nc.sbuf_tensor(f"{name}_Wz", [128, 8, 1056], BF16))
                kWz = c.key("Wz")
                for k in range(8):
                    c.dma("pool", Wz[:, k, 0:1024], w_in_v[:, k, 1536:2560], writes=[(kWz, k)])
                    c.dma("pool", Wz[:, k, 1024:1056], w_in_v[:, k, 4608:4640], writes=[(kWz, k, "d")])
                dtb = Ez(nc.sbuf_tensor(f"{name}_dtb", [128, 32], F32))
                alb = Ez(nc.sbuf_tensor(f"{name}_alb", [128, 32], F32))
                kdtb = c.key("dtb")
                kalb = c.key("alb")
                c.dma("sp", dtb[:], bcast_row(P["dt_bias"], 32), writes=[kdtb])
                c.dma("sp", alb[:], bcast_row(P["A_log"], 32), writes=[kalb])
                pz = Ring(c, esz, f"{name}_pz", [128, 1024], F32, 2, psum=True)
                pd = Ring(c, esz, f"{name}_pd", [128, 512], F32, 2, psum=True)
                zb = Ring(c, esz, f"{name}_zb", [128, 1024], BF16, 3)
                for i in range(NT):
                    z_p, kzp = pz.next()
                    d_p, kdp = pd.next()
                    for cb in range(2):
                        for k in range(8):
                            c.op("pe", lambda: nc.tensor.matmul(z_p[:, cb * 512:(cb + 1) * 512], lhsT=xnT[:, k, i * 128:(i + 1) * 128],
                                                                rhs=Wz[:, k, cb * 512:(cb + 1) * 512], start=(k == 0), stop=(k == 7)),
                                 reads=[(kWz, k), (kxnT, i)], writes=[kzp], signal=(k == 7 and cb == 1))
                    for k in range(8):
                        c.op("pe", lambda: nc.tensor.matmul(d_p[:, 0:32], lhsT=xnT[:, k, i * 128:(i + 1) * 128],
                                                            rhs=Wz[:, k, 1024:1056], start=(k == 0), stop=(k == 7)),
                             reads=[(kWz, k, "d"), (kxnT, i)], writes=[kdp], signal=(k == 7))
                    z_t, kz = zb.next()
                    c.op("act", lambda: nc.scalar.activation(out=z_t[:], in_=z_p[:], func=AF.Silu), reads=[kzp], writes=[kz])
                    c.dma("sp", P["zs"][i * 128:(i + 1) * 128, :], z_t[:], reads=[kz], writes=[("zs_d", i)])
                    c.op("dve", lambda: nc.vector.tensor_tensor(out=DT[:, i, :], in0=d_p[:, 0:32], in1=dtb[:], op=ALU.add),
                         reads=[kdp, kdtb], writes=[kDT])
                c.op("act", lambda: nc.scalar.activation(out=DT[:], in_=DT[:], func=AF.Exp), reads=[kDT], writes=[kDT])
                c.op("act", lambda: nc.scalar.activation(out=DT[:], in_=DT[:], func=AF.Ln, bias=1.0), reads=[kDT], writes=[kDT])
                c.op("act", lambda: nc.scalar.activation(out=alb[:], in_=alb[:], func=AF.Exp), reads=[kalb], writes=[kalb])
                c.op("dve", lambda: nc.vector.tensor_scalar(out=alb[:], in0=alb[:], scalar1=-1.0, scalar2=None, op0=ALU.mult),
                     reads=[kalb], writes=[kalb])
                c.op("dve", lambda: nc.vector.tensor_tensor(out=AA[:], in0=DT[:], in1=alb[:].unsqueeze(1).to_broadcast([128, NT, 32]),
                                                            op=ALU.mult), reads=[kDT, kalb], writes=[kAA])
                c.barrier()
            with ExitStack() as esc:
                Ec = esc.enter_context
                cw = Ec(nc.sbuf_tensor(f"{name}_cw", [128, 16, 4], F32))
                cbias = Ec(nc.sbuf_tensor(f"{name}_cb", [128, 16], F32))
                kcw = c.key("cw")
                for j in range(4):
                    c.dma("sp", cw[:, :, j], P["conv_w"][j].rearrange("(c p) -> p c", p=128), writes=[kcw], allow_slow_non_contiguous=True)
                c.dma("sp", cbias[:], P["conv_b"].rearrange("(c p) -> p c", p=128), writes=[kcw], allow_slow_non_contiguous=True)
                wx = Ring(c, esc, f"{name}_wx", [128, 8, 128], BF16, 3)
                px = Ring(c, esc, f"{name}_px", [128, 1024], F32, 2, psum=True)
                pTr = Ring(c, esc, f"{name}_pTr", [128, 8, 128], BF16, 2, psum=True)
                Rr = Ring(c, esc, f"{name}_R", [128, S + 3], F32, 2)
                acc = Ring(c, esc, f"{name}_acc", [128, S], F32, 2)
                xo = Ring(c, esc, f"{name}_xo", [128, S], BF16, 2)
                for t_, k_ in zip(Rr.t, Rr.k):
                    c.op("pool", lambda: nc.gpsimd.memset(t_[:, 0:2], 0.0), writes=[k_])
                    c.op("pool", lambda: nc.gpsimd.memset(t_[:, S + 2:S + 3], 0.0), writes=[k_])
                def make_ch(ch):
                    T = {}

                    def front():
                            w_t, kw = wx.next()
                            c.dma("pool", w_t[:], w_in_v[:, :, 2560 + ch * 128:2560 + (ch + 1) * 128], writes=[kw])
                            R_t, kR = Rr.next()
                            for half in range(2):
                                p_t, kp = px.next()
                                for k in range(8):
                                    for nb in range(2):
                                        t0 = half * 1024 + nb * 512
                                        c.op("pe", lambda: nc.tensor.matmul(p_t[:, nb * 512:(nb + 1) * 512], lhsT=w_t[:, k, :],
                                                                            rhs=xnT[:, k, t0:t0 + 512], start=(k == 0), stop=(k == 7)),
                                             reads=[kw] + xn_all, writes=[kp], signal=(k == 7 and nb == 1))
                                c.op("act", lambda: nc.scalar.copy(out=R_t[:, 2 + half * 1024:2 + (half + 1) * 1024], in_=p_t[:]),
                                     reads=[kp], writes=[kR])
                            T['R'] = (R_t, kR)

                    def back1():
                            R_t, kR = T['R']
                            a_t, ka = acc.next()
                            c.op("act", lambda: nc.scalar.activation(out=a_t[:], in_=R_t[:, 0:S], func=AF.Identity,
                                                                      scale=cw[:, ch, 0:1], bias=cbias[:, ch:ch + 1]),
                                 reads=[kR, kcw], writes=[ka])
                            c.op("dve", lambda: nc.vector.scalar_tensor_tensor(out=a_t[:], in0=R_t[:, 1:S + 1], scalar=cw[:, ch, 1:2], in1=a_t[:],
                                                                               op0=ALU.mult, op1=ALU.add), reads=[kR, kcw, ka], writes=[ka])
                            c.op("dve", lambda: nc.vector.scalar_tensor_tensor(out=a_t[:], in0=R_t[:, 2:S + 2], scalar=cw[:, ch, 2:3], in1=a_t[:],
                                                                                op0=ALU.mult, op1=ALU.add), reads=[kR, kcw, ka], writes=[ka])
                            c.op("dve", lambda: nc.vector.scalar_tensor_tensor(out=a_t[:], in0=R_t[:, 3:S + 3], scalar=cw[:, ch, 3:4], in1=a_t[:],
                                                                               op0=ALU.mult, op1=ALU.add), reads=[kR, kcw, ka], writes=[ka])
                            if ch < 8:
                                o_t, ko = xo.next()
                                dst = o_t[:]
                                wk = [ko]
                            elif ch < 12:
                                dst = BT[:, ch - 8, :]
                                ko = (kBT, ch - 8)
                                wk = [ko]
                            else:
                                dst = CT[:, ch - 12, :]
                                wk = [(kCT, ch - 12)]
                            T['dst'] = (dst, wk, a_t, ka)

                    def back2():
                            dst, wk, a_t, ka = T['dst']
                            c.op("act", lambda: nc.scalar.activation(out=dst, in_=a_t[:], func=AF.Silu), reads=[ka], writes=wk)
                            if ch < 12:
                                for half in range(2):
                                    t_t, kt_ = pTr.next()
                                    for tt in range(8):
                                        t0 = (half * 8 + tt) * 128
                                        c.op("pe", lambda: nc.tensor.transpose(t_t[:, tt, :], dst[:, t0:t0 + 128], ident[:]),
                                             reads=wk, writes=[kt_], signal=(tt == 7))
                                    if ch < 8:
                                        c.op("dve", lambda: nc.vector.tensor_copy(out=XT[:, half * 8:(half + 1) * 8, ch * 128:(ch + 1) * 128], in_=t_t[:]),
                                             reads=[kt_], writes=[(kXT, ch, half)])
                                    else:
                                        c.op("dve", lambda: nc.vector.tensor_copy(out=Btok[:, half * 8:(half + 1) * 8, (ch - 8) * 128:(ch - 7) * 128],
                                                                                  in_=t_t[:]), reads=[kt_], writes=[(kBtok, ch, half)])
                    return front, back1, back2

                chs = [make_ch(ch) for ch in range(16)]
                chs[0][0]()
                chs[1][0]()
                chs[0][1]()
                for ch in range(16):
                    if ch + 2 < 16:
                        chs[ch + 2][0]()
                    if ch + 1 < 16:
                        chs[ch + 1][1]()
                    chs[ch][2]()
                c.barrier()
        emit_ssd(c, P, XT, BT, CT, Btok, DT, AA, eps, ident, ones32, name)
        with ExitStack() as es4:
            MT = es4.enter_context(nc.sbuf_tensor(f"{name}_MT", [128, 12, S], BF16))
            for k in range(12):
                c.dma("sp", MT[:, k, :], mt_v[:, k, :], writes=[(kMT, "ld", k)])
            Wo = es4.enter_context(nc.sbuf_tensor(f"{name}_Wo", [128, 12, D], BF16))
            kWo = c.key("Wo")
            wov = P["w_out"].rearrange("(k p) n -> p k n", p=128)
            for k in range(12):
                c.dma("pool", Wo[:, k, :], wov[:, k, :], writes=[(kWo, k)])
            py = Ring(c, es4, f"{name}_py", [128, D], F32, 2, psum=True)
            xr = Ring(c, es4, f"{name}_xr", [128, D], F32, 4)
            yo = Ring(c, es4, f"{name}_yo", [128, D], F32, 3)
            xq = []
            for tt in range(3):
                x_t, kx = xr.next()
                c.dma("sp", x_t[:], x_in[tt * 128:(tt + 1) * 128, :], writes=[kx])
                xq.append((x_t, kx))
            for tt in range(NT):
                y_p, kyp = py.next()
                for k in range(12):
                    for ob in range(2):
                        c.op("pe", lambda: nc.tensor.matmul(y_p[:, ob * 512:(ob + 1) * 512], lhsT=MT[:, k, tt * 128:(tt + 1) * 128],
                                                            rhs=Wo[:, k, ob * 512:(ob + 1) * 512], start=(k == 0), stop=(k == 11)),
                             reads=[(kWo, k), (kMT, "ld", k)], writes=[kyp], signal=(k == 11 and ob == 1))
                x_t, kx = xq.pop(0)
                y_t, ky = yo.next()
                c.op("dve", lambda: nc.vector.tensor_tensor(out=y_t[:], in0=x_t[:], in1=y_p[:], op=ALU.add),
                     reads=[kx, kyp], writes=[ky])
                if tt + 3 < NT:
                    x_n, kxn = xr.next()
                    c.dma("sp", x_n[:], x_in[(tt + 3) * 128:(tt + 4) * 128, :], writes=[kxn])
                    xq.append((x_n, kxn))
                c.dma("sp", x_out[tt * 128:(tt + 1) * 128, :], y_t[:], reads=[ky])
            c.barrier()


def emit_ssd(c, P, XT, BT, CT, Btok, DT, AA, eps, ident, ones32, name):
    nc = c.nc
    NHD = 16
    mt_v = P["mt_scr"].rearrange("k p s -> p k s")
    with ExitStack() as es:
        E = es.enter_context
        U = E(nc.sbuf_tensor(f"{name}_U", [128, 128], F32))
        L = E(nc.sbuf_tensor(f"{name}_L", [128, 128], F32))
        NMf = E(nc.sbuf_tensor(f"{name}_NMf", [128, 128], F32))
        NMb = E(nc.sbuf_tensor(f"{name}_NMb", [128, 128], F32))
        Db = E(nc.sbuf_tensor(f"{name}_Db", [128, NHD], F32))
        gnb = E(nc.sbuf_tensor(f"{name}_gnb", [128, 1024], F32))
        kconst = c.key("ssdconst")
        c.dma("sp", U[:], P["tri_u"], writes=[kconst])
        c.dma("sp", L[:], P["tri_l"], writes=[kconst])
        c.dma("sp", NMf[:], P["negm_f"], writes=[kconst])
        c.dma("sp", NMb[:], P["negm_b"], writes=[kconst])
        c.dma("sp", Db[:], bcast_row(P["ssd_D"], NHD), writes=[kconst])
        c.dma("sp", gnb[:], bcast_row(P["out_norm"], 1024), writes=[kconst])
        Hbst = Ring(c, es, f"{name}_Hbst", [128, 1024], BF16, 2)
        Hbld = Ring(c, es, f"{name}_Hbld", [128, 1024], BF16, 2)
        kHb = c.key("Hball")
        Hf32 = E(nc.sbuf_tensor(f"{name}_Hf32", [128, 1024], F32))
        Hb32 = E(nc.sbuf_tensor(f"{name}_Hb32", [128, 1024], F32))
        kHf = c.key("Hf32")
        kHb32 = c.key("Hb32")
        c.op("pool", lambda: nc.gpsimd.memset(Hf32[:], 0.0), writes=[kHf])
        c.op("pool", lambda: nc.gpsimd.memset(Hb32[:], 0.0), writes=[kHb32])
        Hfb = Ring(c, es, f"{name}_Hfb", [128, 1024], BF16, 2)
        pR = Ring(c, es, f"{name}_pR", [128, 4, 128], F32, 2, psum=True)
        pcb = Ring(c, es, f"{name}_pcb", [128, 4, 128], F32, 1, psum=True)
        py = Ring(c, es, f"{name}_pyd", [128, 1024], F32, 1, psum=True)
        pY2 = Ring(c, es, f"{name}_pY2", [128, 1024], F32, 1, psum=True)
        pTn = Ring(c, es, f"{name}_pTn", [128, 8, 128], BF16, 1, psum=True)
        STr = Ring(c, es, f"{name}_ST", [128, 5, 32], F32, 4)
        wsm = Ring(c, es, f"{name}_wsm", [128, 16], F32, 3)
        xcf = Ring(c, es, f"{name}_xcf", [128, 1024], BF16, 2)
        xcb = Ring(c, es, f"{name}_xcb", [128, 1024], BF16, 2)
        xcd = Ring(c, es, f"{name}_xcd", [128, 1024], BF16, 3)
        cbT = Ring(c, es, f"{name}_cbT", [128, 4, 128], F32, 2)
        t1r = Ring(c, es, f"{name}_t1", [128, 4, 128], F32, 3)
        t2r = Ring(c, es, f"{name}_t2", [128, 4, 128], F32, 3)
        Mr = Ring(c, es, f"{name}_M", [128, 4, 128], BF16, 18)
        yar = Ring(c, es, f"{name}_ya", [128, 1024], F32, 2)
        ybr = Ring(c, es, f"{name}_yb", [128, 1024], F32, 2)
        ydr = Ring(c, es, f"{name}_yd", [128, 1024], F32, 2)
        zsr = Ring(c, es, f"{name}_zs", [128, 1024], BF16, 2)
        hhr = Ring(c, es, f"{name}_hh", [128, 1024], F32, 2)
        sqr = Ring(c, es, f"{name}_sqj", [128, 1024], BF16, 1)
        s4r = Ring(c, es, f"{name}_s4", [128, 2, 4], F32, 2)
        hbr = Ring(c, es, f"{name}_hb16", [128, 1024], BF16, 2)
        mst = Ring(c, es, f"{name}_mst", [128, 8, 128], BF16, 2)
        c.barrier()

        def b16(ap2d, n=NHD, d=64):
            return ap2d.unsqueeze(2).to_broadcast([128, n, d])

        def v3(ap2d, d=64):
            return ap2d.rearrange("p (h d) -> p h d", d=d)

        def chunk_stats(ci):
            p_t, kp = pR.next()
            c.op("pe", lambda: nc.tensor.matmul(p_t[:, 0, 0:16], lhsT=U[:], rhs=AA[:, ci, 0:16], start=True, stop=True),
                 reads=[], writes=[kp], signal=False)
            c.op("pe", lambda: nc.tensor.matmul(p_t[:, 0, 16:32], lhsT=L[:], rhs=AA[:, ci, 16:32], start=True, stop=True),
                 reads=[], writes=[kp], signal=False)
            c.op("pe", lambda: nc.tensor.matmul(p_t[:, 0, 32:64], lhsT=ones32[:], rhs=AA[:, ci, 0:32], start=True, stop=True),
                 reads=[], writes=[kp], signal=True)
            st, kst = STr.next()
            c.op("dve", lambda: nc.vector.tensor_copy(out=st[:, 0, :], in_=p_t[:, 0, 0:32]), reads=[kp], writes=[kst])
            c.op("dve", lambda: nc.vector.tensor_tensor(out=st[:, 4, :], in0=p_t[:, 0, 32:64], in1=st[:, 0, :], op=ALU.subtract),
                 reads=[kp, kst], writes=[kst])
            c.op("act", lambda: nc.scalar.activation(out=st[:, 1, :], in_=st[:, 0, :], func=AF.Exp), reads=[kst], writes=[kst])
            c.op("act", lambda: nc.scalar.activation(out=st[:, 2, :], in_=st[:, 4, :], func=AF.Exp), reads=[kst], writes=[kst])
            c.op("act", lambda: nc.scalar.activation(out=st[:, 3, :], in_=p_t[:, 0, 32:64], func=AF.Exp), reads=[kp, kst], writes=[kst])
            return st, kst

        def states_into(ps_t, kps, ci, x_t, kx):
            for g in range(4):
                c.op("pe", lambda: nc.tensor.matmul(ps_t[:, g * 256:(g + 1) * 256], lhsT=Btok[:, ci, g * 128:(g + 1) * 128],
                                                    rhs=x_t[:, g * 256:(g + 1) * 256], start=True, stop=True),
                     reads=[kx], writes=[kps], signal=(g == 3))

        pre_ps = [(pY2.t[0], pY2.k[0]), (py.t[0], py.k[0])]
        order = list(range(NT - 1, -1, -1))

        def pre_front(n):
            ci = order[n]
            st, kst = chunk_stats(ci)
            w_t, kw = wsm.next()
            c.op("dve", lambda: nc.vector.tensor_tensor(out=w_t[:], in0=DT[:, ci, 16:32], in1=st[:, 2, 16:32], op=ALU.mult),
                 reads=[kst], writes=[kw])
            x_t, kx = xcd.next()
            c.op("pool", lambda: nc.gpsimd.tensor_tensor(out=v3(x_t[:]), in0=v3(XT[:, ci, :]), in1=b16(w_t[:]), op=ALU.mult),
                 reads=[kw], writes=[kx])
            ps_t, kps = pre_ps[n % 2]
            states_into(ps_t, kps, ci, x_t, kx)
            return st, kst, ps_t, kps

        fr = pre_front(0)
        for n in range(NT):
            ci = order[n]
            nxt = pre_front(n + 1) if n + 1 < NT else None
            st, kst, ps_t, kps = fr
            hs_t, khs = Hbst.next()
            c.op("act", lambda: nc.scalar.copy(out=hs_t[:], in_=Hb32[:]), reads=[kHb32], writes=[khs])
            c.dma("sp", P["hb_scr"][ci], hs_t[:], reads=[khs], writes=[(kHb, ci)])
            c.op("pool", lambda: nc.gpsimd.tensor_tensor(out=v3(Hb32[:]), in0=v3(Hb32[:]), in1=b16(st[:, 3, 16:32]), op=ALU.mult),
                 reads=[kHb32, kst], writes=[kHb32])
            c.op("dve", lambda: nc.vector.tensor_tensor(out=Hb32[:], in0=Hb32[:], in1=ps_t[:], op=ALU.add),
                 reads=[kHb32, kps], writes=[kHb32])
            fr = nxt

        def stage_a(ci):
            T = {"ci": ci}
            tok = slice(ci * 128, (ci + 1) * 128)
            st, kst = chunk_stats(ci)
            T["st"], T["kst"] = st, kst
            xf_t, kxf = xcf.next()
            xb_t, kxb = xcb.next()
            xd_t, kxd = xcd.next()
            c.op("dve", lambda: nc.vector.tensor_tensor(out=v3(xf_t[:]), in0=v3(XT[:, ci, :]), in1=b16(DT[:, ci, 0:16]), op=ALU.mult),
                 reads=[], writes=[kxf])
            c.op("pool", lambda: nc.gpsimd.tensor_tensor(out=v3(xb_t[:]), in0=v3(XT[:, ci, :]), in1=b16(DT[:, ci, 16:32]), op=ALU.mult),
                 reads=[], writes=[kxb])
            c.op("pool", lambda: nc.gpsimd.tensor_tensor(out=v3(xd_t[:]), in0=v3(xf_t[:]), in1=b16(st[:, 2, 0:16]), op=ALU.mult),
                 reads=[kxf, kst], writes=[kxd])
            T["xf"], T["xb"], T["xd"] = (xf_t, kxf), (xb_t, kxb), (xd_t, kxd)
            yd_t, kyd = ydr.next()
            c.op("pool", lambda: nc.gpsimd.tensor_tensor(out=v3(yd_t[:]), in0=v3(XT[:, ci, :]), in1=b16(Db[:]), op=ALU.mult),
                 reads=[], writes=[kyd])
            T["yd"] = (yd_t, kyd)
            cb_p, kcbp = pcb.next()
            for g in range(4):
                c.op("pe", lambda: nc.tensor.matmul(cb_p[:, g, :], lhsT=BT[:, g, tok], rhs=CT[:, g, tok], start=True, stop=True),
                     reads=[], writes=[kcbp], signal=(g == 3))
            cb_t, kcb = cbT.next()
            c.op("act", lambda: nc.scalar.copy(out=cb_t[:], in_=cb_p[:]), reads=[kcbp], writes=[kcb])
            T["M"] = []
            for g in range(4):
                ms = []
                for d in range(2):
                    tri = U if d == 0 else L
                    NM = NMf if d == 0 else NMb
                    r_p, krp = pR.next()
                    for j in range(4):
                        col = d * 16 + g * 4 + j
                        c.op("pe", lambda: nc.tensor.matmul(r_p[:, j, :], lhsT=AA[:, ci, col:col + 1].to_broadcast([128, 128]),
                                                            rhs=tri[:], start=True, stop=True),
                             reads=[], writes=[krp], signal=(j == 3))
                    c0 = d * 16 + g * 4
                    a_t, ka = t1r.next()
                    for j in range(4):
                        c.op("dve", lambda: nc.vector.scalar_tensor_tensor(out=a_t[:, j, :], in0=r_p[:, j, :], scalar=st[:, 0, c0 + j:c0 + j + 1],
                                                                           in1=NM[:], op0=ALU.subtract, op1=ALU.add),
                             reads=[krp, kst], writes=[ka])
                    e_t, ke = t2r.next()
                    c.op("act", lambda: nc.scalar.activation(out=e_t[:], in_=a_t[:], func=AF.Exp), reads=[ka], writes=[ke])
                    m_t, km = Mr.next()
                    c.op("pool", lambda: nc.gpsimd.tensor_tensor(out=m_t[:], in0=e_t[:], in1=cb_t[:, g, :].unsqueeze(1).to_broadcast([128, 4, 128]),
                                                                 op=ALU.mult), reads=[ke, kcb], writes=[km])
                    ms.append((m_t, km))
                T["M"].append(ms)
            return T

        def stage_b(T, hf_t, khf):
            ci = T["ci"]
            tok = slice(ci * 128, (ci + 1) * 128)
            st, kst = T["st"], T["kst"]
            (xf_t, kxf), (xb_t, kxb), (xd_t, kxd) = T["xf"], T["xb"], T["xd"]
            yd_t, kyd = T["yd"]
            y_p, kyp = py.next()
            for g in range(4):
                ms = T["M"][g]
                for j in range(4):
                    h = g * 4 + j
                    c.op("pe", lambda: nc.tensor.matmul(y_p[:, h * 64:(h + 1) * 64], lhsT=ms[0][0][:, j, :], rhs=xf_t[:, h * 64:(h + 1) * 64],
                                                        start=True, stop=False),
                         reads=[ms[0][1], kxf], writes=[kyp], signal=False)
                    c.op("pe", lambda: nc.tensor.matmul(y_p[:, h * 64:(h + 1) * 64], lhsT=ms[1][0][:, j, :], rhs=xb_t[:, h * 64:(h + 1) * 64],
                                                        start=False, stop=True),
                         reads=[ms[1][1], kxb], writes=[kyp], signal=(j == 3))
            o_p, kop = pY2.next()
            for g in range(4):
                c.op("pe", lambda: nc.tensor.matmul(o_p[:, g * 256:(g + 1) * 256], lhsT=CT[:, g, tok], rhs=hf_t[:, g * 256:(g + 1) * 256],
                                                    start=True, stop=True), reads=[khf], writes=[kop], signal=(g == 3))
            ya_t, kya = yar.next()
            c.op("dve", lambda: nc.vector.tensor_tensor(out=v3(ya_t[:]), in0=v3(o_p[:]), in1=b16(st[:, 1, 0:16]), op=ALU.mult),
                 reads=[kop, kst], writes=[kya])
            o_p, kop = pY2.next()
            hl_t, khl = Hbld.next()
            c.dma("sp", hl_t[:], P["hb_scr"][ci], reads=[(kHb, ci)], writes=[khl])
            for g in range(4):
                c.op("pe", lambda: nc.tensor.matmul(o_p[:, g * 256:(g + 1) * 256], lhsT=CT[:, g, tok], rhs=hl_t[:, g * 256:(g + 1) * 256],
                                                    start=True, stop=True), reads=[khl], writes=[kop], signal=(g == 3))
            yb_t, kyb = ybr.next()
            c.op("dve", lambda: nc.vector.tensor_tensor(out=v3(yb_t[:]), in0=v3(o_p[:]), in1=b16(st[:, 1, 16:32]), op=ALU.mult),
                 reads=[kop, kst], writes=[kyb])
            c.op("pool", lambda: nc.gpsimd.tensor_tensor(out=yb_t[:], in0=yb_t[:], in1=yd_t[:], op=ALU.add), reads=[kyb, kyd], writes=[kyb])
            c.op("dve", lambda: nc.vector.tensor_tensor(out=ya_t[:], in0=ya_t[:], in1=yb_t[:], op=ALU.add), reads=[kya, kyb], writes=[kya])
            s_p, ksp = pY2.next()
            states_into(s_p, ksp, ci, xd_t, kxd)
            c.op("pool", lambda: nc.gpsimd.tensor_tensor(out=v3(Hf32[:]), in0=v3(Hf32[:]), in1=b16(st[:, 3, 0:16]), op=ALU.mult),
                 reads=[kHf, kst], writes=[kHf])
            c.op("dve", lambda: nc.vector.tensor_tensor(out=Hf32[:], in0=Hf32[:], in1=s_p[:], op=ALU.add), reads=[kHf, ksp], writes=[kHf])
            hf_n, khf_n = Hfb.next()
            c.op("act", lambda: nc.scalar.copy(out=hf_n[:], in_=Hf32[:]), reads=[kHf], writes=[khf_n])
            z_t, kz = zsr.next()
            c.dma("sp", z_t[:], P["zs"][ci * 128:(ci + 1) * 128, :], reads=[("zs_d", ci)], writes=[kz])
            hh_t, khh = hhr.next()
            c.op("dve", lambda: nc.vector.tensor_tensor(out=hh_t[:], in0=y_p[:], in1=ya_t[:], op=ALU.add), reads=[kyp, kya], writes=[khh])
            T["hh"] = (hh_t, khh, z_t, kz)
            return hf_n, khf_n

        def stage_b2(T):
            ci = T["ci"]
            hh_t, khh, z_t, kz = T["hh"]
            c.op("pool", lambda: nc.gpsimd.tensor_tensor(out=hh_t[:], in0=hh_t[:], in1=z_t[:], op=ALU.mult), reads=[khh, kz], writes=[khh])
            sq_t, ksq = sqr.next()
            s4, ks4 = s4r.next()
            for gg in range(4):
                c.op("act", lambda: nc.scalar.activation(out=sq_t[:, gg * 256:(gg + 1) * 256], in_=hh_t[:, gg * 256:(gg + 1) * 256],
                                                          func=AF.Square, accum_out=s4[:, 0, gg:gg + 1]),
                     reads=[khh], writes=[ksq, ks4])
            c.op("act", lambda: nc.scalar.activation(out=s4[:, 1, :], in_=s4[:, 0, :], func=AF.Sqrt, scale=1.0 / 256, bias=eps[:]),
                 reads=[ks4], writes=[ks4])
            c.op("dve", lambda: nc.vector.reciprocal(out=s4[:, 1, :], in_=s4[:, 1, :]), reads=[ks4], writes=[ks4])
            hb_t, khb = hbr.next()
            for gg in range(4):
                c.op("dve", lambda: nc.vector.scalar_tensor_tensor(out=hb_t[:, gg * 256:(gg + 1) * 256], in0=hh_t[:, gg * 256:(gg + 1) * 256],
                                                                   scalar=s4[:, 1, gg:gg + 1], in1=gnb[:, gg * 256:(gg + 1) * 256],
                                                                   op0=ALU.mult, op1=ALU.mult),
                     reads=[khh, ks4], writes=[khb])
            T["hb"] = (hb_t, khb)

        def stage_b3(T):
            ci = T["ci"]
            hb_t, khb = T["hb"]
            t_t, kt_ = pTn.next()
            for k in range(8):
                c.op("pe", lambda: nc.tensor.transpose(t_t[:, k, :], hb_t[:, k * 128:(k + 1) * 128], ident[:]),
                     reads=[khb], writes=[kt_], signal=(k == 7))
            m_s, kms = mst.next()
            c.op("act", lambda: nc.scalar.copy(out=m_s[:], in_=t_t[:]), reads=[kt_], writes=[kms])
            c.dma("sp", mt_v[:, 4:12, ci * 128:(ci + 1) * 128], m_s[:], reads=[kms], writes=[("mt_d", "ssd", ci)])

        hf_t, khf = Hfb.next()
        c.op("pool", lambda: nc.gpsimd.memset(hf_t[:], 0.0), writes=[khf])
        Ta = stage_a(0)
        Tp1 = Tp2 = None
        for ci in range(NT):
            Tn = stage_a(ci + 1) if ci + 1 < NT else None
            hf_t, khf = stage_b(Ta, hf_t, khf)
            if Tp1 is not None:
                stage_b2(Tp1)
            if Tp2 is not None:
                stage_b3(Tp2)
            Tp2 = Tp1
            Tp1 = Ta
            Ta = Tn
        stage_b2(Tp1)
        stage_b3(Tp2)
        stage_b3(Tp1)
        c.barrier()


def _rope_tables():
    t = np.arange(S)
    row = (t // 64).astype(np.float32)
    col = (t % 64).astype(np.float32)
    freqs = (np.float32(10000.0) ** (-np.arange(0, 32, 2, dtype=np.float32) / np.float32(32))).astype(np.float32)
    ang = np.concatenate([row[:, None] * freqs, col[:, None] * freqs], -1).astype(np.float32)
    return np.cos(ang).astype(np.float32), np.sin(ang).astype(np.float32)


def _na_index():
    dyi = np.zeros((NTYPES, 128, 128), np.int64)
    dxi = np.zeros((NTYPES, 128, 128), np.int64)
    msk = np.zeros((NTYPES, 128, 128), np.float32)
    rep = {0: 0, 1: 1, 2: 5, 3: 14, 4: 15}
    kk = np.arange(128)
    kr, ck = kk // 64, kk % 64
    for n, (cls, j) in enumerate(NA_TYPES):
        i = rep[cls]
        _, kp0, _ = na_cls(i)
        r = 2 * i + kr[None, :]
        cq = ck[None, :]
        rk = 2 * (kp0 + j) + kr[:, None]
        ckk = ck[:, None]
        rs = np.clip(r - 4, 0, 24)
        vrow = (rk >= rs) & (rk < rs + 8)
        cs = np.clip(cq - 8, 0, 48)
        vcol = (ckk >= cs) & (ckk < cs + 16)
        dyi[n] = np.clip(rk - r + 7, 0, 14)
        dxi[n] = np.clip(ckk - cq, -15, 15) + 15
        msk[n] = np.where(vrow & vcol, 0.0, NEG)
    return dyi, dxi, msk


_PROGRAM = None


def _inputs_spec():
    return [("x", [S, D], F32), ("even_mix_norm", [D], F32), ("even_w_in", [D, 4640], F32), ("na_q_norm", [64], F32),
            ("na_k_norm", [64], F32), ("biasg", [NTYPES, 128, 8, 128], F32), ("namask", [NTYPES, 128, 128], F32),
            ("ssd_conv_w", [4, 2048], F32), ("ssd_conv_b", [2048], F32), ("ssd_dt_bias", [32], F32), ("ssd_A_log", [32], F32),
            ("ssd_D", [16], F32), ("ssd_out_norm", [1024], F32), ("even_w_out", [1536, D], F32),
            ("odd_mix_norm", [D], F32), ("odd_w_qkv", [D, 1536], F32), ("gqa_q_norm", [64], F32), ("gqa_k_norm", [64], F32),
            ("odd_w_out", [D, D], F32), ("ffn_norm0", [D], F32), ("ffn_norm1", [D], F32),
            ("ffn_w13_0", [D, 2 * FH], F32), ("ffn_w13_1", [D, 2 * FH], F32), ("ffn_w2_0", [FH, D], F32), ("ffn_w2_1", [FH, D], F32),
            ("cos", [S, 32], F32), ("sin", [S, 32], F32), ("tri_u", [128, 128], F32), ("tri_l", [128, 128], F32),
            ("negm_f", [128, 128], F32), ("negm_b", [128, 128], F32), ("ident", [128, 128], BF16)]


def build_program(phases=("even", "ffn0", "odd", "ffn1")):
    nc = bass.Bass("TRN2", target_bir_lowering=False)
    A = {}
    for n, sh, dt in _inputs_spec():
        A[n] = nc.dram_tensor(n, sh, dt, kind="ExternalInput").ap()
    out = nc.dram_tensor("out", [S, D], F32, kind="ExternalOutput").ap()
    x1 = nc.dram_tensor("x1_scr", [S, D], F32, kind="Internal").ap()
    x2 = nc.dram_tensor("x2_scr", [S, D], F32, kind="Internal").ap()
    x3 = nc.dram_tensor("x3_scr", [S, D], F32, kind="Internal").ap()
    zs = nc.dram_tensor("zs_scr", [S, 1024], BF16, kind="Internal").ap()
    hb = nc.dram_tensor("hb_scr", [NT, 128, 1024], BF16, kind="Internal").ap()
    mt = nc.dram_tensor("mt_scr", [12, 128, S], BF16, kind="Internal").ap()
    chain = [A["x"], x1, x2, x3, out]
    order = ["even", "ffn0", "odd", "ffn1"]
    active = [p for p in order if p in phases]
    cur = A["x"]
    with ExitStack() as es:
        c = Ctx(nc, es)
        ident = es.enter_context(nc.sbuf_tensor("ident_sb", [128, 128], BF16))
        ones32 = es.enter_context(nc.sbuf_tensor("ones32", [128, 128], F32))
        c.dma("sp", ident[:], A["ident"], writes=["ident"])
        c.op("pool", lambda: nc.gpsimd.memset(ones32[:], 1.0), writes=["ones32"])
        c.barrier()
        for n, ph in enumerate(active):
            dst = out if n == len(active) - 1 else chain[order.index(ph) + 1]
            if ph == "even":
                P = {"mix_norm": A["even_mix_norm"], "w_in": A["even_w_in"], "na_gq": A["na_q_norm"], "na_gk": A["na_k_norm"],
                     "biasg": A["biasg"], "namask": A["namask"], "conv_w": A["ssd_conv_w"], "conv_b": A["ssd_conv_b"],
                     "dt_bias": A["ssd_dt_bias"], "A_log": A["ssd_A_log"], "ssd_D": A["ssd_D"], "out_norm": A["ssd_out_norm"],
                     "w_out": A["even_w_out"], "tri_u": A["tri_u"], "tri_l": A["tri_l"], "negm_f": A["negm_f"], "negm_b": A["negm_b"],
                     "zs": zs, "hb_scr": hb, "mt_scr": mt}
                emit_even(c, cur, dst, P, ident, ones32, "e0")
            elif ph == "ffn0":
                emit_ffn(c, cur, dst, A["ffn_norm0"], A["ffn_w13_0"], A["ffn_w2_0"], ident, "f0")
            elif ph == "odd":
                emit_odd(c, cur, dst, A["odd_mix_norm"], A["odd_w_qkv"], A["gqa_q_norm"], A["gqa_k_norm"], A["odd_w_out"],
                         A["cos"], A["sin"], ident, ones32, "o0")
            elif ph == "ffn1":
                emit_ffn(c, cur, dst, A["ffn_norm1"], A["ffn_w13_1"], A["ffn_w2_1"], ident, "f1")
            cur = dst
        c.finish("sp")
    return nc


def make_in_maps(inputs, xs):
    import ml_dtypes
    f = lambda a: np.ascontiguousarray(np.asarray(a, dtype=np.float32))
    cos, sin = _rope_tables()
    dyi, dxi, msk = _na_index()
    rpb = f(inputs["na_rel_bias"])[0]
    biasg = np.ascontiguousarray(rpb[:, dyi, dxi].transpose(1, 2, 0, 3))
    tri_u = np.triu(np.ones((128, 128), np.float32))
    kk = np.arange(128)
    negm_f = np.where(kk[None, :] >= kk[:, None], 0.0, NEG).astype(np.float32)
    negm_b = np.where(kk[None, :] <= kk[:, None], 0.0, NEG).astype(np.float32)
    shared = {
        "even_mix_norm": f(inputs["even_mix_norm"])[0], "even_w_in": f(inputs["even_w_in"])[0],
        "na_q_norm": f(inputs["na_q_norm"])[0], "na_k_norm": f(inputs["na_k_norm"])[0], "biasg": biasg, "namask": msk,
        "ssd_conv_w": f(inputs["ssd_conv_w"])[0], "ssd_conv_b": f(inputs["ssd_conv_b"])[0],
        "ssd_dt_bias": f(inputs["ssd_dt_bias"])[0].reshape(32), "ssd_A_log": f(inputs["ssd_A_log"])[0].reshape(32),
        "ssd_D": f(inputs["ssd_D"])[0], "ssd_out_norm": f(inputs["ssd_out_norm"])[0], "even_w_out": f(inputs["even_w_out"])[0],
        "odd_mix_norm": f(inputs["odd_mix_norm"])[0], "odd_w_qkv": f(inputs["odd_w_qkv"])[0],
        "gqa_q_norm": f(inputs["gqa_q_norm"])[0], "gqa_k_norm": f(inputs["gqa_k_norm"])[0], "odd_w_out": f(inputs["odd_w_out"])[0],
        "ffn_norm0": f(inputs["ffn_norm"])[0], "ffn_norm1": f(inputs["ffn_norm"])[1],
        "ffn_w13_0": f(inputs["ffn_w13"])[0], "ffn_w13_1": f(inputs["ffn_w13"])[1],
        "ffn_w2_0": f(inputs["ffn_w2"])[0], "ffn_w2_1": f(inputs["ffn_w2"])[1],
        "cos": cos, "sin": sin, "tri_u": tri_u, "tri_l": np.ascontiguousarray(tri_u.T),
        "negm_f": negm_f, "negm_b": negm_b, "ident": np.eye(128).astype(ml_dtypes.bfloat16),
    }
    return [dict(shared, x=np.ascontiguousarray(xb)) for xb in xs]


def kernel(**inputs):
    global _PROGRAM
    x = np.asarray(inputs["x"], dtype=np.float32)
    B = x.shape[0]
    if _PROGRAM is None:
        _PROGRAM = build_program()
    in_maps = make_in_maps(inputs, [x[b] for b in range(B)])
    res = run_bass_kernel_spmd(_PROGRAM, in_maps, core_ids=list(range(B)))
    return np.stack([np.asarray(r["out"], dtype=np.float32) for r in res.results], axis=0)
```

```python
import numpy as np
from contextlib import ExitStack
import concourse.bass as bass
import concourse.mybir as mybir
from concourse.bass_utils import run_bass_kernel_spmd

F32 = mybir.dt.float32
BF16 = mybir.dt.bfloat16
AF = mybir.ActivationFunctionType
ALU = mybir.AluOpType
AX = mybir.AxisListType

S = 2048
D = 1024
NT = 16
FH = 2816
NHC = 22
EPS = 1e-6
NEG = -30000.0
N_DUMMY = 0


class Ctx:
    SAME_ENGINE_SYNC = ("act", "dve", "pool")

    def __init__(self, nc, es, n_dma_sems=32):
        self.nc = nc
        self.es = es
        self.eng = {"pe": nc.tensor, "act": nc.scalar, "dve": nc.vector, "pool": nc.gpsimd, "sp": nc.sync}
        self.sem = {}
        self.cnt = {}
        self.nsem = 0
        for e in self.eng:
            self._new_sem(e)
        self.dsem = [es.enter_context(nc.semaphore(f"dma{i}")) for i in range(n_dma_sems)]
        self.dcnt = [0] * n_dma_sems
        self.dnext = {"hw": 0, "sw": 0}
        self.dhalf = n_dma_sems // 2
        self.waited = {e: {} for e in self.eng}
        self.last_w = {}
        self.readers = {}
        self.pend = {e: ([], []) for e in self.eng}
        self.uid = 0

    def _new_sem(self, e):
        self.sem[e] = self.es.enter_context(self.nc.semaphore(f"s_{e}_{self.nsem}"))
        self.nsem += 1
        self.cnt[e] = 0

    def _wait(self, e, tok):
        sem, val, src = tok
        if src == e and e not in self.SAME_ENGINE_SYNC:
            return
        key = id(sem)
        if self.waited[e].get(key, 0) >= val:
            return
        self.waited[e][key] = val
        self.eng[e].wait_ge(sem, val)

    def _deps(self, e, reads, writes):
        for r in reads:
            t = self.last_w.get(r)
            if t is not None:
                self._wait(e, t)
        for w in writes:
            t = self.last_w.get(w)
            if t is not None:
                self._wait(e, t)
            for t in self.readers.get(w, ()):
                self._wait(e, t)

    def _commit(self, tok, reads, writes):
        for w in writes:
            self.last_w[w] = tok
            self.readers[w] = []
        for r in reads:
            self.readers.setdefault(r, []).append(tok)

    def op(self, e, ins_fn, reads=(), writes=(), signal=True):
        reads = list(reads)
        writes = list(writes)
        self._deps(e, reads, writes)
        ins = ins_fn()
        pr, pw = self.pend[e]
        if not signal:
            pr.extend(reads)
            pw.extend(writes)
            return ins
        if self.cnt[e] >= 30000:
            self._new_sem(e)
        self.cnt[e] += 1
        ins.then_inc(self.sem[e], 1)
        tok = (self.sem[e], self.cnt[e], e)
        self._commit(tok, reads + pr, writes + pw)
        self.pend[e] = ([], [])
        return ins

    def dma(self, q, out, in_, reads=(), writes=(), **kw):
        reads = list(reads)
        writes = list(writes)
        kind = "sw" if q == "pool" else "hw"
        i = self.dnext[kind] + (self.dhalf if kind == "sw" else 0)
        self.dnext[kind] = (self.dnext[kind] + 1) % self.dhalf
        skey = ("__dsem", i)
        self._deps(q, reads, writes + [skey])
        ins = self.eng[q].dma_start(out=out, in_=in_, **kw)
        self.dcnt[i] += 16
        ins.then_inc(self.dsem[i], 16)
        tok = (self.dsem[i], self.dcnt[i], "dma")
        self._commit(tok, reads, writes + [skey])
        return ins

    def finish(self, e="sp"):
        for i, s in enumerate(self.dsem):
            if self.dcnt[i]:
                self._wait(e, (s, self.dcnt[i], "dma"))
        for x in self.eng:
            if self.cnt[x] and (x != e or e in self.SAME_ENGINE_SYNC):
                self._wait(e, (self.sem[x], self.cnt[x], x))

    def barrier(self):
        for e in self.eng:
            assert not self.pend[e][0] and not self.pend[e][1], "pending unsignalled ops at barrier"
        for e in self.eng:
            self.finish(e)
        self.last_w.clear()
        self.readers.clear()

    def key(self, name):
        self.uid += 1
        return f"{name}#{self.uid}"


class Ring:
    def __init__(self, c, es, name, shape, dtype, n, psum=False):
        alloc = c.nc.psum_tensor if psum else c.nc.sbuf_tensor
        self.t = [es.enter_context(alloc(f"{name}{i}", shape, dtype)) for i in range(n)]
        self.k = [c.key(name) for _ in range(n)]
        self.i = -1
        self.n = n

    def next(self):
        self.i = (self.i + 1) % self.n
        return self.t[self.i], self.k[self.i]


def bcast_row(ap_1d, n, parts=128):
    return ap_1d.rearrange("(o n) -> o n", o=1).to_broadcast([parts, n])


def emit_norm_T(c, es, x_dram, g_dram, xnT, xnT_key, ident, name):
    nc = c.nc
    gb = es.enter_context(nc.sbuf_tensor(f"{name}_gb", [128, D], F32))
    kgb = c.key("gb")
    c.dma("sp", gb[:], bcast_row(g_dram, D), writes=[kgb])
    xt = Ring(c, es, f"{name}_xt", [128, D], F32, 4)
    sq = Ring(c, es, f"{name}_sq", [128, D], BF16, 3)
    xs = Ring(c, es, f"{name}_xs", [128, D], BF16, 3)
    st = Ring(c, es, f"{name}_st", [128, 2], F32, 4)
    pT = Ring(c, es, f"{name}_pT", [128, 8, 128], BF16, 2, psum=True)
    eps = es.enter_context(nc.sbuf_tensor(f"{name}_eps", [128, 1], F32))
    keps = c.key("eps")
    c.op("pool", lambda: nc.gpsimd.memset(eps[:], EPS), writes=[keps])
    def chain(i):
        x_t, kx = xt.next()
        c.dma("sp", x_t[:], x_dram[i * 128:(i + 1) * 128, :], writes=[kx])
        s_t, ks = sq.next()
        st_t, kst = st.next()
        c.op("act", lambda: nc.scalar.activation(out=s_t[:], in_=x_t[:], func=AF.Square, accum_out=st_t[:, 0:1]),
             reads=[kx], writes=[ks, kst])
        c.op("act", lambda: nc.scalar.activation(out=st_t[:, 1:2], in_=st_t[:, 0:1], func=AF.Sqrt,
                                                  scale=1.0 / D, bias=eps[:]),
             reads=[kst, keps], writes=[kst])
        c.op("dve", lambda: nc.vector.reciprocal(out=st_t[:, 1:2], in_=st_t[:, 1:2]), reads=[kst], writes=[kst])
        xs_t, kxs = xs.next()
        c.op("dve", lambda: nc.vector.scalar_tensor_tensor(out=xs_t[:], in0=x_t[:], scalar=st_t[:, 1:2], in1=gb[:],
                                                           op0=ALU.mult, op1=ALU.mult),
             reads=[kx, kst, kgb], writes=[kxs])
        return xs_t, kxs

    def tr(i, xs_t, kxs):
        p_t, kp = pT.next()
        for k in range(8):
            c.op("pe", lambda: nc.tensor.transpose(p_t[:, k, :], xs_t[:, k * 128:(k + 1) * 128], ident[:]),
                 reads=[kxs], writes=[kp], signal=(k == 7))
        c.op("act", lambda: nc.scalar.copy(out=xnT[:, :, i * 128:(i + 1) * 128], in_=p_t[:]),
             reads=[kp], writes=[(xnT_key, i)])

    cur = chain(0)
    for i in range(NT):
        nxt = chain(i + 1) if i + 1 < NT else None
        tr(i, *cur)
        cur = nxt


def emit_ffn(c, x_in, x_out, g_dram, w13, w2, ident, name):
    nc = c.nc
    with ExitStack() as es:
        E = es.enter_context
        xnT = E(nc.sbuf_tensor(f"{name}_xnT", [128, 8, S], BF16))
        kxnT = c.key("xnT")
        hT = E(nc.sbuf_tensor(f"{name}_hT", [128, NHC, S], BF16))
        khT = c.key("hT")
        W2 = E(nc.sbuf_tensor(f"{name}_W2", [128, NHC, D], BF16))
        kW2 = c.key("W2")
        with ExitStack() as es1:
            emit_norm_T(c, es1, x_in, g_dram, xnT, kxnT, ident, name)
            c.barrier()
        with ExitStack() as es2:
            xn_all = [(kxnT, i) for i in range(NT)]
            w13v = w13.rearrange("(k p) n -> p k n", p=128)
            wg = Ring(c, es2, f"{name}_wg", [128, 8, 128], BF16, 3)
            wu = Ring(c, es2, f"{name}_wu", [128, 8, 128], BF16, 3)
            pg = Ring(c, es2, f"{name}_pg", [128, 1024], F32, 2, psum=True)
            pu = Ring(c, es2, f"{name}_pu", [128, 1024], F32, 2, psum=True)
            sg = Ring(c, es2, f"{name}_sg", [128, 1024], F32, 2)
            for hc in range(NHC):
                wg_t, kwg = wg.next()
                wu_t, kwu = wu.next()
                c.dma("pool", wg_t[:], w13v[:, :, hc * 128:(hc + 1) * 128], writes=[kwg])
                c.dma("pool", wu_t[:], w13v[:, :, FH + hc * 128:FH + (hc + 1) * 128], writes=[kwu])
                c.dma("pool", W2[:, hc, :], w2[hc * 128:(hc + 1) * 128, :], writes=[(kW2, hc)])
                for th in range(2):
                    pg_t, kpg = pg.next()
                    pu_t, kpu = pu.next()
                    for (w_t, kw, p_t, kp) in ((wg_t, kwg, pg_t, kpg), (wu_t, kwu, pu_t, kpu)):
                        for k in range(8):
                            for nb in range(2):
                                t0 = th * 1024 + nb * 512
                                c.op("pe", lambda: nc.tensor.matmul(p_t[:, nb * 512:(nb + 1) * 512], lhsT=w_t[:, k, :],
                                                                    rhs=xnT[:, k, t0:t0 + 512],
                                                                    start=(k == 0), stop=(k == 7)),
                                     reads=[kw] + xn_all, writes=[kp], signal=(k == 7 and nb == 1))
                    sg_t, ksg = sg.next()
                    c.op("act", lambda: nc.scalar.activation(out=sg_t[:], in_=pg_t[:], func=AF.Silu),
                         reads=[kpg], writes=[ksg])
                    c.op("dve", lambda: nc.vector.tensor_tensor(out=hT[:, hc, th * 1024:(th + 1) * 1024], in0=sg_t[:],
                                                                in1=pu_t[:], op=ALU.mult),
                         reads=[ksg, kpu], writes=[(khT, hc, th)])
            c.barrier()
        py = [E(nc.psum_tensor(f"{name}_py{j}", [128, D], F32)) for j in range(4)]
        kpy = [c.key("py") for _ in range(4)]
        xr = Ring(c, es, f"{name}_xr", [128, D], F32, 4)
        yo = Ring(c, es, f"{name}_yo", [128, D], F32, 3)
        for tg in range(4):
            xl = []
            for j in range(4):
                x_t, kx = xr.next()
                c.dma("sp", x_t[:], x_in[(tg * 4 + j) * 128:(tg * 4 + j + 1) * 128, :], writes=[kx])
                xl.append((x_t, kx))
            for hc in range(NHC):
                w_t, kw = W2[:, hc, :], (kW2, hc)
                for j in range(4):
                    tt = tg * 4 + j
                    for ob in range(2):
                        c.op("pe", lambda: nc.tensor.matmul(py[j][:, ob * 512:(ob + 1) * 512],
                                                            lhsT=hT[:, hc, tt * 128:(tt + 1) * 128],
                                                            rhs=w_t[:, ob * 512:(ob + 1) * 512],
                                                            start=(hc == 0), stop=(hc == NHC - 1)),
                             reads=[kw, (khT, hc, tt // 8)], writes=[kpy[j]],
                             signal=(ob == 1 and (hc == NHC - 1 or j == 3)))
            for j in range(4):
                tt = tg * 4 + j
                x_t, kx = xl[j]
                y_t, ky = yo.next()
                c.op("dve", lambda: nc.vector.tensor_tensor(out=y_t[:], in0=x_t[:], in1=py[j][:], op=ALU.add),
                     reads=[kx, kpy[j]], writes=[ky])
                c.dma("sp", x_out[tt * 128:(tt + 1) * 128, :], y_t[:], reads=[ky])
        c.barrier()


def warm_pe(c, ptile, pkey, src, n=20):
    nc = c.nc
    for j in range(n):
        c.op("pe", lambda: nc.tensor.matmul(ptile[:, 0:512], lhsT=src[:, 0:128], rhs=src[:, 0:512], start=True, stop=True),
             reads=[], writes=[pkey], signal=(j == n - 1))


def run_pipeline(iters, look=2):
    deferred = []
    N = len(iters)
    for n in range(min(look, N)):
        iters[n]["qk"]()
    for n in range(N):
        due = [d for d in deferred if d[0] <= n]
        for d in due:
            d[1]()
            deferred.remove(d)
        if n + look < N:
            iters[n + look]["qk"]()
        iters[n]["exp"]()
        iters[n]["pv"]()
        posts = iters[n]["post_factory"]() if "post_factory" in iters[n] else ()
        for delay, fn in posts:
            if delay == 0:
                fn()
            else:
                deferred.append((n + delay, fn))
    for d in deferred:
        d[1]()


def make_norm_post(c, o_t, ko, hb, dst_ap, dst_key, osb, rd, pb, ones32, nrows=128):
    nc = c.nc
    dp = 64 if hb == 0 else 0
    st = {}

    def evac():
        st["o"], st["ko"] = osb.next()
        c.op("dve", lambda: nc.vector.tensor_copy(out=st["o"][0:nrows, :], in_=o_t[0:nrows, :]), reads=[ko], writes=[st["ko"]])

    def bcast():
        b_t, kb = pb.next()
        c.op("pe", lambda: nc.tensor.matmul(b_t[:, :], lhsT=ones32[dp:dp + 1, :], rhs=st["o"][dp:dp + 1, :],
                                            start=True, stop=True), reads=[st["ko"]], writes=[kb])
        st["r"], st["kr"] = rd.next()
        c.op("dve", lambda: nc.vector.reciprocal(out=st["r"][hb:hb + 64, :], in_=b_t[hb:hb + 64, :]), reads=[kb], writes=[st["kr"]])
        c.op("dve", lambda: nc.vector.tensor_tensor(out=dst_ap, in0=st["o"][hb:hb + 64, :], in1=st["r"][hb:hb + 64, :], op=ALU.mult),
             reads=[st["ko"], st["kr"]], writes=[dst_key])

    return [(0, evac), (2, bcast)]


def emit_qkv_proj(c, xnT, kxnT, wv, gq, gk, cos_d, sin_d, ident, QT, kQT, VA, VB, kVA, NH, NKV, dupk, name, QZ=None, koff=8, Wpre=None):
    nc = c.nc
    HD = 64
    NQK = NH + NKV
    rope = cos_d is not None
    with ExitStack() as es2:
        E2 = es2.enter_context
        if Wpre is None:
            W = E2(nc.sbuf_tensor(f"{name}_W", [128, 8, 1536], BF16))
            kW = c.key("W")
            for k in range(8):
                c.dma("pool", W[:, k, :], wv[:, k, :], writes=[(kW, k)])
        else:
            W, kW = Wpre
        G = E2(nc.sbuf_tensor(f"{name}_G", [128, NQK, HD], F32))
        kG = c.key("G")
        c.dma("sp", G[:, 0:NH, :], gq.rearrange("(o h d) -> o h d", o=1, h=1).to_broadcast([128, NH, HD]), writes=[kG])
        c.dma("sp", G[:, NH:NQK, :], gk.rearrange("(o h d) -> o h d", o=1, h=1).to_broadcast([128, NKV, HD]), writes=[kG])
        c.op("dve", lambda: nc.vector.tensor_scalar(out=G[:, 0:NH, :], in0=G[:, 0:NH, :], scalar1=HD ** -0.5,
                                                    scalar2=None, op0=ALU.mult), reads=[kG], writes=[kG])
        eps = E2(nc.sbuf_tensor(f"{name}_eps2", [128, 1], F32))
        keps = c.key("eps")
        c.op("pool", lambda: nc.gpsimd.memset(eps[:], EPS), writes=[keps])
        if VA.shape[-1] > HD + 1:
            c.op("pool", lambda: nc.gpsimd.memset(VA[:, :, :, HD:], 0.0), writes=[(kVA, "ones")])
        c.op("pool", lambda: nc.gpsimd.memset(VA[:, :, :, HD:HD + 1], 1.0), writes=[(kVA, "ones")])
        c.op("pool", lambda: nc.gpsimd.memset(VB[:, :, :, 0:HD], 0.0), writes=[(kVA, "ones")])
        c.op("pool", lambda: nc.gpsimd.memset(VB[:, :, :, 0:1], 1.0), writes=[(kVA, "ones")])
        if rope:
            cs = E2(nc.sbuf_tensor(f"{name}_cs", [128, NT, 2, 32], F32))
            kcs = c.key("cs")
            c.dma("sp", cs[:, :, 0, :], cos_d.rearrange("(i p) f -> p i f", p=128), writes=[kcs])
            c.dma("sp", cs[:, :, 1, :], sin_d.rearrange("(i p) f -> p i f", p=128), writes=[kcs])
        pq = Ring(c, es2, f"{name}_pq", [128, 1536], F32, 2, psum=True)
        pT = Ring(c, es2, f"{name}_pT2", [128, 8, 128], BF16, 2, psum=True)
        nb_ = 1
        sq = Ring(c, es2, f"{name}_sq2", [128, NQK, HD], F32, nb_)
        st = Ring(c, es2, f"{name}_st2", [128, 2, NQK], F32, 2)
        qn = Ring(c, es2, f"{name}_qn", [128, NQK, HD], F32, 2)
        if rope:
            kdr = Ring(c, es2, f"{name}_kd", [128, NKV, 2, HD], BF16, 2)
            tA = Ring(c, es2, f"{name}_tA", [128, NQK, 32], F32, 1)
            tB = Ring(c, es2, f"{name}_tB", [128, NQK, 32], F32, 1)
            tC = Ring(c, es2, f"{name}_tC", [128, NQK, 32], F32, 1)
            tD = Ring(c, es2, f"{name}_tD", [128, NQK, 32], F32, 1)
            ro = Ring(c, es2, f"{name}_ro", [128, NQK, HD], F32, 1)
        qr = Ring(c, es2, f"{name}_qr", [128, NQK, HD], BF16, 2)
        def make_tile(i):
            T = {}

            def mm():
                p_t, kp = pq.next()
                for cb in range(3):
                    for k in range(8):
                        c.op("pe", lambda: nc.tensor.matmul(p_t[:, cb * 512:(cb + 1) * 512],
                                                            lhsT=xnT[:, k, i * 128:(i + 1) * 128],
                                                            rhs=W[:, k, cb * 512:(cb + 1) * 512],
                                                            start=(k == 0), stop=(k == 7)),
                             reads=[(kW, k), (kxnT, i)], writes=[kp], signal=(k == 7 and cb == 2))
                T['p_t'], T['kp'] = p_t, kp

            def chain():
                p_t, kp = T['p_t'], T['kp']
                pqk = p_t[:, 0:NQK * HD].rearrange("p (h d) -> p h d", d=HD)
                s_t, ks = sq.next()
                st_t, kst = st.next()
                q_t, kq = qn.next()
                r_t, kr = qr.next()
                c.op("act", lambda: nc.scalar.activation(out=s_t[:], in_=pqk, func=AF.Square), reads=[kp], writes=[ks])
                c.op("dve", lambda: nc.vector.tensor_tensor(out=q_t[:], in0=pqk, in1=G[:], op=ALU.mult), reads=[kp, ks, kG], writes=[kq])
                c.op("act", lambda: nc.scalar.copy(out=VA[:, i, :, 0:HD],
                                                   in_=p_t[:, NQK * HD:1536].rearrange("p (g d) -> p g d", d=HD)),
                     reads=[kp, kq], writes=[(kVA, i)])
                c.op("act", lambda: nc.scalar.copy(out=VB[:, i, :, HD:2 * HD],
                                                   in_=p_t[:, NQK * HD:1536].rearrange("p (g d) -> p g d", d=HD)),
                     reads=[kp, kq], writes=[(kVA, i, "b")])
                c.op("dve", lambda: nc.vector.tensor_reduce(out=st_t[:, 0, :], in_=s_t[:], axis=AX.X, op=ALU.add),
                     reads=[ks], writes=[kst])
                c.op("act", lambda: nc.scalar.activation(out=st_t[:, 1, :], in_=st_t[:, 0, :], func=AF.Sqrt,
                                                          scale=1.0 / HD, bias=eps[:]), reads=[kst, keps], writes=[kst])
                c.op("dve", lambda: nc.vector.reciprocal(out=st_t[:, 1, :], in_=st_t[:, 1, :]), reads=[kst], writes=[kst])
                rstd_b = st_t[:, 1, :].unsqueeze(2).to_broadcast([128, NQK, HD])
                if not rope:
                    c.op("dve", lambda: nc.vector.tensor_tensor(out=r_t[:], in0=q_t[:], in1=rstd_b, op=ALU.mult),
                         reads=[kq, kst], writes=[(kr, 0), (kr, 1)])
                else:
                    qv = q_t[:].rearrange("p h (f two) -> p h f two", two=2)
                    x0 = qv[:, :, :, 0]
                    x1 = qv[:, :, :, 1]
                    cosb = cs[:, i, 0, :].unsqueeze(1).to_broadcast([128, NQK, 32])
                    sinb = cs[:, i, 1, :].unsqueeze(1).to_broadcast([128, NQK, 32])
                    a_t, ka = tA.next()
                    b_t, kb = tB.next()
                    c_t, kc = tC.next()
                    d_t, kd = tD.next()
                    o_t, ko = ro.next()
                    ov = o_t[:].rearrange("p h (f two) -> p h f two", two=2)
                    c.op("dve", lambda: nc.vector.tensor_tensor(out=a_t[:], in0=x0, in1=cosb, op=ALU.mult), reads=[kq, kcs], writes=[ka])
                    c.op("dve", lambda: nc.vector.tensor_tensor(out=b_t[:], in0=x1, in1=sinb, op=ALU.mult), reads=[kq, kcs], writes=[kb])
                    c.op("dve", lambda: nc.vector.tensor_tensor(out=ov[:, :, :, 0], in0=a_t[:], in1=b_t[:], op=ALU.subtract),
                         reads=[ka, kb], writes=[(ko, 0)])
                    c.op("pool", lambda: nc.gpsimd.tensor_tensor(out=c_t[:], in0=x0, in1=sinb, op=ALU.mult), reads=[kq, kcs], writes=[kc])
                    c.op("pool", lambda: nc.gpsimd.tensor_tensor(out=d_t[:], in0=x1, in1=cosb, op=ALU.mult), reads=[kq, kcs], writes=[kd])
                    c.op("pool", lambda: nc.gpsimd.tensor_tensor(out=ov[:, :, :, 1], in0=c_t[:], in1=d_t[:], op=ALU.add),
                         reads=[kc, kd], writes=[(ko, 1)])
                    c.op("dve", lambda: nc.vector.tensor_tensor(out=r_t[:], in0=o_t[:], in1=rstd_b, op=ALU.mult),
                         reads=[(ko, 0), (ko, 1), kst], writes=[(kr, 0), (kr, 1)])
                    kd_t, kkd = kdr.next()
                    c.op("pool", lambda: nc.gpsimd.tensor_copy(out=kd_t[:], in_=r_t[:, NH:NQK, :].unsqueeze(2).to_broadcast([128, NKV, 2, HD])),
                         reads=[(kr, 0), (kr, 1)], writes=[kkd])
                T['r_t'], T['kr'] = r_t, kr
                if dupk:
                    T['kd_t'], T['kkd'] = kd_t, kkd

            def tr():
                r_t, kr = T['r_t'], T['kr']
                if dupk:
                    kd_t, kkd = T['kd_t'], T['kkd']
                rflat = r_t[:].rearrange("p h d -> p (h d)")
                if dupk:
                    kflat = kd_t[:].rearrange("p g t d -> p (g t d)")
                t_t, kt_ = pT.next()
                for j in range(8):
                    c.op("pe", lambda: nc.tensor.transpose(t_t[:, j, :], rflat[:, j * 128:(j + 1) * 128], ident[:]),
                         reads=[(kr, 0), (kr, 1)], writes=[kt_], signal=(j == 7))
                if QZ is None:
                    c.op("act", lambda: nc.scalar.copy(out=QT[:, 0:8, i * 128:(i + 1) * 128], in_=t_t[:]),
                         reads=[kt_], writes=[(kQT, i, 0)])
                else:
                    nqc = NH // 2
                    qzv = QZ[:].rearrange("p (c two) s -> p c two s", two=2)
                    c.op("act", lambda: nc.scalar.copy(out=qzv[0:64, :, 0, i * 128:(i + 1) * 128], in_=t_t[0:64, 0:nqc, :]),
                         reads=[kt_, "qz0", "qz1"], writes=[(kQT, i, 0)])
                    c.op("dve", lambda: nc.vector.tensor_copy(out=qzv[64:128, :, 1, i * 128:(i + 1) * 128], in_=t_t[64:128, 0:nqc, :]),
                         reads=[kt_, (kQT, i, 0)], writes=[(kQT, i, 2)])
                    if not dupk:
                        c.op("dve", lambda: nc.vector.tensor_copy(out=QT[:, koff:koff + NKV // 2, i * 128:(i + 1) * 128],
                                                                  in_=t_t[:, nqc:nqc + NKV // 2, :]),
                             reads=[kt_, (kQT, i, 2)], writes=[(kQT, i, 3)])
                if dupk:
                    t_t, kt_ = pT.next()
                    for j in range(NKV):
                        c.op("pe", lambda: nc.tensor.transpose(t_t[:, j, :], kflat[:, j * 128:(j + 1) * 128], ident[:]),
                             reads=[kkd], writes=[kt_], signal=(j == NKV - 1))
                    c.op("act", lambda: nc.scalar.copy(out=QT[:, koff:koff + NKV, i * 128:(i + 1) * 128], in_=t_t[:, 0:NKV, :]),
                         reads=[kt_], writes=[(kQT, i, 1)])
            return mm, chain, tr

        tiles = [make_tile(i) for i in range(NT)]
        tiles[0][0]()
        tiles[1][0]()
        tiles[0][1]()
        for i in range(NT):
            if i + 2 < NT:
                tiles[i + 2][0]()
            if i + 1 < NT:
                tiles[i + 1][1]()
            tiles[i][2]()
        c.barrier()


def emit_odd(c, x_in, x_out, g_dram, wqkv, gq, gk, wo, cos_d, sin_d, ident, ones32, name):
    nc = c.nc
    NH, NKV, HD = 16, 4, 64
    NQK = NH + NKV
    with ExitStack() as es:
        E = es.enter_context
        QT = E(nc.sbuf_tensor(f"{name}_QT", [128, NKV, S], BF16))
        kQT = c.key("QT")
        VAB = E(nc.sbuf_tensor(f"{name}_VAB", [128, NT, NKV, 192], BF16))
        VA = VAB[:, :, :, 0:128]
        VB = VAB[:, :, :, 64:192]
        kVA = c.key("VA")
        QZ = E(nc.sbuf_tensor(f"{name}_QZ", [128, NH, S], BF16))
        c.op("pool", lambda: nc.gpsimd.memset(QZ[:, 0:NH // 2, :], 0.0), writes=["qz0"])
        c.op("pool", lambda: nc.gpsimd.memset(QZ[:, NH // 2:NH, :], 0.0), writes=["qz1"])
        with ExitStack() as es0:
            xnT = es0.enter_context(nc.sbuf_tensor(f"{name}_xnT", [128, 8, S], BF16))
            kxnT = c.key("xnT")
            Wq = es0.enter_context(nc.sbuf_tensor(f"{name}_Wq", [128, 8, 1536], BF16))
            kWq = c.key("Wq")
            wqv = wqkv.rearrange("(k p) n -> p k n", p=128)
            for k in range(8):
                c.dma("pool", Wq[:, k, :], wqv[:, k, :], writes=[(kWq, k)])
            with ExitStack() as es1:
                emit_norm_T(c, es1, x_in, g_dram, xnT, kxnT, ident, name)
                c.barrier()
            emit_qkv_proj(c, xnT, kxnT, wqv, gq, gk, cos_d, sin_d, ident,
                          QT, kQT, VA, VB, kVA, NH, NKV, True, name, QZ=QZ, koff=0, Wpre=(Wq, kWq))
        OT = E(nc.sbuf_tensor(f"{name}_OT", [128, 8, S], BF16))
        kOT = c.key("OT")
        Wo = E(nc.sbuf_tensor(f"{name}_Wo", [128, 8, D], BF16))
        kWo = c.key("Wo")
        wov = wo.rearrange("(k p) n -> p k n", p=128)
        for k in range(8):
            c.dma("pool", Wo[:, k, :], wov[:, k, :], writes=[(kWo, k)])
        with ExitStack() as es3:
            ps = Ring(c, es3, f"{name}_ps", [128, 2, 512], F32, 3, psum=True)
            po = Ring(c, es3, f"{name}_po", [128, 512], F32, 1, psum=True)
            pb = Ring(c, es3, f"{name}_pb", [128, 512], F32, 1, psum=True)
            PT = Ring(c, es3, f"{name}_PT", [128, 2, 512], BF16, 3)
            osb = Ring(c, es3, f"{name}_osb", [128, 512], F32, 2)
            rd = Ring(c, es3, f"{name}_rd", [128, 512], F32, 2)
            iters = []
            for h in range(NH):
                for qb in range(4):
                    grp = {}
                    for kt2 in range(8):
                        def mk(h=h, qb=qb, kt2=kt2, grp=grp):
                            g = h // 4
                            hb = (h % 2) * 64
                            stt = {}

                            def qk():
                                stt["s"], stt["ks"] = ps.next()
                                for _ in range(N_DUMMY):
                                    c.op("pe", lambda: nc.tensor.matmul(stt["s"][:, 0, :], lhsT=QT[:, 0, 0:128], rhs=QT[:, 0, 0:512],
                                                                        start=True, stop=True), reads=[], writes=[stt["ks"]], signal=False)
                                for j in range(2):
                                    kt = kt2 * 2 + j
                                    c.op("pe", lambda: nc.tensor.matmul(stt["s"][:, j, :], lhsT=QT[:, g, kt * 128:(kt + 1) * 128],
                                                                        rhs=QZ[:, h, qb * 512:(qb + 1) * 512], start=True, stop=True),
                                         reads=[], writes=[stt["ks"]], signal=(j == 1))

                            def ex():
                                stt["p"], stt["kp"] = PT.next()
                                c.op("act", lambda: nc.scalar.activation(out=stt["p"][:], in_=stt["s"][:], func=AF.Exp),
                                     reads=[stt["ks"]], writes=[stt["kp"]])

                            def pv():
                                if kt2 == 0:
                                    grp["o"], grp["ko"] = po.next()
                                o_t, ko = grp["o"], grp["ko"]
                                for j in range(2):
                                    kt = kt2 * 2 + j
                                    if hb == 0:
                                        c.op("pe", lambda: nc.tensor.matmul(o_t[:, :], lhsT=VA[:, kt, g, :], rhs=stt["p"][:, j, :],
                                                                            start=(kt == 0), stop=(kt == 15)),
                                             reads=[stt["kp"]], writes=[ko], signal=(j == 1))
                                    else:
                                        c.op("pe", lambda: nc.tensor.matmul(o_t[:, :], lhsT=VB[:, kt, g, :], rhs=stt["p"][:, j, :],
                                                                            start=(kt == 0), stop=(kt == 15)),
                                             reads=[stt["kp"]], writes=[ko], signal=(j == 1))

                            it = {"qk": qk, "exp": ex, "pv": pv}
                            if kt2 == 7:
                                def post_factory():
                                    return make_norm_post(c, grp["o"], grp["ko"], hb,
                                                          OT[hb:hb + 64, h // 2, qb * 512:(qb + 1) * 512], (kOT, h, qb), osb, rd, pb, ones32)
                                it["post_factory"] = post_factory
                            return it
                        iters.append(mk())
            warm_pe(c, pb.t[0], pb.k[0], QT[:, 0, :])
            run_pipeline(iters)
            c.barrier()
        with ExitStack() as es4:
            py = Ring(c, es4, f"{name}_py", [128, D], F32, 2, psum=True)
            xr = Ring(c, es4, f"{name}_xr", [128, D], F32, 4)
            yo = Ring(c, es4, f"{name}_yo", [128, D], F32, 3)
            xq = []
            for tt in range(3):
                x_t, kx = xr.next()
                c.dma("sp", x_t[:], x_in[tt * 128:(tt + 1) * 128, :], writes=[kx])
                xq.append((x_t, kx))
            for tt in range(NT):
                y_p, kyp = py.next()
                for k in range(8):
                    for ob in range(2):
                        c.op("pe", lambda: nc.tensor.matmul(y_p[:, ob * 512:(ob + 1) * 512], lhsT=OT[:, k, tt * 128:(tt + 1) * 128],
                                                            rhs=Wo[:, k, ob * 512:(ob + 1) * 512], start=(k == 0), stop=(k == 7)),
                             reads=[(kWo, k)], writes=[kyp], signal=(k == 7 and ob == 1))
                x_t, kx = xq.pop(0)
                y_t, ky = yo.next()
                c.op("dve", lambda: nc.vector.tensor_tensor(out=y_t[:], in0=x_t[:], in1=y_p[:], op=ALU.add),
                     reads=[kx, kyp], writes=[ky])
                if tt + 3 < NT:
                    x_n, kxn = xr.next()
                    c.dma("sp", x_n[:], x_in[(tt + 3) * 128:(tt + 4) * 128, :], writes=[kxn])
                    xq.append((x_n, kxn))
                c.dma("sp", x_out[tt * 128:(tt + 1) * 128, :], y_t[:], reads=[ky])
            c.barrier()


NA_TYPES = [(0, j) for j in range(4)] + [(1, j) for j in range(4)] + [(2, j) for j in range(5)] + \
           [(3, j) for j in range(4)] + [(4, j) for j in range(4)]
NA_TIX = {t: n for n, t in enumerate(NA_TYPES)}
NTYPES = len(NA_TYPES)


def na_cls(i):
    if i == 0:
        return 0, 0, 4
    if i == 1:
        return 1, 0, 4
    if i == 14:
        return 3, 12, 4
    if i == 15:
        return 4, 12, 4
    return 2, i - 2, 5


def emit_even(c, x_in, x_out, P, ident, ones32, name):
    nc = c.nc
    HD = 64
    w_in_v = P["w_in"].rearrange("(k p) n -> p k n", p=128)
    with ExitStack() as es:
        E = es.enter_context
        mt_v = P["mt_scr"].rearrange("k p s -> p k s")
        kMT = c.key("MT")
        DT = E(nc.sbuf_tensor(f"{name}_DT", [128, NT, 32], F32))
        AA = E(nc.sbuf_tensor(f"{name}_AA", [128, NT, 32], F32))
        kDT = c.key("DT")
        kAA = c.key("AA")
        eps = E(nc.sbuf_tensor(f"{name}_epsE", [128, 1], F32))
        keps = c.key("eps")
        c.op("pool", lambda: nc.gpsimd.memset(eps[:], EPS), writes=[keps])
        with ExitStack() as esn:
            if True:
                En = esn.enter_context
                MT = En(nc.sbuf_tensor(f"{name}_MTn", [128, 4, S], BF16))
                QT = En(nc.sbuf_tensor(f"{name}_QT", [128, 4, S], BF16))
                kQT = c.key("QT")
                QZ = En(nc.sbuf_tensor(f"{name}_QZ", [128, 8, S], BF16))
                c.op("pool", lambda: nc.gpsimd.memset(QZ[:, 0:4, :], 0.0), writes=["qz0"])
                c.op("pool", lambda: nc.gpsimd.memset(QZ[:, 4:8, :], 0.0), writes=["qz1"])
                VAB = En(nc.sbuf_tensor(f"{name}_VAB", [128, NT, 8, 192], BF16))
                VA = VAB[:, :, :, 0:128]
                VB = VAB[:, :, :, 64:192]
                kVA = c.key("VA")
                with ExitStack() as esx:
                    xnT = esx.enter_context(nc.sbuf_tensor(f"{name}_xnT", [128, 8, S], BF16))
                    kxnT = c.key("xnT")
                    Wq = esx.enter_context(nc.sbuf_tensor(f"{name}_Wq", [128, 8, 1536], BF16))
                    kWq = c.key("Wq")
                    for k in range(8):
                        c.dma("pool", Wq[:, k, :], w_in_v[:, k, 0:1536], writes=[(kWq, k)])
                    with ExitStack() as es1:
                        emit_norm_T(c, es1, x_in, P["mix_norm"], xnT, kxnT, ident, name)
                        c.barrier()
                    emit_qkv_proj(c, xnT, kxnT, w_in_v[:, :, 0:1536], P["na_gq"], P["na_gk"], None, None, ident,
                                  QT, kQT, VA, VB, kVA, 8, 8, False, name + "n", QZ=QZ, koff=0, Wpre=(Wq, kWq))
                BM = En(nc.sbuf_tensor(f"{name}_BM", [128, NTYPES, 8, 128], BF16))
                kBM = c.key("BM")
                with ExitStack() as esb:
                    bg = Ring(c, esb, f"{name}_bg", [128, 8, 128], F32, 2)
                    mk = Ring(c, esb, f"{name}_mk", [128, 128], F32, 2)
                    for t in range(NTYPES):
                        b_t, kb = bg.next()
                        m_t, km = mk.next()
                        c.dma("sp", b_t[:], P["biasg"][t], writes=[kb])
                        c.dma("sp", m_t[:], P["namask"][t], writes=[km])
                        c.op("dve", lambda: nc.vector.tensor_tensor(out=BM[:, t, :, :], in0=b_t[:],
                                                                    in1=m_t[:].unsqueeze(1).to_broadcast([128, 8, 128]), op=ALU.add),
                             reads=[kb, km], writes=[kBM])
                    c.barrier()
                with ExitStack() as es3:
                    ps = Ring(c, es3, f"{name}_ps", [128, 8, 128], F32, 3, psum=True)
                    po = Ring(c, es3, f"{name}_po", [128, 512], F32, 1, psum=True)
                    pb = Ring(c, es3, f"{name}_pb", [128, 512], F32, 1, psum=True)
                    PT = Ring(c, es3, f"{name}_PT", [128, 5, 128], BF16, 3)
                    osb = Ring(c, es3, f"{name}_osb", [128, 512], F32, 2)
                    rd = Ring(c, es3, f"{name}_rd", [128, 512], F32, 2)
                    iters = []
                    for h in range(8):
                        for ig in range(4):
                            grp = {}
                            for ii in range(4):
                                def mk(h=h, ig=ig, ii=ii, grp=grp):
                                    hb = (h % 2) * 64
                                    qc = h // 2
                                    kc = 4 + h // 2
                                    i = ig * 4 + ii
                                    cls, kp0, ntl = na_cls(i)
                                    stt = {}

                                    def qk():
                                        stt["s"], stt["ks"] = ps.next()
                                        t0 = NA_TIX[(cls, 0)]
                                        c.op("pe", lambda: nc.tensor.matmul(stt["s"][:, 0:4, :], lhsT=ident[:], rhs=BM[:, t0:t0 + 4, h, :],
                                                                            start=True, stop=False),
                                             reads=[], writes=[stt["ks"]], signal=False)
                                        for j in range(4):
                                            kp = kp0 + j
                                            c.op("pe", lambda: nc.tensor.matmul(stt["s"][:, j, :], lhsT=QT[:, h // 2, kp * 128:(kp + 1) * 128],
                                                                                rhs=QZ[:, h, i * 128:(i + 1) * 128], start=False, stop=(j == 3)),
                                                 reads=[], writes=[stt["ks"]], signal=(j == 3 and ntl == 4))
                                        if ntl == 5:
                                            kp = kp0 + 4
                                            c.op("pe", lambda: nc.tensor.matmul(stt["s"][:, 4, :], lhsT=ident[:], rhs=BM[:, t0 + 4, h, :],
                                                                                start=True, stop=False),
                                                 reads=[], writes=[stt["ks"]], signal=False)
                                            c.op("pe", lambda: nc.tensor.matmul(stt["s"][:, 4, :], lhsT=QT[:, h // 2, kp * 128:(kp + 1) * 128],
                                                                                rhs=QZ[:, h, i * 128:(i + 1) * 128], start=False, stop=True),
                                                 reads=[], writes=[stt["ks"]], signal=True)

                                    def ex():
                                        stt["p"], stt["kp"] = PT.next()
                                        c.op("act", lambda: nc.scalar.activation(out=stt["p"][:, 0:ntl, :], in_=stt["s"][:, 0:ntl, :], func=AF.Exp),
                                             reads=[stt["ks"]], writes=[stt["kp"]])

                                    def pv():
                                        if ii == 0:
                                            grp["o"], grp["ko"] = po.next()
                                        o_t, ko = grp["o"], grp["ko"]
                                        for j in range(ntl):
                                            kp = kp0 + j
                                            if hb == 0:
                                                c.op("pe", lambda: nc.tensor.matmul(o_t[:, ii * 128:(ii + 1) * 128], lhsT=VA[:, kp, h, :],
                                                                                    rhs=stt["p"][:, j, :], start=(j == 0), stop=(j == ntl - 1)),
                                                     reads=[stt["kp"]], writes=[ko], signal=(j == ntl - 1))
                                            else:
                                                c.op("pe", lambda: nc.tensor.matmul(o_t[:, ii * 128:(ii + 1) * 128], lhsT=VB[:, kp, h, :],
                                                                                    rhs=stt["p"][:, j, :], start=(j == 0), stop=(j == ntl - 1)),
                                                     reads=[stt["kp"]], writes=[ko], signal=(j == ntl - 1))

                                    it = {"qk": qk, "exp": ex, "pv": pv}
                                    if ii == 3:
                                        def post_factory():
                                            return make_norm_post(c, grp["o"], grp["ko"], hb,
                                                                  MT[hb:hb + 64, h // 2, ig * 512:(ig + 1) * 512], (kMT, h, ig), osb, rd, pb, ones32,
                                                                  nrows=128)
                                        it["post_factory"] = post_factory
                                    return it
                                iters.append(mk())
                    warm_pe(c, pb.t[0], pb.k[0], QT[:, 0, :])
                    run_pipeline(iters)
                    c.barrier()
                    c.dma("sp", mt_v[:, 0:4, :], MT[:], writes=[("mt_d", "na")])
                    c.barrier()
        XT = E(nc.sbuf_tensor(f"{name}_XT", [128, NT, 1024], BF16))
        kXT = c.key("XT")
        BT = E(nc.sbuf_tensor(f"{name}_BT", [128, 4, S], BF16))
        CT = E(nc.sbuf_tensor(f"{name}_CT", [128, 4, S], BF16))
        kBT = c.key("BT")
        kCT = c.key("CT")
        Btok = E(nc.sbuf_tensor(f"{name}_Btok", [128, NT, 512], BF16))
        kBtok = c.key("Btok")
        with ExitStack() as esx:
            xnT = esx.enter_context(nc.sbuf_tensor(f"{name}_xnTb", [128, 8, S], BF16))
            kxnT = c.key("xnT")
            with ExitStack() as es1:
                emit_norm_T(c, es1, x_in, P["mix_norm"], xnT, kxnT, ident, name + "b")
                c.barrier()
            xn_all = [(kxnT, i) for i in range(NT)]
            with ExitStack() as esz:
                Ez = esz.enter_context
                Wz = Ez(nc.sbuf_tensor(f"{name}_Wz", [128, 8, 1056], BF16))
                kWz = c.key("Wz")
                for k in range(8):
                    c.dma("pool", Wz[:, k, 0:1024], w_in_v[:, k, 1536:2560], writes=[(kWz, k)])
                    c.dma("pool", Wz[:, k, 1024:1056], w_in_v[:, k, 4608:4640], writes=[(kWz, k, "d")])
                dtb = Ez(nc.sbuf_tensor(f"{name}_dtb", [128, 32], F32))
                alb = Ez(nc.sbuf_tensor(f"{name}_alb", [128, 32], F32))
                kdtb = c.key("dtb")
                kalb = c.key("alb")
                c.dma("sp", dtb[:], bcast_row(P["dt_bias"], 32), writes=[kdtb])
                c.dma("sp", alb[:], bcast_row(P["A_log"], 32), writes=[kalb])
                pz = Ring(c, esz, f"{name}_pz", [128, 1024], F32, 2, psum=True)
                pd = Ring(c, esz, f"{name}_pd", [128, 512], F32, 2, psum=True)
                zb = Ring(c, esz, f"{name}_zb", [128, 1024], BF16, 3)
                for i in range(NT):
                    z_p, kzp = pz.next()
                    d_p, kdp = pd.next()
                    for cb in range(2):
                        for k in range(8):
                            c.op("pe", lambda: nc.tensor.matmul(z_p[:, cb * 512:(cb + 1) * 512], lhsT=xnT[:, k, i * 128:(i + 1) * 128],
                                                                rhs=Wz[:, k, cb * 512:(cb + 1) * 512], start=(k == 0), stop=(k == 7)),
                                 reads=[(kWz, k), (kxnT, i)], writes=[kzp], signal=(k == 7 and cb == 1))
                    for k in range(8):
                        c.op("pe", lambda: nc.tensor.matmul(d_p[:, 0:32], lhsT=xnT[:, k, i * 128:(i + 1) * 128],
                                                            rhs=Wz[:, k, 1024:1056], start=(k == 0), stop=(k == 7)),
                             reads=[(kWz, k, "d"), (kxnT, i)], writes=[kdp], signal=(k == 7))
                    z_t, kz = zb.next()
                    c.op("act", lambda: nc.scalar.activation(out=z_t[:], in_=z_p[:], func=AF.Silu), reads=[kzp], writes=[kz])
                    c.dma("sp", P["zs"][i * 128:(i + 1) * 128, :], z_t[:], reads=[kz], writes=[("zs_d", i)])
                    c.op("dve", lambda: nc.vector.tensor_tensor(out=DT[:, i, :], in0=d_p[:, 0:32], in1=dtb[:], op=ALU.add),
                         reads=[kdp, kdtb], writes=[kDT])
                c.op("act", lambda: nc.scalar.activation(out=DT[:], in_=DT[:], func=AF.Exp), reads=[kDT], writes=[kDT])
                c.op("act", lambda: nc.scalar.activation(out=DT[:], in_=DT[:], func=AF.Ln, bias=1.0), reads=[kDT], writes=[kDT])
                c.op("act", lambda: nc.scalar.activation(out=alb[:], in_=alb[:], func=AF.Exp), reads=[kalb], writes=[kalb])
                c.op("dve", lambda: nc.vector.tensor_scalar(out=alb[:], in0=alb[:], scalar1=-1.0, scalar2=None, op0=ALU.mult),
                     reads=[kalb], writes=[kalb])
                c.op("dve", lambda: nc.vector.tensor_tensor(out=AA[:], in0=DT[:], in1=alb[:].unsqueeze(1).to_broadcast([128, NT, 32]),
                                                            op=ALU.mult), reads=[kDT, kalb], writes=[kAA])
                c.barrier()
            with ExitStack() as esc:
                Ec = esc.enter_context
                cw = Ec(nc.sbuf_tensor(f"{name}_cw", [128, 16, 4], F32))
                cbias = Ec(nc.sbuf_tensor(f"{name}_cb", [128, 16], F32))
                kcw = c.key("cw")
                for j in range(4):
                    c.dma("sp", cw[:, :, j], P["conv_w"][j].rearrange("(c p) -> p c", p=128), writes=[kcw], allow_slow_non_contiguous=True)
                c.dma("sp", cbias[:], P["conv_b"].rearrange("(c p) -> p c", p=128), writes=[kcw], allow_slow_non_contiguous=True)
                wx = Ring(c, esc, f"{name}_wx", [128, 8, 128], BF16, 3)
                px = Ring(c, esc, f"{name}_px", [128, 1024], F32, 2, psum=True)
                pTr = Ring(c, esc, f"{name}_pTr", [128, 8, 128], BF16, 2, psum=True)
                Rr = Ring(c, esc, f"{name}_R", [128, S + 3], F32, 2)
                acc = Ring(c, esc, f"{name}_acc", [128, S], F32, 2)
                xo = Ring(c, esc, f"{name}_xo", [128, S], BF16, 2)
                for t_, k_ in zip(Rr.t, Rr.k):
                    c.op("pool", lambda: nc.gpsimd.memset(t_[:, 0:2], 0.0), writes=[k_])
                    c.op("pool", lambda: nc.gpsimd.memset(t_[:, S + 2:S + 3], 0.0), writes=[k_])
                def make_ch(ch):
                    T = {}

                    def front():
                            w_t, kw = wx.next()
                            c.dma("pool", w_t[:], w_in_v[:, :, 2560 + ch * 128:2560 + (ch + 1) * 128], writes=[kw])
                            R_t, kR = Rr.next()
                            for half in range(2):
                                p_t, kp = px.next()
                                for k in range(8):
                                    for nb in range(2):
                                        t0 = half * 1024 + nb * 512
                                        c.op("pe", lambda: nc.tensor.matmul(p_t[:, nb * 512:(nb + 1) * 512], lhsT=w_t[:, k, :],
                                                                            rhs=xnT[:, k, t0:t0 + 512], start=(k == 0), stop=(k == 7)),
                                             reads=[kw] + xn_all, writes=[kp], signal=(k == 7 and nb == 1))
                                c.op("act", lambda: nc.scalar.copy(out=R_t[:, 2 + half * 1024:2 + (half + 1) * 1024], in_=p_t[:]),
                                     reads=[kp], writes=[kR])
                            T['R'] = (R_t, kR)

                    def back1():
                            R_t, kR = T['R']
                            a_t, ka = acc.next()
                            c.op("act", lambda: nc.scalar.activation(out=a_t[:], in_=R_t[:, 0:S], func=AF.Identity,
                                                                      scale=cw[:, ch, 0:1], bias=cbias[:, ch:ch + 1]),
                                 reads=[kR, kcw], writes=[ka])
                            c.op("dve", lambda: nc.vector.scalar_tensor_tensor(out=a_t[:], in0=R_t[:, 1:S + 1], scalar=cw[:, ch, 1:2], in1=a_t[:],
                                                                               op0=ALU.mult, op1=ALU.add), reads=[kR, kcw, ka], writes=[ka])
                            c.op("dve", lambda: nc.vector.scalar_tensor_tensor(out=a_t[:], in0=R_t[:, 2:S + 2], scalar=cw[:, ch, 2:3], in1=a_t[:],
                                                                                op0=ALU.mult, op1=ALU.add), reads=[kR, kcw, ka], writes=[ka])
                            c.op("dve", lambda: nc.vector.scalar_tensor_tensor(out=a_t[:], in0=R_t[:, 3:S + 3], scalar=cw[:, ch, 3:4], in1=a_t[:],
                                                                               op0=ALU.mult, op1=ALU.add), reads=[kR, kcw, ka], writes=[ka])
                            if ch < 8:
                                o_t, ko = xo.next()
                                dst = o_t[:]
                                wk = [ko]
                            elif ch < 12:
                                dst = BT[:, ch - 8, :]
                                ko = (kBT, ch - 8)
                                wk = [ko]
                            else:
                                dst = CT[:, ch - 12, :]
                                wk = [(kCT, ch - 12)]
                            T['dst'] = (dst, wk, a_t, ka)

                    def back2():
                            dst, wk, a_t, ka = T['dst']
                            c.op("act", lambda: nc.scalar.activation(out=dst, in_=a_t[:], func=AF.Silu), reads=[ka], writes=wk)
                            if ch < 12:
                                for half in range(2):
                                    t_t, kt_ = pTr.next()
                                    for tt in range(8):
                                        t0 = (half * 8 + tt) * 128
                                        c.op("pe", lambda: nc.tensor.transpose(t_t[:, tt, :], dst[:, t0:t0 + 128], ident[:]),
                                             reads=wk, writes=[kt_], signal=(tt == 7))
                                    if ch < 8:
                                        c.op("dve", lambda: nc.vector.tensor_copy(out=XT[:, half * 8:(half + 1) * 8, ch * 128:(ch + 1) * 128], in_=t_t[:]),
                                             reads=[kt_], writes=[(kXT, ch, half)])
                                    else:
                                        c.op("dve", lambda: nc.vector.tensor_copy(out=Btok[:, half * 8:(half + 1) * 8, (ch - 8) * 128:(ch - 7) * 128],
                                                                                  in_=t_t[:]), reads=[kt_], writes=[(kBtok, ch, half)])
                    return front, back1, back2

                chs = [make_ch(ch) for ch in range(16)]
                chs[0][0]()
                chs[1][0]()
                chs[0][1]()
                for ch in range(16):
                    if ch + 2 < 16:
                        chs[ch + 2][0]()
                    if ch + 1 < 16:
                        chs[ch + 1][1]()
                    chs[ch][2]()
                c.barrier()
        emit_ssd(c, P, XT, BT, CT, Btok, DT, AA, eps, ident, ones32, name)
        with ExitStack() as es4:
            MT = es4.enter_context(nc.sbuf_tensor(f"{name}_MT", [128, 12, S], BF16))
            for k in range(12):
                c.dma("sp", MT[:, k, :], mt_v[:, k, :], writes=[(kMT, "ld", k)])
            Wo = es4.enter_context(nc.sbuf_tensor(f"{name}_Wo", [128, 12, D], BF16))
            kWo = c.key("Wo")
            wov = P["w_out"].rearrange("(k p) n -> p k n", p=128)
            for k in range(12):
                c.dma("pool", Wo[:, k, :], wov[:, k, :], writes=[(kWo, k)])
            py = Ring(c, es4, f"{name}_py", [128, D], F32, 2, psum=True)
            xr = Ring(c, es4, f"{name}_xr", [128, D], F32, 4)
            yo = Ring(c, es4, f"{name}_yo", [128, D], F32, 3)
            xq = []
            for tt in range(3):
                x_t, kx = xr.next()
                c.dma("sp", x_t[:], x_in[tt * 128:(tt + 1) * 128, :], writes=[kx])
                xq.append((x_t, kx))
            for tt in range(NT):
                y_p, kyp = py.next()
                for k in range(12):
                    for ob in range(2):
                        c.op("pe", lambda: nc.tensor.matmul(y_p[:, ob * 512:(ob + 1) * 512], lhsT=MT[:, k, tt * 128:(tt + 1) * 128],
                                                            rhs=Wo[:, k, ob * 512:(ob + 1) * 512], start=(k == 0), stop=(k == 11)),
                             reads=[(kWo, k), (kMT, "ld", k)], writes=[kyp], signal=(k == 11 and ob == 1))
                x_t, kx = xq.pop(0)
                y_t, ky = yo.next()
                c.op("dve", lambda: nc.vector.tensor_tensor(out=y_t[:], in0=x_t[:], in1=y_p[:], op=ALU.add),
                     reads=[kx, kyp], writes=[ky])
                if tt + 3 < NT:
                    x_n, kxn = xr.next()
                    c.dma("sp", x_n[:], x_in[(tt + 3) * 128:(tt + 4) * 128, :], writes=[kxn])
                    xq.append((x_n, kxn))
                c.dma("sp", x_out[tt * 128:(tt + 1) * 128, :], y_t[:], reads=[ky])
            c.barrier()


def emit_ssd(c, P, XT, BT, CT, Btok, DT, AA, eps, ident, ones32, name):
    nc = c.nc
    NHD = 16
    mt_v = P["mt_scr"].rearrange("k p s -> p k s")
    with ExitStack() as es:
        E = es.enter_context
        U = E(nc.sbuf_tensor(f"{name}_U", [128, 128], F32))
        L = E(nc.sbuf_tensor(f"{name}_L", [128, 128], F32))
        NMf = E(nc.sbuf_tensor(f"{name}_NMf", [128, 128], F32))
        NMb = E(nc.sbuf_tensor(f"{name}_NMb", [128, 128], F32))
        Db = E(nc.sbuf_tensor(f"{name}_Db", [128, NHD], F32))
        gnb = E(nc.sbuf_tensor(f"{name}_gnb", [128, 1024], F32))
        kconst = c.key("ssdconst")
        c.dma("sp", U[:], P["tri_u"], writes=[kconst])
        c.dma("sp", L[:], P["tri_l"], writes=[kconst])
        c.dma("sp", NMf[:], P["negm_f"], writes=[kconst])
        c.dma("sp", NMb[:], P["negm_b"], writes=[kconst])
        c.dma("sp", Db[:], bcast_row(P["ssd_D"], NHD), writes=[kconst])
        c.dma("sp", gnb[:], bcast_row(P["out_norm"], 1024), writes=[kconst])
        Hbst = Ring(c, es, f"{name}_Hbst", [128, 1024], BF16, 2)
        Hbld = Ring(c, es, f"{name}_Hbld", [128, 1024], BF16, 2)
        kHb = c.key("Hball")
        Hf32 = E(nc.sbuf_tensor(f"{name}_Hf32", [128, 1024], F32))
        Hb32 = E(nc.sbuf_tensor(f"{name}_Hb32", [128, 1024], F32))
        kHf = c.key("Hf32")
        kHb32 = c.key("Hb32")
        c.op("pool", lambda: nc.gpsimd.memset(Hf32[:], 0.0), writes=[kHf])
        c.op("pool", lambda: nc.gpsimd.memset(Hb32[:], 0.0), writes=[kHb32])
        Hfb = Ring(c, es, f"{name}_Hfb", [128, 1024], BF16, 2)
        pR = Ring(c, es, f"{name}_pR", [128, 4, 128], F32, 2, psum=True)
        pcb = Ring(c, es, f"{name}_pcb", [128, 4, 128], F32, 1, psum=True)
        py = Ring(c, es, f"{name}_pyd", [128, 1024], F32, 1, psum=True)
        pY2 = Ring(c, es, f"{name}_pY2", [128, 1024], F32, 1, psum=True)
        pTn = Ring(c, es, f"{name}_pTn", [128, 8, 128], BF16, 1, psum=True)
        STr = Ring(c, es, f"{name}_ST", [128, 5, 32], F32, 4)
        wsm = Ring(c, es, f"{name}_wsm", [128, 16], F32, 3)
        xcf = Ring(c, es, f"{name}_xcf", [128, 1024], BF16, 2)
        xcb = Ring(c, es, f"{name}_xcb", [128, 1024], BF16, 2)
        xcd = Ring(c, es, f"{name}_xcd", [128, 1024], BF16, 3)
        cbT = Ring(c, es, f"{name}_cbT", [128, 4, 128], F32, 2)
        t1r = Ring(c, es, f"{name}_t1", [128, 4, 128], F32, 3)
        t2r = Ring(c, es, f"{name}_t2", [128, 4, 128], F32, 3)
        Mr = Ring(c, es, f"{name}_M", [128, 4, 128], BF16, 18)
        yar = Ring(c, es, f"{name}_ya", [128, 1024], F32, 2)
        ybr = Ring(c, es, f"{name}_yb", [128, 1024], F32, 2)
        ydr = Ring(c, es, f"{name}_yd", [128, 1024], F32, 2)
        zsr = Ring(c, es, f"{name}_zs", [128, 1024], BF16, 2)
        hhr = Ring(c, es, f"{name}_hh", [128, 1024], F32, 2)
        sqr = Ring(c, es, f"{name}_sqj", [128, 1024], BF16, 1)
        s4r = Ring(c, es, f"{name}_s4", [128, 2, 4], F32, 2)
        hbr = Ring(c, es, f"{name}_hb16", [128, 1024], BF16, 2)
        mst = Ring(c, es, f"{name}_mst", [128, 8, 128], BF16, 2)
        c.barrier()

        def b16(ap2d, n=NHD, d=64):
            return ap2d.unsqueeze(2).to_broadcast([128, n, d])

        def v3(ap2d, d=64):
            return ap2d.rearrange("p (h d) -> p h d", d=d)

        def chunk_stats(ci):
            p_t, kp = pR.next()
            c.op("pe", lambda: nc.tensor.matmul(p_t[:, 0, 0:16], lhsT=U[:], rhs=AA[:, ci, 0:16], start=True, stop=True),
                 reads=[], writes=[kp], signal=False)
            c.op("pe", lambda: nc.tensor.matmul(p_t[:, 0, 16:32], lhsT=L[:], rhs=AA[:, ci, 16:32], start=True, stop=True),
                 reads=[], writes=[kp], signal=False)
            c.op("pe", lambda: nc.tensor.matmul(p_t[:, 0, 32:64], lhsT=ones32[:], rhs=AA[:, ci, 0:32], start=True, stop=True),
                 reads=[], writes=[kp], signal=True)
            st, kst = STr.next()
            c.op("dve", lambda: nc.vector.tensor_copy(out=st[:, 0, :], in_=p_t[:, 0, 0:32]), reads=[kp], writes=[kst])
            c.op("dve", lambda: nc.vector.tensor_tensor(out=st[:, 4, :], in0=p_t[:, 0, 32:64], in1=st[:, 0, :], op=ALU.subtract),
                 reads=[kp, kst], writes=[kst])
            c.op("act", lambda: nc.scalar.activation(out=st[:, 1, :], in_=st[:, 0, :], func=AF.Exp), reads=[kst], writes=[kst])
            c.op("act", lambda: nc.scalar.activation(out=st[:, 2, :], in_=st[:, 4, :], func=AF.Exp), reads=[kst], writes=[kst])
            c.op("act", lambda: nc.scalar.activation(out=st[:, 3, :], in_=p_t[:, 0, 32:64], func=AF.Exp), reads=[kp, kst], writes=[kst])
            return st, kst

        def states_into(ps_t, kps, ci, x_t, kx):
            for g in range(4):
                c.op("pe", lambda: nc.tensor.matmul(ps_t[:, g * 256:(g + 1) * 256], lhsT=Btok[:, ci, g * 128:(g + 1) * 128],
                                                    rhs=x_t[:, g * 256:(g + 1) * 256], start=True, stop=True),
                     reads=[kx], writes=[kps], signal=(g == 3))

        pre_ps = [(pY2.t[0], pY2.k[0]), (py.t[0], py.k[0])]
        order = list(range(NT - 1, -1, -1))

        def pre_front(n):
            ci = order[n]
            st, kst = chunk_stats(ci)
            w_t, kw = wsm.next()
            c.op("dve", lambda: nc.vector.tensor_tensor(out=w_t[:], in0=DT[:, ci, 16:32], in1=st[:, 2, 16:32], op=ALU.mult),
                 reads=[kst], writes=[kw])
            x_t, kx = xcd.next()
            c.op("pool", lambda: nc.gpsimd.tensor_tensor(out=v3(x_t[:]), in0=v3(XT[:, ci, :]), in1=b16(w_t[:]), op=ALU.mult),
                 reads=[kw], writes=[kx])
            ps_t, kps = pre_ps[n % 2]
            states_into(ps_t, kps, ci, x_t, kx)
            return st, kst, ps_t, kps

        fr = pre_front(0)
        for n in range(NT):
            ci = order[n]
            nxt = pre_front(n + 1) if n + 1 < NT else None
            st, kst, ps_t, kps = fr
            hs_t, khs = Hbst.next()
            c.op("act", lambda: nc.scalar.copy(out=hs_t[:], in_=Hb32[:]), reads=[kHb32], writes=[khs])
            c.dma("sp", P["hb_scr"][ci], hs_t[:], reads=[khs], writes=[(kHb, ci)])
            c.op("pool", lambda: nc.gpsimd.tensor_tensor(out=v3(Hb32[:]), in0=v3(Hb32[:]), in1=b16(st[:, 3, 16:32]), op=ALU.mult),
                 reads=[kHb32, kst], writes=[kHb32])
            c.op("dve", lambda: nc.vector.tensor_tensor(out=Hb32[:], in0=Hb32[:], in1=ps_t[:], op=ALU.add),
                 reads=[kHb32, kps], writes=[kHb32])
            fr = nxt

        def stage_a(ci):
            T = {"ci": ci}
            tok = slice(ci * 128, (ci + 1) * 128)
            st, kst = chunk_stats(ci)
            T["st"], T["kst"] = st, kst
            xf_t, kxf = xcf.next()
            xb_t, kxb = xcb.next()
            xd_t, kxd = xcd.next()
            c.op("dve", lambda: nc.vector.tensor_tensor(out=v3(xf_t[:]), in0=v3(XT[:, ci, :]), in1=b16(DT[:, ci, 0:16]), op=ALU.mult),
                 reads=[], writes=[kxf])
            c.op("pool", lambda: nc.gpsimd.tensor_tensor(out=v3(xb_t[:]), in0=v3(XT[:, ci, :]), in1=b16(DT[:, ci, 16:32]), op=ALU.mult),
                 reads=[], writes=[kxb])
            c.op("pool", lambda: nc.gpsimd.tensor_tensor(out=v3(xd_t[:]), in0=v3(xf_t[:]), in1=b16(st[:, 2, 0:16]), op=ALU.mult),
                 reads=[kxf, kst], writes=[kxd])
            T["xf"], T["xb"], T["xd"] = (xf_t, kxf), (xb_t, kxb), (xd_t, kxd)
            yd_t, kyd = ydr.next()
            c.op("pool", lambda: nc.gpsimd.tensor_tensor(out=v3(yd_t[:]), in0=v3(XT[:, ci, :]), in1=b16(Db[:]), op=ALU.mult),
                 reads=[], writes=[kyd])
            T["yd"] = (yd_t, kyd)
            cb_p, kcbp = pcb.next()
            for g in range(4):
                c.op("pe", lambda: nc.tensor.matmul(cb_p[:, g, :], lhsT=BT[:, g, tok], rhs=CT[:, g, tok], start=True, stop=True),
                     reads=[], writes=[kcbp], signal=(g == 3))
            cb_t, kcb = cbT.next()
            c.op("act", lambda: nc.scalar.copy(out=cb_t[:], in_=cb_p[:]), reads=[kcbp], writes=[kcb])
            T["M"] = []
            for g in range(4):
                ms = []
                for d in range(2):
                    tri = U if d == 0 else L
                    NM = NMf if d == 0 else NMb
                    r_p, krp = pR.next()
                    for j in range(4):
                        col = d * 16 + g * 4 + j
                        c.op("pe", lambda: nc.tensor.matmul(r_p[:, j, :], lhsT=AA[:, ci, col:col + 1].to_broadcast([128, 128]),
                                                            rhs=tri[:], start=True, stop=True),
                             reads=[], writes=[krp], signal=(j == 3))
                    c0 = d * 16 + g * 4
                    a_t, ka = t1r.next()
                    for j in range(4):
                        c.op("dve", lambda: nc.vector.scalar_tensor_tensor(out=a_t[:, j, :], in0=r_p[:, j, :], scalar=st[:, 0, c0 + j:c0 + j + 1],
                                                                           in1=NM[:], op0=ALU.subtract, op1=ALU.add),
                             reads=[krp, kst], writes=[ka])
                    e_t, ke = t2r.next()
                    c.op("act", lambda: nc.scalar.activation(out=e_t[:], in_=a_t[:], func=AF.Exp), reads=[ka], writes=[ke])
                    m_t, km = Mr.next()
                    c.op("pool", lambda: nc.gpsimd.tensor_tensor(out=m_t[:], in0=e_t[:], in1=cb_t[:, g, :].unsqueeze(1).to_broadcast([128, 4, 128]),
                                                                 op=ALU.mult), reads=[ke, kcb], writes=[km])
                    ms.append((m_t, km))
                T["M"].append(ms)
            return T

        def stage_b(T, hf_t, khf):
            ci = T["ci"]
            tok = slice(ci * 128, (ci + 1) * 128)
            st, kst = T["st"], T["kst"]
            (xf_t, kxf), (xb_t, kxb), (xd_t, kxd) = T["xf"], T["xb"], T["xd"]
            yd_t, kyd = T["yd"]
            y_p, kyp = py.next()
            for g in range(4):
                ms = T["M"][g]
                for j in range(4):
                    h = g * 4 + j
                    c.op("pe", lambda: nc.tensor.matmul(y_p[:, h * 64:(h + 1) * 64], lhsT=ms[0][0][:, j, :], rhs=xf_t[:, h * 64:(h + 1) * 64],
                                                        start=True, stop=False),
                         reads=[ms[0][1], kxf], writes=[kyp], signal=False)
                    c.op("pe", lambda: nc.tensor.matmul(y_p[:, h * 64:(h + 1) * 64], lhsT=ms[1][0][:, j, :], rhs=xb_t[:, h * 64:(h + 1) * 64],
                                                        start=False, stop=True),
                         reads=[ms[1][1], kxb], writes=[kyp], signal=(j == 3))
            o_p, kop = pY2.next()
            for g in range(4):
                c.op("pe", lambda: nc.tensor.matmul(o_p[:, g * 256:(g + 1) * 256], lhsT=CT[:, g, tok], rhs=hf_t[:, g * 256:(g + 1) * 256],
                                                    start=True, stop=True), reads=[khf], writes=[kop], signal=(g == 3))
            ya_t, kya = yar.next()
            c.op("dve", lambda: nc.vector.tensor_tensor(out=v3(ya_t[:]), in0=v3(o_p[:]), in1=b16(st[:, 1, 0:16]), op=ALU.mult),
                 reads=[kop, kst], writes=[kya])
            o_p, kop = pY2.next()
            hl_t, khl = Hbld.next()
            c.dma("sp", hl_t[:], P["hb_scr"][ci], reads=[(kHb, ci)], writes=[khl])
            for g in range(4):
                c.op("pe", lambda: nc.tensor.matmul(o_p[:, g * 256:(g + 1) * 256], lhsT=CT[:, g, tok], rhs=hl_t[:, g * 256:(g + 1) * 256],
                                                    start=True, stop=True), reads=[khl], writes=[kop], signal=(g == 3))
            yb_t, kyb = ybr.next()
            c.op("dve", lambda: nc.vector.tensor_tensor(out=v3(yb_t[:]), in0=v3(o_p[:]), in1=b16(st[:, 1, 16:32]), op=ALU.mult),
                 reads=[kop, kst], writes=[kyb])
            c.op("pool", lambda: nc.gpsimd.tensor_tensor(out=yb_t[:], in0=yb_t[:], in1=yd_t[:], op=ALU.add), reads=[kyb, kyd], writes=[kyb])
            c.op("dve", lambda: nc.vector.tensor_tensor(out=ya_t[:], in0=ya_t[:], in1=yb_t[:], op=ALU.add), reads=[kya, kyb], writes=[kya])
            s_p, ksp = pY2.next()
            states_into(s_p, ksp, ci, xd_t, kxd)
            c.op("pool", lambda: nc.gpsimd.tensor_tensor(out=v3(Hf32[:]), in0=v3(Hf32[:]), in1=b16(st[:, 3, 0:16]), op=ALU.mult),
                 reads=[kHf, kst], writes=[kHf])
            c.op("dve", lambda: nc.vector.tensor_tensor(out=Hf32[:], in0=Hf32[:], in1=s_p[:], op=ALU.add), reads=[kHf, ksp], writes=[kHf])
            hf_n, khf_n = Hfb.next()
            c.op("act", lambda: nc.scalar.copy(out=hf_n[:], in_=Hf32[:]), reads=[kHf], writes=[khf_n])
            z_t, kz = zsr.next()
            c.dma("sp", z_t[:], P["zs"][ci * 128:(ci + 1) * 128, :], reads=[("zs_d", ci)], writes=[kz])
            hh_t, khh = hhr.next()
            c.op("dve", lambda: nc.vector.tensor_tensor(out=hh_t[:], in0=y_p[:], in1=ya_t[:], op=ALU.add), reads=[kyp, kya], writes=[khh])
            T["hh"] = (hh_t, khh, z_t, kz)
            return hf_n, khf_n

        def stage_b2(T):
            ci = T["ci"]
            hh_t, khh, z_t, kz = T["hh"]
            c.op("pool", lambda: nc.gpsimd.tensor_tensor(out=hh_t[:], in0=hh_t[:], in1=z_t[:], op=ALU.mult), reads=[khh, kz], writes=[khh])
            sq_t, ksq = sqr.next()
            s4, ks4 = s4r.next()
            for gg in range(4):
                c.op("act", lambda: nc.scalar.activation(out=sq_t[:, gg * 256:(gg + 1) * 256], in_=hh_t[:, gg * 256:(gg + 1) * 256],
                                                          func=AF.Square, accum_out=s4[:, 0, gg:gg + 1]),
                     reads=[khh], writes=[ksq, ks4])
            c.op("act", lambda: nc.scalar.activation(out=s4[:, 1, :], in_=s4[:, 0, :], func=AF.Sqrt, scale=1.0 / 256, bias=eps[:]),
                 reads=[ks4], writes=[ks4])
            c.op("dve", lambda: nc.vector.reciprocal(out=s4[:, 1, :], in_=s4[:, 1, :]), reads=[ks4], writes=[ks4])
            hb_t, khb = hbr.next()
            for gg in range(4):
                c.op("dve", lambda: nc.vector.scalar_tensor_tensor(out=hb_t[:, gg * 256:(gg + 1) * 256], in0=hh_t[:, gg * 256:(gg + 1) * 256],
                                                                   scalar=s4[:, 1, gg:gg + 1], in1=gnb[:, gg * 256:(gg + 1) * 256],
                                                                   op0=ALU.mult, op1=ALU.mult),
                     reads=[khh, ks4], writes=[khb])
            T["hb"] = (hb_t, khb)

        def stage_b3(T):
            ci = T["ci"]
            hb_t, khb = T["hb"]
            t_t, kt_ = pTn.next()
            for k in range(8):
                c.op("pe", lambda: nc.tensor.transpose(t_t[:, k, :], hb_t[:, k * 128:(k + 1) * 128], ident[:]),
                     reads=[khb], writes=[kt_], signal=(k == 7))
            m_s, kms = mst.next()
            c.op("act", lambda: nc.scalar.copy(out=m_s[:], in_=t_t[:]), reads=[kt_], writes=[kms])
            c.dma("sp", mt_v[:, 4:12, ci * 128:(ci + 1) * 128], m_s[:], reads=[kms], writes=[("mt_d", "ssd", ci)])

        hf_t, khf = Hfb.next()
        c.op("pool", lambda: nc.gpsimd.memset(hf_t[:], 0.0), writes=[khf])
        Ta = stage_a(0)
        Tp1 = Tp2 = None
        for ci in range(NT):
            Tn = stage_a(ci + 1) if ci + 1 < NT else None
            hf_t, khf = stage_b(Ta, hf_t, khf)
            if Tp1 is not None:
                stage_b2(Tp1)
            if Tp2 is not None:
                stage_b3(Tp2)
            Tp2 = Tp1
            Tp1 = Ta
            Ta = Tn
        stage_b2(Tp1)
        stage_b3(Tp2)
        stage_b3(Tp1)
        c.barrier()


def _rope_tables():
    t = np.arange(S)
    row = (t // 64).astype(np.float32)
    col = (t % 64).astype(np.float32)
    freqs = (np.float32(10000.0) ** (-np.arange(0, 32, 2, dtype=np.float32) / np.float32(32))).astype(np.float32)
    ang = np.concatenate([row[:, None] * freqs, col[:, None] * freqs], -1).astype(np.float32)
    return np.cos(ang).astype(np.float32), np.sin(ang).astype(np.float32)


def _na_index():
    dyi = np.zeros((NTYPES, 128, 128), np.int64)
    dxi = np.zeros((NTYPES, 128, 128), np.int64)
    msk = np.zeros((NTYPES, 128, 128), np.float32)
    rep = {0: 0, 1: 1, 2: 5, 3: 14, 4: 15}
    kk = np.arange(128)
    kr, ck = kk // 64, kk % 64
    for n, (cls, j) in enumerate(NA_TYPES):
        i = rep[cls]
        _, kp0, _ = na_cls(i)
        r = 2 * i + kr[None, :]
        cq = ck[None, :]
        rk = 2 * (kp0 + j) + kr[:, None]
        ckk = ck[:, None]
        rs = np.clip(r - 4, 0, 24)
        vrow = (rk >= rs) & (rk < rs + 8)
        cs = np.clip(cq - 8, 0, 48)
        vcol = (ckk >= cs) & (ckk < cs + 16)
        dyi[n] = np.clip(rk - r + 7, 0, 14)
        dxi[n] = np.clip(ckk - cq, -15, 15) + 15
        msk[n] = np.where(vrow & vcol, 0.0, NEG)
    return dyi, dxi, msk


_PROGRAM = None


def _inputs_spec():
    return [("x", [S, D], F32), ("even_mix_norm", [D], F32), ("even_w_in", [D, 4640], F32), ("na_q_norm", [64], F32),
            ("na_k_norm", [64], F32), ("biasg", [NTYPES, 128, 8, 128], F32), ("namask", [NTYPES, 128, 128], F32),
            ("ssd_conv_w", [4, 2048], F32), ("ssd_conv_b", [2048], F32), ("ssd_dt_bias", [32], F32), ("ssd_A_log", [32], F32),
            ("ssd_D", [16], F32), ("ssd_out_norm", [1024], F32), ("even_w_out", [1536, D], F32),
            ("odd_mix_norm", [D], F32), ("odd_w_qkv", [D, 1536], F32), ("gqa_q_norm", [64], F32), ("gqa_k_norm", [64], F32),
            ("odd_w_out", [D, D], F32), ("ffn_norm0", [D], F32), ("ffn_norm1", [D], F32),
            ("ffn_w13_0", [D, 2 * FH], F32), ("ffn_w13_1", [D, 2 * FH], F32), ("ffn_w2_0", [FH, D], F32), ("ffn_w2_1", [FH, D], F32),
            ("cos", [S, 32], F32), ("sin", [S, 32], F32), ("tri_u", [128, 128], F32), ("tri_l", [128, 128], F32),
            ("negm_f", [128, 128], F32), ("negm_b", [128, 128], F32), ("ident", [128, 128], BF16)]


def build_program(phases=("even", "ffn0", "odd", "ffn1")):
    nc = bass.Bass("TRN2", target_bir_lowering=False)
    A = {}
    for n, sh, dt in _inputs_spec():
        A[n] = nc.dram_tensor(n, sh, dt, kind="ExternalInput").ap()
    out = nc.dram_tensor("out", [S, D], F32, kind="ExternalOutput").ap()
    x1 = nc.dram_tensor("x1_scr", [S, D], F32, kind="Internal").ap()
    x2 = nc.dram_tensor("x2_scr", [S, D], F32, kind="Internal").ap()
    x3 = nc.dram_tensor("x3_scr", [S, D], F32, kind="Internal").ap()
    zs = nc.dram_tensor("zs_scr", [S, 1024], BF16, kind="Internal").ap()
    hb = nc.dram_tensor("hb_scr", [NT, 128, 1024], BF16, kind="Internal").ap()
    mt = nc.dram_tensor("mt_scr", [12, 128, S], BF16, kind="Internal").ap()
    chain = [A["x"], x1, x2, x3, out]
    order = ["even", "ffn0", "odd", "ffn1"]
    active = [p for p in order if p in phases]
    cur = A["x"]
    with ExitStack() as es:
        c = Ctx(nc, es)
        ident = es.enter_context(nc.sbuf_tensor("ident_sb", [128, 128], BF16))
        ones32 = es.enter_context(nc.sbuf_tensor("ones32", [128, 128], F32))
        c.dma("sp", ident[:], A["ident"], writes=["ident"])
        c.op("pool", lambda: nc.gpsimd.memset(ones32[:], 1.0), writes=["ones32"])
        c.barrier()
        for n, ph in enumerate(active):
            dst = out if n == len(active) - 1 else chain[order.index(ph) + 1]
            if ph == "even":
                P = {"mix_norm": A["even_mix_norm"], "w_in": A["even_w_in"], "na_gq": A["na_q_norm"], "na_gk": A["na_k_norm"],
                     "biasg": A["biasg"], "namask": A["namask"], "conv_w": A["ssd_conv_w"], "conv_b": A["ssd_conv_b"],
                     "dt_bias": A["ssd_dt_bias"], "A_log": A["ssd_A_log"], "ssd_D": A["ssd_D"], "out_norm": A["ssd_out_norm"],
                     "w_out": A["even_w_out"], "tri_u": A["tri_u"], "tri_l": A["tri_l"], "negm_f": A["negm_f"], "negm_b": A["negm_b"],
                     "zs": zs, "hb_scr": hb, "mt_scr": mt}
                emit_even(c, cur, dst, P, ident, ones32, "e0")
            elif ph == "ffn0":
                emit_ffn(c, cur, dst, A["ffn_norm0"], A["ffn_w13_0"], A["ffn_w2_0"], ident, "f0")
            elif ph == "odd":
                emit_odd(c, cur, dst, A["odd_mix_norm"], A["odd_w_qkv"], A["gqa_q_norm"], A["gqa_k_norm"], A["odd_w_out"],
                         A["cos"], A["sin"], ident, ones32, "o0")
            elif ph == "ffn1":
                emit_ffn(c, cur, dst, A["ffn_norm1"], A["ffn_w13_1"], A["ffn_w2_1"], ident, "f1")
            cur = dst
        c.finish("sp")
    return nc


def make_in_maps(inputs, xs):
    import ml_dtypes
    f = lambda a: np.ascontiguousarray(np.asarray(a, dtype=np.float32))
    cos, sin = _rope_tables()
    dyi, dxi, msk = _na_index()
    rpb = f(inputs["na_rel_bias"])[0]
    biasg = np.ascontiguousarray(rpb[:, dyi, dxi].transpose(1, 2, 0, 3))
    tri_u = np.triu(np.ones((128, 128), np.float32))
    kk = np.arange(128)
    negm_f = np.where(kk[None, :] >= kk[:, None], 0.0, NEG).astype(np.float32)
    negm_b = np.where(kk[None, :] <= kk[:, None], 0.0, NEG).astype(np.float32)
    shared = {
        "even_mix_norm": f(inputs["even_mix_norm"])[0], "even_w_in": f(inputs["even_w_in"])[0],
        "na_q_norm": f(inputs["na_q_norm"])[0], "na_k_norm": f(inputs["na_k_norm"])[0], "biasg": biasg, "namask": msk,
        "ssd_conv_w": f(inputs["ssd_conv_w"])[0], "ssd_conv_b": f(inputs["ssd_conv_b"])[0],
        "ssd_dt_bias": f(inputs["ssd_dt_bias"])[0].reshape(32), "ssd_A_log": f(inputs["ssd_A_log"])[0].reshape(32),
        "ssd_D": f(inputs["ssd_D"])[0], "ssd_out_norm": f(inputs["ssd_out_norm"])[0], "even_w_out": f(inputs["even_w_out"])[0],
        "odd_mix_norm": f(inputs["odd_mix_norm"])[0], "odd_w_qkv": f(inputs["odd_w_qkv"])[0],
        "gqa_q_norm": f(inputs["gqa_q_norm"])[0], "gqa_k_norm": f(inputs["gqa_k_norm"])[0], "odd_w_out": f(inputs["odd_w_out"])[0],
        "ffn_norm0": f(inputs["ffn_norm"])[0], "ffn_norm1": f(inputs["ffn_norm"])[1],
        "ffn_w13_0": f(inputs["ffn_w13"])[0], "ffn_w13_1": f(inputs["ffn_w13"])[1],
        "ffn_w2_0": f(inputs["ffn_w2"])[0], "ffn_w2_1": f(inputs["ffn_w2"])[1],
        "cos": cos, "sin": sin, "tri_u": tri_u, "tri_l": np.ascontiguousarray(tri_u.T),
        "negm_f": negm_f, "negm_b": negm_b, "ident": np.eye(128).astype(ml_dtypes.bfloat16),
    }
    return [dict(shared, x=np.ascontiguousarray(xb)) for xb in xs]


def kernel(**inputs):
    global _PROGRAM
    x = np.asarray(inputs["x"], dtype=np.float32)
    B = x.shape[0]
    if _PROGRAM is None:
        _PROGRAM = build_program()
    in_maps = make_in_maps(inputs, [x[b] for b in range(B)])
    res = run_bass_kernel_spmd(_PROGRAM, in_maps, core_ids=list(range(B)))
    return np.stack([np.asarray(r["out"], dtype=np.float32) for r in res.results], axis=0)
```

```python
import numpy as np
from contextlib import ExitStack
import concourse.bass as bass
import concourse.mybir as mybir
from concourse.bass_utils import run_bass_kernel_spmd

F32 = mybir.dt.float32
BF16 = mybir.dt.bfloat16
AF = mybir.ActivationFunctionType
ALU = mybir.AluOpType
AX = mybir.AxisListType

S = 2048
D = 1024
NT = 16
FH = 2816
NHC = 22
EPS = 1e-6
NEG = -30000.0
N_DUMMY = 0


class Ctx:
    SAME_ENGINE_SYNC = ("act", "dve", "pool")

    def __init__(self, nc, es, n_dma_sems=32):
        self.nc = nc
        self.es = es
        self.eng = {"pe": nc.tensor, "act": nc.scalar, "dve": nc.vector, "pool": nc.gpsimd, "sp": nc.sync}
        self.sem = {}
        self.cnt = {}
        self.nsem = 0
        for e in self.eng:
            self._new_sem(e)
        self.dsem = [es.enter_context(nc.semaphore(f"dma{i}")) for i in range(n_dma_sems)]
        self.dcnt = [0] * n_dma_sems
        self.dnext = {"hw": 0, "sw": 0}
        self.dhalf = n_dma_sems // 2
        self.waited = {e: {} for e in self.eng}
        self.last_w = {}
        self.readers = {}
        self.pend = {e: ([], []) for e in self.eng}
        self.uid = 0

    def _new_sem(self, e):
        self.sem[e] = self.es.enter_context(self.nc.semaphore(f"s_{e}_{self.nsem}"))
        self.nsem += 1
        self.cnt[e] = 0

    def _wait(self, e, tok):
        sem, val, src = tok
        if src == e and e not in self.SAME_ENGINE_SYNC:
            return
        key = id(sem)
        if self.waited[e].get(key, 0) >= val:
            return
        self.waited[e][key] = val
        self.eng[e].wait_ge(sem, val)

    def _deps(self, e, reads, writes):
        for r in reads:
            t = self.last_w.get(r)
            if t is not None:
                self._wait(e, t)
        for w in writes:
            t = self.last_w.get(w)
            if t is not None:
                self._wait(e, t)
            for t in self.readers.get(w, ()):
                self._wait(e, t)

    def _commit(self, tok, reads, writes):
        for w in writes:
            self.last_w[w] = tok
            self.readers[w] = []
        for r in reads:
            self.readers.setdefault(r, []).append(tok)

    def op(self, e, ins_fn, reads=(), writes=(), signal=True):
        reads = list(reads)
        writes = list(writes)
        self._deps(e, reads, writes)
        ins = ins_fn()
        pr, pw = self.pend[e]
        if not signal:
            pr.extend(reads)
            pw.extend(writes)
            return ins
        if self.cnt[e] >= 30000:
            self._new_sem(e)
        self.cnt[e] += 1
        ins.then_inc(self.sem[e], 1)
        tok = (self.sem[e], self.cnt[e], e)
        self._commit(tok, reads + pr, writes + pw)
        self.pend[e] = ([], [])
        return ins

    def dma(self, q, out, in_, reads=(), writes=(), **kw):
        reads = list(reads)
        writes = list(writes)
        kind = "sw" if q == "pool" else "hw"
        i = self.dnext[kind] + (self.dhalf if kind == "sw" else 0)
        self.dnext[kind] = (self.dnext[kind] + 1) % self.dhalf
        skey = ("__dsem", i)
        self._deps(q, reads, writes + [skey])
        ins = self.eng[q].dma_start(out=out, in_=in_, **kw)
        self.dcnt[i] += 16
        ins.then_inc(self.dsem[i], 16)
        tok = (self.dsem[i], self.dcnt[i], "dma")
        self._commit(tok, reads, writes + [skey])
        return ins

    def finish(self, e="sp"):
        for i, s in enumerate(self.dsem):
            if self.dcnt[i]:
                self._wait(e, (s, self.dcnt[i], "dma"))
        for x in self.eng:
            if self.cnt[x] and (x != e or e in self.SAME_ENGINE_SYNC):
                self._wait(e, (self.sem[x], self.cnt[x], x))

    def barrier(self):
        for e in self.eng:
            assert not self.pend[e][0] and not self.pend[e][1], "pending unsignalled ops at barrier"
        for e in self.eng:
            self.finish(e)
        self.last_w.clear()
        self.readers.clear()

    def key(self, name):
        self.uid += 1
        return f"{name}#{self.uid}"


class Ring:
    def __init__(self, c, es, name, shape, dtype, n, psum=False):
        alloc = c.nc.psum_tensor if psum else c.nc.sbuf_tensor
        self.t = [es.enter_context(alloc(f"{name}{i}", shape, dtype)) for i in range(n)]
        self.k = [c.key(name) for _ in range(n)]
        self.i = -1
        self.n = n

    def next(self):
        self.i = (self.i + 1) % self.n
        return self.t[self.i], self.k[self.i]


def bcast_row(ap_1d, n, parts=128):
    return ap_1d.rearrange("(o n) -> o n", o=1).to_broadcast([parts, n])


def emit_norm_T(c, es, x_dram, g_dram, xnT, xnT_key, ident, name):
    nc = c.nc
    gb = es.enter_context(nc.sbuf_tensor(f"{name}_gb", [128, D], F32))
    kgb = c.key("gb")
    c.dma("sp", gb[:], bcast_row(g_dram, D), writes=[kgb])
    xt = Ring(c, es, f"{name}_xt", [128, D], F32, 4)
    sq = Ring(c, es, f"{name}_sq", [128, D], BF16, 3)
    xs = Ring(c, es, f"{name}_xs", [128, D], BF16, 3)
    st = Ring(c, es, f"{name}_st", [128, 2], F32, 4)
    pT = Ring(c, es, f"{name}_pT", [128, 8, 128], BF16, 2, psum=True)
    eps = es.enter_context(nc.sbuf_tensor(f"{name}_eps", [128, 1], F32))
    keps = c.key("eps")
    c.op("pool", lambda: nc.gpsimd.memset(eps[:], EPS), writes=[keps])
    def chain(i):
        x_t, kx = xt.next()
        c.dma("sp", x_t[:], x_dram[i * 128:(i + 1) * 128, :], writes=[kx])
        s_t, ks = sq.next()
        st_t, kst = st.next()
        c.op("act", lambda: nc.scalar.activation(out=s_t[:], in_=x_t[:], func=AF.Square, accum_out=st_t[:, 0:1]),
             reads=[kx], writes=[ks, kst])
        c.op("act", lambda: nc.scalar.activation(out=st_t[:, 1:2], in_=st_t[:, 0:1], func=AF.Sqrt,
                                                  scale=1.0 / D, bias=eps[:]),
             reads=[kst, keps], writes=[kst])
        c.op("dve", lambda: nc.vector.reciprocal(out=st_t[:, 1:2], in_=st_t[:, 1:2]), reads=[kst], writes=[kst])
        xs_t, kxs = xs.next()
        c.op("dve", lambda: nc.vector.scalar_tensor_tensor(out=xs_t[:], in0=x_t[:], scalar=st_t[:, 1:2], in1=gb[:],
                                                           op0=ALU.mult, op1=ALU.mult),
             reads=[kx, kst, kgb], writes=[kxs])
        return xs_t, kxs

    def tr(i, xs_t, kxs):
        p_t, kp = pT.next()
        for k in range(8):
            c.op("pe", lambda: nc.tensor.transpose(p_t[:, k, :], xs_t[:, k * 128:(k + 1) * 128], ident[:]),
                 reads=[kxs], writes=[kp], signal=(k == 7))
        c.op("act", lambda: nc.scalar.copy(out=xnT[:, :, i * 128:(i + 1) * 128], in_=p_t[:]),
             reads=[kp], writes=[(xnT_key, i)])

    cur = chain(0)
    for i in range(NT):
        nxt = chain(i + 1) if i + 1 < NT else None
        tr(i, *cur)
        cur = nxt


def emit_ffn(c, x_in, x_out, g_dram, w13, w2, ident, name):
    nc = c.nc
    with ExitStack() as es:
        E = es.enter_context
        xnT = E(nc.sbuf_tensor(f"{name}_xnT", [128, 8, S], BF16))
        kxnT = c.key("xnT")
        hT = E(nc.sbuf_tensor(f"{name}_hT", [128, NHC, S], BF16))
        khT = c.key("hT")
        W2 = E(nc.sbuf_tensor(f"{name}_W2", [128, NHC, D], BF16))
        kW2 = c.key("W2")
        with ExitStack() as es1:
            emit_norm_T(c, es1, x_in, g_dram, xnT, kxnT, ident, name)
            c.barrier()
        with ExitStack() as es2:
            xn_all = [(kxnT, i) for i in range(NT)]
            w13v = w13.rearrange("(k p) n -> p k n", p=128)
            wg = Ring(c, es2, f"{name}_wg", [128, 8, 128], BF16, 3)
            wu = Ring(c, es2, f"{name}_wu", [128, 8, 128], BF16, 3)
            pg = Ring(c, es2, f"{name}_pg", [128, 1024], F32, 2, psum=True)
            pu = Ring(c, es2, f"{name}_pu", [128, 1024], F32, 2, psum=True)
            sg = Ring(c, es2, f"{name}_sg", [128, 1024], F32, 2)
            for hc in range(NHC):
                wg_t, kwg = wg.next()
                wu_t, kwu = wu.next()
                c.dma("pool", wg_t[:], w13v[:, :, hc * 128:(hc + 1) * 128], writes=[kwg])
                c.dma("pool", wu_t[:], w13v[:, :, FH + hc * 128:FH + (hc + 1) * 128], writes=[kwu])
                c.dma("pool", W2[:, hc, :], w2[hc * 128:(hc + 1) * 128, :], writes=[(kW2, hc)])
                for th in range(2):
                    pg_t, kpg = pg.next()
                    pu_t, kpu = pu.next()
                    for (w_t, kw, p_t, kp) in ((wg_t, kwg, pg_t, kpg), (wu_t, kwu, pu_t, kpu)):
                        for k in range(8):
                            for nb in range(2):
                                t0 = th * 1024 + nb * 512
                                c.op("pe", lambda: nc.tensor.matmul(p_t[:, nb * 512:(nb + 1) * 512], lhsT=w_t[:, k, :],
                                                                    rhs=xnT[:, k, t0:t0 + 512],
                                                                    start=(k == 0), stop=(k == 7)),
                                     reads=[kw] + xn_all, writes=[kp], signal=(k == 7 and nb == 1))
                    sg_t, ksg = sg.next()
                    c.op("act", lambda: nc.scalar.activation(out=sg_t[:], in_=pg_t[:], func=AF.Silu),
                         reads=[kpg], writes=[ksg])
                    c.op("dve", lambda: nc.vector.tensor_tensor(out=hT[:, hc, th * 1024:(th + 1) * 1024], in0=sg_t[:],
                                                                in1=pu_t[:], op=ALU.mult),
                         reads=[ksg, kpu], writes=[(khT, hc, th)])
            c.barrier()
        py = [E(nc.psum_tensor(f"{name}_py{j}", [128, D], F32)) for j in range(4)]
        kpy = [c.key("py") for _ in range(4)]
        xr = Ring(c, es, f"{name}_xr", [128, D], F32, 4)
        yo = Ring(c, es, f"{name}_yo", [128, D], F32, 3)
        for tg in range(4):
            xl = []
            for j in range(4):
                x_t, kx = xr.next()
                c.dma("sp", x_t[:], x_in[(tg * 4 + j) * 128:(tg * 4 + j + 1) * 128, :], writes=[kx])
                xl.append((x_t, kx))
            for hc in range(NHC):
                w_t, kw = W2[:, hc, :], (kW2, hc)
                for j in range(4):
                    tt = tg * 4 + j
                    for ob in range(2):
                        c.op("pe", lambda: nc.tensor.matmul(py[j][:, ob * 512:(ob + 1) * 512],
                                                            lhsT=hT[:, hc, tt * 128:(tt + 1) * 128],
                                                            rhs=w_t[:, ob * 512:(ob + 1) * 512],
                                                            start=(hc == 0), stop=(hc == NHC - 1)),
                             reads=[kw, (khT, hc, tt // 8)], writes=[kpy[j]],
                             signal=(ob == 1 and (hc == NHC - 1 or j == 3)))
            for j in range(4):
                tt = tg * 4 + j
                x_t, kx = xl[j]
                y_t, ky = yo.next()
                c.op("dve", lambda: nc.vector.tensor_tensor(out=y_t[:], in0=x_t[:], in1=py[j][:], op=ALU.add),
                     reads=[kx, kpy[j]], writes=[ky])
                c.dma("sp", x_out[tt * 128:(tt + 1) * 128, :], y_t[:], reads=[ky])
        c.barrier()


def warm_pe(c, ptile, pkey, src, n=20):
    nc = c.nc
    for j in range(n):
        c.op("pe", lambda: nc.tensor.matmul(ptile[:, 0:512], lhsT=src[:, 0:128], rhs=src[:, 0:512], start=True, stop=True),
             reads=[], writes=[pkey], signal=(j == n - 1))


def run_pipeline(iters, look=2):
    deferred = []
    N = len(iters)
    for n in range(min(look, N)):
        iters[n]["qk"]()
    for n in range(N):
        due = [d for d in deferred if d[0] <= n]
        for d in due:
            d[1]()
            deferred.remove(d)
        if n + look < N:
            iters[n + look]["qk"]()
        iters[n]["exp"]()
        iters[n]["pv"]()
        posts = iters[n]["post_factory"]() if "post_factory" in iters[n] else ()
        for delay, fn in posts:
            if delay == 0:
                fn()
            else:
                deferred.append((n + delay, fn))
    for d in deferred:
        d[1]()


def make_norm_post(c, o_t, ko, hb, dst_ap, dst_key, osb, rd, pb, ones32, nrows=128):
    nc = c.nc
    dp = 64 if hb == 0 else 0
    st = {}

    def evac():
        st["o"], st["ko"] = osb.next()
        c.op("dve", lambda: nc.vector.tensor_copy(out=st["o"][0:nrows, :], in_=o_t[0:nrows, :]), reads=[ko], writes=[st["ko"]])

    def bcast():
        b_t, kb = pb.next()
        c.op("pe", lambda: nc.tensor.matmul(b_t[:, :], lhsT=ones32[dp:dp + 1, :], rhs=st["o"][dp:dp + 1, :],
                                            start=True, stop=True), reads=[st["ko"]], writes=[kb])
        st["r"], st["kr"] = rd.next()
        c.op("dve", lambda: nc.vector.reciprocal(out=st["r"][hb:hb + 64, :], in_=b_t[hb:hb + 64, :]), reads=[kb], writes=[st["kr"]])
        c.op("dve", lambda: nc.vector.tensor_tensor(out=dst_ap, in0=st["o"][hb:hb + 64, :], in1=st["r"][hb:hb + 64, :], op=ALU.mult),
             reads=[st["ko"], st["kr"]], writes=[dst_key])

    return [(0, evac), (2, bcast)]


def emit_qkv_proj(c, xnT, kxnT, wv, gq, gk, cos_d, sin_d, ident, QT, kQT, VA, VB, kVA, NH, NKV, dupk, name, QZ=None, koff=8, Wpre=None):
    nc = c.nc
    HD = 64
    NQK = NH + NKV
    rope = cos_d is not None
    with ExitStack() as es2:
        E2 = es2.enter_context
        if Wpre is None:
            W = E2(nc.sbuf_tensor(f"{name}_W", [128, 8, 1536], BF16))
            kW = c.key("W")
            for k in range(8):
                c.dma("pool", W[:, k, :], wv[:, k, :], writes=[(kW, k)])
        else:
            W, kW = Wpre
        G = E2(nc.sbuf_tensor(f"{name}_G", [128, NQK, HD], F32))
        kG = c.key("G")
        c.dma("sp", G[:, 0:NH, :], gq.rearrange("(o h d) -> o h d", o=1, h=1).to_broadcast([128, NH, HD]), writes=[kG])
        c.dma("sp", G[:, NH:NQK, :], gk.rearrange("(o h d) -> o h d", o=1, h=1).to_broadcast([128, NKV, HD]), writes=[kG])
        c.op("dve", lambda: nc.vector.tensor_scalar(out=G[:, 0:NH, :], in0=G[:, 0:NH, :], scalar1=HD ** -0.5,
                                                    scalar2=None, op0=ALU.mult), reads=[kG], writes=[kG])
        eps = E2(nc.sbuf_tensor(f"{name}_eps2", [128, 1], F32))
        keps = c.key("eps")
        c.op("pool", lambda: nc.gpsimd.memset(eps[:], EPS), writes=[keps])
        if VA.shape[-1] > HD + 1:
            c.op("pool", lambda: nc.gpsimd.memset(VA[:, :, :, HD:], 0.0), writes=[(kVA, "ones")])
        c.op("pool", lambda: nc.gpsimd.memset(VA[:, :, :, HD:HD + 1], 1.0), writes=[(kVA, "ones")])
        c.op("pool", lambda: nc.gpsimd.memset(VB[:, :, :, 0:HD], 0.0), writes=[(kVA, "ones")])
        c.op("pool", lambda: nc.gpsimd.memset(VB[:, :, :, 0:1], 1.0), writes=[(kVA, "ones")])
        if rope:
            cs = E2(nc.sbuf_tensor(f"{name}_cs", [128, NT, 2, 32], F32))
            kcs = c.key("cs")
            c.dma("sp", cs[:, :, 0, :], cos_d.rearrange("(i p) f -> p i f", p=128), writes=[kcs])
            c.dma("sp", cs[:, :, 1, :], sin_d.rearrange("(i p) f -> p i f", p=128), writes=[kcs])
        pq = Ring(c, es2, f"{name}_pq", [128, 1536], F32, 2, psum=True)
        pT = Ring(c, es2, f"{name}_pT2", [128, 8, 128], BF16, 2, psum=True)
        nb_ = 1
        sq = Ring(c, es2, f"{name}_sq2", [128, NQK, HD], F32, nb_)
        st = Ring(c, es2, f"{name}_st2", [128, 2, NQK], F32, 2)
        qn = Ring(c, es2, f"{name}_qn", [128, NQK, HD], F32, 2)
        if rope:
            kdr = Ring(c, es2, f"{name}_kd", [128, NKV, 2, HD], BF16, 2)
            tA = Ring(c, es2, f"{name}_tA", [128, NQK, 32], F32, 1)
            tB = Ring(c, es2, f"{name}_tB", [128, NQK, 32], F32, 1)
            tC = Ring(c, es2, f"{name}_tC", [128, NQK, 32], F32, 1)
            tD = Ring(c, es2, f"{name}_tD", [128, NQK, 32], F32, 1)
            ro = Ring(c, es2, f"{name}_ro", [128, NQK, HD], F32, 1)
        qr = Ring(c, es2, f"{name}_qr", [128, NQK, HD], BF16, 2)
        def make_tile(i):
            T = {}

            def mm():
                p_t, kp = pq.next()
                for cb in range(3):
                    for k in range(8):
                        c.op("pe", lambda: nc.tensor.matmul(p_t[:, cb * 512:(cb + 1) * 512],
                                                            lhsT=xnT[:, k, i * 128:(i + 1) * 128],
                                                            rhs=W[:, k, cb * 512:(cb + 1) * 512],
                                                            start=(k == 0), stop=(k == 7)),
                             reads=[(kW, k), (kxnT, i)], writes=[kp], signal=(k == 7 and cb == 2))
                T['p_t'], T['kp'] = p_t, kp

            def chain():
                p_t, kp = T['p_t'], T['kp']
                pqk = p_t[:, 0:NQK * HD].rearrange("p (h d) -> p h d", d=HD)
                s_t, ks = sq.next()
                st_t, kst = st.next()
                q_t, kq = qn.next()
                r_t, kr = qr.next()
                c.op("act", lambda: nc.scalar.activation(out=s_t[:], in_=pqk, func=AF.Square), reads=[kp], writes=[ks])
                c.op("dve", lambda: nc.vector.tensor_tensor(out=q_t[:], in0=pqk, in1=G[:], op=ALU.mult), reads=[kp, ks, kG], writes=[kq])
                c.op("act", lambda: nc.scalar.copy(out=VA[:, i, :, 0:HD],
                                                   in_=p_t[:, NQK * HD:1536].rearrange("p (g d) -> p g d", d=HD)),
                     reads=[kp, kq], writes=[(kVA, i)])
                c.op("act", lambda: nc.scalar.copy(out=VB[:, i, :, HD:2 * HD],
                                                   in_=p_t[:, NQK * HD:1536].rearrange("p (g d) -> p g d", d=HD)),
                     reads=[kp, kq], writes=[(kVA, i, "b")])
                c.op("dve", lambda: nc.vector.tensor_reduce(out=st_t[:, 0, :], in_=s_t[:], axis=AX.X, op=ALU.add),
                     reads=[ks], writes=[kst])
                c.op("act", lambda: nc.scalar.activation(out=st_t[:, 1, :], in_=st_t[:, 0, :], func=AF.Sqrt,
                                                          scale=1.0 / HD, bias=eps[:]), reads=[kst, keps], writes=[kst])
                c.op("dve", lambda: nc.vector.reciprocal(out=st_t[:, 1, :], in_=st_t[:, 1, :]), reads=[kst], writes=[kst])
                rstd_b = st_t[:, 1, :].unsqueeze(2).to_broadcast([128, NQK, HD])
                if not rope:
                    c.op("dve", lambda: nc.vector.tensor_tensor(out=r_t[:], in0=q_t[:], in1=rstd_b, op=ALU.mult),
                         reads=[kq, kst], writes=[(kr, 0), (kr, 1)])
                else:
                    qv = q_t[:].rearrange("p h (f two) -> p h f two", two=2)
                    x0 = qv[:, :, :, 0]
                    x1 = qv[:, :, :, 1]
                    cosb = cs[:, i, 0, :].unsqueeze(1).to_broadcast([128, NQK, 32])
                    sinb = cs[:, i, 1, :].unsqueeze(1).to_broadcast([128, NQK, 32])
                    a_t, ka = tA.next()
                    b_t, kb = tB.next()
                    c_t, kc = tC.next()
                    d_t, kd = tD.next()
                    o_t, ko = ro.next()
                    ov = o_t[:].rearrange("p h (f two) -> p h f two", two=2)
                    c.op("dve", lambda: nc.vector.tensor_tensor(out=a_t[:], in0=x0, in1=cosb, op=ALU.mult), reads=[kq, kcs], writes=[ka])
                    c.op("dve", lambda: nc.vector.tensor_tensor(out=b_t[:], in0=x1, in1=sinb, op=ALU.mult), reads=[kq, kcs], writes=[kb])
                    c.op("dve", lambda: nc.vector.tensor_tensor(out=ov[:, :, :, 0], in0=a_t[:], in1=b_t[:], op=ALU.subtract),
                         reads=[ka, kb], writes=[(ko, 0)])
                    c.op("pool", lambda: nc.gpsimd.tensor_tensor(out=c_t[:], in0=x0, in1=sinb, op=ALU.mult), reads=[kq, kcs], writes=[kc])
                    c.op("pool", lambda: nc.gpsimd.tensor_tensor(out=d_t[:], in0=x1, in1=cosb, op=ALU.mult), reads=[kq, kcs], writes=[kd])
                    c.op("pool", lambda: nc.gpsimd.tensor_tensor(out=ov[:, :, :, 1], in0=c_t[:], in1=d_t[:], op=ALU.add),
                         reads=[kc, kd], writes=[(ko, 1)])
                    c.op("dve", lambda: nc.vector.tensor_tensor(out=r_t[:], in0=o_t[:], in1=rstd_b, op=ALU.mult),
                         reads=[(ko, 0), (ko, 1), kst], writes=[(kr, 0), (kr, 1)])
                    kd_t, kkd = kdr.next()
                    c.op("pool", lambda: nc.gpsimd.tensor_copy(out=kd_t[:], in_=r_t[:, NH:NQK, :].unsqueeze(2).to_broadcast([128, NKV, 2, HD])),
                         reads=[(kr, 0), (kr, 1)], writes=[kkd])
                T['r_t'], T['kr'] = r_t, kr
                if dupk:
                    T['kd_t'], T['kkd'] = kd_t, kkd

            def tr():
                r_t, kr = T['r_t'], T['kr']
                if dupk:
                    kd_t, kkd = T['kd_t'], T['kkd']
                rflat = r_t[:].rearrange("p h d -> p (h d)")
                if dupk:
                    kflat = kd_t[:].rearrange("p g t d -> p (g t d)")
                t_t, kt_ = pT.next()
                for j in range(8):
                    c.op("pe", lambda: nc.tensor.transpose(t_t[:, j, :], rflat[:, j * 128:(j + 1) * 128], ident[:]),
                         reads=[(kr, 0), (kr, 1)], writes=[kt_], signal=(j == 7))
                if QZ is None:
                    c.op("act", lambda: nc.scalar.copy(out=QT[:, 0:8, i * 128:(i + 1) * 128], in_=t_t[:]),
                         reads=[kt_], writes=[(kQT, i, 0)])
                else:
                    nqc = NH // 2
                    qzv = QZ[:].rearrange("p (c two) s -> p c two s", two=2)
                    c.op("act", lambda: nc.scalar.copy(out=qzv[0:64, :, 0, i * 128:(i + 1) * 128], in_=t_t[0:64, 0:nqc, :]),
                         reads=[kt_, "qz0", "qz1"], writes=[(kQT, i, 0)])
                    c.op("dve", lambda: nc.vector.tensor_copy(out=qzv[64:128, :, 1, i * 128:(i + 1) * 128], in_=t_t[64:128, 0:nqc, :]),
                         reads=[kt_, (kQT, i, 0)], writes=[(kQT, i, 2)])
                    if not dupk:
                        c.op("dve", lambda: nc.vector.tensor_copy(out=QT[:, koff:koff + NKV // 2, i * 128:(i + 1) * 128],
                                                                  in_=t_t[:, nqc:nqc + NKV // 2, :]),
                             reads=[kt_, (kQT, i, 2)], writes=[(kQT, i, 3)])
                if dupk:
                    t_t, kt_ = pT.next()
                    for j in range(NKV):
                        c.op("pe", lambda: nc.tensor.transpose(t_t[:, j, :], kflat[:, j * 128:(j + 1) * 128], ident[:]),
                             reads=[kkd], writes=[kt_], signal=(j == NKV - 1))
                    c.op("act", lambda: nc.scalar.copy(out=QT[:, koff:koff + NKV, i * 128:(i + 1) * 128], in_=t_t[:, 0:NKV, :]),
                         reads=[kt_], writes=[(kQT, i, 1)])
            return mm, chain, tr

        tiles = [make_tile(i) for i in range(NT)]
        tiles[0][0]()
        tiles[1][0]()
        tiles[0][1]()
        for i in range(NT):
            if i + 2 < NT:
                tiles[i + 2][0]()
            if i + 1 < NT:
                tiles[i + 1][1]()
            tiles[i][2]()
        c.barrier()


def emit_odd(c, x_in, x_out, g_dram, wqkv, gq, gk, wo, cos_d, sin_d, ident, ones32, name):
    nc = c.nc
    NH, NKV, HD = 16, 4, 64
    NQK = NH + NKV
    with ExitStack() as es:
        E = es.enter_context
        QT = E(nc.sbuf_tensor(f"{name}_QT", [128, NKV, S], BF16))
        kQT = c.key("QT")
        VAB = E(nc.sbuf_tensor(f"{name}_VAB", [128, NT, NKV, 192], BF16))
        VA = VAB[:, :, :, 0:128]
        VB = VAB[:, :, :, 64:192]
        kVA = c.key("VA")
        QZ = E(nc.sbuf_tensor(f"{name}_QZ", [128, NH, S], BF16))
        c.op("pool", lambda: nc.gpsimd.memset(QZ[:, 0:NH // 2, :], 0.0), writes=["qz0"])
        c.op("pool", lambda: nc.gpsimd.memset(QZ[:, NH // 2:NH, :], 0.0), writes=["qz1"])
        with ExitStack() as es0:
            xnT = es0.enter_context(nc.sbuf_tensor(f"{name}_xnT", [128, 8, S], BF16))
            kxnT = c.key("xnT")
            Wq = es0.enter_context(nc.sbuf_tensor(f"{name}_Wq", [128, 8, 1536], BF16))
            kWq = c.key("Wq")
            wqv = wqkv.rearrange("(k p) n -> p k n", p=128)
            for k in range(8):
                c.dma("pool", Wq[:, k, :], wqv[:, k, :], writes=[(kWq, k)])
            with ExitStack() as es1:
                emit_norm_T(c, es1, x_in, g_dram, xnT, kxnT, ident, name)
                c.barrier()
            emit_qkv_proj(c, xnT, kxnT, wqv, gq, gk, cos_d, sin_d, ident,
                          QT, kQT, VA, VB, kVA, NH, NKV, True, name, QZ=QZ, koff=0, Wpre=(Wq, kWq))
        OT = E(nc.sbuf_tensor(f"{name}_OT", [128, 8, S], BF16))
        kOT = c.key("OT")
        Wo = E(nc.sbuf_tensor(f"{name}_Wo", [128, 8, D], BF16))
        kWo = c.key("Wo")
        wov = wo.rearrange("(k p) n -> p k n", p=128)
        for k in range(8):
            c.dma("pool", Wo[:, k, :], wov[:, k, :], writes=[(kWo, k)])
        with ExitStack() as es3:
            ps = Ring(c, es3, f"{name}_ps", [128, 2, 512], F32, 3, psum=True)
            po = Ring(c, es3, f"{name}_po", [128, 512], F32, 1, psum=True)
            pb = Ring(c, es3, f"{name}_pb", [128, 512], F32, 1, psum=True)
            PT = Ring(c, es3, f"{name}_PT", [128, 2, 512], BF16, 3)
            osb = Ring(c, es3, f"{name}_osb", [128, 512], F32, 2)
            rd = Ring(c, es3, f"{name}_rd", [128, 512], F32, 2)
            iters = []
            for h in range(NH):
                for qb in range(4):
                    grp = {}
                    for kt2 in range(8):
                        def mk(h=h, qb=qb, kt2=kt2, grp=grp):
                            g = h // 4
                            hb = (h % 2) * 64
                            stt = {}

                            def qk():
                                stt["s"], stt["ks"] = ps.next()
                                for _ in range(N_DUMMY):
                                    c.op("pe", lambda: nc.tensor.matmul(stt["s"][:, 0, :], lhsT=QT[:, 0, 0:128], rhs=QT[:, 0, 0:512],
                                                                        start=True, stop=True), reads=[], writes=[stt["ks"]], signal=False)
                                for j in range(2):
                                    kt = kt2 * 2 + j
                                    c.op("pe", lambda: nc.tensor.matmul(stt["s"][:, j, :], lhsT=QT[:, g, kt * 128:(kt + 1) * 128],
                                                                        rhs=QZ[:, h, qb * 512:(qb + 1) * 512], start=True, stop=True),
                                         reads=[], writes=[stt["ks"]], signal=(j == 1))

                            def ex():
                                stt["p"], stt["kp"] = PT.next()
                                c.op("act", lambda: nc.scalar.activation(out=stt["p"][:], in_=stt["s"][:], func=AF.Exp),
                                     reads=[stt["ks"]], writes=[stt["kp"]])

                            def pv():
                                if kt2 == 0:
                                    grp["o"], grp["ko"] = po.next()
                                o_t, ko = grp["o"], grp["ko"]
                                for j in range(2):
                                    kt = kt2 * 2 + j
                                    if hb == 0:
                                        c.op("pe", lambda: nc.tensor.matmul(o_t[:, :], lhsT=VA[:, kt, g, :], rhs=stt["p"][:, j, :],
                                                                            start=(kt == 0), stop=(kt == 15)),
                                             reads=[stt["kp"]], writes=[ko], signal=(j == 1))
                                    else:
                                        c.op("pe", lambda: nc.tensor.matmul(o_t[:, :], lhsT=VB[:, kt, g, :], rhs=stt["p"][:, j, :],
                                                                            start=(kt == 0), stop=(kt == 15)),
                                             reads=[stt["kp"]], writes=[ko], signal=(j == 1))

                            it = {"qk": qk, "exp": ex, "pv": pv}
                            if kt2 == 7:
                                def post_factory():
                                    return make_norm_post(c, grp["o"], grp["ko"], hb,
                                                          OT[hb:hb + 64, h // 2, qb * 512:(qb + 1) * 512], (kOT, h, qb), osb, rd, pb, ones32)
                                it["post_factory"] = post_factory
                            return it
                        iters.append(mk())
            warm_pe(c, pb.t[0], pb.k[0], QT[:, 0, :])
            run_pipeline(iters)
            c.barrier()
        with ExitStack() as es4:
            py = Ring(c, es4, f"{name}_py", [128, D], F32, 2, psum=True)
            xr = Ring(c, es4, f"{name}_xr", [128, D], F32, 4)
            yo = Ring(c, es4, f"{name}_yo", [128, D], F32, 3)
            xq = []
            for tt in range(3):
                x_t, kx = xr.next()
                c.dma("sp", x_t[:], x_in[tt * 128:(tt + 1) * 128, :], writes=[kx])
                xq.append((x_t, kx))
            for tt in range(NT):
                y_p, kyp = py.next()
                for k in range(8):
                    for ob in range(2):
                        c.op("pe", lambda: nc.tensor.matmul(y_p[:, ob * 512:(ob + 1) * 512], lhsT=OT[:, k, tt * 128:(tt + 1) * 128],
                                                            rhs=Wo[:, k, ob * 512:(ob + 1) * 512], start=(k == 0), stop=(k == 7)),
                             reads=[(kWo, k)], writes=[kyp], signal=(k == 7 and ob == 1))
                x_t, kx = xq.pop(0)
                y_t, ky = yo.next()
                c.op("dve", lambda: nc.vector.tensor_tensor(out=y_t[:], in0=x_t[:], in1=y_p[:], op=ALU.add),
                     reads=[kx, kyp], writes=[ky])
                if tt + 3 < NT:
                    x_n, kxn = xr.next()
                    c.dma("sp", x_n[:], x_in[(tt + 3) * 128:(tt + 4) * 128, :], writes=[kxn])
                    xq.append((x_n, kxn))
                c.dma("sp", x_out[tt * 128:(tt + 1) * 128, :], y_t[:], reads=[ky])
            c.barrier()


NA_TYPES = [(0, j) for j in range(4)] + [(1, j) for j in range(4)] + [(2, j) for j in range(5)] + \
           [(3, j) for j in range(4)] + [(4, j) for j in range(4)]
NA_TIX = {t: n for n, t in enumerate(NA_TYPES)}
NTYPES = len(NA_TYPES)


def na_cls(i):
    if i == 0:
        return 0, 0, 4
    if i == 1:
        return 1, 0, 4
    if i == 14:
        return 3, 12, 4
    if i == 15:
        return 4, 12, 4
    return 2, i - 2, 5


def emit_even(c, x_in, x_out, P, ident, ones32, name):
    nc = c.nc
    HD = 64
    w_in_v = P["w_in"].rearrange("(k p) n -> p k n", p=128)
    with ExitStack() as es:
        E = es.enter_context
        mt_v = P["mt_scr"].rearrange("k p s -> p k s")
        kMT = c.key("MT")
        DT = E(nc.sbuf_tensor(f"{name}_DT", [128, NT, 32], F32))
        AA = E(nc.sbuf_tensor(f"{name}_AA", [128, NT, 32], F32))
        kDT = c.key("DT")
        kAA = c.key("AA")
        eps = E(nc.sbuf_tensor(f"{name}_epsE", [128, 1], F32))
        keps = c.key("eps")
        c.op("pool", lambda: nc.gpsimd.memset(eps[:], EPS), writes=[keps])
        with ExitStack() as esn:
            if True:
                En = esn.enter_context
                MT = En(nc.sbuf_tensor(f"{name}_MTn", [128, 4, S], BF16))
                QT = En(nc.sbuf_tensor(f"{name}_QT", [128, 4, S], BF16))
                kQT = c.key("QT")
                QZ = En(nc.sbuf_tensor(f"{name}_QZ", [128, 8, S], BF16))
                c.op("pool", lambda: nc.gpsimd.memset(QZ[:, 0:4, :], 0.0), writes=["qz0"])
                c.op("pool", lambda: nc.gpsimd.memset(QZ[:, 4:8, :], 0.0), writes=["qz1"])
                VAB = En(nc.sbuf_tensor(f"{name}_VAB", [128, NT, 8, 192], BF16))
                VA = VAB[:, :, :, 0:128]
                VB = VAB[:, :, :, 64:192]
                kVA = c.key("VA")
                with ExitStack() as esx:
                    xnT = esx.enter_context(nc.sbuf_tensor(f"{name}_xnT", [128, 8, S], BF16))
                    kxnT = c.key("xnT")
                    Wq = esx.enter_context(nc.sbuf_tensor(f"{name}_Wq", [128, 8, 1536], BF16))
                    kWq = c.key("Wq")
                    for k in range(8):
                        c.dma("pool", Wq[:, k, :], w_in_v[:, k, 0:1536], writes=[(kWq, k)])
                    with ExitStack() as es1:
                        emit_norm_T(c, es1, x_in, P["mix_norm"], xnT, kxnT, ident, name)
                        c.barrier()
                    emit_qkv_proj(c, xnT, kxnT, w_in_v[:, :, 0:1536], P["na_gq"], P["na_gk"], None, None, ident,
                                  QT, kQT, VA, VB, kVA, 8, 8, False, name + "n", QZ=QZ, koff=0, Wpre=(Wq, kWq))
                BM = En(nc.sbuf_tensor(f"{name}_BM", [128, NTYPES, 8, 128], BF16))
                kBM = c.key("BM")
                with ExitStack() as esb:
                    bg = Ring(c, esb, f"{name}_bg", [128, 8, 128], F32, 3)
                    mk = Ring(c, esb, f"{name}_mk", [128, 128], F32, 3)
                    for t in range(NTYPES):
                        b_t, kb = bg.next()
                        m_t, km = mk.next()
                        c.dma("sp" if t % 2 == 0 else "pool", b_t[:], P["biasg"][t], writes=[kb])
                        c.dma("sp", m_t[:], P["namask"][t], writes=[km])
                        c.op("dve", lambda: nc.vector.tensor_tensor(out=BM[:, t, :, :], in0=b_t[:],
                                                                    in1=m_t[:].unsqueeze(1).to_broadcast([128, 8, 128]), op=ALU.add),
                             reads=[kb, km], writes=[kBM])
                    c.barrier()
                with ExitStack() as es3:
                    ps = Ring(c, es3, f"{name}_ps", [128, 8, 128], F32, 3, psum=True)
                    po = Ring(c, es3, f"{name}_po", [128, 512], F32, 1, psum=True)
                    pb = Ring(c, es3, f"{name}_pb", [128, 512], F32, 1, psum=True)
                    PT = Ring(c, es3, f"{name}_PT", [128, 5, 128], BF16, 3)
                    osb = Ring(c, es3, f"{name}_osb", [128, 512], F32, 2)
                    rd = Ring(c, es3, f"{name}_rd", [128, 512], F32, 2)
                    iters = []
                    for h in range(8):
                        for ig in range(4):
                            grp = {}
                            for ii in range(4):
                                def mk(h=h, ig=ig, ii=ii, grp=grp):
                                    hb = (h % 2) * 64
                                    qc = h // 2
                                    kc = 4 + h // 2
                                    i = ig * 4 + ii
                                    cls, kp0, ntl = na_cls(i)
                                    stt = {}

                                    def qk():
                                        stt["s"], stt["ks"] = ps.next()
                                        t0 = NA_TIX[(cls, 0)]
                                        c.op("pe", lambda: nc.tensor.matmul(stt["s"][:, 0:4, :], lhsT=ident[:], rhs=BM[:, t0:t0 + 4, h, :],
                                                                            start=True, stop=False),
                                             reads=[], writes=[stt["ks"]], signal=False)
                                        for j in range(4):
                                            kp = kp0 + j
                                            c.op("pe", lambda: nc.tensor.matmul(stt["s"][:, j, :], lhsT=QT[:, h // 2, kp * 128:(kp + 1) * 128],
                                                                                rhs=QZ[:, h, i * 128:(i + 1) * 128], start=False, stop=(j == 3)),
                                                 reads=[], writes=[stt["ks"]], signal=(j == 3 and ntl == 4))
                                        if ntl == 5:
                                            kp = kp0 + 4
                                            c.op("pe", lambda: nc.tensor.matmul(stt["s"][:, 4, :], lhsT=ident[:], rhs=BM[:, t0 + 4, h, :],
                                                                                start=True, stop=False),
                                                 reads=[], writes=[stt["ks"]], signal=False)
                                            c.op("pe", lambda: nc.tensor.matmul(stt["s"][:, 4, :], lhsT=QT[:, h // 2, kp * 128:(kp + 1) * 128],
                                                                                rhs=QZ[:, h, i * 128:(i + 1) * 128], start=False, stop=True),
                                                 reads=[], writes=[stt["ks"]], signal=True)

                                    def ex():
                                        stt["p"], stt["kp"] = PT.next()
                                        c.op("act", lambda: nc.scalar.activation(out=stt["p"][:, 0:ntl, :], in_=stt["s"][:, 0:ntl, :], func=AF.Exp),
                                             reads=[stt["ks"]], writes=[stt["kp"]])

                                    def pv():
                                        if ii == 0:
                                            grp["o"], grp["ko"] = po.next()
                                        o_t, ko = grp["o"], grp["ko"]
                                        for j in range(ntl):
                                            kp = kp0 + j
                                            if hb == 0:
                                                c.op("pe", lambda: nc.tensor.matmul(o_t[:, ii * 128:(ii + 1) * 128], lhsT=VA[:, kp, h, :],
                                                                                    rhs=stt["p"][:, j, :], start=(j == 0), stop=(j == ntl - 1)),
                                                     reads=[stt["kp"]], writes=[ko], signal=(j == ntl - 1))
                                            else:
                                                c.op("pe", lambda: nc.tensor.matmul(o_t[:, ii * 128:(ii + 1) * 128], lhsT=VB[:, kp, h, :],
                                                                                    rhs=stt["p"][:, j, :], start=(j == 0), stop=(j == ntl - 1)),
                                                     reads=[stt["kp"]], writes=[ko], signal=(j == ntl - 1))

                                    it = {"qk": qk, "exp": ex, "pv": pv}
                                    if ii == 3:
                                        def post_factory():
                                            return make_norm_post(c, grp["o"], grp["ko"], hb,
                                                                  MT[hb:hb + 64, h // 2, ig * 512:(ig + 1) * 512], (kMT, h, ig), osb, rd, pb, ones32,
                                                                  nrows=128)
                                        it["post_factory"] = post_factory
                                    return it
                                iters.append(mk())
                    warm_pe(c, pb.t[0], pb.k[0], QT[:, 0, :])
                    run_pipeline(iters)
                    c.barrier()
                    c.dma("sp", mt_v[:, 0:4, :], MT[:], writes=[("mt_d", "na")])
                    c.barrier()
        XT = E(nc.sbuf_tensor(f"{name}_XT", [128, NT, 1024], BF16))
        kXT = c.key("XT")
        BT = E(nc.sbuf_tensor(f"{name}_BT", [128, 4, S], BF16))
        CT = E(nc.sbuf_tensor(f"{name}_CT", [128, 4, S], BF16))
        kBT = c.key("BT")
        kCT = c.key("CT")
        Btok = E(nc.sbuf_tensor(f"{name}_Btok", [128, NT, 512], BF16))
        kBtok = c.key("Btok")
        with ExitStack() as esx:
            xnT = esx.enter_context(nc.sbuf_tensor(f"{name}_xnTb", [128, 8, S], BF16))
            kxnT = c.key("xnT")
            Wz = esx.enter_context(nc.sbuf_tensor(f"{name}_Wz", [128, 8, 1056], BF16))
            kWz = c.key("Wz")
            for k in range(8):
                c.dma("pool", Wz[:, k, 0:1024], w_in_v[:, k, 1536:2560], writes=[(kWz, k)])
                c.dma("pool", Wz[:, k, 1024:1056], w_in_v[:, k, 4608:4640], writes=[(kWz, k, "d")])
            with ExitStack() as es1:
                emit_norm_T(c, es1, x_in, P["mix_norm"], xnT, kxnT, ident, name + "b")
                c.barrier()
            xn_all = [(kxnT, i) for i in range(NT)]
            with ExitStack() as esz:
                Ez = esz.enter_context
                dtb = Ez(nc.sbuf_tensor(f"{name}_dtb", [128, 32], F32))
                alb = Ez(nc.sbuf_tensor(f"{name}_alb", [128, 32], F32))
                kdtb = c.key("dtb")
                kalb = c.key("alb")
                c.dma("sp", dtb[:], bcast_row(P["dt_bias"], 32), writes=[kdtb])
                c.dma("sp", alb[:], bcast_row(P["A_log"], 32), writes=[kalb])
                pz = Ring(c, esz, f"{name}_pz", [128, 1024], F32, 2, psum=True)
                pd = Ring(c, esz, f"{name}_pd", [128, 512], F32, 2, psum=True)
                zb = Ring(c, esz, f"{name}_zb", [128, 1024], BF16, 3)
                for i in range(NT):
                    z_p, kzp = pz.next()
                    d_p, kdp = pd.next()
                    for cb in range(2):
                        for k in range(8):
                            c.op("pe", lambda: nc.tensor.matmul(z_p[:, cb * 512:(cb + 1) * 512], lhsT=xnT[:, k, i * 128:(i + 1) * 128],
                                                                rhs=Wz[:, k, cb * 512:(cb + 1) * 512], start=(k == 0), stop=(k == 7)),
                                 reads=[(kWz, k), (kxnT, i)], writes=[kzp], signal=(k == 7 and cb == 1))
                    for k in range(8):
                        c.op("pe", lambda: nc.tensor.matmul(d_p[:, 0:32], lhsT=xnT[:, k, i * 128:(i + 1) * 128],
                                                            rhs=Wz[:, k, 1024:1056], start=(k == 0), stop=(k == 7)),
                             reads=[(kWz, k, "d"), (kxnT, i)], writes=[kdp], signal=(k == 7))
                    z_t, kz = zb.next()
                    c.op("act", lambda: nc.scalar.activation(out=z_t[:], in_=z_p[:], func=AF.Silu), reads=[kzp], writes=[kz])
                    c.dma("sp", P["zs"][i * 128:(i + 1) * 128, :], z_t[:], reads=[kz], writes=[("zs_d", i)])
                    c.op("dve", lambda: nc.vector.tensor_tensor(out=DT[:, i, :], in0=d_p[:, 0:32], in1=dtb[:], op=ALU.add),
                         reads=[kdp, kdtb], writes=[kDT])
                c.op("act", lambda: nc.scalar.activation(out=DT[:], in_=DT[:], func=AF.Exp), reads=[kDT], writes=[kDT])
                c.op("act", lambda: nc.scalar.activation(out=DT[:], in_=DT[:], func=AF.Ln, bias=1.0), reads=[kDT], writes=[kDT])
                c.op("act", lambda: nc.scalar.activation(out=alb[:], in_=alb[:], func=AF.Exp), reads=[kalb], writes=[kalb])
                c.op("dve", lambda: nc.vector.tensor_scalar(out=alb[:], in0=alb[:], scalar1=-1.0, scalar2=None, op0=ALU.mult),
                     reads=[kalb], writes=[kalb])
                c.op("dve", lambda: nc.vector.tensor_tensor(out=AA[:], in0=DT[:], in1=alb[:].unsqueeze(1).to_broadcast([128, NT, 32]),
                                                            op=ALU.mult), reads=[kDT, kalb], writes=[kAA])
                c.barrier()
            with ExitStack() as esc:
                Ec = esc.enter_context
                cw = Ec(nc.sbuf_tensor(f"{name}_cw", [128, 16, 4], F32))
                cbias = Ec(nc.sbuf_tensor(f"{name}_cb", [128, 16], F32))
                kcw = c.key("cw")
                for j in range(4):
                    c.dma("sp", cw[:, :, j], P["conv_w"][j].rearrange("(c p) -> p c", p=128), writes=[kcw], allow_slow_non_contiguous=True)
                c.dma("sp", cbias[:], P["conv_b"].rearrange("(c p) -> p c", p=128), writes=[kcw], allow_slow_non_contiguous=True)
                wx = Ring(c, esc, f"{name}_wx", [128, 8, 128], BF16, 3)
                px = Ring(c, esc, f"{name}_px", [128, 1024], F32, 2, psum=True)
                pTr = Ring(c, esc, f"{name}_pTr", [128, 8, 128], BF16, 2, psum=True)
                Rr = Ring(c, esc, f"{name}_R", [128, S + 3], F32, 2)
                acc = Ring(c, esc, f"{name}_acc", [128, S], F32, 2)
                xo = Ring(c, esc, f"{name}_xo", [128, S], BF16, 2)
                for t_, k_ in zip(Rr.t, Rr.k):
                    c.op("pool", lambda: nc.gpsimd.memset(t_[:, 0:2], 0.0), writes=[k_])
                    c.op("pool", lambda: nc.gpsimd.memset(t_[:, S + 2:S + 3], 0.0), writes=[k_])
                def make_ch(ch):
                    T = {}

                    def front():
                            w_t, kw = wx.next()
                            c.dma("pool", w_t[:], w_in_v[:, :, 2560 + ch * 128:2560 + (ch + 1) * 128], writes=[kw])
                            R_t, kR = Rr.next()
                            for half in range(2):
                                p_t, kp = px.next()
                                for k in range(8):
                                    for nb in range(2):
                                        t0 = half * 1024 + nb * 512
                                        c.op("pe", lambda: nc.tensor.matmul(p_t[:, nb * 512:(nb + 1) * 512], lhsT=w_t[:, k, :],
                                                                            rhs=xnT[:, k, t0:t0 + 512], start=(k == 0), stop=(k == 7)),
                                             reads=[kw] + xn_all, writes=[kp], signal=(k == 7 and nb == 1))
                                c.op("act", lambda: nc.scalar.copy(out=R_t[:, 2 + half * 1024:2 + (half + 1) * 1024], in_=p_t[:]),
                                     reads=[kp], writes=[kR])
                            T['R'] = (R_t, kR)

                    def back1():
                            R_t, kR = T['R']
                            a_t, ka = acc.next()
                            c.op("act", lambda: nc.scalar.activation(out=a_t[:], in_=R_t[:, 0:S], func=AF.Identity,
                                                                      scale=cw[:, ch, 0:1], bias=cbias[:, ch:ch + 1]),
                                 reads=[kR, kcw], writes=[ka])
                            c.op("dve", lambda: nc.vector.scalar_tensor_tensor(out=a_t[:], in0=R_t[:, 1:S + 1], scalar=cw[:, ch, 1:2], in1=a_t[:],
                                                                               op0=ALU.mult, op1=ALU.add), reads=[kR, kcw, ka], writes=[ka])
                            c.op("dve", lambda: nc.vector.scalar_tensor_tensor(out=a_t[:], in0=R_t[:, 2:S + 2], scalar=cw[:, ch, 2:3], in1=a_t[:],
                                                                                op0=ALU.mult, op1=ALU.add), reads=[kR, kcw, ka], writes=[ka])
                            c.op("dve", lambda: nc.vector.scalar_tensor_tensor(out=a_t[:], in0=R_t[:, 3:S + 3], scalar=cw[:, ch, 3:4], in1=a_t[:],
                                                                               op0=ALU.mult, op1=ALU.add), reads=[kR, kcw, ka], writes=[ka])
                            if ch < 8:
                                o_t, ko = xo.next()
                                dst = o_t[:]
                                wk = [ko]
                            elif ch < 12:
                                dst = BT[:, ch - 8, :]
                                ko = (kBT, ch - 8)
                                wk = [ko]
                            else:
                                dst = CT[:, ch - 12, :]
                                wk = [(kCT, ch - 12)]
                            T['dst'] = (dst, wk, a_t, ka)

                    def back2():
                            dst, wk, a_t, ka = T['dst']
                            c.op("act", lambda: nc.scalar.activation(out=dst, in_=a_t[:], func=AF.Silu), reads=[ka], writes=wk)
                            if ch < 12:
                                for half in range(2):
                                    t_t, kt_ = pTr.next()
                                    for tt in range(8):
                                        t0 = (half * 8 + tt) * 128
                                        c.op("pe", lambda: nc.tensor.transpose(t_t[:, tt, :], dst[:, t0:t0 + 128], ident[:]),
                                             reads=wk, writes=[kt_], signal=(tt == 7))
                                    if ch < 8:
                                        c.op("dve", lambda: nc.vector.tensor_copy(out=XT[:, half * 8:(half + 1) * 8, ch * 128:(ch + 1) * 128], in_=t_t[:]),
                                             reads=[kt_], writes=[(kXT, ch, half)])
                                    else:
                                        c.op("dve", lambda: nc.vector.tensor_copy(out=Btok[:, half * 8:(half + 1) * 8, (ch - 8) * 128:(ch - 7) * 128],
                                                                                  in_=t_t[:]), reads=[kt_], writes=[(kBtok, ch, half)])
                    return front, back1, back2

                chs = [make_ch(ch) for ch in range(16)]
                chs[0][0]()
                chs[1][0]()
                chs[0][1]()
                for ch in range(16):
                    if ch + 2 < 16:
                        chs[ch + 2][0]()
                    if ch + 1 < 16:
                        chs[ch + 1][1]()
                    chs[ch][2]()
                c.barrier()
        emit_ssd(c, P, XT, BT, CT, Btok, DT, AA, eps, ident, ones32, name)
        with ExitStack() as es4:
            MT = es4.enter_context(nc.sbuf_tensor(f"{name}_MT", [128, 12, S], BF16))
            for k in range(12):
                c.dma("sp", MT[:, k, :], mt_v[:, k, :], writes=[(kMT, "ld", k)])
            Wo = es4.enter_context(nc.sbuf_tensor(f"{name}_Wo", [128, 12, D], BF16))
            kWo = c.key("Wo")
            wov = P["w_out"].rearrange("(k p) n -> p k n", p=128)
            for k in range(12):
                c.dma("pool", Wo[:, k, :], wov[:, k, :], writes=[(kWo, k)])
            py = Ring(c, es4, f"{name}_py", [128, D], F32, 2, psum=True)
            xr = Ring(c, es4, f"{name}_xr", [128, D], F32, 4)
            yo = Ring(c, es4, f"{name}_yo", [128, D], F32, 3)
            xq = []
            for tt in range(3):
                x_t, kx = xr.next()
                c.dma("sp", x_t[:], x_in[tt * 128:(tt + 1) * 128, :], writes=[kx])
                xq.append((x_t, kx))
            for tt in range(NT):
                y_p, kyp = py.next()
                for k in range(12):
                    for ob in range(2):
                        c.op("pe", lambda: nc.tensor.matmul(y_p[:, ob * 512:(ob + 1) * 512], lhsT=MT[:, k, tt * 128:(tt + 1) * 128],
                                                            rhs=Wo[:, k, ob * 512:(ob + 1) * 512], start=(k == 0), stop=(k == 11)),
                             reads=[(kWo, k), (kMT, "ld", k)], writes=[kyp], signal=(k == 11 and ob == 1))
                x_t, kx = xq.pop(0)
                y_t, ky = yo.next()
                c.op("dve", lambda: nc.vector.tensor_tensor(out=y_t[:], in0=x_t[:], in1=y_p[:], op=ALU.add),
                     reads=[kx, kyp], writes=[ky])
                if tt + 3 < NT:
                    x_n, kxn = xr.next()
                    c.dma("sp", x_n[:], x_in[(tt + 3) * 128:(tt + 4) * 128, :], writes=[kxn])
                    xq.append((x_n, kxn))
                c.dma("sp", x_out[tt * 128:(tt + 1) * 128, :], y_t[:], reads=[ky])
            c.barrier()


def emit_ssd(c, P, XT, BT, CT, Btok, DT, AA, eps, ident, ones32, name):
    nc = c.nc
    NHD = 16
    mt_v = P["mt_scr"].rearrange("k p s -> p k s")
    with ExitStack() as es:
        E = es.enter_context
        U = E(nc.sbuf_tensor(f"{name}_U", [128, 128], F32))
        L = E(nc.sbuf_tensor(f"{name}_L", [128, 128], F32))
        NMf = E(nc.sbuf_tensor(f"{name}_NMf", [128, 128], F32))
        NMb = E(nc.sbuf_tensor(f"{name}_NMb", [128, 128], F32))
        Db = E(nc.sbuf_tensor(f"{name}_Db", [128, NHD], F32))
        gnb = E(nc.sbuf_tensor(f"{name}_gnb", [128, 1024], F32))
        kconst = c.key("ssdconst")
        c.dma("sp", U[:], P["tri_u"], writes=[kconst])
        c.dma("sp", L[:], P["tri_l"], writes=[kconst])
        c.dma("sp", NMf[:], P["negm_f"], writes=[kconst])
        c.dma("sp", NMb[:], P["negm_b"], writes=[kconst])
        c.dma("sp", Db[:], bcast_row(P["ssd_D"], NHD), writes=[kconst])
        c.dma("sp", gnb[:], bcast_row(P["out_norm"], 1024), writes=[kconst])
        Hbst = Ring(c, es, f"{name}_Hbst", [128, 1024], BF16, 2)
        Hbld = Ring(c, es, f"{name}_Hbld", [128, 1024], BF16, 2)
        kHb = c.key("Hball")
        Hf32 = E(nc.sbuf_tensor(f"{name}_Hf32", [128, 1024], F32))
        Hb32 = E(nc.sbuf_tensor(f"{name}_Hb32", [128, 1024], F32))
        kHf = c.key("Hf32")
        kHb32 = c.key("Hb32")
        c.op("pool", lambda: nc.gpsimd.memset(Hf32[:], 0.0), writes=[kHf])
        c.op("pool", lambda: nc.gpsimd.memset(Hb32[:], 0.0), writes=[kHb32])
        Hfb = Ring(c, es, f"{name}_Hfb", [128, 1024], BF16, 2)
        pR = Ring(c, es, f"{name}_pR", [128, 4, 128], F32, 2, psum=True)
        pcb = Ring(c, es, f"{name}_pcb", [128, 4, 128], F32, 1, psum=True)
        py = Ring(c, es, f"{name}_pyd", [128, 1024], F32, 1, psum=True)
        pY2 = Ring(c, es, f"{name}_pY2", [128, 1024], F32, 1, psum=True)
        pTn = Ring(c, es, f"{name}_pTn", [128, 8, 128], BF16, 1, psum=True)
        STr = Ring(c, es, f"{name}_ST", [128, 5, 32], F32, 4)
        wsm = Ring(c, es, f"{name}_wsm", [128, 16], F32, 3)
        xcf = Ring(c, es, f"{name}_xcf", [128, 1024], BF16, 2)
        xcb = Ring(c, es, f"{name}_xcb", [128, 1024], BF16, 2)
        xcd = Ring(c, es, f"{name}_xcd", [128, 1024], BF16, 3)
        cbT = Ring(c, es, f"{name}_cbT", [128, 4, 128], F32, 2)
        t1r = Ring(c, es, f"{name}_t1", [128, 4, 128], F32, 3)
        t2r = Ring(c, es, f"{name}_t2", [128, 4, 128], F32, 3)
        Mr = Ring(c, es, f"{name}_M", [128, 4, 128], BF16, 18)
        yar = Ring(c, es, f"{name}_ya", [128, 1024], F32, 2)
        ybr = Ring(c, es, f"{name}_yb", [128, 1024], F32, 2)
        ydr = Ring(c, es, f"{name}_yd", [128, 1024], F32, 2)
        zsr = Ring(c, es, f"{name}_zs", [128, 1024], BF16, 2)
        hhr = Ring(c, es, f"{name}_hh", [128, 1024], F32, 2)
        sqr = Ring(c, es, f"{name}_sqj", [128, 1024], BF16, 1)
        s4r = Ring(c, es, f"{name}_s4", [128, 2, 4], F32, 2)
        hbr = Ring(c, es, f"{name}_hb16", [128, 1024], BF16, 2)
        mst = Ring(c, es, f"{name}_mst", [128, 8, 128], BF16, 2)
        c.barrier()

        def b16(ap2d, n=NHD, d=64):
            return ap2d.unsqueeze(2).to_broadcast([128, n, d])

        def v3(ap2d, d=64):
            return ap2d.rearrange("p (h d) -> p h d", d=d)

        def chunk_stats(ci):
            p_t, kp = pR.next()
            c.op("pe", lambda: nc.tensor.matmul(p_t[:, 0, 0:16], lhsT=U[:], rhs=AA[:, ci, 0:16], start=True, stop=True),
                 reads=[], writes=[kp], signal=False)
            c.op("pe", lambda: nc.tensor.matmul(p_t[:, 0, 16:32], lhsT=L[:], rhs=AA[:, ci, 16:32], start=True, stop=True),
                 reads=[], writes=[kp], signal=False)
            c.op("pe", lambda: nc.tensor.matmul(p_t[:, 0, 32:64], lhsT=ones32[:], rhs=AA[:, ci, 0:32], start=True, stop=True),
                 reads=[], writes=[kp], signal=True)
            st, kst = STr.next()
            c.op("dve", lambda: nc.vector.tensor_copy(out=st[:, 0, :], in_=p_t[:, 0, 0:32]), reads=[kp], writes=[kst])
            c.op("dve", lambda: nc.vector.tensor_tensor(out=st[:, 4, :], in0=p_t[:, 0, 32:64], in1=st[:, 0, :], op=ALU.subtract),
                 reads=[kp, kst], writes=[kst])
            c.op("act", lambda: nc.scalar.activation(out=st[:, 1, :], in_=st[:, 0, :], func=AF.Exp), reads=[kst], writes=[kst])
            c.op("act", lambda: nc.scalar.activation(out=st[:, 2, :], in_=st[:, 4, :], func=AF.Exp), reads=[kst], writes=[kst])
            c.op("act", lambda: nc.scalar.activation(out=st[:, 3, :], in_=p_t[:, 0, 32:64], func=AF.Exp), reads=[kp, kst], writes=[kst])
            return st, kst

        def states_into(ps_t, kps, ci, x_t, kx):
            for g in range(4):
                c.op("pe", lambda: nc.tensor.matmul(ps_t[:, g * 256:(g + 1) * 256], lhsT=Btok[:, ci, g * 128:(g + 1) * 128],
                                                    rhs=x_t[:, g * 256:(g + 1) * 256], start=True, stop=True),
                     reads=[kx], writes=[kps], signal=(g == 3))

        pre_ps = [(pY2.t[0], pY2.k[0]), (py.t[0], py.k[0])]
        order = list(range(NT - 1, -1, -1))

        def pre_front(n):
            ci = order[n]
            st, kst = chunk_stats(ci)
            w_t, kw = wsm.next()
            c.op("dve", lambda: nc.vector.tensor_tensor(out=w_t[:], in0=DT[:, ci, 16:32], in1=st[:, 2, 16:32], op=ALU.mult),
                 reads=[kst], writes=[kw])
            x_t, kx = xcd.next()
            c.op("pool", lambda: nc.gpsimd.tensor_tensor(out=v3(x_t[:]), in0=v3(XT[:, ci, :]), in1=b16(w_t[:]), op=ALU.mult),
                 reads=[kw], writes=[kx])
            ps_t, kps = pre_ps[n % 2]
            states_into(ps_t, kps, ci, x_t, kx)
            return st, kst, ps_t, kps

        fr = pre_front(0)
        for n in range(NT):
            ci = order[n]
            nxt = pre_front(n + 1) if n + 1 < NT else None
            st, kst, ps_t, kps = fr
            hs_t, khs = Hbst.next()
            c.op("act", lambda: nc.scalar.copy(out=hs_t[:], in_=Hb32[:]), reads=[kHb32], writes=[khs])
            c.dma("sp", P["hb_scr"][ci], hs_t[:], reads=[khs], writes=[(kHb, ci)])
            c.op("pool", lambda: nc.gpsimd.tensor_tensor(out=v3(Hb32[:]), in0=v3(Hb32[:]), in1=b16(st[:, 3, 16:32]), op=ALU.mult),
                 reads=[kHb32, kst], writes=[kHb32])
            c.op("dve", lambda: nc.vector.tensor_tensor(out=Hb32[:], in0=Hb32[:], in1=ps_t[:], op=ALU.add),
                 reads=[kHb32, kps], writes=[kHb32])
            fr = nxt

        def stage_a(ci):
            T = {"ci": ci}
            tok = slice(ci * 128, (ci + 1) * 128)
            st, kst = chunk_stats(ci)
            T["st"], T["kst"] = st, kst
            xf_t, kxf = xcf.next()
            xb_t, kxb = xcb.next()
            xd_t, kxd = xcd.next()
            c.op("dve", lambda: nc.vector.tensor_tensor(out=v3(xf_t[:]), in0=v3(XT[:, ci, :]), in1=b16(DT[:, ci, 0:16]), op=ALU.mult),
                 reads=[], writes=[kxf])
            c.op("pool", lambda: nc.gpsimd.tensor_tensor(out=v3(xb_t[:]), in0=v3(XT[:, ci, :]), in1=b16(DT[:, ci, 16:32]), op=ALU.mult),
                 reads=[], writes=[kxb])
            c.op("pool", lambda: nc.gpsimd.tensor_tensor(out=v3(xd_t[:]), in0=v3(xf_t[:]), in1=b16(st[:, 2, 0:16]), op=ALU.mult),
                 reads=[kxf, kst], writes=[kxd])
            T["xf"], T["xb"], T["xd"] = (xf_t, kxf), (xb_t, kxb), (xd_t, kxd)
            yd_t, kyd = ydr.next()
            c.op("pool", lambda: nc.gpsimd.tensor_tensor(out=v3(yd_t[:]), in0=v3(XT[:, ci, :]), in1=b16(Db[:]), op=ALU.mult),
                 reads=[], writes=[kyd])
            T["yd"] = (yd_t, kyd)
            cb_p, kcbp = pcb.next()
            for g in range(4):
                c.op("pe", lambda: nc.tensor.matmul(cb_p[:, g, :], lhsT=BT[:, g, tok], rhs=CT[:, g, tok], start=True, stop=True),
                     reads=[], writes=[kcbp], signal=(g == 3))
            cb_t, kcb = cbT.next()
            c.op("act", lambda: nc.scalar.copy(out=cb_t[:], in_=cb_p[:]), reads=[kcbp], writes=[kcb])
            T["M"] = []
            for g in range(4):
                ms = []
                for d in range(2):
                    tri = U if d == 0 else L
                    NM = NMf if d == 0 else NMb
                    r_p, krp = pR.next()
                    for j in range(4):
                        col = d * 16 + g * 4 + j
                        c.op("pe", lambda: nc.tensor.matmul(r_p[:, j, :], lhsT=AA[:, ci, col:col + 1].to_broadcast([128, 128]),
                                                            rhs=tri[:], start=True, stop=True),
                             reads=[], writes=[krp], signal=(j == 3))
                    c0 = d * 16 + g * 4
                    a_t, ka = t1r.next()
                    for j in range(4):
                        c.op("dve", lambda: nc.vector.scalar_tensor_tensor(out=a_t[:, j, :], in0=r_p[:, j, :], scalar=st[:, 0, c0 + j:c0 + j + 1],
                                                                           in1=NM[:], op0=ALU.subtract, op1=ALU.add),
                             reads=[krp, kst], writes=[ka])
                    e_t, ke = t2r.next()
                    c.op("act", lambda: nc.scalar.activation(out=e_t[:], in_=a_t[:], func=AF.Exp), reads=[ka], writes=[ke])
                    m_t, km = Mr.next()
                    c.op("pool", lambda: nc.gpsimd.tensor_tensor(out=m_t[:], in0=e_t[:], in1=cb_t[:, g, :].unsqueeze(1).to_broadcast([128, 4, 128]),
                                                                 op=ALU.mult), reads=[ke, kcb], writes=[km])
                    ms.append((m_t, km))
                T["M"].append(ms)
            return T

        def stage_b(T, hf_t, khf):
            ci = T["ci"]
            tok = slice(ci * 128, (ci + 1) * 128)
            st, kst = T["st"], T["kst"]
            (xf_t, kxf), (xb_t, kxb), (xd_t, kxd) = T["xf"], T["xb"], T["xd"]
            yd_t, kyd = T["yd"]
            y_p, kyp = py.next()
            for g in range(4):
                ms = T["M"][g]
                for j in range(4):
                    h = g * 4 + j
                    c.op("pe", lambda: nc.tensor.matmul(y_p[:, h * 64:(h + 1) * 64], lhsT=ms[0][0][:, j, :], rhs=xf_t[:, h * 64:(h + 1) * 64],
                                                        start=True, stop=False),
                         reads=[ms[0][1], kxf], writes=[kyp], signal=False)
                    c.op("pe", lambda: nc.tensor.matmul(y_p[:, h * 64:(h + 1) * 64], lhsT=ms[1][0][:, j, :], rhs=xb_t[:, h * 64:(h + 1) * 64],
                                                        start=False, stop=True),
                         reads=[ms[1][1], kxb], writes=[kyp], signal=(j == 3))
            o_p, kop = pY2.next()
            for g in range(4):
                c.op("pe", lambda: nc.tensor.matmul(o_p[:, g * 256:(g + 1) * 256], lhsT=CT[:, g, tok], rhs=hf_t[:, g * 256:(g + 1) * 256],
                                                    start=True, stop=True), reads=[khf], writes=[kop], signal=(g == 3))
            ya_t, kya = yar.next()
            c.op("dve", lambda: nc.vector.tensor_tensor(out=v3(ya_t[:]), in0=v3(o_p[:]), in1=b16(st[:, 1, 0:16]), op=ALU.mult),
                 reads=[kop, kst], writes=[kya])
            o_p, kop = pY2.next()
            hl_t, khl = Hbld.next()
            c.dma("sp", hl_t[:], P["hb_scr"][ci], reads=[(kHb, ci)], writes=[khl])
            for g in range(4):
                c.op("pe", lambda: nc.tensor.matmul(o_p[:, g * 256:(g + 1) * 256], lhsT=CT[:, g, tok], rhs=hl_t[:, g * 256:(g + 1) * 256],
                                                    start=True, stop=True), reads=[khl], writes=[kop], signal=(g == 3))
            yb_t, kyb = ybr.next()
            c.op("dve", lambda: nc.vector.tensor_tensor(out=v3(yb_t[:]), in0=v3(o_p[:]), in1=b16(st[:, 1, 16:32]), op=ALU.mult),
                 reads=[kop, kst], writes=[kyb])
            c.op("pool", lambda: nc.gpsimd.tensor_tensor(out=yb_t[:], in0=yb_t[:], in1=yd_t[:], op=ALU.add), reads=[kyb, kyd], writes=[kyb])
            c.op("dve", lambda: nc.vector.tensor_tensor(out=ya_t[:], in0=ya_t[:], in1=yb_t[:], op=ALU.add), reads=[kya, kyb], writes=[kya])
            s_p, ksp = pY2.next()
            states_into(s_p, ksp, ci, xd_t, kxd)
            c.op("pool", lambda: nc.gpsimd.tensor_tensor(out=v3(Hf32[:]), in0=v3(Hf32[:]), in1=b16(st[:, 3, 0:16]), op=ALU.mult),
                 reads=[kHf, kst], writes=[kHf])
            c.op("dve", lambda: nc.vector.tensor_tensor(out=Hf32[:], in0=Hf32[:], in1=s_p[:], op=ALU.add), reads=[kHf, ksp], writes=[kHf])
            hf_n, khf_n = Hfb.next()
            c.op("act", lambda: nc.scalar.copy(out=hf_n[:], in_=Hf32[:]), reads=[kHf], writes=[khf_n])
            z_t, kz = zsr.next()
            c.dma("sp", z_t[:], P["zs"][ci * 128:(ci + 1) * 128, :], reads=[("zs_d", ci)], writes=[kz])
            hh_t, khh = hhr.next()
            c.op("dve", lambda: nc.vector.tensor_tensor(out=hh_t[:], in0=y_p[:], in1=ya_t[:], op=ALU.add), reads=[kyp, kya], writes=[khh])
            T["hh"] = (hh_t, khh, z_t, kz)
            return hf_n, khf_n

        def stage_b2(T):
            ci = T["ci"]
            hh_t, khh, z_t, kz = T["hh"]
            c.op("pool", lambda: nc.gpsimd.tensor_tensor(out=hh_t[:], in0=hh_t[:], in1=z_t[:], op=ALU.mult), reads=[khh, kz], writes=[khh])
            sq_t, ksq = sqr.next()
            s4, ks4 = s4r.next()
            for gg in range(4):
                c.op("act", lambda: nc.scalar.activation(out=sq_t[:, gg * 256:(gg + 1) * 256], in_=hh_t[:, gg * 256:(gg + 1) * 256],
                                                          func=AF.Square, accum_out=s4[:, 0, gg:gg + 1]),
                     reads=[khh], writes=[ksq, ks4])
            c.op("act", lambda: nc.scalar.activation(out=s4[:, 1, :], in_=s4[:, 0, :], func=AF.Sqrt, scale=1.0 / 256, bias=eps[:]),
                 reads=[ks4], writes=[ks4])
            c.op("dve", lambda: nc.vector.reciprocal(out=s4[:, 1, :], in_=s4[:, 1, :]), reads=[ks4], writes=[ks4])
            hb_t, khb = hbr.next()
            for gg in range(4):
                c.op("dve", lambda: nc.vector.scalar_tensor_tensor(out=hb_t[:, gg * 256:(gg + 1) * 256], in0=hh_t[:, gg * 256:(gg + 1) * 256],
                                                                   scalar=s4[:, 1, gg:gg + 1], in1=gnb[:, gg * 256:(gg + 1) * 256],
                                                                   op0=ALU.mult, op1=ALU.mult),
                     reads=[khh, ks4], writes=[khb])
            T["hb"] = (hb_t, khb)

        def stage_b3(T):
            ci = T["ci"]
            hb_t, khb = T["hb"]
            t_t, kt_ = pTn.next()
            for k in range(8):
                c.op("pe", lambda: nc.tensor.transpose(t_t[:, k, :], hb_t[:, k * 128:(k + 1) * 128], ident[:]),
                     reads=[khb], writes=[kt_], signal=(k == 7))
            m_s, kms = mst.next()
            c.op("act", lambda: nc.scalar.copy(out=m_s[:], in_=t_t[:]), reads=[kt_], writes=[kms])
            c.dma("sp", mt_v[:, 4:12, ci * 128:(ci + 1) * 128], m_s[:], reads=[kms], writes=[("mt_d", "ssd", ci)])

        hf_t, khf = Hfb.next()
        c.op("pool", lambda: nc.gpsimd.memset(hf_t[:], 0.0), writes=[khf])
        Ta = stage_a(0)
        Tp1 = Tp2 = None
        for ci in range(NT):
            Tn = stage_a(ci + 1) if ci + 1 < NT else None
            hf_t, khf = stage_b(Ta, hf_t, khf)
            if Tp1 is not None:
                stage_b2(Tp1)
            if Tp2 is not None:
                stage_b3(Tp2)
            Tp2 = Tp1
            Tp1 = Ta
            Ta = Tn
        stage_b2(Tp1)
        stage_b3(Tp2)
        stage_b3(Tp1)
        c.barrier()


def _rope_tables():
    t = np.arange(S)
    row = (t // 64).astype(np.float32)
    col = (t % 64).astype(np.float32)
    freqs = (np.float32(10000.0) ** (-np.arange(0, 32, 2, dtype=np.float32) / np.float32(32))).astype(np.float32)
    ang = np.concatenate([row[:, None] * freqs, col[:, None] * freqs], -1).astype(np.float32)
    return np.cos(ang).astype(np.float32), np.sin(ang).astype(np.float32)


def _na_index():
    dyi = np.zeros((NTYPES, 128, 128), np.int64)
    dxi = np.zeros((NTYPES, 128, 128), np.int64)
    msk = np.zeros((NTYPES, 128, 128), np.float32)
    rep = {0: 0, 1: 1, 2: 5, 3: 14, 4: 15}
    kk = np.arange(128)
    kr, ck = kk // 64, kk % 64
    for n, (cls, j) in enumerate(NA_TYPES):
        i = rep[cls]
        _, kp0, _ = na_cls(i)
        r = 2 * i + kr[None, :]
        cq = ck[None, :]
        rk = 2 * (kp0 + j) + kr[:, None]
        ckk = ck[:, None]
        rs = np.clip(r - 4, 0, 24)
        vrow = (rk >= rs) & (rk < rs + 8)
        cs = np.clip(cq - 8, 0, 48)
        vcol = (ckk >= cs) & (ckk < cs + 16)
        dyi[n] = np.clip(rk - r + 7, 0, 14)
        dxi[n] = np.clip(ckk - cq, -15, 15) + 15
        msk[n] = np.where(vrow & vcol, 0.0, NEG)
    return dyi, dxi, msk


_PROGRAM = None


def _inputs_spec():
    return [("x", [S, D], F32), ("even_mix_norm", [D], F32), ("even_w_in", [D, 4640], F32), ("na_q_norm", [64], F32),
            ("na_k_norm", [64], F32), ("biasg", [NTYPES, 128, 8, 128], F32), ("namask", [NTYPES, 128, 128], F32),
            ("ssd_conv_w", [4, 2048], F32), ("ssd_conv_b", [2048], F32), ("ssd_dt_bias", [32], F32), ("ssd_A_log", [32], F32),
            ("ssd_D", [16], F32), ("ssd_out_norm", [1024], F32), ("even_w_out", [1536, D], F32),
            ("odd_mix_norm", [D], F32), ("odd_w_qkv", [D, 1536], F32), ("gqa_q_norm", [64], F32), ("gqa_k_norm", [64], F32),
            ("odd_w_out", [D, D], F32), ("ffn_norm0", [D], F32), ("ffn_norm1", [D], F32),
            ("ffn_w13_0", [D, 2 * FH], F32), ("ffn_w13_1", [D, 2 * FH], F32), ("ffn_w2_0", [FH, D], F32), ("ffn_w2_1", [FH, D], F32),
            ("cos", [S, 32], F32), ("sin", [S, 32], F32), ("tri_u", [128, 128], F32), ("tri_l", [128, 128], F32),
            ("negm_f", [128, 128], F32), ("negm_b", [128, 128], F32), ("ident", [128, 128], BF16)]


def build_program(phases=("even", "ffn0", "odd", "ffn1")):
    nc = bass.Bass("TRN2", target_bir_lowering=False)
    A = {}
    for n, sh, dt in _inputs_spec():
        A[n] = nc.dram_tensor(n, sh, dt, kind="ExternalInput").ap()
    out = nc.dram_tensor("out", [S, D], F32, kind="ExternalOutput").ap()
    x1 = nc.dram_tensor("x1_scr", [S, D], F32, kind="Internal").ap()
    x2 = nc.dram_tensor("x2_scr", [S, D], F32, kind="Internal").ap()
    x3 = nc.dram_tensor("x3_scr", [S, D], F32, kind="Internal").ap()
    zs = nc.dram_tensor("zs_scr", [S, 1024], BF16, kind="Internal").ap()
    hb = nc.dram_tensor("hb_scr", [NT, 128, 1024], BF16, kind="Internal").ap()
    mt = nc.dram_tensor("mt_scr", [12, 128, S], BF16, kind="Internal").ap()
    chain = [A["x"], x1, x2, x3, out]
    order = ["even", "ffn0", "odd", "ffn1"]
    active = [p for p in order if p in phases]
    cur = A["x"]
    with ExitStack() as es:
        c = Ctx(nc, es)
        ident = es.enter_context(nc.sbuf_tensor("ident_sb", [128, 128], BF16))
        ones32 = es.enter_context(nc.sbuf_tensor("ones32", [128, 128], F32))
        c.dma("sp", ident[:], A["ident"], writes=["ident"])
        c.op("pool", lambda: nc.gpsimd.memset(ones32[:], 1.0), writes=["ones32"])
        c.barrier()
        for n, ph in enumerate(active):
            dst = out if n == len(active) - 1 else chain[order.index(ph) + 1]
            if ph == "even":
                P = {"mix_norm": A["even_mix_norm"], "w_in": A["even_w_in"], "na_gq": A["na_q_norm"], "na_gk": A["na_k_norm"],
                     "biasg": A["biasg"], "namask": A["namask"], "conv_w": A["ssd_conv_w"], "conv_b": A["ssd_conv_b"],
                     "dt_bias": A["ssd_dt_bias"], "A_log": A["ssd_A_log"], "ssd_D": A["ssd_D"], "out_norm": A["ssd_out_norm"],
                     "w_out": A["even_w_out"], "tri_u": A["tri_u"], "tri_l": A["tri_l"], "negm_f": A["negm_f"], "negm_b": A["negm_b"],
                     "zs": zs, "hb_scr": hb, "mt_scr": mt}
                emit_even(c, cur, dst, P, ident, ones32, "e0")
            elif ph == "ffn0":
                emit_ffn(c, cur, dst, A["ffn_norm0"], A["ffn_w13_0"], A["ffn_w2_0"], ident, "f0")
            elif ph == "odd":
                emit_odd(c, cur, dst, A["odd_mix_norm"], A["odd_w_qkv"], A["gqa_q_norm"], A["gqa_k_norm"], A["odd_w_out"],
                         A["cos"], A["sin"], ident, ones32, "o0")
            elif ph == "ffn1":
                emit_ffn(c, cur, dst, A["ffn_norm1"], A["ffn_w13_1"], A["ffn_w2_1"], ident, "f1")
            cur = dst
        c.finish("sp")
    return nc


def make_in_maps(inputs, xs):
    import ml_dtypes
    f = lambda a: np.ascontiguousarray(np.asarray(a, dtype=np.float32))
    cos, sin = _rope_tables()
    dyi, dxi, msk = _na_index()
    rpb = f(inputs["na_rel_bias"])[0]
    biasg = np.ascontiguousarray(rpb[:, dyi, dxi].transpose(1, 2, 0, 3))
    tri_u = np.triu(np.ones((128, 128), np.float32))
    kk = np.arange(128)
    negm_f = np.where(kk[None, :] >= kk[:, None], 0.0, NEG).astype(np.float32)
    negm_b = np.where(kk[None, :] <= kk[:, None], 0.0, NEG).astype(np.float32)
    shared = {
        "even_mix_norm": f(inputs["even_mix_norm"])[0], "even_w_in": f(inputs["even_w_in"])[0],
        "na_q_norm": f(inputs["na_q_norm"])[0], "na_k_norm": f(inputs["na_k_norm"])[0], "biasg": biasg, "namask": msk,
        "ssd_conv_w": f(inputs["ssd_conv_w"])[0], "ssd_conv_b": f(inputs["ssd_conv_b"])[0],
        "ssd_dt_bias": f(inputs["ssd_dt_bias"])[0].reshape(32), "ssd_A_log": f(inputs["ssd_A_log"])[0].reshape(32),
        "ssd_D": f(inputs["ssd_D"])[0], "ssd_out_norm": f(inputs["ssd_out_norm"])[0], "even_w_out": f(inputs["even_w_out"])[0],
        "odd_mix_norm": f(inputs["odd_mix_norm"])[0], "odd_w_qkv": f(inputs["odd_w_qkv"])[0],
        "gqa_q_norm": f(inputs["gqa_q_norm"])[0], "gqa_k_norm": f(inputs["gqa_k_norm"])[0], "odd_w_out": f(inputs["odd_w_out"])[0],
        "ffn_norm0": f(inputs["ffn_norm"])[0], "ffn_norm1": f(inputs["ffn_norm"])[1],
        "ffn_w13_0": f(inputs["ffn_w13"])[0], "ffn_w13_1": f(inputs["ffn_w13"])[1],
        "ffn_w2_0": f(inputs["ffn_w2"])[0], "ffn_w2_1": f(inputs["ffn_w2"])[1],
        "cos": cos, "sin": sin, "tri_u": tri_u, "tri_l": np.ascontiguousarray(tri_u.T),
        "negm_f": negm_f, "negm_b": negm_b, "ident": np.eye(128).astype(ml_dtypes.bfloat16),
    }
    return [dict(shared, x=np.ascontiguousarray(xb)) for xb in xs]


def kernel(**inputs):
    global _PROGRAM
    x = np.asarray(inputs["x"], dtype=np.float32)
    B = x.shape[0]
    if _PROGRAM is None:
        _PROGRAM = build_program()
    in_maps = make_in_maps(inputs, [x[b] for b in range(B)])
    res = run_bass_kernel_spmd(_PROGRAM, in_maps, core_ids=list(range(B)))
    return np.stack([np.asarray(r["out"], dtype=np.float32) for r in res.results], axis=0)
```

```python
import numpy as np
from contextlib import ExitStack
import concourse.bass as bass
import concourse.mybir as mybir
from concourse.bass_utils import run_bass_kernel_spmd

F32 = mybir.dt.float32
BF16 = mybir.dt.bfloat16
AF = mybir.ActivationFunctionType
ALU = mybir.AluOpType
AX = mybir.AxisListType

S = 2048
D = 1024
NT = 16
FH = 2816
NHC = 22
EPS = 1e-6
NEG = -30000.0
N_DUMMY = 0


class Ctx:
    SAME_ENGINE_SYNC = ("act", "dve", "pool")

    def __init__(self, nc, es, n_dma_sems=32):
        self.nc = nc
        self.es = es
        self.eng = {"pe": nc.tensor, "act": nc.scalar, "dve": nc.vector, "pool": nc.gpsimd, "sp": nc.sync}
        self.sem = {}
        self.cnt = {}
        self.nsem = 0
        for e in self.eng:
            self._new_sem(e)
        self.dsem = [es.enter_context(nc.semaphore(f"dma{i}")) for i in range(n_dma_sems)]
        self.dcnt = [0] * n_dma_sems
        self.dnext = {"hw": 0, "sw": 0}
        self.dhalf = n_dma_sems // 2
        self.waited = {e: {} for e in self.eng}
        self.last_w = {}
        self.readers = {}
        self.pend = {e: ([], []) for e in self.eng}
        self.uid = 0

    def _new_sem(self, e):
        self.sem[e] = self.es.enter_context(self.nc.semaphore(f"s_{e}_{self.nsem}"))
        self.nsem += 1
        self.cnt[e] = 0

    def _wait(self, e, tok):
        sem, val, src = tok
        if src == e and e not in self.SAME_ENGINE_SYNC:
            return
        key = id(sem)
        if self.waited[e].get(key, 0) >= val:
            return
        self.waited[e][key] = val
        self.eng[e].wait_ge(sem, val)

    def _deps(self, e, reads, writes):
        for r in reads:
            t = self.last_w.get(r)
            if t is not None:
                self._wait(e, t)
        for w in writes:
            t = self.last_w.get(w)
            if t is not None:
                self._wait(e, t)
            for t in self.readers.get(w, ()):
                self._wait(e, t)

    def _commit(self, tok, reads, writes):
        for w in writes:
            self.last_w[w] = tok
            self.readers[w] = []
        for r in reads:
            self.readers.setdefault(r, []).append(tok)

    def op(self, e, ins_fn, reads=(), writes=(), signal=True):
        reads = list(reads)
        writes = list(writes)
        self._deps(e, reads, writes)
        ins = ins_fn()
        pr, pw = self.pend[e]
        if not signal:
            pr.extend(reads)
            pw.extend(writes)
            return ins
        if self.cnt[e] >= 30000:
            self._new_sem(e)
        self.cnt[e] += 1
        ins.then_inc(self.sem[e], 1)
        tok = (self.sem[e], self.cnt[e], e)
        self._commit(tok, reads + pr, writes + pw)
        self.pend[e] = ([], [])
        return ins

    def dma(self, q, out, in_, reads=(), writes=(), **kw):
        reads = list(reads)
        writes = list(writes)
        kind = "sw" if q == "pool" else "hw"
        i = self.dnext[kind] + (self.dhalf if kind == "sw" else 0)
        self.dnext[kind] = (self.dnext[kind] + 1) % self.dhalf
        skey = ("__dsem", i)
        self._deps(q, reads, writes + [skey])
        ins = self.eng[q].dma_start(out=out, in_=in_, **kw)
        self.dcnt[i] += 16
        ins.then_inc(self.dsem[i], 16)
        tok = (self.dsem[i], self.dcnt[i], "dma")
        self._commit(tok, reads, writes + [skey])
        return ins

    def finish(self, e="sp"):
        for i, s in enumerate(self.dsem):
            if self.dcnt[i]:
                self._wait(e, (s, self.dcnt[i], "dma"))
        for x in self.eng:
            if self.cnt[x] and (x != e or e in self.SAME_ENGINE_SYNC):
                self._wait(e, (self.sem[x], self.cnt[x], x))

    def barrier(self):
        for e in self.eng:
            assert not self.pend[e][0] and not self.pend[e][1], "pending unsignalled ops at barrier"
        for e in self.eng:
            self.finish(e)
        self.last_w.clear()
        self.readers.clear()

    def key(self, name):
        self.uid += 1
        return f"{name}#{self.uid}"


class Ring:
    def __init__(self, c, es, name, shape, dtype, n, psum=False):
        alloc = c.nc.psum_tensor if psum else c.nc.sbuf_tensor
        self.t = [es.enter_context(alloc(f"{name}{i}", shape, dtype)) for i in range(n)]
        self.k = [c.key(name) for _ in range(n)]
        self.i = -1
        self.n = n

    def next(self):
        self.i = (self.i + 1) % self.n
        return self.t[self.i], self.k[self.i]


def bcast_row(ap_1d, n, parts=128):
    return ap_1d.rearrange("(o n) -> o n", o=1).to_broadcast([parts, n])


def emit_norm_T(c, es, x_dram, g_dram, xnT, xnT_key, ident, name):
    nc = c.nc
    gb = es.enter_context(nc.sbuf_tensor(f"{name}_gb", [128, D], F32))
    kgb = c.key("gb")
    c.dma("sp", gb[:], bcast_row(g_dram, D), writes=[kgb])
    xt = Ring(c, es, f"{name}_xt", [128, D], F32, 4)
    sq = Ring(c, es, f"{name}_sq", [128, D], BF16, 3)
    xs = Ring(c, es, f"{name}_xs", [128, D], BF16, 3)
    st = Ring(c, es, f"{name}_st", [128, 2], F32, 4)
    pT = Ring(c, es, f"{name}_pT", [128, 8, 128], BF16, 2, psum=True)
    eps = es.enter_context(nc.sbuf_tensor(f"{name}_eps", [128, 1], F32))
    keps = c.key("eps")
    c.op("pool", lambda: nc.gpsimd.memset(eps[:], EPS), writes=[keps])
    def chain(i):
        x_t, kx = xt.next()
        c.dma("sp", x_t[:], x_dram[i * 128:(i + 1) * 128, :], writes=[kx])
        s_t, ks = sq.next()
        st_t, kst = st.next()
        c.op("act", lambda: nc.scalar.activation(out=s_t[:], in_=x_t[:], func=AF.Square, accum_out=st_t[:, 0:1]),
             reads=[kx], writes=[ks, kst])
        c.op("act", lambda: nc.scalar.activation(out=st_t[:, 1:2], in_=st_t[:, 0:1], func=AF.Sqrt,
                                                  scale=1.0 / D, bias=eps[:]),
             reads=[kst, keps], writes=[kst])
        c.op("dve", lambda: nc.vector.reciprocal(out=st_t[:, 1:2], in_=st_t[:, 1:2]), reads=[kst], writes=[kst])
        xs_t, kxs = xs.next()
        c.op("dve", lambda: nc.vector.scalar_tensor_tensor(out=xs_t[:], in0=x_t[:], scalar=st_t[:, 1:2], in1=gb[:],
                                                           op0=ALU.mult, op1=ALU.mult),
             reads=[kx, kst, kgb], writes=[kxs])
        return xs_t, kxs

    def tr(i, xs_t, kxs):
        p_t, kp = pT.next()
        for k in range(8):
            c.op("pe", lambda: nc.tensor.transpose(p_t[:, k, :], xs_t[:, k * 128:(k + 1) * 128], ident[:]),
                 reads=[kxs], writes=[kp], signal=(k == 7))
        c.op("act", lambda: nc.scalar.copy(out=xnT[:, :, i * 128:(i + 1) * 128], in_=p_t[:]),
             reads=[kp], writes=[(xnT_key, i)])

    cur = chain(0)
    for i in range(NT):
        nxt = chain(i + 1) if i + 1 < NT else None
        tr(i, *cur)
        cur = nxt


def emit_ffn(c, x_in, x_out, g_dram, w13, w2, ident, name):
    nc = c.nc
    with ExitStack() as es:
        E = es.enter_context
        xnT = E(nc.sbuf_tensor(f"{name}_xnT", [128, 8, S], BF16))
        kxnT = c.key("xnT")
        hT = E(nc.sbuf_tensor(f"{name}_hT", [128, NHC, S], BF16))
        khT = c.key("hT")
        W2 = E(nc.sbuf_tensor(f"{name}_W2", [128, NHC, D], BF16))
        kW2 = c.key("W2")
        with ExitStack() as es1:
            emit_norm_T(c, es1, x_in, g_dram, xnT, kxnT, ident, name)
            c.barrier()
        with ExitStack() as es2:
            xn_all = [(kxnT, i) for i in range(NT)]
            w13v = w13.rearrange("(k p) n -> p k n", p=128)
            wg = Ring(c, es2, f"{name}_wg", [128, 8, 128], BF16, 3)
            wu = Ring(c, es2, f"{name}_wu", [128, 8, 128], BF16, 3)
            pg = Ring(c, es2, f"{name}_pg", [128, 1024], F32, 2, psum=True)
            pu = Ring(c, es2, f"{name}_pu", [128, 1024], F32, 2, psum=True)
            sg = Ring(c, es2, f"{name}_sg", [128, 1024], F32, 2)
            for hc in range(NHC):
                wg_t, kwg = wg.next()
                wu_t, kwu = wu.next()
                c.dma("pool", wg_t[:], w13v[:, :, hc * 128:(hc + 1) * 128], writes=[kwg])
                c.dma("pool", wu_t[:], w13v[:, :, FH + hc * 128:FH + (hc + 1) * 128], writes=[kwu])
                c.dma("pool", W2[:, hc, :], w2[hc * 128:(hc + 1) * 128, :], writes=[(kW2, hc)])
                for th in range(2):
                    pg_t, kpg = pg.next()
                    pu_t, kpu = pu.next()
                    for (w_t, kw, p_t, kp) in ((wg_t, kwg, pg_t, kpg), (wu_t, kwu, pu_t, kpu)):
                        for k in range(8):
                            for nb in range(2):
                                t0 = th * 1024 + nb * 512
                                c.op("pe", lambda: nc.tensor.matmul(p_t[:, nb * 512:(nb + 1) * 512], lhsT=w_t[:, k, :],
                                                                    rhs=xnT[:, k, t0:t0 + 512],
                                                                    start=(k == 0), stop=(k == 7)),
                                     reads=[kw] + xn_all, writes=[kp], signal=(k == 7 and nb == 1))
                    sg_t, ksg = sg.next()
                    c.op("act", lambda: nc.scalar.activation(out=sg_t[:], in_=pg_t[:], func=AF.Silu),
                         reads=[kpg], writes=[ksg])
                    c.op("dve", lambda: nc.vector.tensor_tensor(out=hT[:, hc, th * 1024:(th + 1) * 1024], in0=sg_t[:],
                                                                in1=pu_t[:], op=ALU.mult),
                         reads=[ksg, kpu], writes=[(khT, hc, th)])
            c.barrier()
        py = [E(nc.psum_tensor(f"{name}_py{j}", [128, D], F32)) for j in range(4)]
        kpy = [c.key("py") for _ in range(4)]
        xr = Ring(c, es, f"{name}_xr", [128, D], F32, 4)
        yo = Ring(c, es, f"{name}_yo", [128, D], F32, 3)
        for tg in range(4):
            xl = []
            for j in range(4):
                x_t, kx = xr.next()
                c.dma("sp", x_t[:], x_in[(tg * 4 + j) * 128:(tg * 4 + j + 1) * 128, :], writes=[kx])
                xl.append((x_t, kx))
            for hc in range(NHC):
                w_t, kw = W2[:, hc, :], (kW2, hc)
                for j in range(4):
                    tt = tg * 4 + j
                    for ob in range(2):
                        c.op("pe", lambda: nc.tensor.matmul(py[j][:, ob * 512:(ob + 1) * 512],
                                                            lhsT=hT[:, hc, tt * 128:(tt + 1) * 128],
                                                            rhs=w_t[:, ob * 512:(ob + 1) * 512],
                                                            start=(hc == 0), stop=(hc == NHC - 1)),
                             reads=[kw, (khT, hc, tt // 8)], writes=[kpy[j]],
                             signal=(ob == 1 and (hc == NHC - 1 or j == 3)))
            for j in range(4):
                tt = tg * 4 + j
                x_t, kx = xl[j]
                y_t, ky = yo.next()
                c.op("dve", lambda: nc.vector.tensor_tensor(out=y_t[:], in0=x_t[:], in1=py[j][:], op=ALU.add),
                     reads=[kx, kpy[j]], writes=[ky])
                c.dma("sp", x_out[tt * 128:(tt + 1) * 128, :], y_t[:], reads=[ky])
        c.barrier()


def warm_pe(c, ptile, pkey, src, n=20):
    nc = c.nc
    for j in range(n):
        c.op("pe", lambda: nc.tensor.matmul(ptile[:, 0:512], lhsT=src[:, 0:128], rhs=src[:, 0:512], start=True, stop=True),
             reads=[], writes=[pkey], signal=(j == n - 1))


def run_pipeline(iters, look=2):
    deferred = []
    N = len(iters)
    for n in range(min(look, N)):
        iters[n]["qk"]()
    for n in range(N):
        due = [d for d in deferred if d[0] <= n]
        for d in due:
            d[1]()
            deferred.remove(d)
        if n + look < N:
            iters[n + look]["qk"]()
        iters[n]["exp"]()
        iters[n]["pv"]()
        posts = iters[n]["post_factory"]() if "post_factory" in iters[n] else ()
        for delay, fn in posts:
            if delay == 0:
                fn()
            else:
                deferred.append((n + delay, fn))
    for d in deferred:
        d[1]()


def make_norm_post(c, o_t, ko, hb, dst_ap, dst_key, osb, rd, pb, ones32, nrows=128):
    nc = c.nc
    dp = 64 if hb == 0 else 0
    st = {}

    def evac():
        st["o"], st["ko"] = osb.next()
        c.op("dve", lambda: nc.vector.tensor_copy(out=st["o"][0:nrows, :], in_=o_t[0:nrows, :]), reads=[ko], writes=[st["ko"]])

    def bcast():
        b_t, kb = pb.next()
        c.op("pe", lambda: nc.tensor.matmul(b_t[:, :], lhsT=ones32[dp:dp + 1, :], rhs=st["o"][dp:dp + 1, :],
                                            start=True, stop=True), reads=[st["ko"]], writes=[kb])
        st["r"], st["kr"] = rd.next()
        c.op("dve", lambda: nc.vector.reciprocal(out=st["r"][hb:hb + 64, :], in_=b_t[hb:hb + 64, :]), reads=[kb], writes=[st["kr"]])
        c.op("dve", lambda: nc.vector.tensor_tensor(out=dst_ap, in0=st["o"][hb:hb + 64, :], in1=st["r"][hb:hb + 64, :], op=ALU.mult),
             reads=[st["ko"], st["kr"]], writes=[dst_key])

    return [(0, evac), (2, bcast)]


def emit_qkv_proj(c, xnT, kxnT, wv, gq, gk, cos_d, sin_d, ident, QT, kQT, VA, VB, kVA, NH, NKV, dupk, name, QZ=None, koff=8, Wpre=None):
    nc = c.nc
    HD = 64
    NQK = NH + NKV
    rope = cos_d is not None
    with ExitStack() as es2:
        E2 = es2.enter_context
        if Wpre is None:
            W = E2(nc.sbuf_tensor(f"{name}_W", [128, 8, 1536], BF16))
            kW = c.key("W")
            for k in range(8):
                c.dma("pool", W[:, k, :], wv[:, k, :], writes=[(kW, k)])
        else:
            W, kW = Wpre
        G = E2(nc.sbuf_tensor(f"{name}_G", [128, NQK, HD], F32))
        kG = c.key("G")
        c.dma("sp", G[:, 0:NH, :], gq.rearrange("(o h d) -> o h d", o=1, h=1).to_broadcast([128, NH, HD]), writes=[kG])
        c.dma("sp", G[:, NH:NQK, :], gk.rearrange("(o h d) -> o h d", o=1, h=1).to_broadcast([128, NKV, HD]), writes=[kG])
        c.op("dve", lambda: nc.vector.tensor_scalar(out=G[:, 0:NH, :], in0=G[:, 0:NH, :], scalar1=HD ** -0.5,
                                                    scalar2=None, op0=ALU.mult), reads=[kG], writes=[kG])
        eps = E2(nc.sbuf_tensor(f"{name}_eps2", [128, 1], F32))
        keps = c.key("eps")
        c.op("pool", lambda: nc.gpsimd.memset(eps[:], EPS), writes=[keps])
        if VA.shape[-1] > HD + 1:
            c.op("pool", lambda: nc.gpsimd.memset(VA[:, :, :, HD:], 0.0), writes=[(kVA, "ones")])
        c.op("pool", lambda: nc.gpsimd.memset(VA[:, :, :, HD:HD + 1], 1.0), writes=[(kVA, "ones")])
        c.op("pool", lambda: nc.gpsimd.memset(VB[:, :, :, 0:HD], 0.0), writes=[(kVA, "ones")])
        c.op("pool", lambda: nc.gpsimd.memset(VB[:, :, :, 0:1], 1.0), writes=[(kVA, "ones")])
        if rope:
            cs = E2(nc.sbuf_tensor(f"{name}_cs", [128, NT, 2, 32], F32))
            kcs = c.key("cs")
            c.dma("sp", cs[:, :, 0, :], cos_d.rearrange("(i p) f -> p i f", p=128), writes=[kcs])
            c.dma("sp", cs[:, :, 1, :], sin_d.rearrange("(i p) f -> p i f", p=128), writes=[kcs])
        pq = Ring(c, es2, f"{name}_pq", [128, 1536], F32, 2, psum=True)
        pT = Ring(c, es2, f"{name}_pT2", [128, 8, 128], BF16, 2, psum=True)
        nb_ = 1
        sq = Ring(c, es2, f"{name}_sq2", [128, NQK, HD], F32, nb_)
        st = Ring(c, es2, f"{name}_st2", [128, 2, NQK], F32, 2)
        qn = Ring(c, es2, f"{name}_qn", [128, NQK, HD], F32, 2)
        if rope:
            kdr = Ring(c, es2, f"{name}_kd", [128, NKV, 2, HD], BF16, 2)
            tA = Ring(c, es2, f"{name}_tA", [128, NQK, 32], F32, 1)
            tB = Ring(c, es2, f"{name}_tB", [128, NQK, 32], F32, 1)
            tC = Ring(c, es2, f"{name}_tC", [128, NQK, 32], F32, 1)
            tD = Ring(c, es2, f"{name}_tD", [128, NQK, 32], F32, 1)
            ro = Ring(c, es2, f"{name}_ro", [128, NQK, HD], F32, 1)
        qr = Ring(c, es2, f"{name}_qr", [128, NQK, HD], BF16, 2)
        def make_tile(i):
            T = {}

            def mm():
                p_t, kp = pq.next()
                for cb in range(3):
                    for k in range(8):
                        c.op("pe", lambda: nc.tensor.matmul(p_t[:, cb * 512:(cb + 1) * 512],
                                                            lhsT=xnT[:, k, i * 128:(i + 1) * 128],
                                                            rhs=W[:, k, cb * 512:(cb + 1) * 512],
                                                            start=(k == 0), stop=(k == 7)),
                             reads=[(kW, k), (kxnT, i)], writes=[kp], signal=(k == 7 and cb == 2))
                T['p_t'], T['kp'] = p_t, kp

            def chain():
                p_t, kp = T['p_t'], T['kp']
                pqk = p_t[:, 0:NQK * HD].rearrange("p (h d) -> p h d", d=HD)
                s_t, ks = sq.next()
                st_t, kst = st.next()
                q_t, kq = qn.next()
                r_t, kr = qr.next()
                c.op("act", lambda: nc.scalar.activation(out=s_t[:], in_=pqk, func=AF.Square), reads=[kp], writes=[ks])
                c.op("dve", lambda: nc.vector.tensor_tensor(out=q_t[:], in0=pqk, in1=G[:], op=ALU.mult), reads=[kp, ks, kG], writes=[kq])
                c.op("act", lambda: nc.scalar.copy(out=VA[:, i, :, 0:HD],
                                                   in_=p_t[:, NQK * HD:1536].rearrange("p (g d) -> p g d", d=HD)),
                     reads=[kp, kq], writes=[(kVA, i)])
                c.op("act", lambda: nc.scalar.copy(out=VB[:, i, :, HD:2 * HD],
                                                   in_=p_t[:, NQK * HD:1536].rearrange("p (g d) -> p g d", d=HD)),
                     reads=[kp, kq], writes=[(kVA, i, "b")])
                c.op("dve", lambda: nc.vector.tensor_reduce(out=st_t[:, 0, :], in_=s_t[:], axis=AX.X, op=ALU.add),
                     reads=[ks], writes=[kst])
                c.op("act", lambda: nc.scalar.activation(out=st_t[:, 1, :], in_=st_t[:, 0, :], func=AF.Sqrt,
                                                          scale=1.0 / HD, bias=eps[:]), reads=[kst, keps], writes=[kst])
                c.op("dve", lambda: nc.vector.reciprocal(out=st_t[:, 1, :], in_=st_t[:, 1, :]), reads=[kst], writes=[kst])
                rstd_b = st_t[:, 1, :].unsqueeze(2).to_broadcast([128, NQK, HD])
                if not rope:
                    c.op("dve", lambda: nc.vector.tensor_tensor(out=r_t[:], in0=q_t[:], in1=rstd_b, op=ALU.mult),
                         reads=[kq, kst], writes=[(kr, 0), (kr, 1)])
                else:
                    qv = q_t[:].rearrange("p h (f two) -> p h f two", two=2)
                    x0 = qv[:, :, :, 0]
                    x1 = qv[:, :, :, 1]
                    cosb = cs[:, i, 0, :].unsqueeze(1).to_broadcast([128, NQK, 32])
                    sinb = cs[:, i, 1, :].unsqueeze(1).to_broadcast([128, NQK, 32])
                    a_t, ka = tA.next()
                    b_t, kb = tB.next()
                    c_t, kc = tC.next()
                    d_t, kd = tD.next()
                    o_t, ko = ro.next()
                    ov = o_t[:].rearrange("p h (f two) -> p h f two", two=2)
                    c.op("dve", lambda: nc.vector.tensor_tensor(out=a_t[:], in0=x0, in1=cosb, op=ALU.mult), reads=[kq, kcs], writes=[ka])
                    c.op("dve", lambda: nc.vector.tensor_tensor(out=b_t[:], in0=x1, in1=sinb, op=ALU.mult), reads=[kq, kcs], writes=[kb])
                    c.op("dve", lambda: nc.vector.tensor_tensor(out=ov[:, :, :, 0], in0=a_t[:], in1=b_t[:], op=ALU.subtract),
                         reads=[ka, kb], writes=[(ko, 0)])
                    c.op("pool", lambda: nc.gpsimd.tensor_tensor(out=c_t[:], in0=x0, in1=sinb, op=ALU.mult), reads=[kq, kcs], writes=[kc])
                    c.op("pool", lambda: nc.gpsimd.tensor_tensor(out=d_t[:], in0=x1, in1=cosb, op=ALU.mult), reads=[kq, kcs], writes=[kd])
                    c.op("pool", lambda: nc.gpsimd.tensor_tensor(out=ov[:, :, :, 1], in0=c_t[:], in1=d_t[:], op=ALU.add),
                         reads=[kc, kd], writes=[(ko, 1)])
                    c.op("dve", lambda: nc.vector.tensor_tensor(out=r_t[:], in0=o_t[:], in1=rstd_b, op=ALU.mult),
                         reads=[(ko, 0), (ko, 1), kst], writes=[(kr, 0), (kr, 1)])
                    kd_t, kkd = kdr.next()
                    c.op("pool", lambda: nc.gpsimd.tensor_copy(out=kd_t[:], in_=r_t[:, NH:NQK, :].unsqueeze(2).to_broadcast([128, NKV, 2, HD])),
                         reads=[(kr, 0), (kr, 1)], writes=[kkd])
                T['r_t'], T['kr'] = r_t, kr
                if dupk:
                    T['kd_t'], T['kkd'] = kd_t, kkd

            def tr():
                r_t, kr = T['r_t'], T['kr']
                if dupk:
                    kd_t, kkd = T['kd_t'], T['kkd']
                rflat = r_t[:].rearrange("p h d -> p (h d)")
                if dupk:
                    kflat = kd_t[:].rearrange("p g t d -> p (g t d)")
                t_t, kt_ = pT.next()
                for j in range(8):
                    c.op("pe", lambda: nc.tensor.transpose(t_t[:, j, :], rflat[:, j * 128:(j + 1) * 128], ident[:]),
                         reads=[(kr, 0), (kr, 1)], writes=[kt_], signal=(j == 7))
                if QZ is None:
                    c.op("act", lambda: nc.scalar.copy(out=QT[:, 0:8, i * 128:(i + 1) * 128], in_=t_t[:]),
                         reads=[kt_], writes=[(kQT, i, 0)])
                else:
                    nqc = NH // 2
                    qzv = QZ[:].rearrange("p (c two) s -> p c two s", two=2)
                    c.op("act", lambda: nc.scalar.copy(out=qzv[0:64, :, 0, i * 128:(i + 1) * 128], in_=t_t[0:64, 0:nqc, :]),
                         reads=[kt_, "qz0", "qz1"], writes=[(kQT, i, 0)])
                    c.op("dve", lambda: nc.vector.tensor_copy(out=qzv[64:128, :, 1, i * 128:(i + 1) * 128], in_=t_t[64:128, 0:nqc, :]),
                         reads=[kt_, (kQT, i, 0)], writes=[(kQT, i, 2)])
                    if not dupk:
                        c.op("dve", lambda: nc.vector.tensor_copy(out=QT[:, koff:koff + NKV // 2, i * 128:(i + 1) * 128],
                                                                  in_=t_t[:, nqc:nqc + NKV // 2, :]),
                             reads=[kt_, (kQT, i, 2)], writes=[(kQT, i, 3)])
                if dupk:
                    t_t, kt_ = pT.next()
                    for j in range(NKV):
                        c.op("pe", lambda: nc.tensor.transpose(t_t[:, j, :], kflat[:, j * 128:(j + 1) * 128], ident[:]),
                             reads=[kkd], writes=[kt_], signal=(j == NKV - 1))
                    c.op("act", lambda: nc.scalar.copy(out=QT[:, koff:koff + NKV, i * 128:(i + 1) * 128], in_=t_t[:, 0:NKV, :]),
                         reads=[kt_], writes=[(kQT, i, 1)])
            return mm, chain, tr

        tiles = [make_tile(i) for i in range(NT)]
        tiles[0][0]()
        tiles[1][0]()
        tiles[0][1]()
        for i in range(NT):
            if i + 2 < NT:
                tiles[i + 2][0]()
            if i + 1 < NT:
                tiles[i + 1][1]()
            tiles[i][2]()
        c.barrier()


def emit_odd(c, x_in, x_out, g_dram, wqkv, gq, gk, wo, cos_d, sin_d, ident, ones32, name):
    nc = c.nc
    NH, NKV, HD = 16, 4, 64
    NQK = NH + NKV
    with ExitStack() as es:
        E = es.enter_context
        QT = E(nc.sbuf_tensor(f"{name}_QT", [128, NKV, S], BF16))
        kQT = c.key("QT")
        VAB = E(nc.sbuf_tensor(f"{name}_VAB", [128, NT, NKV, 192], BF16))
        VA = VAB[:, :, :, 0:128]
        VB = VAB[:, :, :, 64:192]
        kVA = c.key("VA")
        QZ = E(nc.sbuf_tensor(f"{name}_QZ", [128, NH, S], BF16))
        with ExitStack() as es0:
            xnT = es0.enter_context(nc.sbuf_tensor(f"{name}_xnT", [128, 8, S], BF16))
            kxnT = c.key("xnT")
            Wq = es0.enter_context(nc.sbuf_tensor(f"{name}_Wq", [128, 8, 1536], BF16))
            kWq = c.key("Wq")
            wqv = wqkv.rearrange("(k p) n -> p k n", p=128)
            for k in range(8):
                c.dma("pool", Wq[:, k, :], wqv[:, k, :], writes=[(kWq, k)])
            c.op("pool", lambda: nc.gpsimd.memset(QZ[:, 0:NH // 2, :], 0.0), writes=["qz0"])
            c.op("pool", lambda: nc.gpsimd.memset(QZ[:, NH // 2:NH, :], 0.0), writes=["qz1"])
            with ExitStack() as es1:
                emit_norm_T(c, es1, x_in, g_dram, xnT, kxnT, ident, name)
                c.barrier()
            emit_qkv_proj(c, xnT, kxnT, wqv, gq, gk, cos_d, sin_d, ident,
                          QT, kQT, VA, VB, kVA, NH, NKV, True, name, QZ=QZ, koff=0, Wpre=(Wq, kWq))
        OT = E(nc.sbuf_tensor(f"{name}_OT", [128, 8, S], BF16))
        kOT = c.key("OT")
        Wo = E(nc.sbuf_tensor(f"{name}_Wo", [128, 8, D], BF16))
        kWo = c.key("Wo")
        wov = wo.rearrange("(k p) n -> p k n", p=128)
        for k in range(8):
            c.dma("pool", Wo[:, k, :], wov[:, k, :], writes=[(kWo, k)])
        with ExitStack() as es3:
            ps = Ring(c, es3, f"{name}_ps", [128, 2, 512], F32, 3, psum=True)
            po = Ring(c, es3, f"{name}_po", [128, 512], F32, 1, psum=True)
            pb = Ring(c, es3, f"{name}_pb", [128, 512], F32, 1, psum=True)
            PT = Ring(c, es3, f"{name}_PT", [128, 2, 512], BF16, 3)
            osb = Ring(c, es3, f"{name}_osb", [128, 512], F32, 2)
            rd = Ring(c, es3, f"{name}_rd", [128, 512], F32, 2)
            iters = []
            for h in range(NH):
                for qb in range(4):
                    grp = {}
                    for kt2 in range(8):
                        def mk(h=h, qb=qb, kt2=kt2, grp=grp):
                            g = h // 4
                            hb = (h % 2) * 64
                            stt = {}

                            def qk():
                                stt["s"], stt["ks"] = ps.next()
                                for _ in range(N_DUMMY):
                                    c.op("pe", lambda: nc.tensor.matmul(stt["s"][:, 0, :], lhsT=QT[:, 0, 0:128], rhs=QT[:, 0, 0:512],
                                                                        start=True, stop=True), reads=[], writes=[stt["ks"]], signal=False)
                                for j in range(2):
                                    kt = kt2 * 2 + j
                                    c.op("pe", lambda: nc.tensor.matmul(stt["s"][:, j, :], lhsT=QT[:, g, kt * 128:(kt + 1) * 128],
                                                                        rhs=QZ[:, h, qb * 512:(qb + 1) * 512], start=True, stop=True),
                                         reads=[], writes=[stt["ks"]], signal=(j == 1))

                            def ex():
                                stt["p"], stt["kp"] = PT.next()
                                c.op("act", lambda: nc.scalar.activation(out=stt["p"][:], in_=stt["s"][:], func=AF.Exp),
                                     reads=[stt["ks"]], writes=[stt["kp"]])

                            def pv():
                                if kt2 == 0:
                                    grp["o"], grp["ko"] = po.next()
                                o_t, ko = grp["o"], grp["ko"]
                                for j in range(2):
                                    kt = kt2 * 2 + j
                                    if hb == 0:
                                        c.op("pe", lambda: nc.tensor.matmul(o_t[:, :], lhsT=VA[:, kt, g, :], rhs=stt["p"][:, j, :],
                                                                            start=(kt == 0), stop=(kt == 15)),
                                             reads=[stt["kp"]], writes=[ko], signal=(j == 1))
                                    else:
                                        c.op("pe", lambda: nc.tensor.matmul(o_t[:, :], lhsT=VB[:, kt, g, :], rhs=stt["p"][:, j, :],
                                                                            start=(kt == 0), stop=(kt == 15)),
                                             reads=[stt["kp"]], writes=[ko], signal=(j == 1))

                            it = {"qk": qk, "exp": ex, "pv": pv}
                            if kt2 == 7:
                                def post_factory():
                                    return make_norm_post(c, grp["o"], grp["ko"], hb,
                                                          OT[hb:hb + 64, h // 2, qb * 512:(qb + 1) * 512], (kOT, h, qb), osb, rd, pb, ones32)
                                it["post_factory"] = post_factory
                            return it
                        iters.append(mk())
            warm_pe(c, pb.t[0], pb.k[0], QT[:, 0, :])
            run_pipeline(iters)
            c.barrier()
        with ExitStack() as es4:
            py = Ring(c, es4, f"{name}_py", [128, D], F32, 2, psum=True)
            xr = Ring(c, es4, f"{name}_xr", [128, D], F32, 4)
            yo = Ring(c, es4, f"{name}_yo", [128, D], F32, 3)
            xq = []
            for tt in range(3):
                x_t, kx = xr.next()
                c.dma("sp", x_t[:], x_in[tt * 128:(tt + 1) * 128, :], writes=[kx])
                xq.append((x_t, kx))
            for tt in range(NT):
                y_p, kyp = py.next()
                for k in range(8):
                    for ob in range(2):
                        c.op("pe", lambda: nc.tensor.matmul(y_p[:, ob * 512:(ob + 1) * 512], lhsT=OT[:, k, tt * 128:(tt + 1) * 128],
                                                            rhs=Wo[:, k, ob * 512:(ob + 1) * 512], start=(k == 0), stop=(k == 7)),
                             reads=[(kWo, k)], writes=[kyp], signal=(k == 7 and ob == 1))
                x_t, kx = xq.pop(0)
                y_t, ky = yo.next()
                c.op("dve", lambda: nc.vector.tensor_tensor(out=y_t[:], in0=x_t[:], in1=y_p[:], op=ALU.add),
                     reads=[kx, kyp], writes=[ky])
                if tt + 3 < NT:
                    x_n, kxn = xr.next()
                    c.dma("sp", x_n[:], x_in[(tt + 3) * 128:(tt + 4) * 128, :], writes=[kxn])
                    xq.append((x_n, kxn))
                c.dma("sp", x_out[tt * 128:(tt + 1) * 128, :], y_t[:], reads=[ky])
            c.barrier()


NA_TYPES = [(0, j) for j in range(4)] + [(1, j) for j in range(4)] + [(2, j) for j in range(5)] + \
           [(3, j) for j in range(4)] + [(4, j) for j in range(4)]
NA_TIX = {t: n for n, t in enumerate(NA_TYPES)}
NTYPES = len(NA_TYPES)


def na_cls(i):
    if i == 0:
        return 0, 0, 4
    if i == 1:
        return 1, 0, 4
    if i == 14:
        return 3, 12, 4
    if i == 15:
        return 4, 12, 4
    return 2, i - 2, 5


def emit_even(c, x_in, x_out, P, ident, ones32, name):
    nc = c.nc
    HD = 64
    w_in_v = P["w_in"].rearrange("(k p) n -> p k n", p=128)
    with ExitStack() as es:
        E = es.enter_context
        mt_v = P["mt_scr"].rearrange("k p s -> p k s")
        kMT = c.key("MT")
        DT = E(nc.sbuf_tensor(f"{name}_DT", [128, NT, 32], F32))
        AA = E(nc.sbuf_tensor(f"{name}_AA", [128, NT, 32], F32))
        kDT = c.key("DT")
        kAA = c.key("AA")
        eps = E(nc.sbuf_tensor(f"{name}_epsE", [128, 1], F32))
        keps = c.key("eps")
        c.op("pool", lambda: nc.gpsimd.memset(eps[:], EPS), writes=[keps])
        with ExitStack() as esn:
            if True:
                En = esn.enter_context
                MT = En(nc.sbuf_tensor(f"{name}_MTn", [128, 4, S], BF16))
                QT = En(nc.sbuf_tensor(f"{name}_QT", [128, 4, S], BF16))
                kQT = c.key("QT")
                QZ = En(nc.sbuf_tensor(f"{name}_QZ", [128, 8, S], BF16))
                VAB = En(nc.sbuf_tensor(f"{name}_VAB", [128, NT, 8, 192], BF16))
                VA = VAB[:, :, :, 0:128]
                VB = VAB[:, :, :, 64:192]
                kVA = c.key("VA")
                with ExitStack() as esx:
                    xnT = esx.enter_context(nc.sbuf_tensor(f"{name}_xnT", [128, 8, S], BF16))
                    kxnT = c.key("xnT")
                    Wq = esx.enter_context(nc.sbuf_tensor(f"{name}_Wq", [128, 8, 1536], BF16))
                    kWq = c.key("Wq")
                    for k in range(8):
                        c.dma("pool", Wq[:, k, :], w_in_v[:, k, 0:1536], writes=[(kWq, k)])
                    c.op("pool", lambda: nc.gpsimd.memset(QZ[:, 0:4, :], 0.0), writes=["qz0"])
                    c.op("pool", lambda: nc.gpsimd.memset(QZ[:, 4:8, :], 0.0), writes=["qz1"])
                    with ExitStack() as es1:
                        emit_norm_T(c, es1, x_in, P["mix_norm"], xnT, kxnT, ident, name)
                        c.barrier()
                    emit_qkv_proj(c, xnT, kxnT, w_in_v[:, :, 0:1536], P["na_gq"], P["na_gk"], None, None, ident,
                                  QT, kQT, VA, VB, kVA, 8, 8, False, name + "n", QZ=QZ, koff=0, Wpre=(Wq, kWq))
                BM = En(nc.sbuf_tensor(f"{name}_BM", [128, NTYPES, 8, 128], BF16))
                kBM = c.key("BM")
                with ExitStack() as esb:
                    bg = Ring(c, esb, f"{name}_bg", [128, 8, 128], F32, 3)
                    mk = Ring(c, esb, f"{name}_mk", [128, 128], F32, 3)
                    for t in range(NTYPES):
                        b_t, kb = bg.next()
                        m_t, km = mk.next()
                        c.dma("sp" if t % 2 == 0 else "pool", b_t[:], P["biasg"][t], writes=[kb])
                        c.dma("sp", m_t[:], P["namask"][t], writes=[km])
                        c.op("dve", lambda: nc.vector.tensor_tensor(out=BM[:, t, :, :], in0=b_t[:],
                                                                    in1=m_t[:].unsqueeze(1).to_broadcast([128, 8, 128]), op=ALU.add),
                             reads=[kb, km], writes=[kBM])
                    c.barrier()
                with ExitStack() as es3:
                    ps = Ring(c, es3, f"{name}_ps", [128, 8, 128], F32, 3, psum=True)
                    po = Ring(c, es3, f"{name}_po", [128, 512], F32, 1, psum=True)
                    pb = Ring(c, es3, f"{name}_pb", [128, 512], F32, 1, psum=True)
                    PT = Ring(c, es3, f"{name}_PT", [128, 5, 128], BF16, 3)
                    osb = Ring(c, es3, f"{name}_osb", [128, 512], F32, 2)
                    rd = Ring(c, es3, f"{name}_rd", [128, 512], F32, 2)
                    iters = []
                    for h in range(8):
                        for ig in range(4):
                            grp = {}
                            for ii in range(4):
                                def mk(h=h, ig=ig, ii=ii, grp=grp):
                                    hb = (h % 2) * 64
                                    qc = h // 2
                                    kc = 4 + h // 2
                                    i = ig * 4 + ii
                                    cls, kp0, ntl = na_cls(i)
                                    stt = {}

                                    def qk():
                                        stt["s"], stt["ks"] = ps.next()
                                        t0 = NA_TIX[(cls, 0)]
                                        c.op("pe", lambda: nc.tensor.matmul(stt["s"][:, 0:4, :], lhsT=ident[:], rhs=BM[:, t0:t0 + 4, h, :],
                                                                            start=True, stop=False),
                                             reads=[], writes=[stt["ks"]], signal=False)
                                        for j in range(4):
                                            kp = kp0 + j
                                            c.op("pe", lambda: nc.tensor.matmul(stt["s"][:, j, :], lhsT=QT[:, h // 2, kp * 128:(kp + 1) * 128],
                                                                                rhs=QZ[:, h, i * 128:(i + 1) * 128], start=False, stop=(j == 3)),
                                                 reads=[], writes=[stt["ks"]], signal=(j == 3 and ntl == 4))
                                        if ntl == 5:
                                            kp = kp0 + 4
                                            c.op("pe", lambda: nc.tensor.matmul(stt["s"][:, 4, :], lhsT=ident[:], rhs=BM[:, t0 + 4, h, :],
                                                                                start=True, stop=False),
                                                 reads=[], writes=[stt["ks"]], signal=False)
                                            c.op("pe", lambda: nc.tensor.matmul(stt["s"][:, 4, :], lhsT=QT[:, h // 2, kp * 128:(kp + 1) * 128],
                                                                                rhs=QZ[:, h, i * 128:(i + 1) * 128], start=False, stop=True),
                                                 reads=[], writes=[stt["ks"]], signal=True)

                                    def ex():
                                        stt["p"], stt["kp"] = PT.next()
                                        c.op("act", lambda: nc.scalar.activation(out=stt["p"][:, 0:ntl, :], in_=stt["s"][:, 0:ntl, :], func=AF.Exp),
                                             reads=[stt["ks"]], writes=[stt["kp"]])

                                    def pv():
                                        if ii == 0:
                                            grp["o"], grp["ko"] = po.next()
                                        o_t, ko = grp["o"], grp["ko"]
                                        for j in range(ntl):
                                            kp = kp0 + j
                                            if hb == 0:
                                                c.op("pe", lambda: nc.tensor.matmul(o_t[:, ii * 128:(ii + 1) * 128], lhsT=VA[:, kp, h, :],
                                                                                    rhs=stt["p"][:, j, :], start=(j == 0), stop=(j == ntl - 1)),
                                                     reads=[stt["kp"]], writes=[ko], signal=(j == ntl - 1))
                                            else:
                                                c.op("pe", lambda: nc.tensor.matmul(o_t[:, ii * 128:(ii + 1) * 128], lhsT=VB[:, kp, h, :],
                                                                                    rhs=stt["p"][:, j, :], start=(j == 0), stop=(j == ntl - 1)),
                                                     reads=[stt["kp"]], writes=[ko], signal=(j == ntl - 1))

                                    it = {"qk": qk, "exp": ex, "pv": pv}
                                    if ii == 3:
                                        def post_factory():
                                            return make_norm_post(c, grp["o"], grp["ko"], hb,
                                                                  MT[hb:hb + 64, h // 2, ig * 512:(ig + 1) * 512], (kMT, h, ig), osb, rd, pb, ones32,
                                                                  nrows=128)
                                        it["post_factory"] = post_factory
                                    return it
                                iters.append(mk())
                    warm_pe(c, pb.t[0], pb.k[0], QT[:, 0, :])
                    run_pipeline(iters)
                    c.barrier()
                    c.dma("sp", mt_v[:, 0:4, :], MT[:], writes=[("mt_d", "na")])
                    c.barrier()
        XT = E(nc.sbuf_tensor(f"{name}_XT", [128, NT, 1024], BF16))
        kXT = c.key("XT")
        BT = E(nc.sbuf_tensor(f"{name}_BT", [128, 4, S], BF16))
        CT = E(nc.sbuf_tensor(f"{name}_CT", [128, 4, S], BF16))
        kBT = c.key("BT")
        kCT = c.key("CT")
        Btok = E(nc.sbuf_tensor(f"{name}_Btok", [128, NT, 512], BF16))
        kBtok = c.key("Btok")
        with ExitStack() as esx:
            xnT = esx.enter_context(nc.sbuf_tensor(f"{name}_xnTb", [128, 8, S], BF16))
            kxnT = c.key("xnT")
            Wz = esx.enter_context(nc.sbuf_tensor(f"{name}_Wz", [128, 8, 1056], BF16))
            kWz = c.key("Wz")
            for k in range(8):
                c.dma("pool", Wz[:, k, 0:1024], w_in_v[:, k, 1536:2560], writes=[(kWz, k)])
                c.dma("pool", Wz[:, k, 1024:1056], w_in_v[:, k, 4608:4640], writes=[(kWz, k, "d")])
            with ExitStack() as es1:
                emit_norm_T(c, es1, x_in, P["mix_norm"], xnT, kxnT, ident, name + "b")
                c.barrier()
            xn_all = [(kxnT, i) for i in range(NT)]
            with ExitStack() as esz:
                Ez = esz.enter_context
                dtb = Ez(nc.sbuf_tensor(f"{name}_dtb", [128, 32], F32))
                alb = Ez(nc.sbuf_tensor(f"{name}_alb", [128, 32], F32))
                kdtb = c.key("dtb")
                kalb = c.key("alb")
                c.dma("sp", dtb[:], bcast_row(P["dt_bias"], 32), writes=[kdtb])
                c.dma("sp", alb[:], bcast_row(P["A_log"], 32), writes=[kalb])
                pz = Ring(c, esz, f"{name}_pz", [128, 1024], F32, 2, psum=True)
                pd = Ring(c, esz, f"{name}_pd", [128, 512], F32, 2, psum=True)
                zb = Ring(c, esz, f"{name}_zb", [128, 1024], BF16, 3)
                for i in range(NT):
                    z_p, kzp = pz.next()
                    d_p, kdp = pd.next()
                    for cb in range(2):
                        for k in range(8):
                            c.op("pe", lambda: nc.tensor.matmul(z_p[:, cb * 512:(cb + 1) * 512], lhsT=xnT[:, k, i * 128:(i + 1) * 128],
                                                                rhs=Wz[:, k, cb * 512:(cb + 1) * 512], start=(k == 0), stop=(k == 7)),
                                 reads=[(kWz, k), (kxnT, i)], writes=[kzp], signal=(k == 7 and cb == 1))
                    for k in range(8):
                        c.op("pe", lambda: nc.tensor.matmul(d_p[:, 0:32], lhsT=xnT[:, k, i * 128:(i + 1) * 128],
                                                            rhs=Wz[:, k, 1024:1056], start=(k == 0), stop=(k == 7)),
                             reads=[(kWz, k, "d"), (kxnT, i)], writes=[kdp], signal=(k == 7))
                    z_t, kz = zb.next()
                    c.op("act", lambda: nc.scalar.activation(out=z_t[:], in_=z_p[:], func=AF.Silu), reads=[kzp], writes=[kz])
                    c.dma("sp", P["zs"][i * 128:(i + 1) * 128, :], z_t[:], reads=[kz], writes=[("zs_d", i)])
                    c.op("dve", lambda: nc.vector.tensor_tensor(out=DT[:, i, :], in0=d_p[:, 0:32], in1=dtb[:], op=ALU.add),
                         reads=[kdp, kdtb], writes=[kDT])
                c.op("act", lambda: nc.scalar.activation(out=DT[:], in_=DT[:], func=AF.Exp), reads=[kDT], writes=[kDT])
                c.op("act", lambda: nc.scalar.activation(out=DT[:], in_=DT[:], func=AF.Ln, bias=1.0), reads=[kDT], writes=[kDT])
                c.op("act", lambda: nc.scalar.activation(out=alb[:], in_=alb[:], func=AF.Exp), reads=[kalb], writes=[kalb])
                c.op("dve", lambda: nc.vector.tensor_scalar(out=alb[:], in0=alb[:], scalar1=-1.0, scalar2=None, op0=ALU.mult),
                     reads=[kalb], writes=[kalb])
                c.op("dve", lambda: nc.vector.tensor_tensor(out=AA[:], in0=DT[:], in1=alb[:].unsqueeze(1).to_broadcast([128, NT, 32]),
                                                            op=ALU.mult), reads=[kDT, kalb], writes=[kAA])
                c.barrier()
            with ExitStack() as esc:
                Ec = esc.enter_context
                cw = Ec(nc.sbuf_tensor(f"{name}_cw", [128, 16, 4], F32))
                cbias = Ec(nc.sbuf_tensor(f"{name}_cb", [128, 16], F32))
                kcw = c.key("cw")
                for j in range(4):
                    c.dma("sp", cw[:, :, j], P["conv_w"][j].rearrange("(c p) -> p c", p=128), writes=[kcw], allow_slow_non_contiguous=True)
                c.dma("sp", cbias[:], P["conv_b"].rearrange("(c p) -> p c", p=128), writes=[kcw], allow_slow_non_contiguous=True)
                wx = Ring(c, esc, f"{name}_wx", [128, 8, 128], BF16, 3)
                px = Ring(c, esc, f"{name}_px", [128, 1024], F32, 2, psum=True)
                pTr = Ring(c, esc, f"{name}_pTr", [128, 8, 128], BF16, 2, psum=True)
                Rr = Ring(c, esc, f"{name}_R", [128, S + 3], F32, 2)
                acc = Ring(c, esc, f"{name}_acc", [128, S], F32, 2)
                xo = Ring(c, esc, f"{name}_xo", [128, S], BF16, 2)
                for t_, k_ in zip(Rr.t, Rr.k):
                    c.op("pool", lambda: nc.gpsimd.memset(t_[:, 0:2], 0.0), writes=[k_])
                    c.op("pool", lambda: nc.gpsimd.memset(t_[:, S + 2:S + 3], 0.0), writes=[k_])
                def make_ch(ch):
                    T = {}

                    def front():
                            w_t, kw = wx.next()
                            c.dma("pool", w_t[:], w_in_v[:, :, 2560 + ch * 128:2560 + (ch + 1) * 128], writes=[kw])
                            R_t, kR = Rr.next()
                            for half in range(2):
                                p_t, kp = px.next()
                                for k in range(8):
                                    for nb in range(2):
                                        t0 = half * 1024 + nb * 512
                                        c.op("pe", lambda: nc.tensor.matmul(p_t[:, nb * 512:(nb + 1) * 512], lhsT=w_t[:, k, :],
                                                                            rhs=xnT[:, k, t0:t0 + 512], start=(k == 0), stop=(k == 7)),
                                             reads=[kw] + xn_all, writes=[kp], signal=(k == 7 and nb == 1))
                                c.op("act", lambda: nc.scalar.copy(out=R_t[:, 2 + half * 1024:2 + (half + 1) * 1024], in_=p_t[:]),
                                     reads=[kp], writes=[kR])
                            T['R'] = (R_t, kR)

                    def back1():
                            R_t, kR = T['R']
                            a_t, ka = acc.next()
                            c.op("act", lambda: nc.scalar.activation(out=a_t[:], in_=R_t[:, 0:S], func=AF.Identity,
                                                                      scale=cw[:, ch, 0:1], bias=cbias[:, ch:ch + 1]),
                                 reads=[kR, kcw], writes=[ka])
                            c.op("dve", lambda: nc.vector.scalar_tensor_tensor(out=a_t[:], in0=R_t[:, 1:S + 1], scalar=cw[:, ch, 1:2], in1=a_t[:],
                                                                               op0=ALU.mult, op1=ALU.add), reads=[kR, kcw, ka], writes=[ka])
                            c.op("dve", lambda: nc.vector.scalar_tensor_tensor(out=a_t[:], in0=R_t[:, 2:S + 2], scalar=cw[:, ch, 2:3], in1=a_t[:],
                                                                                op0=ALU.mult, op1=ALU.add), reads=[kR, kcw, ka], writes=[ka])
                            c.op("dve", lambda: nc.vector.scalar_tensor_tensor(out=a_t[:], in0=R_t[:, 3:S + 3], scalar=cw[:, ch, 3:4], in1=a_t[:],
                                                                               op0=ALU.mult, op1=ALU.add), reads=[kR, kcw, ka], writes=[ka])
                            if ch < 8:
                                o_t, ko = xo.next()
                                dst = o_t[:]
                                wk = [ko]
                            elif ch < 12:
                                dst = BT[:, ch - 8, :]
                                ko = (kBT, ch - 8)
                                wk = [ko]
                            else:
                                dst = CT[:, ch - 12, :]
                                wk = [(kCT, ch - 12)]
                            T['dst'] = (dst, wk, a_t, ka)

                    def back2():
                            dst, wk, a_t, ka = T['dst']
                            c.op("act", lambda: nc.scalar.activation(out=dst, in_=a_t[:], func=AF.Silu), reads=[ka], writes=wk)
                            if ch < 12:
                                for half in range(2):
                                    t_t, kt_ = pTr.next()
                                    for tt in range(8):
                                        t0 = (half * 8 + tt) * 128
                                        c.op("pe", lambda: nc.tensor.transpose(t_t[:, tt, :], dst[:, t0:t0 + 128], ident[:]),
                                             reads=wk, writes=[kt_], signal=(tt == 7))
                                    if ch < 8:
                                        c.op("dve", lambda: nc.vector.tensor_copy(out=XT[:, half * 8:(half + 1) * 8, ch * 128:(ch + 1) * 128], in_=t_t[:]),
                                             reads=[kt_], writes=[(kXT, ch, half)])
                                    else:
                                        c.op("dve", lambda: nc.vector.tensor_copy(out=Btok[:, half * 8:(half + 1) * 8, (ch - 8) * 128:(ch - 7) * 128],
                                                                                  in_=t_t[:]), reads=[kt_], writes=[(kBtok, ch, half)])
                    return front, back1, back2

                chs = [make_ch(ch) for ch in range(16)]
                chs[0][0]()
                chs[1][0]()
                chs[0][1]()
                for ch in range(16):
                    if ch + 2 < 16:
                        chs[ch + 2][0]()
                    if ch + 1 < 16:
                        chs[ch + 1][1]()
                    chs[ch][2]()
                c.barrier()
        emit_ssd(c, P, XT, BT, CT, Btok, DT, AA, eps, ident, ones32, name)
        with ExitStack() as es4:
            MT = es4.enter_context(nc.sbuf_tensor(f"{name}_MT", [128, 12, S], BF16))
            for k in range(12):
                c.dma("sp", MT[:, k, :], mt_v[:, k, :], writes=[(kMT, "ld", k)])
            Wo = es4.enter_context(nc.sbuf_tensor(f"{name}_Wo", [128, 12, D], BF16))
            kWo = c.key("Wo")
            wov = P["w_out"].rearrange("(k p) n -> p k n", p=128)
            for k in range(12):
                c.dma("pool", Wo[:, k, :], wov[:, k, :], writes=[(kWo, k)])
            py = Ring(c, es4, f"{name}_py", [128, D], F32, 2, psum=True)
            xr = Ring(c, es4, f"{name}_xr", [128, D], F32, 4)
            yo = Ring(c, es4, f"{name}_yo", [128, D], F32, 3)
            xq = []
            for tt in range(3):
                x_t, kx = xr.next()
                c.dma("sp", x_t[:], x_in[tt * 128:(tt + 1) * 128, :], writes=[kx])
                xq.append((x_t, kx))
            for tt in range(NT):
                y_p, kyp = py.next()
                for k in range(12):
                    for ob in range(2):
                        c.op("pe", lambda: nc.tensor.matmul(y_p[:, ob * 512:(ob + 1) * 512], lhsT=MT[:, k, tt * 128:(tt + 1) * 128],
                                                            rhs=Wo[:, k, ob * 512:(ob + 1) * 512], start=(k == 0), stop=(k == 11)),
                             reads=[(kWo, k), (kMT, "ld", k)], writes=[kyp], signal=(k == 11 and ob == 1))
                x_t, kx = xq.pop(0)
                y_t, ky = yo.next()
                c.op("dve", lambda: nc.vector.tensor_tensor(out=y_t[:], in0=x_t[:], in1=y_p[:], op=ALU.add),
                     reads=[kx, kyp], writes=[ky])
                if tt + 3 < NT:
                    x_n, kxn = xr.next()
                    c.dma("sp", x_n[:], x_in[(tt + 3) * 128:(tt + 4) * 128, :], writes=[kxn])
                    xq.append((x_n, kxn))
                c.dma("sp", x_out[tt * 128:(tt + 1) * 128, :], y_t[:], reads=[ky])
            c.barrier()


def emit_ssd(c, P, XT, BT, CT, Btok, DT, AA, eps, ident, ones32, name):
    nc = c.nc
    NHD = 16
    mt_v = P["mt_scr"].rearrange("k p s -> p k s")
    with ExitStack() as es:
        E = es.enter_context
        U = E(nc.sbuf_tensor(f"{name}_U", [128, 128], F32))
        L = E(nc.sbuf_tensor(f"{name}_L", [128, 128], F32))
        NMf = E(nc.sbuf_tensor(f"{name}_NMf", [128, 128], F32))
        NMb = E(nc.sbuf_tensor(f"{name}_NMb", [128, 128], F32))
        Db = E(nc.sbuf_tensor(f"{name}_Db", [128, NHD], F32))
        gnb = E(nc.sbuf_tensor(f"{name}_gnb", [128, 1024], F32))
        kconst = c.key("ssdconst")
        c.dma("sp", U[:], P["tri_u"], writes=[kconst])
        c.dma("sp", L[:], P["tri_l"], writes=[kconst])
        c.dma("sp", NMf[:], P["negm_f"], writes=[kconst])
        c.dma("sp", NMb[:], P["negm_b"], writes=[kconst])
        c.dma("sp", Db[:], bcast_row(P["ssd_D"], NHD), writes=[kconst])
        c.dma("sp", gnb[:], bcast_row(P["out_norm"], 1024), writes=[kconst])
        Hbst = Ring(c, es, f"{name}_Hbst", [128, 1024], BF16, 2)
        Hbld = Ring(c, es, f"{name}_Hbld", [128, 1024], BF16, 2)
        kHb = c.key("Hball")
        Hf32 = E(nc.sbuf_tensor(f"{name}_Hf32", [128, 1024], F32))
        Hb32 = E(nc.sbuf_tensor(f"{name}_Hb32", [128, 1024], F32))
        kHf = c.key("Hf32")
        kHb32 = c.key("Hb32")
        c.op("pool", lambda: nc.gpsimd.memset(Hf32[:], 0.0), writes=[kHf])
        c.op("pool", lambda: nc.gpsimd.memset(Hb32[:], 0.0), writes=[kHb32])
        Hfb = Ring(c, es, f"{name}_Hfb", [128, 1024], BF16, 2)
        pR = Ring(c, es, f"{name}_pR", [128, 4, 128], F32, 2, psum=True)
        pcb = Ring(c, es, f"{name}_pcb", [128, 4, 128], F32, 1, psum=True)
        py = Ring(c, es, f"{name}_pyd", [128, 1024], F32, 1, psum=True)
        pY2 = Ring(c, es, f"{name}_pY2", [128, 1024], F32, 1, psum=True)
        pTn = Ring(c, es, f"{name}_pTn", [128, 8, 128], BF16, 1, psum=True)
        STr = Ring(c, es, f"{name}_ST", [128, 5, 32], F32, 4)
        wsm = Ring(c, es, f"{name}_wsm", [128, 16], F32, 3)
        xcf = Ring(c, es, f"{name}_xcf", [128, 1024], BF16, 2)
        xcb = Ring(c, es, f"{name}_xcb", [128, 1024], BF16, 2)
        xcd = Ring(c, es, f"{name}_xcd", [128, 1024], BF16, 3)
        cbT = Ring(c, es, f"{name}_cbT", [128, 4, 128], F32, 2)
        t1r = Ring(c, es, f"{name}_t1", [128, 4, 128], F32, 3)
        t2r = Ring(c, es, f"{name}_t2", [128, 4, 128], F32, 3)
        Mr = Ring(c, es, f"{name}_M", [128, 4, 128], BF16, 18)
        yar = Ring(c, es, f"{name}_ya", [128, 1024], F32, 2)
        ybr = Ring(c, es, f"{name}_yb", [128, 1024], F32, 2)
        ydr = Ring(c, es, f"{name}_yd", [128, 1024], F32, 2)
        zsr = Ring(c, es, f"{name}_zs", [128, 1024], BF16, 2)
        hhr = Ring(c, es, f"{name}_hh", [128, 1024], F32, 2)
        sqr = Ring(c, es, f"{name}_sqj", [128, 1024], BF16, 1)
        s4r = Ring(c, es, f"{name}_s4", [128, 2, 4], F32, 2)
        hbr = Ring(c, es, f"{name}_hb16", [128, 1024], BF16, 2)
        mst = Ring(c, es, f"{name}_mst", [128, 8, 128], BF16, 2)
        c.barrier()

        def b16(ap2d, n=NHD, d=64):
            return ap2d.unsqueeze(2).to_broadcast([128, n, d])

        def v3(ap2d, d=64):
            return ap2d.rearrange("p (h d) -> p h d", d=d)

        def chunk_stats(ci):
            p_t, kp = pR.next()
            c.op("pe", lambda: nc.tensor.matmul(p_t[:, 0, 0:16], lhsT=U[:], rhs=AA[:, ci, 0:16], start=True, stop=True),
                 reads=[], writes=[kp], signal=False)
            c.op("pe", lambda: nc.tensor.matmul(p_t[:, 0, 16:32], lhsT=L[:], rhs=AA[:, ci, 16:32], start=True, stop=True),
                 reads=[], writes=[kp], signal=False)
            c.op("pe", lambda: nc.tensor.matmul(p_t[:, 0, 32:64], lhsT=ones32[:], rhs=AA[:, ci, 0:32], start=True, stop=True),
                 reads=[], writes=[kp], signal=True)
            st, kst = STr.next()
            c.op("dve", lambda: nc.vector.tensor_copy(out=st[:, 0, :], in_=p_t[:, 0, 0:32]), reads=[kp], writes=[kst])
            c.op("dve", lambda: nc.vector.tensor_tensor(out=st[:, 4, :], in0=p_t[:, 0, 32:64], in1=st[:, 0, :], op=ALU.subtract),
                 reads=[kp, kst], writes=[kst])
            c.op("act", lambda: nc.scalar.activation(out=st[:, 1, :], in_=st[:, 0, :], func=AF.Exp), reads=[kst], writes=[kst])
            c.op("act", lambda: nc.scalar.activation(out=st[:, 2, :], in_=st[:, 4, :], func=AF.Exp), reads=[kst], writes=[kst])
            c.op("act", lambda: nc.scalar.activation(out=st[:, 3, :], in_=p_t[:, 0, 32:64], func=AF.Exp), reads=[kp, kst], writes=[kst])
            return st, kst

        def states_into(ps_t, kps, ci, x_t, kx):
            for g in range(4):
                c.op("pe", lambda: nc.tensor.matmul(ps_t[:, g * 256:(g + 1) * 256], lhsT=Btok[:, ci, g * 128:(g + 1) * 128],
                                                    rhs=x_t[:, g * 256:(g + 1) * 256], start=True, stop=True),
                     reads=[kx], writes=[kps], signal=(g == 3))

        pre_ps = [(pY2.t[0], pY2.k[0]), (py.t[0], py.k[0])]
        order = list(range(NT - 1, -1, -1))

        def pre_front(n):
            ci = order[n]
            st, kst = chunk_stats(ci)
            w_t, kw = wsm.next()
            c.op("dve", lambda: nc.vector.tensor_tensor(out=w_t[:], in0=DT[:, ci, 16:32], in1=st[:, 2, 16:32], op=ALU.mult),
                 reads=[kst], writes=[kw])
            x_t, kx = xcd.next()
            c.op("pool", lambda: nc.gpsimd.tensor_tensor(out=v3(x_t[:]), in0=v3(XT[:, ci, :]), in1=b16(w_t[:]), op=ALU.mult),
                 reads=[kw], writes=[kx])
            ps_t, kps = pre_ps[n % 2]
            states_into(ps_t, kps, ci, x_t, kx)
            return st, kst, ps_t, kps

        fr = pre_front(0)
        for n in range(NT):
            ci = order[n]
            nxt = pre_front(n + 1) if n + 1 < NT else None
            st, kst, ps_t, kps = fr
            hs_t, khs = Hbst.next()
            c.op("act", lambda: nc.scalar.copy(out=hs_t[:], in_=Hb32[:]), reads=[kHb32], writes=[khs])
            c.dma("sp", P["hb_scr"][ci], hs_t[:], reads=[khs], writes=[(kHb, ci)])
            c.op("pool", lambda: nc.gpsimd.tensor_tensor(out=v3(Hb32[:]), in0=v3(Hb32[:]), in1=b16(st[:, 3, 16:32]), op=ALU.mult),
                 reads=[kHb32, kst], writes=[kHb32])
            c.op("dve", lambda: nc.vector.tensor_tensor(out=Hb32[:], in0=Hb32[:], in1=ps_t[:], op=ALU.add),
                 reads=[kHb32, kps], writes=[kHb32])
            fr = nxt

        def stage_a(ci):
            T = {"ci": ci}
            tok = slice(ci * 128, (ci + 1) * 128)
            st, kst = chunk_stats(ci)
            T["st"], T["kst"] = st, kst
            xf_t, kxf = xcf.next()
            xb_t, kxb = xcb.next()
            xd_t, kxd = xcd.next()
            c.op("dve", lambda: nc.vector.tensor_tensor(out=v3(xf_t[:]), in0=v3(XT[:, ci, :]), in1=b16(DT[:, ci, 0:16]), op=ALU.mult),
                 reads=[], writes=[kxf])
            c.op("pool", lambda: nc.gpsimd.tensor_tensor(out=v3(xb_t[:]), in0=v3(XT[:, ci, :]), in1=b16(DT[:, ci, 16:32]), op=ALU.mult),
                 reads=[], writes=[kxb])
            c.op("pool", lambda: nc.gpsimd.tensor_tensor(out=v3(xd_t[:]), in0=v3(xf_t[:]), in1=b16(st[:, 2, 0:16]), op=ALU.mult),
                 reads=[kxf, kst], writes=[kxd])
            T["xf"], T["xb"], T["xd"] = (xf_t, kxf), (xb_t, kxb), (xd_t, kxd)
            yd_t, kyd = ydr.next()
            c.op("pool", lambda: nc.gpsimd.tensor_tensor(out=v3(yd_t[:]), in0=v3(XT[:, ci, :]), in1=b16(Db[:]), op=ALU.mult),
                 reads=[], writes=[kyd])
            T["yd"] = (yd_t, kyd)
            cb_p, kcbp = pcb.next()
            for g in range(4):
                c.op("pe", lambda: nc.tensor.matmul(cb_p[:, g, :], lhsT=BT[:, g, tok], rhs=CT[:, g, tok], start=True, stop=True),
                     reads=[], writes=[kcbp], signal=(g == 3))
            cb_t, kcb = cbT.next()
            c.op("act", lambda: nc.scalar.copy(out=cb_t[:], in_=cb_p[:]), reads=[kcbp], writes=[kcb])
            T["M"] = []
            for g in range(4):
                ms = []
                for d in range(2):
                    tri = U if d == 0 else L
                    NM = NMf if d == 0 else NMb
                    r_p, krp = pR.next()
                    for j in range(4):
                        col = d * 16 + g * 4 + j
                        c.op("pe", lambda: nc.tensor.matmul(r_p[:, j, :], lhsT=AA[:, ci, col:col + 1].to_broadcast([128, 128]),
                                                            rhs=tri[:], start=True, stop=True),
                             reads=[], writes=[krp], signal=(j == 3))
                    c0 = d * 16 + g * 4
                    a_t, ka = t1r.next()
                    for j in range(4):
                        c.op("dve", lambda: nc.vector.scalar_tensor_tensor(out=a_t[:, j, :], in0=r_p[:, j, :], scalar=st[:, 0, c0 + j:c0 + j + 1],
                                                                           in1=NM[:], op0=ALU.subtract, op1=ALU.add),
                             reads=[krp, kst], writes=[ka])
                    e_t, ke = t2r.next()
                    c.op("act", lambda: nc.scalar.activation(out=e_t[:], in_=a_t[:], func=AF.Exp), reads=[ka], writes=[ke])
                    m_t, km = Mr.next()
                    c.op("pool", lambda: nc.gpsimd.tensor_tensor(out=m_t[:], in0=e_t[:], in1=cb_t[:, g, :].unsqueeze(1).to_broadcast([128, 4, 128]),
                                                                 op=ALU.mult), reads=[ke, kcb], writes=[km])
                    ms.append((m_t, km))
                T["M"].append(ms)
            return T

        def stage_b(T, hf_t, khf):
            ci = T["ci"]
            tok = slice(ci * 128, (ci + 1) * 128)
            st, kst = T["st"], T["kst"]
            (xf_t, kxf), (xb_t, kxb), (xd_t, kxd) = T["xf"], T["xb"], T["xd"]
            yd_t, kyd = T["yd"]
            y_p, kyp = py.next()
            for g in range(4):
                ms = T["M"][g]
                for j in range(4):
                    h = g * 4 + j
                    c.op("pe", lambda: nc.tensor.matmul(y_p[:, h * 64:(h + 1) * 64], lhsT=ms[0][0][:, j, :], rhs=xf_t[:, h * 64:(h + 1) * 64],
                                                        start=True, stop=False),
                         reads=[ms[0][1], kxf], writes=[kyp], signal=False)
                    c.op("pe", lambda: nc.tensor.matmul(y_p[:, h * 64:(h + 1) * 64], lhsT=ms[1][0][:, j, :], rhs=xb_t[:, h * 64:(h + 1) * 64],
                                                        start=False, stop=True),
                         reads=[ms[1][1], kxb], writes=[kyp], signal=(j == 3))
            o_p, kop = pY2.next()
            for g in range(4):
                c.op("pe", lambda: nc.tensor.matmul(o_p[:, g * 256:(g + 1) * 256], lhsT=CT[:, g, tok], rhs=hf_t[:, g * 256:(g + 1) * 256],
                                                    start=True, stop=True), reads=[khf], writes=[kop], signal=(g == 3))
            ya_t, kya = yar.next()
            c.op("dve", lambda: nc.vector.tensor_tensor(out=v3(ya_t[:]), in0=v3(o_p[:]), in1=b16(st[:, 1, 0:16]), op=ALU.mult),
                 reads=[kop, kst], writes=[kya])
            o_p, kop = pY2.next()
            hl_t, khl = Hbld.next()
            c.dma("sp", hl_t[:], P["hb_scr"][ci], reads=[(kHb, ci)], writes=[khl])
            for g in range(4):
                c.op("pe", lambda: nc.tensor.matmul(o_p[:, g * 256:(g + 1) * 256], lhsT=CT[:, g, tok], rhs=hl_t[:, g * 256:(g + 1) * 256],
                                                    start=True, stop=True), reads=[khl], writes=[kop], signal=(g == 3))
            yb_t, kyb = ybr.next()
            c.op("dve", lambda: nc.vector.tensor_tensor(out=v3(yb_t[:]), in0=v3(o_p[:]), in1=b16(st[:, 1, 16:32]), op=ALU.mult),
                 reads=[kop, kst], writes=[kyb])
            c.op("pool", lambda: nc.gpsimd.tensor_tensor(out=yb_t[:], in0=yb_t[:], in1=yd_t[:], op=ALU.add), reads=[kyb, kyd], writes=[kyb])
            c.op("dve", lambda: nc.vector.tensor_tensor(out=ya_t[:], in0=ya_t[:], in1=yb_t[:], op=ALU.add), reads=[kya, kyb], writes=[kya])
            s_p, ksp = pY2.next()
            states_into(s_p, ksp, ci, xd_t, kxd)
            c.op("pool", lambda: nc.gpsimd.tensor_tensor(out=v3(Hf32[:]), in0=v3(Hf32[:]), in1=b16(st[:, 3, 0:16]), op=ALU.mult),
                 reads=[kHf, kst], writes=[kHf])
            c.op("dve", lambda: nc.vector.tensor_tensor(out=Hf32[:], in0=Hf32[:], in1=s_p[:], op=ALU.add), reads=[kHf, ksp], writes=[kHf])
            hf_n, khf_n = Hfb.next()
            c.op("act", lambda: nc.scalar.copy(out=hf_n[:], in_=Hf32[:]), reads=[kHf], writes=[khf_n])
            z_t, kz = zsr.next()
            c.dma("sp", z_t[:], P["zs"][ci * 128:(ci + 1) * 128, :], reads=[("zs_d", ci)], writes=[kz])
            hh_t, khh = hhr.next()
            c.op("dve", lambda: nc.vector.tensor_tensor(out=hh_t[:], in0=y_p[:], in1=ya_t[:], op=ALU.add), reads=[kyp, kya], writes=[khh])
            T["hh"] = (hh_t, khh, z_t, kz)
            return hf_n, khf_n

        def stage_b2(T):
            ci = T["ci"]
            hh_t, khh, z_t, kz = T["hh"]
            c.op("pool", lambda: nc.gpsimd.tensor_tensor(out=hh_t[:], in0=hh_t[:], in1=z_t[:], op=ALU.mult), reads=[khh, kz], writes=[khh])
            sq_t, ksq = sqr.next()
            s4, ks4 = s4r.next()
            for gg in range(4):
                c.op("act", lambda: nc.scalar.activation(out=sq_t[:, gg * 256:(gg + 1) * 256], in_=hh_t[:, gg * 256:(gg + 1) * 256],
                                                          func=AF.Square, accum_out=s4[:, 0, gg:gg + 1]),
                     reads=[khh], writes=[ksq, ks4])
            c.op("act", lambda: nc.scalar.activation(out=s4[:, 1, :], in_=s4[:, 0, :], func=AF.Sqrt, scale=1.0 / 256, bias=eps[:]),
                 reads=[ks4], writes=[ks4])
            c.op("dve", lambda: nc.vector.reciprocal(out=s4[:, 1, :], in_=s4[:, 1, :]), reads=[ks4], writes=[ks4])
            hb_t, khb = hbr.next()
            for gg in range(4):
                c.op("dve", lambda: nc.vector.scalar_tensor_tensor(out=hb_t[:, gg * 256:(gg + 1) * 256], in0=hh_t[:, gg * 256:(gg + 1) * 256],
                                                                   scalar=s4[:, 1, gg:gg + 1], in1=gnb[:, gg * 256:(gg + 1) * 256],
                                                                   op0=ALU.mult, op1=ALU.mult),
                     reads=[khh, ks4], writes=[khb])
            T["hb"] = (hb_t, khb)

        def stage_b3(T):
            ci = T["ci"]
            hb_t, khb = T["hb"]
            t_t, kt_ = pTn.next()
            for k in range(8):
                c.op("pe", lambda: nc.tensor.transpose(t_t[:, k, :], hb_t[:, k * 128:(k + 1) * 128], ident[:]),
                     reads=[khb], writes=[kt_], signal=(k == 7))
            m_s, kms = mst.next()
            c.op("act", lambda: nc.scalar.copy(out=m_s[:], in_=t_t[:]), reads=[kt_], writes=[kms])
            c.dma("sp", mt_v[:, 4:12, ci * 128:(ci + 1) * 128], m_s[:], reads=[kms], writes=[("mt_d", "ssd", ci)])

        hf_t, khf = Hfb.next()
        c.op("pool", lambda: nc.gpsimd.memset(hf_t[:], 0.0), writes=[khf])
        Ta = stage_a(0)
        Tp1 = Tp2 = None
        for ci in range(NT):
            Tn = stage_a(ci + 1) if ci + 1 < NT else None
            hf_t, khf = stage_b(Ta, hf_t, khf)
            if Tp1 is not None:
                stage_b2(Tp1)
            if Tp2 is not None:
                stage_b3(Tp2)
            Tp2 = Tp1
            Tp1 = Ta
            Ta = Tn
        stage_b2(Tp1)
        stage_b3(Tp2)
        stage_b3(Tp1)
        c.barrier()


def _rope_tables():
    t = np.arange(S)
    row = (t // 64).astype(np.float32)
    col = (t % 64).astype(np.float32)
    freqs = (np.float32(10000.0) ** (-np.arange(0, 32, 2, dtype=np.float32) / np.float32(32))).astype(np.float32)
    ang = np.concatenate([row[:, None] * freqs, col[:, None] * freqs], -1).astype(np.float32)
    return np.cos(ang).astype(np.float32), np.sin(ang).astype(np.float32)


def _na_index():
    dyi = np.zeros((NTYPES, 128, 128), np.int64)
    dxi = np.zeros((NTYPES, 128, 128), np.int64)
    msk = np.zeros((NTYPES, 128, 128), np.float32)
    rep = {0: 0, 1: 1, 2: 5, 3: 14, 4: 15}
    kk = np.arange(128)
    kr, ck = kk // 64, kk % 64
    for n, (cls, j) in enumerate(NA_TYPES):
        i = rep[cls]
        _, kp0, _ = na_cls(i)
        r = 2 * i + kr[None, :]
        cq = ck[None, :]
        rk = 2 * (kp0 + j) + kr[:, None]
        ckk = ck[:, None]
        rs = np.clip(r - 4, 0, 24)
        vrow = (rk >= rs) & (rk < rs + 8)
        cs = np.clip(cq - 8, 0, 48)
        vcol = (ckk >= cs) & (ckk < cs + 16)
        dyi[n] = np.clip(rk - r + 7, 0, 14)
        dxi[n] = np.clip(ckk - cq, -15, 15) + 15
        msk[n] = np.where(vrow & vcol, 0.0, NEG)
    return dyi, dxi, msk


_PROGRAM = None


def _inputs_spec():
    return [("x", [S, D], F32), ("even_mix_norm", [D], F32), ("even_w_in", [D, 4640], F32), ("na_q_norm", [64], F32),
            ("na_k_norm", [64], F32), ("biasg", [NTYPES, 128, 8, 128], F32), ("namask", [NTYPES, 128, 128], F32),
            ("ssd_conv_w", [4, 2048], F32), ("ssd_conv_b", [2048], F32), ("ssd_dt_bias", [32], F32), ("ssd_A_log", [32], F32),
            ("ssd_D", [16], F32), ("ssd_out_norm", [1024], F32), ("even_w_out", [1536, D], F32),
            ("odd_mix_norm", [D], F32), ("odd_w_qkv", [D, 1536], F32), ("gqa_q_norm", [64], F32), ("gqa_k_norm", [64], F32),
            ("odd_w_out", [D, D], F32), ("ffn_norm0", [D], F32), ("ffn_norm1", [D], F32),
            ("ffn_w13_0", [D, 2 * FH], F32), ("ffn_w13_1", [D, 2 * FH], F32), ("ffn_w2_0", [FH, D], F32), ("ffn_w2_1", [FH, D], F32),
            ("cos", [S, 32], F32), ("sin", [S, 32], F32), ("tri_u", [128, 128], F32), ("tri_l", [128, 128], F32),
            ("negm_f", [128, 128], F32), ("negm_b", [128, 128], F32), ("ident", [128, 128], BF16)]


def build_program(phases=("even", "ffn0", "odd", "ffn1")):
    nc = bass.Bass("TRN2", target_bir_lowering=False)
    A = {}
    for n, sh, dt in _inputs_spec():
        A[n] = nc.dram_tensor(n, sh, dt, kind="ExternalInput").ap()
    out = nc.dram_tensor("out", [S, D], F32, kind="ExternalOutput").ap()
    x1 = nc.dram_tensor("x1_scr", [S, D], F32, kind="Internal").ap()
    x2 = nc.dram_tensor("x2_scr", [S, D], F32, kind="Internal").ap()
    x3 = nc.dram_tensor("x3_scr", [S, D], F32, kind="Internal").ap()
    zs = nc.dram_tensor("zs_scr", [S, 1024], BF16, kind="Internal").ap()
    hb = nc.dram_tensor("hb_scr", [NT, 128, 1024], BF16, kind="Internal").ap()
    mt = nc.dram_tensor("mt_scr", [12, 128, S], BF16, kind="Internal").ap()
    chain = [A["x"], x1, x2, x3, out]
    order = ["even", "ffn0", "odd", "ffn1"]
    active = [p for p in order if p in phases]
    cur = A["x"]
    with ExitStack() as es:
        c = Ctx(nc, es)
        ident = es.enter_context(nc.sbuf_tensor("ident_sb", [128, 128], BF16))
        ones32 = es.enter_context(nc.sbuf_tensor("ones32", [128, 128], F32))
        c.dma("sp", ident[:], A["ident"], writes=["ident"])
        c.op("pool", lambda: nc.gpsimd.memset(ones32[:], 1.0), writes=["ones32"])
        c.barrier()
        for n, ph in enumerate(active):
            dst = out if n == len(active) - 1 else chain[order.index(ph) + 1]
            if ph == "even":
                P = {"mix_norm": A["even_mix_norm"], "w_in": A["even_w_in"], "na_gq": A["na_q_norm"], "na_gk": A["na_k_norm"],
                     "biasg": A["biasg"], "namask": A["namask"], "conv_w": A["ssd_conv_w"], "conv_b": A["ssd_conv_b"],
                     "dt_bias": A["ssd_dt_bias"], "A_log": A["ssd_A_log"], "ssd_D": A["ssd_D"], "out_norm": A["ssd_out_norm"],
                     "w_out": A["even_w_out"], "tri_u": A["tri_u"], "tri_l": A["tri_l"], "negm_f": A["negm_f"], "negm_b": A["negm_b"],
                     "zs": zs, "hb_scr": hb, "mt_scr": mt}
                emit_even(c, cur, dst, P, ident, ones32, "e0")
            elif ph == "ffn0":
                emit_ffn(c, cur, dst, A["ffn_norm0"], A["ffn_w13_0"], A["ffn_w2_0"], ident, "f0")
            elif ph == "odd":
                emit_odd(c, cur, dst, A["odd_mix_norm"], A["odd_w_qkv"], A["gqa_q_norm"], A["gqa_k_norm"], A["odd_w_out"],
                         A["cos"], A["sin"], ident, ones32, "o0")
            elif ph == "ffn1":
                emit_ffn(c, cur, dst, A["ffn_norm1"], A["ffn_w13_1"], A["ffn_w2_1"], ident, "f1")
            cur = dst
        c.finish("sp")
    return nc


def make_in_maps(inputs, xs):
    import ml_dtypes
    f = lambda a: np.ascontiguousarray(np.asarray(a, dtype=np.float32))
    cos, sin = _rope_tables()
    dyi, dxi, msk = _na_index()
    rpb = f(inputs["na_rel_bias"])[0]
    biasg = np.ascontiguousarray(rpb[:, dyi, dxi].transpose(1, 2, 0, 3))
    tri_u = np.triu(np.ones((128, 128), np.float32))
    kk = np.arange(128)
    negm_f = np.where(kk[None, :] >= kk[:, None], 0.0, NEG).astype(np.float32)
    negm_b = np.where(kk[None, :] <= kk[:, None], 0.0, NEG).astype(np.float32)
    shared = {
        "even_mix_norm": f(inputs["even_mix_norm"])[0], "even_w_in": f(inputs["even_w_in"])[0],
        "na_q_norm": f(inputs["na_q_norm"])[0], "na_k_norm": f(inputs["na_k_norm"])[0], "biasg": biasg, "namask": msk,
        "ssd_conv_w": f(inputs["ssd_conv_w"])[0], "ssd_conv_b": f(inputs["ssd_conv_b"])[0],
        "ssd_dt_bias": f(inputs["ssd_dt_bias"])[0].reshape(32), "ssd_A_log": f(inputs["ssd_A_log"])[0].reshape(32),
        "ssd_D": f(inputs["ssd_D"])[0], "ssd_out_norm": f(inputs["ssd_out_norm"])[0], "even_w_out": f(inputs["even_w_out"])[0],
        "odd_mix_norm": f(inputs["odd_mix_norm"])[0], "odd_w_qkv": f(inputs["odd_w_qkv"])[0],
        "gqa_q_norm": f(inputs["gqa_q_norm"])[0], "gqa_k_norm": f(inputs["gqa_k_norm"])[0], "odd_w_out": f(inputs["odd_w_out"])[0],
        "ffn_norm0": f(inputs["ffn_norm"])[0], "ffn_norm1": f(inputs["ffn_norm"])[1],
        "ffn_w13_0": f(inputs["ffn_w13"])[0], "ffn_w13_1": f(inputs["ffn_w13"])[1],
        "ffn_w2_0": f(inputs["ffn_w2"])[0], "ffn_w2_1": f(inputs["ffn_w2"])[1],
        "cos": cos, "sin": sin, "tri_u": tri_u, "tri_l": np.ascontiguousarray(tri_u.T),
        "negm_f": negm_f, "negm_b": negm_b, "ident": np.eye(128).astype(ml_dtypes.bfloat16),
    }
    return [dict(shared, x=np.ascontiguousarray(xb)) for xb in xs]


def kernel(**inputs):
    global _PROGRAM
    x = np.asarray(inputs["x"], dtype=np.float32)
    B = x.shape[0]
    if _PROGRAM is None:
        _PROGRAM = build_program()
    in_maps = make_in_maps(inputs, [x[b] for b in range(B)])
    res = run_bass_kernel_spmd(_PROGRAM, in_maps, core_ids=list(range(B)))
    return np.stack([np.asarray(r["out"], dtype=np.float32) for r in res.results], axis=0)
```
